# Optimizing a Trainium2 kernel written in Bass

```python
import math
import jax
import jax.numpy as jnp
from jax import lax
import numpy as np


D_MODEL = 1024
BATCH = 8
SEQ = 8192
DEPTH = 4

CTX_LEN = 256
GRID_W = 64
HEAD_DIM = 64
ROPE_PAIRS = HEAD_DIM // 4
ROPE_BASE = 10000.0
Q_BLOCK = 128
NORM_EPS = 1e-6

DIFF_HEADS = 4
DIFF_V_DIM = 2 * HEAD_DIM
GQA_HEADS = 8
GQA_KV_HEADS = 2
GQA_GROUP = GQA_HEADS // GQA_KV_HEADS

DIFF_Q_COLS = DIFF_HEADS * 2 * HEAD_DIM
DIFF_K_COLS = DIFF_HEADS * 2 * HEAD_DIM
DIFF_V_COLS = DIFF_HEADS * DIFF_V_DIM
GQA_Q_COLS = GQA_HEADS * HEAD_DIM
GQA_KV_COLS = GQA_KV_HEADS * HEAD_DIM
ATTN_IN_COLS = DIFF_Q_COLS + DIFF_K_COLS + DIFF_V_COLS + GQA_Q_COLS + 2 * GQA_KV_COLS
ATTN_OUT_COLS = DIFF_V_COLS + GQA_Q_COLS
ATTN_SPLITS = (DIFF_Q_COLS,
               DIFF_Q_COLS + DIFF_K_COLS,
               DIFF_Q_COLS + DIFF_K_COLS + DIFF_V_COLS,
               DIFF_Q_COLS + DIFF_K_COLS + DIFF_V_COLS + GQA_Q_COLS,
               DIFF_Q_COLS + DIFF_K_COLS + DIFF_V_COLS + GQA_Q_COLS + GQA_KV_COLS)

SSM_D_INNER = 2 * D_MODEL
SSM_HEAD_DIM = 64
SSM_HEADS = SSM_D_INNER // SSM_HEAD_DIM
SSM_GROUPS = 4
SSM_HEADS_PER_GROUP = SSM_HEADS // SSM_GROUPS
SSM_STATE = 128
SSM_CONV = 3
SSM_CHUNK = 128
SSM_XBC = SSM_D_INNER + 2 * SSM_GROUPS * SSM_STATE
SSM_IN_COLS = SSM_D_INNER + SSM_XBC + 2 * SSM_HEADS

D_FF = 2816
FFN_CONV = 3

N_ATTN_LAYERS = (DEPTH + 1) // 2
N_SSM_LAYERS = DEPTH // 2

kernel_name = 'hybrid_diffattn_gqa_ssd_convglu_trunk'


def rms_norm(x, g):
    x32 = x.astype(jnp.float32)
    y = x32 * lax.rsqrt(jnp.mean(x32 * x32, axis=-1, keepdims=True) + NORM_EPS)
    return (y * g.astype(jnp.float32)).astype(x.dtype)


def modulate(x, g, shift, scale):
    return rms_norm(x, g) * (1.0 + scale) + shift


def dw_conv_centred(x, w, b):
    k = w.shape[0]
    pad = k // 2
    t = x.shape[1]
    xp = jnp.pad(x, ((0, 0), (pad, pad), (0, 0)))
    out = b
    for i in range(k):
        out = out + xp[:, i:i + t] * w[i]
    return out


def axial_rope_tables(n):
    rows = n // GRID_W
    row = jnp.broadcast_to(jnp.arange(rows)[:, None], (rows, GRID_W)).reshape(n)
    col = jnp.broadcast_to(jnp.arange(GRID_W)[None, :], (rows, GRID_W)).reshape(n)
    inv_freq = 1.0 / (ROPE_BASE ** (jnp.arange(ROPE_PAIRS, dtype=jnp.float32) / ROPE_PAIRS))
    ang = jnp.concatenate([row.astype(jnp.float32)[:, None] * inv_freq,
                           col.astype(jnp.float32)[:, None] * inv_freq], axis=-1)
    return jnp.cos(ang)[:, None, :], jnp.sin(ang)[:, None, :]


def apply_axial_rope(t, cos, sin):
    cos = cos.astype(t.dtype)
    sin = sin.astype(t.dtype)

    def rot(v, cs, sn):
        v1, v2 = v[..., :ROPE_PAIRS], v[..., ROPE_PAIRS:]
        return jnp.concatenate([v1 * cs - v2 * sn, v2 * cs + v1 * sn], axis=-1)

    half = 2 * ROPE_PAIRS
    return jnp.concatenate([rot(t[..., :half], cos[..., :ROPE_PAIRS], sin[..., :ROPE_PAIRS]),
                            rot(t[..., half:], cos[..., ROPE_PAIRS:], sin[..., ROPE_PAIRS:])], axis=-1)


def _to_blocks(t):
    b, n = t.shape[:2]
    return jnp.moveaxis(t.reshape((b, n // Q_BLOCK, Q_BLOCK) + t.shape[2:]), 1, 0)


def _from_blocks(t):
    nb, b, q = t.shape[:3]
    return jnp.moveaxis(t, 0, 1).reshape((b, nb * q) + t.shape[3:])


def attention_sweep(qa1, qa2, qb, ka1, ka2, va, kb, vb, lam):
    scale = HEAD_DIM ** -0.5

    def block(qs):
        q1, q2, qg = qs
        s1 = jnp.einsum('bqhd,bkhd->bhqk', q1, ka1).astype(jnp.float32) * scale
        s2 = jnp.einsum('bqhd,bkhd->bhqk', q2, ka2).astype(jnp.float32) * scale
        p = jax.nn.softmax(s1, axis=-1) - lam * jax.nn.softmax(s2, axis=-1)
        oa = jnp.einsum('bhqk,bkhe->bqhe', p.astype(va.dtype), va)
        b, nq = qg.shape[:2]
        qg = qg.reshape(b, nq, GQA_KV_HEADS, GQA_GROUP, HEAD_DIM)
        sg = jnp.einsum('bqgrd,bkgd->bgrqk', qg, kb).astype(jnp.float32) * scale
        pg = jax.nn.softmax(sg, axis=-1)
        ob = jnp.einsum('bgrqk,bkgd->bqgrd', pg.astype(vb.dtype), vb).reshape(b, nq, GQA_HEADS, HEAD_DIM)
        return oa, ob

    oa, ob = lax.map(block, (_to_blocks(qa1), _to_blocks(qa2), _to_blocks(qb)))
    return _from_blocks(oa), _from_blocks(ob)


def attn_qkv(h, w_in, q_norm_g, k_norm_g, rope):
    b, n = h.shape[:2]
    qa, ka, va, qb, kb, vb = jnp.split(h @ w_in, ATTN_SPLITS, axis=-1)
    qa = qa.reshape(b, n, DIFF_HEADS, 2, HEAD_DIM)
    ka = ka.reshape(b, n, DIFF_HEADS, 2, HEAD_DIM)
    va = va.reshape(b, n, DIFF_HEADS, DIFF_V_DIM)
    qb = rms_norm(qb.reshape(b, n, GQA_HEADS, HEAD_DIM), q_norm_g)
    kb = rms_norm(kb.reshape(b, n, GQA_KV_HEADS, HEAD_DIM), k_norm_g)
    vb = vb.reshape(b, n, GQA_KV_HEADS, HEAD_DIM)
    qa1, qa2, ka1, ka2 = qa[..., 0, :], qa[..., 1, :], ka[..., 0, :], ka[..., 1, :]
    if rope is not None:
        cos, sin = rope
        qa1, qa2, ka1, ka2, qb, kb = [apply_axial_rope(t, cos, sin) for t in (qa1, qa2, ka1, ka2, qb, kb)]
    return qa1, qa2, ka1, ka2, va, qb, kb, vb


def hybrid_attention_mixer(h, hc, w_in, w_out, lq1, lk1, lq2, lk2, subln_g, q_norm_g, k_norm_g,
                           rope, lambda_init, with_ctx):
    f32 = jnp.float32
    lam = (jnp.exp(jnp.sum(lq1.astype(f32) * lk1.astype(f32)))
           - jnp.exp(jnp.sum(lq2.astype(f32) * lk2.astype(f32))) + lambda_init)
    qa1, qa2, ka1, ka2, va, qb, kb, vb = attn_qkv(h, w_in, q_norm_g, k_norm_g, rope)
    cqa1, cqa2, cka1, cka2, cva, cqb, ckb, cvb = attn_qkv(hc, w_in, q_norm_g, k_norm_g, None)

    def cat(ctx_t, lat_t):
        return jnp.concatenate([ctx_t, lat_t], axis=1)

    def project(oa, ob):
        b, n = oa.shape[:2]
        oa = rms_norm(oa, subln_g) * (1.0 - lambda_init)
        o = jnp.concatenate([oa.reshape(b, n, DIFF_V_COLS), ob.reshape(b, n, GQA_Q_COLS)], axis=-1)
        return o @ w_out

    y = project(*attention_sweep(qa1, qa2, qb, cat(cka1, ka1), cat(cka2, ka2), cat(cva, va),
                                 cat(ckb, kb), cat(cvb, vb), lam))
    yc = project(*attention_sweep(cqa1, cqa2, cqb, cka1, cka2, cva, ckb, cvb, lam)) if with_ctx else None
    return y, yc


def ssd_scan(x, dt, bm, cm, a, state):
    b, n = x.shape[:2]
    nc = n // SSM_CHUNK
    f32 = jnp.float32

    def chunks(t):
        return jnp.moveaxis(t.astype(f32).reshape((b, nc, SSM_CHUNK) + t.shape[2:]), 1, 0)

    xs = chunks(x.reshape(b, n, SSM_GROUPS, SSM_HEADS_PER_GROUP, SSM_HEAD_DIM))
    dts = chunks(dt.reshape(b, n, SSM_GROUPS, SSM_HEADS_PER_GROUP))
    bs = chunks(bm)
    cs = chunks(cm)
    a = a.astype(f32).reshape(SSM_GROUPS, SSM_HEADS_PER_GROUP)
    lower = jnp.tril(jnp.ones((SSM_CHUNK, SSM_CHUNK), dtype=bool))[None, :, :, None, None]

    def step(s, inp):
        xc, dtc, bc, cc = inp
        cum = jnp.cumsum(dtc * a, axis=1)
        seg = cum[:, :, None] - cum[:, None, :]
        decay = jnp.exp(jnp.where(lower, seg, -jnp.inf))
        cb = jnp.einsum('bign,bjgn->bijg', cc, bc)
        w = cb[..., None] * decay * dtc[:, None]
        y = jnp.einsum('bijgr,bjgrp->bigrp', w, xc)
        y = y + jnp.einsum('bign,bgrpn->bigrp', cc, s) * jnp.exp(cum)[..., None]
        to_end = jnp.exp(cum[:, -1:] - cum) * dtc
        s = s * jnp.exp(cum[:, -1])[..., None, None] + jnp.einsum('bjgr,bjgn,bjgrp->bgrpn', to_end, bc, xc)
        return s, y

    state, ys = lax.scan(step, state.astype(f32), (xs, dts, bs, cs))
    y = jnp.moveaxis(ys, 0, 1).reshape(b, n, SSM_HEADS, SSM_HEAD_DIM)
    return y.astype(x.dtype), state


def ssm_pre(h, w_in, conv_w, conv_b):
    b, n = h.shape[:2]
    z, xbc, dt = jnp.split(h @ w_in, (SSM_D_INNER, SSM_D_INNER + SSM_XBC), axis=-1)
    xbc = jax.nn.silu(dw_conv_centred(xbc, conv_w, conv_b))
    xs, bm, cm = jnp.split(xbc, (SSM_D_INNER, SSM_D_INNER + SSM_GROUPS * SSM_STATE), axis=-1)
    return (z,
            xs.reshape(b, n, SSM_HEADS, SSM_HEAD_DIM),
            bm.reshape(b, n, SSM_GROUPS, SSM_STATE),
            cm.reshape(b, n, SSM_GROUPS, SSM_STATE),
            dt.reshape(b, n, 2, SSM_HEADS))


def bidir_ssd_mixer(h, hc, w_in, conv_w, conv_b, dt_bias, a_log, d_skip, norm_g, w_out, with_ctx):
    f32 = jnp.float32
    lat = ssm_pre(h, w_in, conv_w, conv_b)
    ctx_p = ssm_pre(hc, w_in, conv_w, conv_b)
    a = -jnp.exp(a_log.astype(f32))
    zero = jnp.zeros((h.shape[0], SSM_GROUPS, SSM_HEADS_PER_GROUP, SSM_HEAD_DIM, SSM_STATE), f32)

    def flip(t):
        return t[:, ::-1]

    def direction(d):
        rev = d == 1

        def prep(parts):
            _, xs, bm, cm, dt = parts
            dt_d = jax.nn.softplus((dt[:, :, d] + dt_bias[d]).astype(f32))
            seqs = (xs, dt_d, bm, cm)
            return tuple(flip(t) for t in seqs) if rev else seqs

        yc, s_ctx = ssd_scan(*prep(ctx_p), a[d], zero)
        yl, _ = ssd_scan(*prep(lat), a[d], s_ctx)
        if rev:
            yc, yl = flip(yc), flip(yl)
        return yl, yc

    yl_f, yc_f = direction(0)
    yl_b, yc_b = direction(1)

    def finish(parts, y):
        z, xs = parts[0], parts[1]
        b, n = xs.shape[:2]
        y = (y + d_skip[:, None] * xs).reshape(b, n, SSM_D_INNER)
        return rms_norm(y * jax.nn.silu(z), norm_g) @ w_out

    y = finish(lat, yl_f + yl_b)
    yc = finish(ctx_p, yc_f + yc_b) if with_ctx else None
    return y, yc


def conv_glu_ffn(h, w_in, conv_w, conv_b, w_out):
    val, gate = jnp.split(h @ w_in, 2, axis=-1)
    gate = dw_conv_centred(gate, conv_w, conv_b)
    return (jax.nn.gelu(gate, approximate=False) * val) @ w_out


def setup_inputs(seed: int = 0) -> dict:
    key = jax.random.key(seed)
    ks = iter(jax.random.split(key, 40))
    f32 = jnp.float32

    def nrm(shape, scale):
        return jax.random.normal(next(ks), shape, f32) * scale

    def gain(shape):
        return 1.0 + nrm(shape, 0.02)

    na, ns = N_ATTN_LAYERS, N_SSM_LAYERS
    x = nrm((BATCH, SEQ, D_MODEL), 1.0)
    c = nrm((BATCH, D_MODEL), 1.0)
    ctx = nrm((BATCH, CTX_LEN, D_MODEL), 1.0)
    c_ctx = nrm((D_MODEL,), 1.0)
    mod_w = nrm((DEPTH, D_MODEL, 6 * D_MODEL), 0.5 * D_MODEL ** -0.5)
    mod_b = nrm((DEPTH, 6 * D_MODEL), 0.02)
    norm_mix_g = gain((DEPTH, D_MODEL))
    norm_ffn_g = gain((DEPTH, D_MODEL))
    attn_w_in = nrm((na, D_MODEL, ATTN_IN_COLS), D_MODEL ** -0.5)
    attn_w_out = nrm((na, ATTN_OUT_COLS, D_MODEL), ATTN_OUT_COLS ** -0.5)
    diff_lq1 = nrm((na, HEAD_DIM), 0.1)
    diff_lk1 = nrm((na, HEAD_DIM), 0.1)
    diff_lq2 = nrm((na, HEAD_DIM), 0.1)
    diff_lk2 = nrm((na, HEAD_DIM), 0.1)
    diff_subln_g = gain((na, DIFF_V_DIM))
    gqa_q_norm_g = gain((na, HEAD_DIM))
    gqa_k_norm_g = gain((na, HEAD_DIM))
    ssm_w_in = nrm((ns, D_MODEL, SSM_IN_COLS), D_MODEL ** -0.5)
    ssm_conv_w = nrm((ns, SSM_CONV, SSM_XBC), SSM_CONV ** -0.5)
    ssm_conv_b = nrm((ns, SSM_XBC), 0.02)
    u = jax.random.uniform(next(ks), (ns, 2, SSM_HEADS), f32)
    dt0 = jnp.exp(u * (math.log(0.1) - math.log(0.001)) + math.log(0.001))
    ssm_dt_bias = dt0 + jnp.log(-jnp.expm1(-dt0))
    ssm_a_log = jnp.log(jax.random.uniform(next(ks), (ns, 2, SSM_HEADS), f32, minval=1.0, maxval=16.0))
    ssm_d = 1.0 + nrm((ns, SSM_HEADS), 0.1)
    ssm_norm_g = gain((ns, SSM_D_INNER))
    ssm_w_out = nrm((ns, SSM_D_INNER, D_MODEL), SSM_D_INNER ** -0.5)
    ffn_w_in = nrm((DEPTH, D_MODEL, 2 * D_FF), D_MODEL ** -0.5)
    ffn_conv_w = nrm((DEPTH, FFN_CONV, D_FF), FFN_CONV ** -0.5)
    ffn_conv_b = nrm((DEPTH, D_FF), 0.02)
    ffn_w_out = nrm((DEPTH, D_FF, D_MODEL), D_FF ** -0.5)
    final_norm_g = gain((D_MODEL,))
    return {'x': x, 'c': c, 'ctx': ctx, 'c_ctx': c_ctx,
            'mod_w': mod_w, 'mod_b': mod_b, 'norm_mix_g': norm_mix_g, 'norm_ffn_g': norm_ffn_g,
            'attn_w_in': attn_w_in, 'attn_w_out': attn_w_out,
            'diff_lq1': diff_lq1, 'diff_lk1': diff_lk1, 'diff_lq2': diff_lq2, 'diff_lk2': diff_lk2,
            'diff_subln_g': diff_subln_g, 'gqa_q_norm_g': gqa_q_norm_g, 'gqa_k_norm_g': gqa_k_norm_g,
            'ssm_w_in': ssm_w_in, 'ssm_conv_w': ssm_conv_w, 'ssm_conv_b': ssm_conv_b,
            'ssm_dt_bias': ssm_dt_bias, 'ssm_a_log': ssm_a_log, 'ssm_d': ssm_d,
            'ssm_norm_g': ssm_norm_g, 'ssm_w_out': ssm_w_out,
            'ffn_w_in': ffn_w_in, 'ffn_conv_w': ffn_conv_w, 'ffn_conv_b': ffn_conv_b, 'ffn_w_out': ffn_w_out,
            'final_norm_g': final_norm_g}


def reference(x, c, ctx, c_ctx, mod_w, mod_b, norm_mix_g, norm_ffn_g, attn_w_in, attn_w_out,
              diff_lq1, diff_lk1, diff_lq2, diff_lk2, diff_subln_g, gqa_q_norm_g, gqa_k_norm_g,
              ssm_w_in, ssm_conv_w, ssm_conv_b, ssm_dt_bias, ssm_a_log, ssm_d, ssm_norm_g, ssm_w_out,
              ffn_w_in, ffn_conv_w, ffn_conv_b, ffn_w_out, final_norm_g):
    rope = axial_rope_tables(x.shape[1])
    for layer in range(DEPTH):
        with_ctx = layer < DEPTH - 1
        mod = jax.nn.silu(c) @ mod_w[layer] + mod_b[layer]
        mod_c = jax.nn.silu(c_ctx) @ mod_w[layer] + mod_b[layer]
        sh1, sc1, g1, sh2, sc2, g2 = jnp.split(mod[:, None, :], 6, axis=-1)
        sh1c, sc1c, g1c, sh2c, sc2c, g2c = jnp.split(mod_c[None, None, :], 6, axis=-1)
        h = modulate(x, norm_mix_g[layer], sh1, sc1)
        hc = modulate(ctx, norm_mix_g[layer], sh1c, sc1c)
        i = layer // 2
        if layer % 2 == 0:
            lambda_init = 0.8 - 0.6 * math.exp(-0.3 * layer)
            y, yc = hybrid_attention_mixer(h, hc, attn_w_in[i], attn_w_out[i], diff_lq1[i], diff_lk1[i],
                                           diff_lq2[i], diff_lk2[i], diff_subln_g[i], gqa_q_norm_g[i],
                                           gqa_k_norm_g[i], rope, lambda_init, with_ctx)
        else:
            y, yc = bidir_ssd_mixer(h, hc, ssm_w_in[i], ssm_conv_w[i], ssm_conv_b[i], ssm_dt_bias[i],
                                    ssm_a_log[i], ssm_d[i], ssm_norm_g[i], ssm_w_out[i], with_ctx)
        x = x + g1 * y
        h = modulate(x, norm_ffn_g[layer], sh2, sc2)
        x = x + g2 * conv_glu_ffn(h, ffn_w_in[layer], ffn_conv_w[layer], ffn_conv_b[layer], ffn_w_out[layer])
        if with_ctx:
            ctx = ctx + g1c * yc
            hc = modulate(ctx, norm_ffn_g[layer], sh2c, sc2c)
            ctx = ctx + g2c * conv_glu_ffn(hc, ffn_w_in[layer], ffn_conv_w[layer], ffn_conv_b[layer],
                                           ffn_w_out[layer])
    return rms_norm(x, final_norm_g)
```

```python
import math
from contextlib import ExitStack

import numpy as np
import concourse.bass as bass
import concourse.mybir as mybir
from concourse.bass_utils import run_bass_kernel_spmd

F32 = mybir.dt.float32
BF16 = mybir.dt.bfloat16
AF = mybir.ActivationFunctionType
ALU = mybir.AluOpType


class Sem:
    def __init__(self, h, name):
        self.h = h
        self.cnt = 0
        self.name = name


class Buf:
    def __init__(self, ap=None, name=""):
        self.ap = ap
        self.name = name
        self.last_w = {}
        self.readers = {}
        self.gen_deps = {}
        self.dsem = None


def _merge(dst, src):
    for s, v in src.items():
        if dst.get(s, 0) < v:
            dst[s] = v


class FW:
    def __init__(self, nc, n_dma_sems=48, n_spare=28):
        self.nc = nc
        self.es = ExitStack()
        self.streams = {
            "pe": nc.tensor,
            "act": nc.scalar,
            "dve": nc.vector,
            "pool": nc.gpsimd,
            "sp": nc.sync,
        }
        self.esem = {}
        for k in ("pe", "act", "dve", "pool"):
            self.esem[k] = Sem(self.es.enter_context(nc.semaphore("e_" + k)), k)
        self.known = {k: {} for k in self.streams}
        self.free_dsems = [
            Sem(self.es.enter_context(nc.semaphore("d%d" % i)), "d%d" % i) for i in range(n_dma_sems)
        ]
        self.spare = [Sem(self.es.enter_context(nc.semaphore("x%d" % i)), "x%d" % i) for i in range(n_spare)]
        self.phase_bufs = []
        self.pers_bufs = []
        self.n_inst = 0
        self.n_wait = 0

    def buf(self, ap=None, name="", persistent=False):
        b = Buf(ap, name)
        (self.pers_bufs if persistent else self.phase_bufs).append(b)
        return b

    def _wait_for(self, stream, deps, fold=False):
        eng = self.streams[stream]
        kn = self.known[stream]
        need = [(s, v) for s, v in deps.items() if kn.get(s, 0) < v]
        last = None
        if fold and need:
            last = need.pop()
            kn[last[0]] = last[1]
        for s, v in need:
            eng.wait_ge(s.h, v)
            kn[s] = v
            self.n_wait += 1
        return last

    def _deps(self, stream, reads, writes, waw=True):
        deps = {}
        for r in reads:
            _merge(deps, r.last_w)
        for w in writes:
            if waw or w.readers:
                _merge(deps, w.last_w)
                _merge(deps, w.readers)
            else:
                _merge(deps, w.gen_deps)
        if stream == "pe":
            deps.pop(self.esem["pe"], None)
        return deps

    def _commit(self, t, reads, writes, waw):
        for w in writes:
            if waw or w.readers:
                g = dict(w.last_w)
                _merge(g, w.readers)
                w.gen_deps = g
                w.last_w = dict(t)
                w.readers = {}
            else:
                _merge(w.last_w, t)
        for r in reads:
            _merge(r.readers, t)

    def op(self, stream, fn, reads=(), writes=(), ticket=True, waw=True):
        deps = self._deps(stream, reads, writes, waw)
        last = self._wait_for(stream, deps, fold=True)
        inst = fn(self.streams[stream])
        if last is not None:
            inst._wait_ge(last[0].h, last[1])
        self.n_inst += 1
        if ticket:
            s = self.esem[stream]
            if s.cnt >= 30000:
                s = self.spare.pop()
                self.esem[stream] = s
            s.cnt += 1
            inst.then_inc(s.h, 1)
            self._commit({s: s.cnt}, reads, writes, waw)
        return inst

    def dma(self, out_ap, in_ap, owner, reads=(), writes=(), stream="sp", waw=True, **kw):
        deps = self._deps(stream, reads, writes, waw)
        last = self._wait_for(stream, deps, fold=True)
        if owner.dsem is None:
            owner.dsem = self.free_dsems.pop(0)
        s = owner.dsem
        s.cnt += 16
        inst = self.streams[stream].dma_start(out=out_ap, in_=in_ap, **kw)
        if last is not None:
            inst._wait_ge(last[0].h, last[1])
        inst.then_inc(s.h, 16)
        self.n_inst += 1
        self._commit({s: s.cnt}, reads, writes, waw)

    def barrier(self, clear=True):
        alld = {}
        for s in self.esem.values():
            if s.cnt:
                alld[s] = s.cnt
        for b in self.phase_bufs + self.pers_bufs:
            if b.dsem is not None:
                alld[b.dsem] = b.dsem.cnt
        for st in self.streams:
            d = dict(alld)
            self._wait_for(st, d)
        for b in self.phase_bufs + self.pers_bufs:
            if b.dsem is not None:
                self.free_dsems.append(b.dsem)
                b.dsem = None
            b.last_w = {}
            b.readers = {}
            b.gen_deps = {}
        if clear:
            self.phase_bufs = []

    def close(self):
        self.es.close()


D = 1024
KD = D // 128
DEPTH = 4
CTX = 256
GRID_W = 64
HD = 64
EPS = 1e-6
DFF = 2816
NFF = DFF // 128
SSM_DI = 2048
SSM_XBC = 3072
SSM_IN = 5184
SSM_H = 32
NCORES = 8
ATT_FM = 1664
ATT_EXT = 2 * ATT_FM + 640


class Cfg:
    def __init__(self, T):
        self.T = T
        self.C = CTX
        self.CS = 1
        self.LS = CTX + 3
        self.NP = CTX + T + 4
        self.NTOK = CTX + T
        self.seqs = [("ctx", self.CS, CTX, 1, 0), ("lat", self.LS, T, 0, CTX)]


def _partner():
    p = np.arange(64)
    return np.where((p % 32) < 16, p + 16, p - 16)


class VecLayout:
    def __init__(self):
        self.off = {}
        self.n = 0
        self.cols = []

    def add(self, name, arr):
        arr = np.ascontiguousarray(arr, dtype=np.float32).reshape(128, -1)
        self.off[name] = (self.n, arr.shape[1])
        self.n += arr.shape[1]
        self.cols.append(arr)

    def build(self):
        return np.ascontiguousarray(np.concatenate(self.cols, axis=1))


def fm(v):
    v = np.asarray(v, dtype=np.float32)
    return np.ascontiguousarray(v.reshape(-1, 128).T)


def prep_shared(inp, T):
    cfg = Cfg(T)
    vl = VecLayout()
    rows = {}
    for l in range(DEPTH):
        vl.add("modb%d" % l, fm(inp["mod_b"][l]))
        vl.add("gmix%d" % l, fm(inp["norm_mix_g"][l]))
        vl.add("gffn%d" % l, fm(inp["norm_ffn_g"][l]))
        cw = inp["ffn_conv_w"][l]
        for i in range(3):
            vl.add("fcw%d_%d" % (l, i), fm(cw[i]))
        vl.add("fcb%d" % l, fm(inp["ffn_conv_b"][l]))
    vl.add("gfin", fm(inp["final_norm_g"]))
    pt = _partner()
    for i in range(DEPTH // 2 + DEPTH % 2):
        gq = np.asarray(inp["gqa_q_norm_g"][i], np.float32)
        gk = np.asarray(inp["gqa_k_norm_g"][i], np.float32)
        vl.add("gq%d" % i, np.concatenate([gq, gq])[:, None])
        vl.add("gqs%d" % i, np.concatenate([gq[pt], gq[pt]])[:, None])
        vl.add("gk%d" % i, np.concatenate([gk, gk])[:, None])
        vl.add("gks%d" % i, np.concatenate([gk[pt], gk[pt]])[:, None])
        vl.add("subg%d" % i, np.asarray(inp["diff_subln_g"][i], np.float32)[:, None])
    for i in range(DEPTH // 2):
        cw = inp["ssm_conv_w"][i]
        for j in range(3):
            vl.add("scw%d_%d" % (i, j), fm(cw[j]))
        vl.add("scb%d" % i, fm(inp["ssm_conv_b"][i]))
    vecs = vl.build()

    rl = VecLayout()

    def radd(name, v):
        v = np.asarray(v, np.float32).reshape(1, -1)
        rl.off[name] = (rl.n, v.shape[1])
        rl.n += v.shape[1]
        rl.cols.append(v)

    for i in range((DEPTH + 1) // 2):
        radd("lq1_%d" % i, inp["diff_lq1"][i])
        radd("lk1_%d" % i, inp["diff_lk1"][i])
        radd("lq2_%d" % i, inp["diff_lq2"][i])
        radd("lk2_%d" % i, inp["diff_lk2"][i])
    for i in range(DEPTH // 2):
        radd("dtb%d" % i, inp["ssm_dt_bias"][i])
        radd("alog%d" % i, inp["ssm_a_log"][i])
        radd("ssd%d" % i, inp["ssm_d"][i])
        radd("sng%d" % i, inp["ssm_norm_g"][i])
    rowv = np.ascontiguousarray(np.concatenate(rl.cols, axis=1))

    w_ext = []
    for i in range((DEPTH + 1) // 2):
        w = np.asarray(inp["attn_w_in"][i])
        qa, ka, va = w[:, 0:512], w[:, 512:1024], w[:, 1024:1536]
        qb, kb, vb = w[:, 1536:2048], w[:, 2048:2176], w[:, 2176:2304]
        qbp = np.concatenate(
            [np.concatenate([qb[:, j * 64:(j + 1) * 64], qb[:, (4 + j) * 64:(5 + j) * 64]], axis=1) for j in range(4)],
            axis=1)
        fmw = np.concatenate([qa, ka, qbp, kb], axis=1)
        idx = (np.arange(ATT_FM) // 64) * 64 + pt[np.arange(ATT_FM) % 64]
        w_ext.append(np.concatenate([fmw, fmw[:, idx], va, vb], axis=1))
    w_ext = np.ascontiguousarray(np.stack(w_ext))

    NP = cfg.NP
    cosT = np.ones((128, NP), np.float32)
    sinT = np.zeros((128, NP), np.float32)
    t = np.arange(T)
    row = (t // GRID_W).astype(np.float32)
    col = (t % GRID_W).astype(np.float32)
    inv = (1.0 / (10000.0 ** (np.arange(16, dtype=np.float32) / 16))).astype(np.float32)
    for p in range(64):
        q, i = p // 16, p % 16
        ang = ((row if q < 2 else col) * inv[i]).astype(np.float32)
        sgn = -1.0 if q in (0, 2) else 1.0
        for rep in (0, 64):
            cosT[p + rep, cfg.LS:cfg.LS + T] = np.cos(ang)
            sinT[p + rep, cfg.LS:cfg.LS + T] = sgn * np.sin(ang)

    tt = np.arange(128)
    consts = {
        "ident": np.eye(128, dtype=np.float32),
        "ML_f": (tt[:, None] > tt[None, :]).astype(np.float32),
        "MR_f": (tt[:, None] <= tt[None, :]).astype(np.float32),
        "ML_b": (tt[:, None] < tt[None, :]).astype(np.float32),
        "MR_b": (tt[:, None] >= tt[None, :]).astype(np.float32),
    }
    bd = np.zeros((128, 128), np.float32)
    bd[:64, :64] = 1
    bd[64:, 64:] = 1
    consts["bd"] = bd
    cmat = np.ascontiguousarray(
        np.concatenate([consts[k] for k in ("ident", "ML_f", "MR_f", "ML_b", "MR_b", "bd")], axis=1))

    shared = {
        "vecs": vecs, "rowv": rowv, "w_ext": w_ext, "cosT": cosT, "sinT": sinT, "cmat": cmat,
        "mod_w": np.asarray(inp["mod_w"], np.float32),
        "attn_w_out": np.asarray(inp["attn_w_out"], np.float32),
        "ssm_w_in": np.asarray(inp["ssm_w_in"], np.float32),
        "ssm_w_out": np.asarray(inp["ssm_w_out"], np.float32),
        "ffn_w_in": np.asarray(inp["ffn_w_in"], np.float32),
        "ffn_w_out": np.asarray(inp["ffn_w_out"], np.float32),
    }
    return cfg, vl.off, rl.off, shared


def prep_core(inp, b, T):
    xT = np.ascontiguousarray(np.asarray(inp["x"][b, :T]).T)
    cxT = np.ascontiguousarray(np.asarray(inp["ctx"][b]).T)
    cT = np.stack([fm(inp["c"][b]), fm(inp["c_ctx"])], axis=2)
    return {"xT": xT, "cxT": cxT, "cT": np.ascontiguousarray(cT.reshape(128, 16))}


class TB:
    def __init__(self, t, b, bs=None):
        self.t = t
        self.b = b
        self.bs = bs


def blocks_of(n, size):
    out = []
    t0 = 0
    while t0 < n:
        out.append((t0, min(size, n - t0)))
        t0 += size
    return out


class Prog:
    def __init__(self, T, voff, roff, nv, nr, n_layers=DEPTH, debug=False):
        self.cfg = Cfg(T)
        self.voff, self.roff = voff, roff
        self.debug = debug
        self.n_layers = n_layers
        nc = bass.Bass("TRN2", target_bir_lowering=False)
        self.nc = nc
        self.fw = FW(nc)
        cfg = self.cfg
        NP, NTOK = cfg.NP, cfg.NTOK
        di = lambda name, shape, dt=F32: nc.dram_tensor(name, shape, dt, kind="ExternalInput").ap()
        self.i_xT = di("xT", [D, T])
        self.i_cxT = di("cxT", [D, CTX])
        self.i_cT = di("cT", [128, 16])
        self.i_vecs = di("vecs", [128, nv])
        self.i_rowv = di("rowv", [1, nr])
        self.i_wext = di("w_ext", [(DEPTH + 1) // 2, D, ATT_EXT])
        self.i_cos = di("cosT", [128, NP])
        self.i_sin = di("sinT", [128, NP])
        self.i_cmat = di("cmat", [128, 768])
        self.i_modw = di("mod_w", [DEPTH, D, 6 * D])
        self.i_awo = di("attn_w_out", [(DEPTH + 1) // 2, D, D])
        self.i_swi = di("ssm_w_in", [DEPTH // 2, D, SSM_IN])
        self.i_swo = di("ssm_w_out", [DEPTH // 2, SSM_DI, D])
        self.i_fwi = di("ffn_w_in", [DEPTH, D, 2 * DFF])
        self.i_fwo = di("ffn_w_out", [DEPTH, DFF, D])
        self.o_out = nc.dram_tensor("outT", [D, T], F32, kind="ExternalOutput").ap()
        self.dbg = {}
        kind = "ExternalOutput" if debug else "Internal"

        def scr(name, shape, dt):
            ap = nc.dram_tensor(name, shape, dt, kind=kind).ap()
            if debug:
                self.dbg[name] = ap
            return ap

        self.XA = scr("XA", [D, NP], F32)
        self.XB = scr("XB", [D, NP], F32)
        self.HT = scr("HT", [D, NP], BF16)
        self.QT = scr("QT", [D, NP], BF16)
        self.KT = scr("KT", [5 * 128, NP], BF16)
        self.VT = scr("VT", [NTOK, 640], BF16)
        self.OT = scr("OT", [D, NP], BF16)
        self.UT = scr("UT", [DFF, NP], BF16)
        self.XBCT = scr("XBCT", [SSM_XBC, NP], BF16)
        self.ZT = scr("ZT", [NTOK, SSM_DI], F32)
        self.DTT = scr("DTT", [NTOK, 64], F32)
        self.YF = scr("YF", [NTOK, SSM_DI], F32)
        self.YGT = scr("YGT", [SSM_DI, NP], BF16)
        self.dram_b = {}
        self.pes = ExitStack()
        self.ph = None
        self._names = 0

    def _nm(self, name):
        self._names += 1
        return "%s_%d" % (name, self._names)

    def sb(self, name, shape, dt=F32, nb=0, pers=False):
        st = self.pes if pers else self.ph
        t = st.enter_context(self.nc.sbuf_tensor(self._nm(name), shape, dt))
        b = self.fw.buf(name=name, persistent=pers)
        bs = [self.fw.buf(name=name + str(i), persistent=pers) for i in range(nb)] if nb else None
        return TB(t, b, bs)

    def ps(self, name, shape, dt=F32):
        t = self.ph.enter_context(self.nc.psum_tensor(self._nm(name), shape, dt))
        return TB(t, self.fw.buf(name=name))

    def db(self, ap):
        k = ap.name if hasattr(ap, "name") else id(ap)
        if k not in self.dram_b:
            self.dram_b[k] = self.fw.buf(name="dram", persistent=True)
        return self.dram_b[k]

    def begin(self):
        self.ph = ExitStack()

    def end(self):
        self.fw.barrier()
        self.ph.close()
        self.ph = None

    def vec(self, name, j=0, w=1):
        o, n = self.voff[name]
        return self.vecs.t[:, o + j:o + j + w]

    def load_w(self, dst, src_rows, ncols, col0=0, dst_col0=0, kc=None):
        fw = self.fw
        kc = kc if kc is not None else dst.t.shape[1]
        PIECE = 1408
        main_ph = self.ph
        self.ph = ExitStack()
        wst = [self.sb("wst", [128, PIECE], F32) for _ in range(3)]
        wi = 0
        for k in range(kc):
            for (c0, n) in blocks_of(ncols, PIECE):
                st = wst[wi % 3]
                eng = ("dve", "pool", "act")[wi % 3]
                wi += 1
                src = src_rows(k)[:, col0 + c0:col0 + c0 + n]
                fw.dma(st.t[:, 0:n], src, st.b, writes=[st.b])
                d = dst.t[:, k, dst_col0 + c0:dst_col0 + c0 + n]
                if eng == "act":
                    fw.op("act", lambda e, d=d, st=st, n=n: e.activation(out=d, in_=st.t[:, 0:n], func=AF.Copy),
                          reads=[st.b], writes=[dst.b], waw=False)
                else:
                    fw.op(eng, lambda e, d=d, st=st, n=n: e.tensor_copy(out=d, in_=st.t[:, 0:n]),
                          reads=[st.b], writes=[dst.b], waw=False)
        fw.barrier(clear=False)
        self.ph.close()
        self.ph = main_ph

    def rstd(self, ss, out, tmp, scale, n):
        fw = self.fw
        fw.op("act", lambda e: e.activation(out=tmp.t[:, 0:n], in_=ss.t[:, 0:n], func=AF.Sqrt, scale=scale, bias=EPS),
              reads=[ss.b], writes=[tmp.b])
        fw.op("dve", lambda e: e.reciprocal(out=out.t[:, 0:n], in_=tmp.t[:, 0:n]), reads=[tmp.b], writes=[out.b])

    def norm_mod(self, x, n, ss, sq, rs, tmp, gs, sh, out, out_dt_bf16=True):
        fw = self.fw
        for k in range(KD):
            fw.op("act", lambda e, k=k: e.activation(out=sq.t[:, k, 0:n], in_=x.t[:, k, 0:n], func=AF.Square),
                  reads=[x.b], writes=[sq.b], waw=False)
        for k in range(KD):
            fw.op("pe", lambda e, k=k: e.matmul(ss.t[:, 0:n], lhsT=self.ones_bf.t[:, :], rhs=sq.t[:, k, 0:n],
                                                start=(k == 0), stop=(k == KD - 1)),
                  reads=[sq.b, self.ones_bf.b], writes=[ss.b], ticket=(k == KD - 1))
        self.rstd(ss, rs, tmp, 1.0 / D, n)
        for k in range(KD):
            if sh is None:
                fw.op("dve", lambda e, k=k: e.scalar_tensor_tensor(
                    out=out.t[:, k, 0:n], in0=x.t[:, k, 0:n], scalar=gs(k), in1=rs.t[:, 0:n],
                    op0=ALU.mult, op1=ALU.mult), reads=[x.b, rs.b], writes=[out.b], waw=False)
            else:
                tm = self._nm_tmp[k % 2]
                fw.op("dve", lambda e, k=k, tm=tm: e.scalar_tensor_tensor(
                    out=tm.t[:, 0:n], in0=x.t[:, k, 0:n], scalar=gs(k), in1=rs.t[:, 0:n],
                    op0=ALU.mult, op1=ALU.mult), reads=[x.b, rs.b], writes=[tm.b])
                fw.op("act", lambda e, k=k, tm=tm: e.activation(out=out.t[:, k, 0:n], in_=tm.t[:, 0:n],
                                                         func=AF.Identity, bias=sh(k), scale=1.0),
                      reads=[tm.b], writes=[out.b], waw=False)

    def norm_tiles(self, nmax=512):
        self._nm_tmp = [self.sb("nmt", [128, nmax], F32) for _ in range(2)]
        return dict(ss=self.ps("ss", [128, 512], F32), sq=self.sb("sq", [128, KD, nmax], BF16),
                    rs=self.sb("rs", [128, nmax], F32), tmp=self.sb("rtmp", [128, nmax], F32))

    def load_cols(self, dst, dview, seq, t0, n, halo, extra_reads=()):
        fw = self.fw
        _, start, ln, _, _ = seq
        lo, hi = t0 - halo, t0 + n + halo
        clo, chi = max(lo, 0), min(hi, ln)
        if clo > lo:
            fw.op("pool", lambda e: e.memset(dst.t[:, :, 0:clo - lo], 0.0), writes=[dst.b])
        if chi < hi:
            fw.op("pool", lambda e: e.memset(dst.t[:, :, chi - lo:hi - lo], 0.0), writes=[dst.b],
                  waw=(clo == lo))
        fw.dma(dst.t[:, :, clo - lo:chi - lo], dview[:, :, start + clo:start + chi], dst.b,
               reads=[self.db(dview)], writes=[dst.b], waw=(clo == lo and chi == hi))

    def dvv(self, l, v, which):
        base = ((l * 2 + v) * 6 + which) * 8
        return lambda k: self.dv.t[:, base + k:base + k + 1]

    def setup(self):
        fw, cfg = self.fw, self.cfg
        nv = self.i_vecs.shape[1]
        self.vecs = self.sb("vecs", [128, nv], F32, pers=True)
        self.cm = self.sb("cmat", [128, 768], F32, pers=True)
        self.cmb = self.sb("cmatb", [128, 768], BF16, pers=True)
        self.ones_bf = self.sb("ones", [128, 128], BF16, pers=True)
        self.dv = self.sb("dv", [128, DEPTH * 2 * 6 * 8], F32, pers=True)
        self.begin()
        fw.dma(self.vecs.t[:, :], self.i_vecs[:, :], self.vecs.b, writes=[self.vecs.b])
        fw.dma(self.cm.t[:, :], self.i_cmat[:, :], self.cm.b, writes=[self.cm.b])
        fw.op("dve", lambda e: e.tensor_copy(out=self.cmb.t[:, :], in_=self.cm.t[:, :]), reads=[self.cm.b],
              writes=[self.cmb.b])
        fw.op("pool", lambda e: e.memset(self.ones_bf.t[:, :], 1.0), writes=[self.ones_bf.b])
        dummy = self.fw.buf(name="x0")
        XAv = self.XA.rearrange("(k p) n -> p k n", p=128)
        xin = self.i_xT.rearrange("(k p) n -> p k n", p=128)
        cin = self.i_cxT.rearrange("(k p) n -> p k n", p=128)
        for k in range(KD):
            fw.dma(XAv[:, k, cfg.LS:cfg.LS + cfg.T], xin[:, k, :], dummy, writes=[self.db(self.XA)], waw=False)
            fw.dma(XAv[:, k, cfg.CS:cfg.CS + cfg.C], cin[:, k, :], dummy, writes=[self.db(self.XA)], waw=False)
        self.end()

    def ident_bf(self):
        return self.cmb.t[:, 0:128]

    def cmask(self, name, bf=True):
        j = ("ident", "ML_f", "MR_f", "ML_b", "MR_b", "bd").index(name)
        return (self.cmb if bf else self.cm).t[:, j * 128:(j + 1) * 128]

    def mod_phase(self):
        fw = self.fw
        self.begin()
        ct = self.sb("ct", [128, 16], F32)
        sc = self.sb("sc", [128, 16], BF16)
        fw.dma(ct.t[:, :], self.i_cT[:, :], ct.b, writes=[ct.b])
        fw.op("act", lambda e: e.activation(out=sc.t[:, :], in_=ct.t[:, :], func=AF.Silu), reads=[ct.b], writes=[sc.b])
        modT = self.sb("modT", [128, DEPTH, 48, 2], F32)
        wst = [self.sb("mwst", [128, KD, 512], F32) for _ in range(2)]
        wb = [self.sb("mwb", [128, KD, 512], BF16) for _ in range(2)]
        pm = [self.ps("pm", [128, 512], F32) for _ in range(2)]
        it = 0
        for l in range(self.n_layers):
            mw = self.i_modw[l].rearrange("(k p) n -> p k n", p=128)
            for cg in range(12):
                s_, b_, p_ = wst[it % 2], wb[it % 2], pm[it % 2]
                fw.dma(s_.t[:, :, :], mw[:, :, cg * 512:(cg + 1) * 512], s_.b, writes=[s_.b])
                eng = "dve" if it % 2 == 0 else "pool"
                fw.op(eng, lambda e, s_=s_, b_=b_: e.tensor_copy(out=b_.t[:, :, :], in_=s_.t[:, :, :]),
                      reads=[s_.b], writes=[b_.b])
                for j in range(4):
                    for k in range(KD):
                        fw.op("pe", lambda e, j=j, k=k, b_=b_, p_=p_: e.matmul(
                            p_.t[:, 2 * j:2 * j + 2], lhsT=b_.t[:, k, j * 128:(j + 1) * 128],
                            rhs=sc.t[:, 2 * k:2 * k + 2], start=(k == 0), stop=(k == KD - 1)),
                              reads=[b_.b, sc.b], writes=[p_.b], ticket=(k == KD - 1 and j == 3))
                for j in range(4):
                    fw.op("dve", lambda e, j=j, p_=p_, l=l, cg=cg: e.tensor_scalar(
                        out=modT.t[:, l, cg * 4 + j, :], in0=p_.t[:, 2 * j:2 * j + 2],
                        scalar1=self.vec("modb%d" % l, cg * 4 + j), scalar2=None, op0=ALU.add),
                          reads=[p_.b], writes=[modT.b], waw=False)
                it += 1
        for l in range(self.n_layers):
            for v in range(2):
                def dvs(which):
                    base = ((l * 2 + v) * 6 + which) * 8
                    return self.dv.t[:, base:base + 8]
                for which, (sci, gname) in ((0, (8, "gmix%d" % l)), (3, (32, "gffn%d" % l))):
                    fw.op("dve", lambda e, which=which, sci=sci, gname=gname, l=l, v=v, dvs=dvs: e.scalar_tensor_tensor(
                        out=dvs(which), in0=modT.t[:, l, sci:sci + 8, v], scalar=1.0, in1=self.vec(gname, 0, 8),
                        op0=ALU.add, op1=ALU.mult), reads=[modT.b], writes=[self.dv.b], waw=False)
                for which, c0 in ((1, 0), (2, 16), (4, 24), (5, 40)):
                    fw.op("dve", lambda e, which=which, c0=c0, l=l, v=v, dvs=dvs: e.tensor_copy(
                        out=dvs(which), in_=modT.t[:, l, c0:c0 + 8, v]), reads=[modT.b], writes=[self.dv.b], waw=False)
        self.end()

    def attn_inproj(self, l, X):
        fw, cfg = self.fw, self.cfg
        i = l // 2
        self.begin()
        W = self.sb("wA", [128, KD, ATT_EXT], BF16)
        self.load_w(W, lambda k: self.i_wext[i, k * 128:(k + 1) * 128, :], ATT_EXT)
        nt = self.norm_tiles()
        xs = [self.sb("x", [128, KD, 512], F32) for _ in range(2)]
        hTs = [self.sb("hT", [128, KD, 512], BF16) for _ in range(2)]
        cs = [self.sb("cos", [128, 512], F32) for _ in range(2)]
        sn = [self.sb("sin", [128, 512], F32) for _ in range(2)]
        qko = [self.sb("qko", [128, 13, 512], BF16) for _ in range(1)]
        vo = [self.sb("vo", [128, 4, 640], BF16) for _ in range(1)]
        pP = [self.ps("pP", [128, 512]) for _ in range(2)]
        pS = [self.ps("pS", [128, 512]) for _ in range(2)]
        ssq = self.ps("ssq", [128, 512])
        pV = self.ps("pV", [128, 1024])
        sqn = self.sb("sqn", [128, 512], BF16)
        rq, tq = self.sb("rq", [128, 512]), self.sb("tq", [128, 512])
        Aq = [self.sb("Aq", [128, 512]) for _ in range(2)]
        Bq = [self.sb("Bq", [128, 512]) for _ in range(2)]
        t1 = [self.sb("t1", [128, 512]) for _ in range(2)]
        t2 = [self.sb("t2", [128, 512]) for _ in range(2)]
        Xv = X.rearrange("(k p) n -> p k n", p=128)
        QTv = self.QT.rearrange("(k p) n -> p k n", p=128)
        KTv = self.KT.rearrange("(k p) n -> p k n", p=128)
        bi = 0
        for seq in cfg.seqs:
            _, start, ln, v, row0 = seq
            lat = (v == 0)
            for (t0, n) in blocks_of(ln, 512):
                x, hT, qo, vv = xs[bi % 2], hTs[bi % 2], qko[0], vo[0]
                c_, s_ = cs[bi % 2], sn[bi % 2]
                bi += 1
                self.load_cols(x, Xv, seq, t0, n, 0)
                if lat:
                    fw.dma(c_.t[:, 0:n], self.i_cos[:, start + t0:start + t0 + n], c_.b, writes=[c_.b])
                    fw.dma(s_.t[:, 0:n], self.i_sin[:, start + t0:start + t0 + n], s_.b, writes=[s_.b])
                self.norm_mod(x, n, nt["ss"], nt["sq"], nt["rs"], nt["tmp"], self.dvv(l, v, 0), self.dvv(l, v, 1), hT)
                for j in range(13):
                    P, S = pP[j % 2], pS[j % 2]
                    for k in range(KD):
                        fw.op("pe", lambda e, k=k, j=j, P=P: e.matmul(
                            P.t[:, 0:n], lhsT=W.t[:, k, j * 128:(j + 1) * 128], rhs=hT.t[:, k, 0:n],
                            start=(k == 0), stop=(k == KD - 1)), reads=[W.b, hT.b], writes=[P.b], ticket=(k == KD - 1))
                    if lat:
                        for k in range(KD):
                            fw.op("pe", lambda e, k=k, j=j, S=S: e.matmul(
                                S.t[:, 0:n], lhsT=W.t[:, k, ATT_FM + j * 128:ATT_FM + (j + 1) * 128],
                                rhs=hT.t[:, k, 0:n], start=(k == 0), stop=(k == KD - 1)),
                                  reads=[W.b, hT.b], writes=[S.b], ticket=(k == KD - 1))
                    normed = j >= 8
                    A, B = P, S
                    if normed:
                        fw.op("act", lambda e, P=P: e.activation(out=sqn.t[:, 0:n], in_=P.t[:, 0:n], func=AF.Square),
                              reads=[P.b], writes=[sqn.b])
                        fw.op("pe", lambda e: e.matmul(ssq.t[:, 0:n], lhsT=self.cmask("bd"), rhs=sqn.t[:, 0:n],
                                                       start=True, stop=True), reads=[sqn.b, self.cmb.b], writes=[ssq.b])
                        self.rstd(ssq, rq, tq, 1.0 / HD, n)
                        gn, gsn = ("gq%d" % i, "gqs%d" % i) if j < 12 else ("gk%d" % i, "gks%d" % i)
                        A = Aq[j % 2]
                        fw.op("dve", lambda e, P=P, A=A, gn=gn: e.scalar_tensor_tensor(
                            out=A.t[:, 0:n], in0=P.t[:, 0:n], scalar=self.vec(gn), in1=rq.t[:, 0:n],
                            op0=ALU.mult, op1=ALU.mult), reads=[P.b, rq.b], writes=[A.b])
                        if lat:
                            B = Bq[j % 2]
                            fw.op("dve", lambda e, S=S, B=B, gsn=gsn: e.scalar_tensor_tensor(
                                out=B.t[:, 0:n], in0=S.t[:, 0:n], scalar=self.vec(gsn), in1=rq.t[:, 0:n],
                                op0=ALU.mult, op1=ALU.mult), reads=[S.b, rq.b], writes=[B.b])
                    if lat:
                        a1, a2 = t1[j % 2], t2[j % 2]
                        fw.op("dve", lambda e, A=A, a1=a1: e.tensor_tensor(
                            out=a1.t[:, 0:n], in0=A.t[:, 0:n], in1=c_.t[:, 0:n], op=ALU.mult),
                              reads=[A.b, c_.b], writes=[a1.b])
                        fw.op("dve", lambda e, B=B, a2=a2: e.tensor_tensor(
                            out=a2.t[:, 0:n], in0=B.t[:, 0:n], in1=s_.t[:, 0:n], op=ALU.mult),
                              reads=[B.b, s_.b], writes=[a2.b])
                        fw.op("pool", lambda e, a1=a1, a2=a2, j=j: e.tensor_tensor(
                            out=qo.t[:, j, 0:n], in0=a1.t[:, 0:n], in1=a2.t[:, 0:n], op=ALU.add),
                              reads=[a1.b, a2.b], writes=[qo.b], waw=False)
                    elif normed:
                        fw.op("pool", lambda e, A=A, j=j: e.tensor_copy(out=qo.t[:, j, 0:n], in_=A.t[:, 0:n]),
                              reads=[A.b], writes=[qo.b], waw=False)
                    else:
                        fw.op("act", lambda e, A=A, j=j: e.activation(out=qo.t[:, j, 0:n], in_=A.t[:, 0:n], func=AF.Copy),
                              reads=[A.b], writes=[qo.b], waw=False)
                c0, c1 = start + t0, start + t0 + n
                for (dst, d0, s0, w) in ((QTv, 0, 0, 4), (KTv, 0, 4, 4), (QTv, 4, 8, 4), (KTv, 4, 12, 1)):
                    fw.dma(dst[:, d0:d0 + w, c0:c1], qo.t[:, s0:s0 + w, 0:n], qo.b, reads=[qo.b],
                           writes=[self.db(dst)], waw=False, stream="pool")
                nsub = n // 128
                for sub in range(nsub):
                    for (cc, w) in ((0, 512), (512, 128)):
                        for k in range(KD):
                            fw.op("pe", lambda e, k=k, cc=cc, w=w, sub=sub: e.matmul(
                                pV.t[:, cc:cc + w], lhsT=hT.t[:, k, sub * 128:(sub + 1) * 128],
                                rhs=W.t[:, k, 2 * ATT_FM + cc:2 * ATT_FM + cc + w], start=(k == 0), stop=(k == KD - 1)),
                                  reads=[W.b, hT.b], writes=[pV.b], ticket=(k == KD - 1 and cc == 512))
                    fw.op("act" if sub % 2 == 0 else "dve",
                          (lambda e, sub=sub: e.activation(out=vv.t[:, sub, :], in_=pV.t[:, 0:640], func=AF.Copy))
                          if sub % 2 == 0 else
                          (lambda e, sub=sub: e.tensor_copy(out=vv.t[:, sub, :], in_=pV.t[:, 0:640])),
                          reads=[pV.b], writes=[vv.b], waw=False)
                r0 = row0 + t0
                fw.dma(self.VT[r0:r0 + n, :].rearrange("(s p) c -> p s c", p=128), vv.t[:, 0:nsub, :], vv.b,
                       reads=[vv.b], writes=[self.db(self.VT)], waw=False, stream="pool")
        self.end()

    def attn_core(self, l):
        fw, cfg = self.fw, self.cfg
        i = l // 2
        lam_init = 0.8 - 0.6 * math.exp(-0.3 * l)
        NTOK, C, T = cfg.NTOK, cfg.C, cfg.T
        NKT = NTOK // 128
        self.begin()
        ro = self.roff["lq1_%d" % i][0]
        rv = self.sb("lqk", [128, 256])
        fw.dma(rv.t[:, :], self.i_rowv[0:1, ro:ro + 256].broadcast_to([128, 256]), rv.b, writes=[rv.b])
        prod = self.sb("prod", [128, 128])
        fw.op("dve", lambda e: e.tensor_tensor(out=prod.t[:, 0:64], in0=rv.t[:, 0:64], in1=rv.t[:, 64:128], op=ALU.mult),
              reads=[rv.b], writes=[prod.b])
        fw.op("dve", lambda e: e.tensor_tensor(out=prod.t[:, 64:128], in0=rv.t[:, 128:192], in1=rv.t[:, 192:256],
                                               op=ALU.mult), reads=[rv.b], writes=[prod.b], waw=False)
        s12 = self.sb("s12", [128, 2])
        for q in range(2):
            fw.op("dve", lambda e, q=q: e.reduce_sum(out=s12.t[:, q:q + 1], in_=prod.t[:, q * 64:(q + 1) * 64],
                                                     axis=mybir.AxisListType.X), reads=[prod.b], writes=[s12.b], waw=False)
        e12 = self.sb("e12", [128, 2])
        fw.op("act", lambda e: e.activation(out=e12.t[:, :], in_=s12.t[:, :], func=AF.Exp), reads=[s12.b], writes=[e12.b])
        nlam = self.sb("nlam", [128, 1])
        fw.op("dve", lambda e: e.tensor_tensor(out=nlam.t[:, :], in0=e12.t[:, 1:2], in1=e12.t[:, 0:1], op=ALU.subtract),
              reads=[e12.b], writes=[nlam.b])
        fw.op("dve", lambda e: e.tensor_scalar(out=nlam.t[:, :], in0=nlam.t[:, :], scalar1=-lam_init, scalar2=None,
                                               op0=ALU.add), reads=[nlam.b], writes=[nlam.b])
        sg = self.sb("sg", [128, 1])
        fw.op("dve", lambda e: e.tensor_scalar(out=sg.t[:, :], in0=self.vec("subg%d" % i), scalar1=1.0 - lam_init,
                                               scalar2=None, op0=ALU.mult), reads=[self.vecs.b], writes=[sg.b])
        Ks = [self.sb("K", [128, NTOK], BF16) for _ in range(2)]
        Vs = [self.sb("V", [128, NKT, 128], BF16) for _ in range(2)]
        Vg = [self.sb("Vg", [128, NKT, 128], BF16) for _ in range(2)]
        Qs = [self.sb("Q", [128, 512], BF16) for _ in range(2)]
        Pt = [self.sb("P", [128, 1024], BF16) for _ in range(2)]
        Sp = [self.ps("S", [128, 1024]) for _ in range(2)]
        Op = [self.ps("O", [128, 512]) for _ in range(2)]
        Lp = [self.ps("L", [128, 512]) for _ in range(2)]
        r_ = [self.sb("r", [128, 512]) for _ in range(2)]
        on = [self.sb("on", [128, 512]) for _ in range(2)]
        oa = self.sb("oa", [128, 512])
        sqo = self.sb("sqo", [128, 512], BF16)
        rs, tmp = self.sb("rso", [128, 512]), self.sb("tmpo", [128, 512])
        Lacc = [self.sb("Lacc", [128, 512]) for _ in range(2)]
        Lb = [self.sb("Lb", [128, 512], BF16) for _ in range(2)]
        Lf = self.sb("Lf", [128, 512])
        hb = self.sb("hb", [128, 512], BF16)
        h32 = self.sb("h32", [128, 512])
        lb = self.sb("lb", [128, 512], BF16)
        ost = [self.sb("ost", [128, 2, 512], BF16) for _ in range(2)]
        QTv = self.QT.rearrange("(k p) n -> p k n", p=128)
        KTv = self.KT.rearrange("(k p) n -> p k n", p=128)
        OTv = self.OT.rearrange("(k p) n -> p k n", p=128)
        v3 = lambda t, n: t.t[:, :].rearrange("p (s c) -> p s c", c=512)[:, :, 0:n]

        def load_kv(u):
            if u > 4:
                return
            K, V = Ks[u % 2], Vs[u % 2]
            kc = u if u < 4 else 4
            vcol = u * 128 if u < 4 else 512
            fw.dma(K.t[:, 0:C], KTv[:, kc, cfg.CS:cfg.CS + C], K.b, reads=[self.db(self.KT)], writes=[K.b])
            fw.dma(K.t[:, C:NTOK], KTv[:, kc, cfg.LS:cfg.LS + T], K.b, reads=[self.db(self.KT)], writes=[K.b], waw=False)
            vsrc = self.VT[:, vcol:vcol + 128].rearrange("(s p) c -> p s c", p=128)
            for (s0, ns) in blocks_of(NKT, 16):
                fw.dma(V.t[:, s0:s0 + ns, :], vsrc[:, s0:s0 + ns, :], V.b, reads=[self.db(self.VT)], writes=[V.b],
                       waw=(s0 == 0))
            if u == 4:
                for sub in range(2):
                    fw.op("pool", lambda e, sub=sub: e.memset(Vg[sub].t[:, :, 64:128], 1.0), writes=[Vg[sub].b])
                    fw.op("pool" if sub == 0 else "dve", lambda e, sub=sub: e.tensor_copy(
                        out=Vg[sub].t[:, :, 0:64], in_=V.t[:, :, sub * 64:(sub + 1) * 64]),
                          reads=[V.b], writes=[Vg[sub].b], waw=False)

        qblocks = []
        for seq in cfg.seqs:
            _, start, ln, v, row0 = seq
            kts = list(range(0, C // 128)) if v == 1 else list(range(NKT))
            for (t0, n) in blocks_of(ln, 512):
                qblocks.append((start + t0, n, kts))
        load_kv(0)
        qi = 0
        for u in range(8):
            if u + 1 < 8:
                load_kv(u + 1)
            K, V = Ks[min(u, 4) % 2], Vs[min(u, 4) % 2]
            diff = u < 4
            for (c0, n, kts) in qblocks:
                Q = Qs[qi % 2]
                os_ = ost[qi % 2]
                qi += 1
                fw.dma(Q.t[:, 0:n], QTv[:, u, c0:c0 + n], Q.b, reads=[self.db(self.QT)], writes=[Q.b])

                def emit_S(kt):
                    S = Sp[kt % 2]
                    for sub in range(2):
                        fw.op("pe", lambda e, sub=sub, S=S, kt=kt: e.matmul(
                            S.t[:, sub * 512:sub * 512 + n], lhsT=K.t[sub * 64:(sub + 1) * 64, kt * 128:(kt + 1) * 128],
                            rhs=Q.t[sub * 64:(sub + 1) * 64, 0:n], start=True, stop=True),
                              reads=[K.b, Q.b], writes=[S.b], ticket=(sub == 1), waw=(sub == 0))

                def emit_E(kt):
                    S, P = Sp[kt % 2], Pt[kt % 2]
                    fw.op("act", lambda e, S=S, P=P: e.activation(out=v3(P, n), in_=v3(S, n), func=AF.Exp,
                                                                  scale=HD ** -0.5), reads=[S.b], writes=[P.b])

                def emit_PV(kt, first, last):
                    P = Pt[kt % 2]
                    for sub in range(2):
                        rhs = P.t[:, sub * 512:sub * 512 + n]
                        if diff:
                            fw.op("pe", lambda e, sub=sub, rhs=rhs: e.matmul(
                                Op[sub].t[:, 0:n], lhsT=V.t[:, kt, :], rhs=rhs, start=first, stop=last),
                                  reads=[V.b, P.b], writes=[Op[sub].b] if (first or last) else [],
                                  ticket=(last or sub == 1))
                            eng = "dve" if sub == 0 else "pool"
                            La = Lacc[sub]
                            if first:
                                fw.op(eng, lambda e, La=La, rhs=rhs: e.tensor_copy(out=La.t[:, 0:n], in_=rhs),
                                      reads=[P.b], writes=[La.b])
                            else:
                                fw.op(eng, lambda e, La=La, rhs=rhs: e.tensor_tensor(
                                    out=La.t[:, 0:n], in0=La.t[:, 0:n], in1=rhs, op=ALU.add),
                                      reads=[P.b, La.b], writes=[La.b])
                        else:
                            fw.op("pe", lambda e, sub=sub, rhs=rhs: e.matmul(
                                Op[sub].t[:, 0:n], lhsT=Vg[sub].t[:, kt, :], rhs=rhs, start=first, stop=last),
                                  reads=[Vg[sub].b, P.b], writes=[Op[sub].b] if (first or last) else [],
                                  ticket=(last or sub == 1))

                emit_S(kts[0])
                for idx, kt in enumerate(kts):
                    if idx + 1 < len(kts):
                        emit_S(kts[idx + 1])
                    emit_E(kt)
                    emit_PV(kt, idx == 0, idx == len(kts) - 1)
                if diff:
                    for sub in range(2):
                        fw.op("dve", lambda e, sub=sub: e.tensor_copy(out=Lb[sub].t[:, 0:n], in_=Lacc[sub].t[:, 0:n]),
                              reads=[Lacc[sub].b], writes=[Lb[sub].b])
                        fw.op("pe", lambda e, sub=sub: e.matmul(Lp[sub].t[:, 0:n], lhsT=self.ones_bf.t[:, :],
                                                                rhs=Lb[sub].t[:, 0:n], start=True, stop=True),
                              reads=[Lb[sub].b, self.ones_bf.b], writes=[Lp[sub].b])
                        fw.op("dve", lambda e, sub=sub: e.reciprocal(out=r_[sub].t[:, 0:n], in_=Lp[sub].t[:, 0:n]),
                              reads=[Lp[sub].b], writes=[r_[sub].b])
                    for sub in range(2):
                        fw.op("dve", lambda e, sub=sub: e.tensor_tensor(
                            out=on[sub].t[:, 0:n], in0=Op[sub].t[:, 0:n], in1=r_[sub].t[:, 0:n], op=ALU.mult),
                              reads=[Op[sub].b, r_[sub].b], writes=[on[sub].b])
                    fw.op("dve", lambda e: e.scalar_tensor_tensor(
                        out=oa.t[:, 0:n], in0=on[1].t[:, 0:n], scalar=nlam.t[:, 0:1], in1=on[0].t[:, 0:n],
                        op0=ALU.mult, op1=ALU.add), reads=[on[0].b, on[1].b, nlam.b], writes=[oa.b])
                    fw.op("act", lambda e: e.activation(out=sqo.t[:, 0:n], in_=oa.t[:, 0:n], func=AF.Square),
                          reads=[oa.b], writes=[sqo.b])
                    ssb = Lp[0]
                    fw.op("pe", lambda e: e.matmul(ssb.t[:, 0:n], lhsT=self.ones_bf.t[:, :], rhs=sqo.t[:, 0:n],
                                                   start=True, stop=True), reads=[sqo.b, self.ones_bf.b], writes=[ssb.b])
                    self.rstd(ssb, rs, tmp, 1.0 / 128, n)
                    fw.op("dve", lambda e: e.scalar_tensor_tensor(
                        out=os_.t[:, 0, 0:n], in0=oa.t[:, 0:n], scalar=sg.t[:, 0:1], in1=rs.t[:, 0:n],
                        op0=ALU.mult, op1=ALU.mult), reads=[oa.b, rs.b, sg.b], writes=[os_.b])
                    fw.dma(OTv[:, u, c0:c0 + n], os_.t[:, 0, 0:n], os_.b, reads=[os_.b], writes=[self.db(self.OT)],
                           waw=False, stream="pool")
                else:
                    j = u - 4
                    H = slice(64, 128)
                    idb = self.cmb.t[64:128, 64:128]
                    for sub in range(2):
                        fw.op("act", lambda e, sub=sub: e.activation(out=Lf.t[H, 0:n], in_=Op[sub].t[H, 0:n], func=AF.Copy),
                              reads=[Op[sub].b], writes=[Lf.b])
                        fw.op("dve", lambda e: e.reciprocal(out=Lf.t[H, 0:n], in_=Lf.t[H, 0:n]), reads=[Lf.b], writes=[Lf.b])
                        fw.op("dve", lambda e: e.tensor_copy(out=hb.t[H, 0:n], in_=Lf.t[H, 0:n]), reads=[Lf.b], writes=[hb.b])
                        fw.op("dve", lambda e: e.tensor_copy(out=h32.t[H, 0:n], in_=hb.t[H, 0:n]), reads=[hb.b], writes=[h32.b])
                        fw.op("dve", lambda e: e.tensor_tensor(out=lb.t[H, 0:n], in0=Lf.t[H, 0:n], in1=h32.t[H, 0:n],
                                                               op=ALU.subtract), reads=[Lf.b, h32.b], writes=[lb.b])
                        fw.op("pe", lambda e, sub=sub: e.matmul(Lp[sub].t[0:64, 0:n], lhsT=idb, rhs=hb.t[H, 0:n],
                                                                start=True, stop=False),
                              reads=[hb.b, self.cmb.b], writes=[Lp[sub].b], ticket=False)
                        fw.op("pe", lambda e, sub=sub: e.matmul(Lp[sub].t[0:64, 0:n], lhsT=idb, rhs=lb.t[H, 0:n],
                                                                start=False, stop=True),
                              reads=[lb.b, hb.b, self.cmb.b], writes=[Lp[sub].b])
                        fw.op("act", lambda e, sub=sub: e.activation(out=r_[sub].t[0:64, 0:n], in_=Lp[sub].t[0:64, 0:n],
                                                                     func=AF.Copy), reads=[Lp[sub].b], writes=[r_[sub].b])
                        fw.op("dve", lambda e, sub=sub: e.tensor_tensor(
                            out=os_.t[0:64, sub, 0:n], in0=Op[sub].t[0:64, 0:n], in1=r_[sub].t[0:64, 0:n], op=ALU.mult),
                              reads=[Op[sub].b, r_[sub].b], writes=[os_.b], waw=(sub == 0))
                    for sub in range(2):
                        hd = 4 * sub + j
                        f0 = 512 + hd * 64
                        fw.dma(self.OT[f0:f0 + 64, c0:c0 + n], os_.t[0:64, sub, 0:n], os_.b, reads=[os_.b],
                               writes=[self.db(self.OT)], waw=False, stream="pool")
        self.end()

    def proj_res_norm(self, l, Xin, Xout, SRC, kc, w_rows):
        fw, cfg = self.fw, self.cfg
        self.begin()
        W = self.sb("wo", [128, kc, D], BF16)
        self.load_w(W, w_rows, D)
        nt = self.norm_tiles()
        srcs = [self.sb("src", [128, kc, 512], BF16) for _ in range(2)]
        xs = [self.sb("x", [128, KD, 512], F32) for _ in range(2)]
        x1s = [self.sb("x1", [128, KD, 512], F32) for _ in range(2)]
        hs = [self.sb("h", [128, KD, 512], BF16) for _ in range(2)]
        pp = [self.ps("pp", [128, 512]) for _ in range(2)]
        Xiv = Xin.rearrange("(k p) n -> p k n", p=128)
        Xov = Xout.rearrange("(k p) n -> p k n", p=128)
        Sv = SRC.rearrange("(k p) n -> p k n", p=128)
        HTv = self.HT.rearrange("(k p) n -> p k n", p=128)
        bi = 0
        for seq in cfg.seqs:
            _, start, ln, v, row0 = seq
            if v == 1 and l == DEPTH - 1:
                continue
            for (t0, n) in blocks_of(ln, 512):
                s_, x, x1, h = srcs[bi % 2], xs[bi % 2], x1s[bi % 2], hs[bi % 2]
                bi += 1
                self.load_cols(s_, Sv, seq, t0, n, 0)
                self.load_cols(x, Xiv, seq, t0, n, 0)
                g1 = self.dvv(l, v, 2)
                for c in range(KD):
                    P = pp[c % 2]
                    for k in range(kc):
                        fw.op("pe", lambda e, k=k, c=c, P=P: e.matmul(
                            P.t[:, 0:n], lhsT=W.t[:, k, c * 128:(c + 1) * 128], rhs=s_.t[:, k, 0:n],
                            start=(k == 0), stop=(k == kc - 1)), reads=[W.b, s_.b], writes=[P.b], ticket=(k == kc - 1))
                    fw.op("dve", lambda e, c=c, P=P: e.scalar_tensor_tensor(
                        out=x1.t[:, c, 0:n], in0=P.t[:, 0:n], scalar=g1(c), in1=x.t[:, c, 0:n],
                        op0=ALU.mult, op1=ALU.add), reads=[P.b, x.b], writes=[x1.b], waw=False)
                c0 = start + t0
                fw.dma(Xov[:, :, c0:c0 + n], x1.t[:, :, 0:n], x1.b, reads=[x1.b], writes=[self.db(Xout)], waw=False,
                       stream="pool")
                self.norm_mod(x1, n, nt["ss"], nt["sq"], nt["rs"], nt["tmp"], self.dvv(l, v, 3), self.dvv(l, v, 4), h)
                fw.dma(HTv[:, :, c0:c0 + n], h.t[:, :, 0:n], h.b, reads=[h.b], writes=[self.db(self.HT)], waw=False,
                       stream="pool")
        self.end()

    def ffn_in(self, l):
        fw, cfg = self.fw, self.cfg
        self.begin()
        W = self.sb("wf", [128, KD, 2 * DFF], BF16)
        self.load_w(W, lambda k: self.i_fwi[l, k * 128:(k + 1) * 128, :], 2 * DFF)
        hs = [self.sb("h", [128, KD, 512], BF16) for _ in range(2)]
        uo = [self.sb("uo", [128, NFF, 512], BF16) for _ in range(2)]
        pg = [self.ps("pg", [128, 512]) for _ in range(2)]
        pv = [self.ps("pv", [128, 512]) for _ in range(2)]
        tt = [self.sb("t", [128, 512]) for _ in range(2)]
        ge = [self.sb("ge", [128, 512]) for _ in range(2)]
        HTv = self.HT.rearrange("(k p) n -> p k n", p=128)
        UTv = self.UT.rearrange("(k p) n -> p k n", p=128)
        bi = 0
        for seq in cfg.seqs:
            _, start, ln, v, row0 = seq
            if v == 1 and l == DEPTH - 1:
                continue
            for (t0, n) in blocks_of(ln, 510):
                h, u = hs[bi % 2], uo[bi % 2]
                bi += 1
                self.load_cols(h, HTv, seq, t0, n, 1)
                N = n + 2
                for c in range(NFF):
                    G, Vv, t, g = pg[c % 2], pv[c % 2], tt[c % 2], ge[c % 2]
                    for k in range(KD):
                        fw.op("pe", lambda e, k=k, c=c, G=G: e.matmul(
                            G.t[:, 0:N], lhsT=W.t[:, k, DFF + c * 128:DFF + (c + 1) * 128], rhs=h.t[:, k, 0:N],
                            start=(k == 0), stop=(k == KD - 1)), reads=[W.b, h.b], writes=[G.b], ticket=(k == KD - 1))
                    for k in range(KD):
                        fw.op("pe", lambda e, k=k, c=c, Vv=Vv: e.matmul(
                            Vv.t[:, 0:N], lhsT=W.t[:, k, c * 128:(c + 1) * 128], rhs=h.t[:, k, 0:N],
                            start=(k == 0), stop=(k == KD - 1)), reads=[W.b, h.b], writes=[Vv.b], ticket=(k == KD - 1))
                    w0, w1, w2 = (self.vec("fcw%d_%d" % (l, q), c) for q in range(3))
                    bb = self.vec("fcb%d" % l, c)
                    fw.op("dve", lambda e, G=G, t=t, w0=w0, bb=bb: e.tensor_scalar(
                        out=t.t[:, 0:n], in0=G.t[:, 0:n], scalar1=w0, scalar2=bb, op0=ALU.mult, op1=ALU.add),
                          reads=[G.b], writes=[t.b])
                    fw.op("dve", lambda e, G=G, t=t, w1=w1: e.scalar_tensor_tensor(
                        out=t.t[:, 0:n], in0=G.t[:, 1:n + 1], scalar=w1, in1=t.t[:, 0:n], op0=ALU.mult, op1=ALU.add),
                          reads=[G.b, t.b], writes=[t.b])
                    fw.op("dve", lambda e, G=G, t=t, w2=w2: e.scalar_tensor_tensor(
                        out=t.t[:, 0:n], in0=G.t[:, 2:n + 2], scalar=w2, in1=t.t[:, 0:n], op0=ALU.mult, op1=ALU.add),
                          reads=[G.b, t.b], writes=[t.b])
                    fw.op("act", lambda e, t=t, g=g: e.activation(out=g.t[:, 0:n], in_=t.t[:, 0:n], func=AF.Gelu),
                          reads=[t.b], writes=[g.b])
                    fw.op("dve", lambda e, g=g, Vv=Vv, c=c: e.tensor_tensor(
                        out=u.t[:, c, 0:n], in0=g.t[:, 0:n], in1=Vv.t[:, 1:n + 1], op=ALU.mult),
                          reads=[g.b, Vv.b], writes=[u.b], waw=False)
                c0 = start + t0
                fw.dma(UTv[:, :, c0:c0 + n], u.t[:, :, 0:n], u.b, reads=[u.b], writes=[self.db(self.UT)], waw=False,
                       stream="pool")
        self.end()

    def ffn_out(self, l, Xin, Xout):
        fw, cfg = self.fw, self.cfg
        self.begin()
        W = self.sb("wf2", [128, NFF, D], BF16)
        self.load_w(W, lambda k: self.i_fwo[l, k * 128:(k + 1) * 128, :], D)
        us = [self.sb("u", [128, NFF, 512], BF16) for _ in range(2)]
        xs = [self.sb("x", [128, KD, 512], F32) for _ in range(2)]
        x2s = [self.sb("x2", [128, KD, 512], F32) for _ in range(2)]
        pp = [self.ps("pp", [128, 512]) for _ in range(2)]
        Xiv = Xin.rearrange("(k p) n -> p k n", p=128)
        Xov = Xout.rearrange("(k p) n -> p k n", p=128)
        UTv = self.UT.rearrange("(k p) n -> p k n", p=128)
        bi = 0
        for seq in cfg.seqs:
            _, start, ln, v, row0 = seq
            if v == 1 and l == DEPTH - 1:
                continue
            for (t0, n) in blocks_of(ln, 512):
                u, x, x2 = us[bi % 2], xs[bi % 2], x2s[bi % 2]
                bi += 1
                self.load_cols(u, UTv, seq, t0, n, 0)
                self.load_cols(x, Xiv, seq, t0, n, 0)
                g2 = self.dvv(l, v, 5)
                for c in range(KD):
                    P = pp[c % 2]
                    for k in range(NFF):
                        fw.op("pe", lambda e, k=k, c=c, P=P: e.matmul(
                            P.t[:, 0:n], lhsT=W.t[:, k, c * 128:(c + 1) * 128], rhs=u.t[:, k, 0:n],
                            start=(k == 0), stop=(k == NFF - 1)), reads=[W.b, u.b], writes=[P.b], ticket=(k == NFF - 1))
                    fw.op("dve", lambda e, c=c, P=P: e.scalar_tensor_tensor(
                        out=x2.t[:, c, 0:n], in0=P.t[:, 0:n], scalar=g2(c), in1=x.t[:, c, 0:n],
                        op0=ALU.mult, op1=ALU.add), reads=[P.b, x.b], writes=[x2.b], waw=False)
                c0 = start + t0
                fw.dma(Xov[:, :, c0:c0 + n], x2.t[:, :, 0:n], x2.b, reads=[x2.b], writes=[self.db(Xout)], waw=False,
                       stream="pool")
        self.end()

    def final_norm(self, X):
        fw, cfg = self.fw, self.cfg
        self.begin()
        nt = self.norm_tiles()
        xs = [self.sb("x", [128, KD, 512], F32) for _ in range(2)]
        os_ = [self.sb("o", [128, KD, 512], F32) for _ in range(2)]
        Xv = X.rearrange("(k p) n -> p k n", p=128)
        Ov = self.o_out.rearrange("(k p) n -> p k n", p=128)
        seq = cfg.seqs[1]
        outb = self.fw.buf(name="outT")
        for bi, (t0, n) in enumerate(blocks_of(cfg.T, 512)):
            x, o = xs[bi % 2], os_[bi % 2]
            self.load_cols(x, Xv, seq, t0, n, 0)
            self.norm_mod(x, n, nt["ss"], nt["sq"], nt["rs"], nt["tmp"], lambda k: self.vec("gfin", k), None, o)
            fw.dma(Ov[:, :, t0:t0 + n], o.t[:, :, 0:n], o.b, reads=[o.b], writes=[outb], waw=False, stream="pool")
        self.end()


    def ssm_inproj(self, l, X):
        fw, cfg = self.fw, self.cfg
        i = l // 2
        self.begin()
        W = self.sb("wS", [128, KD, SSM_IN], BF16)
        self.load_w(W, lambda k: self.i_swi[i, k * 128:(k + 1) * 128, :], SSM_IN)
        nt = self.norm_tiles()
        x = self.sb("x", [128, KD, 512], F32)
        hTs = [self.sb("hT", [128, KD, 512], BF16) for _ in range(2)]
        xo = self.sb("xo", [128, 24, 512], BF16)
        zo = [self.sb("zo", [128, SSM_DI], F32) for _ in range(2)]
        dto = [self.sb("dto", [128, 64], F32) for _ in range(2)]
        dta = [self.sb("dta", [128, 64], F32) for _ in range(2)]
        dtb = self.sb("dtb", [128, 64], F32)
        ro = self.roff["dtb%d" % i][0]
        fw.dma(dtb.t[:, :], self.i_rowv[0:1, ro:ro + 64].broadcast_to([128, 64]), dtb.b, writes=[dtb.b])
        pP = [self.ps("pP", [128, 512]) for _ in range(2)]
        pz = [self.ps("pz", [128, 512]) for _ in range(2)]
        pdt = self.ps("pdt", [128, 512])
        tt = [self.sb("t", [128, 512]) for _ in range(2)]
        Xv = X.rearrange("(k p) n -> p k n", p=128)
        XBv = self.XBCT.rearrange("(k p) n -> p k n", p=128)
        bi = 0
        zi = 0
        for seq in cfg.seqs:
            _, start, ln, v, row0 = seq
            for (t0, n) in blocks_of(ln, 510):
                hT = hTs[bi % 2]
                bi += 1
                N = n + 2
                self.load_cols(x, Xv, seq, t0, n, 1)
                self.norm_mod(x, N, nt["ss"], nt["sq"], nt["rs"], nt["tmp"], self.dvv(l, v, 0), self.dvv(l, v, 1), hT)
                if t0 == 0:
                    fw.op("pool", lambda e: e.memset(hT.t[:, :, 0:1], 0.0), writes=[hT.b])
                if t0 + n == ln:
                    fw.op("pool", lambda e: e.memset(hT.t[:, :, n + 1:n + 2], 0.0), writes=[hT.b])
                for c in range(24):
                    P, t = pP[c % 2], tt[c % 2]
                    for k in range(KD):
                        fw.op("pe", lambda e, k=k, c=c, P=P: e.matmul(
                            P.t[:, 0:N], lhsT=W.t[:, k, SSM_DI + c * 128:SSM_DI + (c + 1) * 128], rhs=hT.t[:, k, 0:N],
                            start=(k == 0), stop=(k == KD - 1)), reads=[W.b, hT.b], writes=[P.b], ticket=(k == KD - 1))
                    w0, w1, w2 = (self.vec("scw%d_%d" % (i, q), c) for q in range(3))
                    bb = self.vec("scb%d" % i, c)
                    fw.op("dve", lambda e, P=P, t=t, w0=w0, bb=bb: e.tensor_scalar(
                        out=t.t[:, 0:n], in0=P.t[:, 0:n], scalar1=w0, scalar2=bb, op0=ALU.mult, op1=ALU.add),
                          reads=[P.b], writes=[t.b])
                    fw.op("dve", lambda e, P=P, t=t, w1=w1: e.scalar_tensor_tensor(
                        out=t.t[:, 0:n], in0=P.t[:, 1:n + 1], scalar=w1, in1=t.t[:, 0:n], op0=ALU.mult, op1=ALU.add),
                          reads=[P.b, t.b], writes=[t.b])
                    fw.op("dve", lambda e, P=P, t=t, w2=w2: e.scalar_tensor_tensor(
                        out=t.t[:, 0:n], in0=P.t[:, 2:n + 2], scalar=w2, in1=t.t[:, 0:n], op0=ALU.mult, op1=ALU.add),
                          reads=[P.b, t.b], writes=[t.b])
                    fw.op("act", lambda e, t=t, c=c: e.activation(out=xo.t[:, c, 0:n], in_=t.t[:, 0:n], func=AF.Silu),
                          reads=[t.b], writes=[xo.b], waw=False)
                c0 = start + t0
                fw.dma(XBv[:, :, c0:c0 + n], xo.t[:, :, 0:n], xo.b, reads=[xo.b], writes=[self.db(self.XBCT)],
                       waw=False, stream="pool")
                for (s0, m) in blocks_of(n, 128):
                    z, dt_, da = zo[zi % 2], dto[zi % 2], dta[zi % 2]
                    zi += 1
                    for q in range(4):
                        Pz = pz[q % 2]
                        for k in range(KD):
                            fw.op("pe", lambda e, k=k, q=q, Pz=Pz: e.matmul(
                                Pz.t[0:m, :], lhsT=hT.t[:, k, 1 + s0:1 + s0 + m], rhs=W.t[:, k, q * 512:(q + 1) * 512],
                                start=(k == 0), stop=(k == KD - 1)), reads=[W.b, hT.b], writes=[Pz.b], ticket=(k == KD - 1))
                        if q % 2 == 0:
                            fw.op("act", lambda e, q=q, Pz=Pz: e.activation(out=z.t[0:m, q * 512:(q + 1) * 512],
                                                                            in_=Pz.t[0:m, :], func=AF.Copy),
                                  reads=[Pz.b], writes=[z.b], waw=False)
                        else:
                            fw.op("dve", lambda e, q=q, Pz=Pz: e.tensor_copy(out=z.t[0:m, q * 512:(q + 1) * 512],
                                                                             in_=Pz.t[0:m, :]),
                                  reads=[Pz.b], writes=[z.b], waw=False)
                    r0 = row0 + t0 + s0
                    fw.dma(self.ZT[r0:r0 + m, :], z.t[0:m, :], z.b, reads=[z.b], writes=[self.db(self.ZT)], waw=False,
                           stream="pool")
                    for k in range(KD):
                        fw.op("pe", lambda e, k=k: e.matmul(
                            pdt.t[0:m, 0:64], lhsT=hT.t[:, k, 1 + s0:1 + s0 + m], rhs=W.t[:, k, SSM_DI + SSM_XBC:SSM_IN],
                            start=(k == 0), stop=(k == KD - 1)), reads=[W.b, hT.b], writes=[pdt.b], ticket=(k == KD - 1))
                    fw.op("dve", lambda e: e.tensor_tensor(out=da.t[0:m, :], in0=pdt.t[0:m, 0:64], in1=dtb.t[0:m, :], op=ALU.add),
                          reads=[pdt.b, dtb.b], writes=[da.b])
                    fw.op("act", lambda e: e.activation(out=da.t[0:m, :], in_=da.t[0:m, :], func=AF.Exp),
                          reads=[da.b], writes=[da.b])
                    fw.op("act", lambda e: e.activation(out=dt_.t[0:m, :], in_=da.t[0:m, :], func=AF.Ln, bias=1.0),
                          reads=[da.b], writes=[dt_.b])
                    fw.dma(self.DTT[r0:r0 + m, :], dt_.t[0:m, :], dt_.b, reads=[dt_.b], writes=[self.db(self.DTT)],
                           waw=False, stream="pool")
        self.end()

    def ssm_scan(self, l, d):
        fw, cfg = self.fw, self.cfg
        i = l // 2
        NKT = cfg.NTOK // 128
        nct = cfg.C // 128
        self.begin()
        bc3 = lambda ap: ap.unsqueeze(2).broadcast_to([128, 8, 64])
        ML = self.cmask("ML_f" if d == 0 else "ML_b")
        MR = self.cmask("MR_f" if d == 0 else "MR_b")
        MRf = self.cmask("MR_f" if d == 0 else "MR_b", bf=False)
        arow = self.sb("arow", [128, 32])
        ro = self.roff["alog%d" % i][0] + 32 * d
        fw.dma(arow.t[:, :], self.i_rowv[0:1, ro:ro + 32].broadcast_to([128, 32]), arow.b, writes=[arow.b])
        fw.op("act", lambda e: e.activation(out=arow.t[:, :], in_=arow.t[:, :], func=AF.Exp), reads=[arow.b], writes=[arow.b])
        fw.op("dve", lambda e: e.tensor_scalar(out=arow.t[:, :], in0=arow.t[:, :], scalar1=-1.0, scalar2=None, op0=ALU.mult),
              reads=[arow.b], writes=[arow.b])
        if d == 1:
            drow = self.sb("drow", [128, 32])
            ro = self.roff["ssd%d" % i][0]
            fw.dma(drow.t[:, :], self.i_rowv[0:1, ro:ro + 32].broadcast_to([128, 32]), drow.b, writes=[drow.b])
            ng = self.sb("ng", [128, SSM_DI])
            ro = self.roff["sng%d" % i][0]
            fw.dma(ng.t[:, :], self.i_rowv[0:1, ro:ro + SSM_DI].broadcast_to([128, SSM_DI]), ng.b, writes=[ng.b])
        S = self.sb("S", [128, SSM_DI], F32)
        Sb = self.sb("Sb", [128, SSM_DI], BF16)
        fw.op("pool", lambda e: e.memset(S.t[:, :], 0.0), writes=[S.b])
        fw.op("pool", lambda e: e.memset(Sb.t[:, :], 0.0), writes=[Sb.b])
        xbcs = [self.sb("xbc", [128, 24, 128], BF16) for _ in range(2)]
        dts = [self.sb("dt", [128, 64], F32) for _ in range(2)]
        xtok = self.sb("xtok", [128, 2560], BF16)
        a = self.sb("a", [128, 32])
        ahb = self.sb("ahb", [128, 32], BF16)
        ah = self.sb("ah", [128, 32])
        al = self.sb("al", [128, 32])
        a2 = self.sb("a2", [128, 64], BF16)
        E = self.sb("E", [128, 96])
        e2 = self.sb("e2", [128, 32])
        cbm = self.sb("cbm", [128, 4, 128])
        Rhs = [self.sb("Rh", [128, 512], BF16) for _ in range(2)]
        Rls = [self.sb("Rl", [128, 512], BF16) for _ in range(2)]
        dec = [self.sb("dec", [128, 512]) for _ in range(2)]
        wT4 = [self.sb("wT4", [128, 512], BF16) for _ in range(2)]
        tmpy = self.sb("tmpy", [128, 512])
        xdt = self.sb("xdt", [128, SSM_DI], BF16)
        xwa = self.sb("xwa", [128, SSM_DI], BF16)
        yo = [self.sb("yo", [128, SSM_DI], F32) for _ in range(2)]
        pT = self.ps("pT", [128, 1024], BF16)
        pE = self.ps("pE", [128, 512])
        pcb = self.ps("pcb", [128, 512])
        pseg = [self.ps("pseg", [128, 512]) for _ in range(2)]
        pY = self.ps("pY", [128, 512])
        pYs = self.ps("pYs", [128, 512])
        pdS = self.ps("pdS", [128, 512])
        if d == 1:
            yfs = [self.sb("yf", [128, SSM_DI], F32) for _ in range(2)]
            zs = [self.sb("z", [128, SSM_DI], F32) for _ in range(2)]
            sz = self.sb("sz", [128, SSM_DI], F32)
            ssq = self.sb("ssq", [128, 1])
            sqj = self.sb("sqj", [128, SSM_DI], BF16)
            rs1, rt1 = self.sb("rs1", [128, 1]), self.sb("rt1", [128, 1])
            ygn = self.sb("ygn", [128, SSM_DI], BF16)
            ygT = [self.sb("ygT", [128, 16, 128], BF16) for _ in range(2)]
        XBv = self.XBCT.rearrange("(k p) n -> p k n", p=128)
        YGv = self.YGT.rearrange("(k p) n -> p k n", p=128)
        order = list(range(NKT)) if d == 0 else (list(range(nct - 1, -1, -1)) + list(range(NKT - 1, nct - 1, -1)))
        hi = 0
        for ci, kt in enumerate(order):
            xbc, dt_ = xbcs[ci % 2], dts[ci % 2]
            r0 = kt * 128
            c0 = (cfg.CS + r0) if kt < nct else (cfg.LS + r0 - cfg.C)
            fw.dma(xbc.t[:, :, :], XBv[:, :, c0:c0 + 128], xbc.b, reads=[self.db(self.XBCT)], writes=[xbc.b])
            fw.dma(dt_.t[:, :], self.DTT[r0:r0 + 128, :], dt_.b, reads=[self.db(self.DTT)], writes=[dt_.b])
            dtd = dt_.t[:, 32 * d:32 * d + 32]
            if d == 1:
                yf, z = yfs[ci % 2], zs[ci % 2]
                fw.dma(yf.t[:, :], self.YF[r0:r0 + 128, :], yf.b, reads=[self.db(self.YF)], writes=[yf.b])
                fw.dma(z.t[:, :], self.ZT[r0:r0 + 128, :], z.b, reads=[self.db(self.ZT)], writes=[z.b])
            for q in range(5):
                for j in range(4):
                    c = q * 4 + j
                    fw.op("pe", lambda e, c=c, j=j: e.transpose(pT.t[:, j * 128:(j + 1) * 128], xbc.t[:, c, :], self.ident_bf()),
                          reads=[xbc.b, self.cmb.b], writes=[pT.b], ticket=(j == 3), waw=(j == 0))
                if q % 2 == 0:
                    fw.op("act", lambda e, q=q: e.activation(out=xtok.t[:, q * 512:(q + 1) * 512], in_=pT.t[:, 0:512], func=AF.Copy),
                          reads=[pT.b], writes=[xtok.b], waw=False)
                else:
                    fw.op("dve", lambda e, q=q: e.tensor_copy(out=xtok.t[:, q * 512:(q + 1) * 512], in_=pT.t[:, 0:512]),
                          reads=[pT.b], writes=[xtok.b], waw=False)
            fw.op("dve", lambda e: e.tensor_tensor(out=a.t[:, :], in0=dtd, in1=arow.t[:, :], op=ALU.mult),
                  reads=[dt_.b, arow.b], writes=[a.b])
            fw.op("dve", lambda e: e.tensor_copy(out=ahb.t[:, :], in_=a.t[:, :]), reads=[a.b], writes=[ahb.b])
            fw.op("dve", lambda e: e.tensor_copy(out=ah.t[:, :], in_=ahb.t[:, :]), reads=[ahb.b], writes=[ah.b])
            fw.op("dve", lambda e: e.tensor_tensor(out=al.t[:, :], in0=a.t[:, :], in1=ah.t[:, :], op=ALU.subtract),
                  reads=[a.b, ah.b], writes=[al.b])
            fw.op("dve", lambda e: e.tensor_copy(out=a2.t[:, 0:32], in_=ah.t[:, :]), reads=[ah.b], writes=[a2.b])
            fw.op("dve", lambda e: e.tensor_copy(out=a2.t[:, 32:64], in_=al.t[:, :]), reads=[al.b], writes=[a2.b], waw=False)
            for q, lhs in enumerate((MR, ML, self.ones_bf.t[:, :])):
                for hl in range(2):
                    fw.op("pe", lambda e, q=q, lhs=lhs, hl=hl: e.matmul(
                        pE.t[:, q * 32:(q + 1) * 32], lhsT=lhs, rhs=a2.t[:, hl * 32:(hl + 1) * 32],
                        start=(hl == 0), stop=(hl == 1)), reads=[a2.b, self.cmb.b, self.ones_bf.b], writes=[pE.b],
                          ticket=(q == 2 and hl == 1), waw=(q == 0 and hl == 0))
            fw.op("act", lambda e: e.activation(out=E.t[:, :], in_=pE.t[:, 0:96], func=AF.Exp), reads=[pE.b], writes=[E.b])
            fw.op("dve", lambda e: e.tensor_tensor(out=e2.t[:, :], in0=E.t[:, 32:64], in1=dtd, op=ALU.mult),
                  reads=[E.b, dt_.b], writes=[e2.b])
            for g in range(4):
                fw.op("pe", lambda e, g=g: e.matmul(pcb.t[:, g * 128:(g + 1) * 128], lhsT=xbc.t[:, 16 + g, :],
                                                    rhs=xbc.t[:, 20 + g, :], start=True, stop=True),
                      reads=[xbc.b], writes=[pcb.b], ticket=(g == 3), waw=(g == 0))
            for g in range(4):
                fw.op("dve", lambda e, g=g: e.tensor_tensor(out=cbm.t[:, g, :], in0=pcb.t[:, g * 128:(g + 1) * 128],
                                                            in1=MRf, op=ALU.mult),
                      reads=[pcb.b, self.cm.b], writes=[cbm.b], waw=(g == 0))
            x3 = xtok.t[:, 0:SSM_DI].rearrange("p (h c) -> p h c", c=64)
            fw.op("pool", lambda e: e.tensor_tensor(out=xdt.t[:, :].rearrange("p (h c) -> p h c", c=64), in0=x3,
                                                    in1=dtd.unsqueeze(2).broadcast_to([128, 32, 64]), op=ALU.mult),
                  reads=[xtok.b, dt_.b], writes=[xdt.b])
            fw.op("pool", lambda e: e.tensor_tensor(out=xwa.t[:, :].rearrange("p (h c) -> p h c", c=64), in0=x3,
                                                    in1=e2.t[:, :].unsqueeze(2).broadcast_to([128, 32, 64]), op=ALU.mult),
                  reads=[xtok.b, e2.b], writes=[xwa.b])
            for g in range(4):
                for hq in range(2):
                    ps_ = pseg[hi % 2]
                    dc = dec[hi % 2]
                    Rh, Rl, w4 = Rhs[hi % 2], Rls[hi % 2], wT4[hi % 2]
                    hi += 1
                    h0 = g * 8 + hq * 4
                    fw.op("dve", lambda e, h0=h0, Rh=Rh: e.tensor_tensor(
                        out=Rh.t[:, :].rearrange("p (r c) -> p r c", c=128),
                        in0=MR.unsqueeze(1).broadcast_to([128, 4, 128]),
                        in1=ah.t[:, h0:h0 + 4].unsqueeze(2).broadcast_to([128, 4, 128]), op=ALU.mult),
                          reads=[ah.b, self.cmb.b], writes=[Rh.b])
                    for r4 in range(4):
                        fw.op("act", lambda e, h0=h0, r4=r4, Rl=Rl: e.activation(
                            out=Rl.t[:, r4 * 128:(r4 + 1) * 128], in_=MR, func=AF.Copy, scale=al.t[:, h0 + r4:h0 + r4 + 1]),
                              reads=[al.b, self.cmb.b], writes=[Rl.b], waw=(r4 == 0))
                    fw.op("pe", lambda e, Rh=Rh, ps_=ps_: e.matmul(ps_.t[:, :], lhsT=ML, rhs=Rh.t[:, :], start=True, stop=False),
                          reads=[Rh.b, self.cmb.b], writes=[ps_.b], ticket=False)
                    fw.op("pe", lambda e, Rl=Rl, ps_=ps_: e.matmul(ps_.t[:, :], lhsT=ML, rhs=Rl.t[:, :], start=False, stop=True),
                          reads=[Rl.b, Rh.b, self.cmb.b], writes=[ps_.b])
                    fw.op("act", lambda e, ps_=ps_, dc=dc: e.activation(out=dc.t[:, :], in_=ps_.t[:, :], func=AF.Exp),
                          reads=[ps_.b], writes=[dc.b])
                    fw.op("dve", lambda e, w4=w4, dc=dc, g=g: e.tensor_tensor(
                        out=w4.t[:, :].rearrange("p (r c) -> p r c", c=128), in0=dc.t[:, :].rearrange("p (r c) -> p r c", c=128),
                        in1=cbm.t[:, g, :].unsqueeze(1).broadcast_to([128, 4, 128]), op=ALU.mult),
                          reads=[dc.b, cbm.b], writes=[w4.b])
                    for r4 in range(4):
                        r = hq * 4 + r4
                        h = g * 8 + r
                        fw.op("pe", lambda e, r=r, r4=r4, h=h, w4=w4: e.matmul(
                            pY.t[:, r * 64:(r + 1) * 64], lhsT=w4.t[:, r4 * 128:(r4 + 1) * 128], rhs=xdt.t[:, h * 64:(h + 1) * 64],
                            start=True, stop=True), reads=[w4.b, xdt.b], writes=[pY.b], ticket=(r4 == 3), waw=(r == 0))
                fw.op("pe", lambda e, g=g: e.matmul(pYs.t[:, :], lhsT=xbc.t[:, 20 + g, :], rhs=Sb.t[:, g * 512:(g + 1) * 512],
                                                    start=True, stop=True), reads=[xbc.b, Sb.b], writes=[pYs.b])
                fw.op("dve", lambda e, g=g: e.tensor_tensor(
                    out=tmpy.t[:, :].rearrange("p (r c) -> p r c", c=64), in0=pYs.t[:, :].rearrange("p (r c) -> p r c", c=64),
                    in1=bc3(E.t[:, g * 8:(g + 1) * 8]), op=ALU.mult), reads=[pYs.b, E.b], writes=[tmpy.b])
                yo_ = yo[ci % 2]
                gs_ = slice(g * 512, (g + 1) * 512)
                fw.op("dve", lambda e, yo_=yo_, gs_=gs_: e.tensor_tensor(out=yo_.t[:, gs_], in0=pY.t[:, :], in1=tmpy.t[:, :],
                                                                        op=ALU.add),
                      reads=[pY.b, tmpy.b], writes=[yo_.b], waw=(g == 0))
                fw.op("pe", lambda e, g=g, gs_=gs_: e.matmul(
                    pdS.t[:, :], lhsT=xtok.t[:, 2048 + g * 128:2048 + (g + 1) * 128], rhs=xwa.t[:, gs_], start=True, stop=True),
                      reads=[xtok.b, xwa.b], writes=[pdS.b])
                fw.op("dve", lambda e, g=g, gs_=gs_: e.tensor_tensor(
                    out=S.t[:, gs_].rearrange("p (r c) -> p r c", c=64), in0=S.t[:, gs_].rearrange("p (r c) -> p r c", c=64),
                    in1=bc3(E.t[:, 64 + g * 8:64 + (g + 1) * 8]), op=ALU.mult), reads=[S.b, E.b], writes=[S.b])
                fw.op("dve", lambda e, gs_=gs_: e.tensor_tensor(out=S.t[:, gs_], in0=S.t[:, gs_], in1=pdS.t[:, :], op=ALU.add),
                      reads=[S.b, pdS.b], writes=[S.b])
                fw.op("act", lambda e, gs_=gs_: e.activation(out=Sb.t[:, gs_], in_=S.t[:, gs_], func=AF.Copy),
                      reads=[S.b], writes=[Sb.b])
            yo_ = yo[ci % 2]
            if d == 0:
                fw.dma(self.YF[r0:r0 + 128, :], yo_.t[:, :], yo_.b, reads=[yo_.b], writes=[self.db(self.YF)], waw=False,
                       stream="pool")
            else:
                fw.op("pool", lambda e: e.tensor_tensor(out=yo_.t[:, :], in0=yo_.t[:, :], in1=yf.t[:, :], op=ALU.add),
                      reads=[yo_.b, yf.b], writes=[yo_.b])
                for g in range(4):
                    gs_ = slice(g * 512, (g + 1) * 512)
                    fw.op("dve", lambda e, g=g, gs_=gs_: e.tensor_tensor(
                        out=tmpy.t[:, :].rearrange("p (r c) -> p r c", c=64),
                        in0=xtok.t[:, gs_].rearrange("p (r c) -> p r c", c=64),
                        in1=bc3(drow.t[:, g * 8:(g + 1) * 8]), op=ALU.mult), reads=[xtok.b, drow.b], writes=[tmpy.b])
                    fw.op("dve", lambda e, gs_=gs_: e.tensor_tensor(out=yo_.t[:, gs_], in0=yo_.t[:, gs_], in1=tmpy.t[:, :],
                                                                    op=ALU.add), reads=[yo_.b, tmpy.b], writes=[yo_.b])
                fw.op("act", lambda e: e.activation(out=sz.t[:, :], in_=z.t[:, :], func=AF.Silu), reads=[z.b], writes=[sz.b])
                fw.op("dve", lambda e: e.tensor_tensor(out=yo_.t[:, :], in0=yo_.t[:, :], in1=sz.t[:, :], op=ALU.mult),
                      reads=[yo_.b, sz.b], writes=[yo_.b])
                fw.op("pool", lambda e: e.memset(ssq.t[:, :], 0.0), writes=[ssq.b])
                fw.op("act", lambda e: e.activation(out=sqj.t[:, :], in_=yo_.t[:, :], func=AF.Square, accum_out=ssq.t[:, :]),
                      reads=[yo_.b], writes=[sqj.b, ssq.b])
                fw.op("act", lambda e: e.activation(out=rt1.t[:, :], in_=ssq.t[:, :], func=AF.Sqrt, scale=1.0 / SSM_DI, bias=EPS),
                      reads=[ssq.b], writes=[rt1.b])
                fw.op("dve", lambda e: e.reciprocal(out=rs1.t[:, :], in_=rt1.t[:, :]), reads=[rt1.b], writes=[rs1.b])
                fw.op("dve", lambda e: e.scalar_tensor_tensor(out=ygn.t[:, :], in0=yo_.t[:, :], scalar=rs1.t[:, 0:1],
                                                              in1=ng.t[:, :], op0=ALU.mult, op1=ALU.mult),
                      reads=[yo_.b, rs1.b, ng.b], writes=[ygn.b])
                yt = ygT[ci % 2]
                for q in range(4):
                    for j in range(4):
                        c = q * 4 + j
                        fw.op("pe", lambda e, c=c, j=j: e.transpose(pT.t[:, j * 128:(j + 1) * 128],
                                                                    ygn.t[:, c * 128:(c + 1) * 128], self.ident_bf()),
                              reads=[ygn.b, self.cmb.b], writes=[pT.b], ticket=(j == 3), waw=(j == 0))
                    if q % 2 == 0:
                        fw.op("act", lambda e, q=q: e.activation(out=yt.t[:, q * 4:(q + 1) * 4, :],
                                                                 in_=pT.t[:, 0:512].rearrange("p (a b) -> p a b", b=128), func=AF.Copy),
                              reads=[pT.b], writes=[yt.b], waw=False)
                    else:
                        fw.op("dve", lambda e, q=q: e.tensor_copy(out=yt.t[:, q * 4:(q + 1) * 4, :],
                                                                  in_=pT.t[:, 0:512].rearrange("p (a b) -> p a b", b=128)),
                              reads=[pT.b], writes=[yt.b], waw=False)
                fw.dma(YGv[:, :, c0:c0 + 128], yt.t[:, :, :], yt.b, reads=[yt.b], writes=[self.db(self.YGT)], waw=False,
                       stream="pool")
        self.end()


def build_program(T, voff, roff, nv, nr, n_layers=DEPTH, debug=False):
    p = Prog(T, voff, roff, nv, nr, n_layers=n_layers, debug=debug)
    p.setup()
    p.mod_phase()
    Xa, Xb = p.XA, p.XB
    for l in range(n_layers):
        if l % 2 == 0:
            i = l // 2
            p.attn_inproj(l, Xa)
            p.attn_core(l)
            p.proj_res_norm(l, Xa, Xb, p.OT, KD, lambda k, i=i: p.i_awo[i, k * 128:(k + 1) * 128, :])
        else:
            i = l // 2
            p.ssm_inproj(l, Xa)
            p.ssm_scan(l, 0)
            p.ssm_scan(l, 1)
            p.proj_res_norm(l, Xa, Xb, p.YGT, SSM_DI // 128, lambda k, i=i: p.i_swo[i, k * 128:(k + 1) * 128, :])
        p.ffn_in(l)
        p.ffn_out(l, Xb, Xa)
    p.final_norm(Xa)
    p.pes.close()
    p.fw.close()
    return p


_CACHE = {}


def run(inputs, T=8192, n_layers=DEPTH, debug=False, cores=NCORES):
    cfg, voff, roff, shared = prep_shared(inputs, T)
    key = (T, n_layers, debug)
    if key not in _CACHE:
        _CACHE[key] = build_program(T, voff, roff, shared["vecs"].shape[1], shared["rowv"].shape[1],
                                    n_layers=n_layers, debug=debug)
    p = _CACHE[key]
    in_maps = []
    for b in range(cores):
        m = dict(shared)
        m.update(prep_core(inputs, b, T))
        in_maps.append(m)
    res = run_bass_kernel_spmd(p.nc, in_maps, core_ids=list(range(cores)))
    return p, res


def kernel(**inputs):
    p, res = run(inputs)
    out = np.stack([np.ascontiguousarray(r["outT"].T) for r in res.results], axis=0)
    return out.astype(np.float32)
```

```python
import math
from contextlib import ExitStack

import numpy as np
import concourse.bass as bass
import concourse.mybir as mybir
from concourse.bass_utils import run_bass_kernel_spmd

F32 = mybir.dt.float32
BF16 = mybir.dt.bfloat16
AF = mybir.ActivationFunctionType
ALU = mybir.AluOpType


class Sem:
    def __init__(self, h, name):
        self.h = h
        self.cnt = 0
        self.name = name


class Buf:
    def __init__(self, ap=None, name=""):
        self.ap = ap
        self.name = name
        self.last_w = {}
        self.readers = {}
        self.gen_deps = {}
        self.dsem = None


def _merge(dst, src):
    for s, v in src.items():
        if dst.get(s, 0) < v:
            dst[s] = v


class FW:
    def __init__(self, nc, n_dma_sems=48, n_spare=28):
        self.nc = nc
        self.es = ExitStack()
        self.streams = {
            "pe": nc.tensor,
            "act": nc.scalar,
            "dve": nc.vector,
            "pool": nc.gpsimd,
            "sp": nc.sync,
        }
        self.esem = {}
        for k in ("pe", "act", "dve", "pool"):
            self.esem[k] = Sem(self.es.enter_context(nc.semaphore("e_" + k)), k)
        self.known = {k: {} for k in self.streams}
        self.free_dsems = [
            Sem(self.es.enter_context(nc.semaphore("d%d" % i)), "d%d" % i) for i in range(n_dma_sems)
        ]
        self.spare = [Sem(self.es.enter_context(nc.semaphore("x%d" % i)), "x%d" % i) for i in range(n_spare)]
        self.phase_bufs = []
        self.pers_bufs = []
        self.n_inst = 0
        self.n_wait = 0

    def buf(self, ap=None, name="", persistent=False):
        b = Buf(ap, name)
        (self.pers_bufs if persistent else self.phase_bufs).append(b)
        return b

    def _wait_for(self, stream, deps, fold=False):
        eng = self.streams[stream]
        kn = self.known[stream]
        need = [(s, v) for s, v in deps.items() if kn.get(s, 0) < v]
        last = None
        if fold and need:
            last = need.pop()
            kn[last[0]] = last[1]
        for s, v in need:
            eng.wait_ge(s.h, v)
            kn[s] = v
            self.n_wait += 1
        return last

    def _deps(self, stream, reads, writes, waw=True):
        deps = {}
        for r in reads:
            _merge(deps, r.last_w)
        for w in writes:
            if waw or w.readers:
                _merge(deps, w.last_w)
                _merge(deps, w.readers)
            else:
                _merge(deps, w.gen_deps)
        if stream == "pe":
            deps.pop(self.esem["pe"], None)
        return deps

    def _commit(self, t, reads, writes, waw):
        for w in writes:
            if waw or w.readers:
                g = dict(w.last_w)
                _merge(g, w.readers)
                w.gen_deps = g
                w.last_w = dict(t)
                w.readers = {}
            else:
                _merge(w.last_w, t)
        for r in reads:
            _merge(r.readers, t)

    def op(self, stream, fn, reads=(), writes=(), ticket=True, waw=True):
        deps = self._deps(stream, reads, writes, waw)
        last = self._wait_for(stream, deps, fold=True)
        inst = fn(self.streams[stream])
        if last is not None:
            inst._wait_ge(last[0].h, last[1])
        self.n_inst += 1
        if ticket:
            s = self.esem[stream]
            if s.cnt >= 30000:
                s = self.spare.pop()
                self.esem[stream] = s
            s.cnt += 1
            inst.then_inc(s.h, 1)
            self._commit({s: s.cnt}, reads, writes, waw)
        return inst

    def dma(self, out_ap, in_ap, owner, reads=(), writes=(), stream="sp", waw=True, **kw):
        deps = self._deps(stream, reads, writes, waw)
        last = self._wait_for(stream, deps, fold=True)
        if owner.dsem is None:
            owner.dsem = self.free_dsems.pop(0)
        s = owner.dsem
        s.cnt += 16
        inst = self.streams[stream].dma_start(out=out_ap, in_=in_ap, **kw)
        if last is not None:
            inst._wait_ge(last[0].h, last[1])
        inst.then_inc(s.h, 16)
        self.n_inst += 1
        self._commit({s: s.cnt}, reads, writes, waw)

    def barrier(self, clear=True):
        alld = {}
        for s in self.esem.values():
            if s.cnt:
                alld[s] = s.cnt
        for b in self.phase_bufs + self.pers_bufs:
            if b.dsem is not None:
                alld[b.dsem] = b.dsem.cnt
        for st in self.streams:
            d = dict(alld)
            self._wait_for(st, d)
        for b in self.phase_bufs + self.pers_bufs:
            if b.dsem is not None:
                self.free_dsems.append(b.dsem)
                b.dsem = None
            b.last_w = {}
            b.readers = {}
            b.gen_deps = {}
        if clear:
            self.phase_bufs = []

    def close(self):
        self.es.close()


D = 1024
KD = D // 128
DEPTH = 4
CTX = 256
GRID_W = 64
HD = 64
EPS = 1e-6
DFF = 2816
NFF = DFF // 128
SSM_DI = 2048
SSM_XBC = 3072
SSM_IN = 5184
SSM_H = 32
NCORES = 8
ATT_FM = 1664
ATT_EXT = 2 * ATT_FM + 640


class Cfg:
    def __init__(self, T):
        self.T = T
        self.C = CTX
        self.CS = 1
        self.LS = CTX + 3
        self.NP = CTX + T + 4
        self.NTOK = CTX + T
        self.seqs = [("ctx", self.CS, CTX, 1, 0), ("lat", self.LS, T, 0, CTX)]


def _partner():
    p = np.arange(64)
    return np.where((p % 32) < 16, p + 16, p - 16)


class VecLayout:
    def __init__(self):
        self.off = {}
        self.n = 0
        self.cols = []

    def add(self, name, arr):
        arr = np.ascontiguousarray(arr, dtype=np.float32).reshape(128, -1)
        self.off[name] = (self.n, arr.shape[1])
        self.n += arr.shape[1]
        self.cols.append(arr)

    def build(self):
        return np.ascontiguousarray(np.concatenate(self.cols, axis=1))


def fm(v):
    v = np.asarray(v, dtype=np.float32)
    return np.ascontiguousarray(v.reshape(-1, 128).T)


def prep_shared(inp, T):
    cfg = Cfg(T)
    vl = VecLayout()
    rows = {}
    for l in range(DEPTH):
        vl.add("modb%d" % l, fm(inp["mod_b"][l]))
        vl.add("gmix%d" % l, fm(inp["norm_mix_g"][l]))
        vl.add("gffn%d" % l, fm(inp["norm_ffn_g"][l]))
        cw = inp["ffn_conv_w"][l]
        for i in range(3):
            vl.add("fcw%d_%d" % (l, i), fm(cw[i]))
        vl.add("fcb%d" % l, fm(inp["ffn_conv_b"][l]))
    vl.add("gfin", fm(inp["final_norm_g"]))
    pt = _partner()
    for i in range(DEPTH // 2 + DEPTH % 2):
        gq = np.asarray(inp["gqa_q_norm_g"][i], np.float32)
        gk = np.asarray(inp["gqa_k_norm_g"][i], np.float32)
        vl.add("gq%d" % i, np.concatenate([gq, gq])[:, None])
        vl.add("gqs%d" % i, np.concatenate([gq[pt], gq[pt]])[:, None])
        vl.add("gk%d" % i, np.concatenate([gk, gk])[:, None])
        vl.add("gks%d" % i, np.concatenate([gk[pt], gk[pt]])[:, None])
        vl.add("subg%d" % i, np.asarray(inp["diff_subln_g"][i], np.float32)[:, None])
    for i in range(DEPTH // 2):
        cw = inp["ssm_conv_w"][i]
        for j in range(3):
            vl.add("scw%d_%d" % (i, j), fm(cw[j]))
        vl.add("scb%d" % i, fm(inp["ssm_conv_b"][i]))
    vecs = vl.build()

    rl = VecLayout()

    def radd(name, v):
        v = np.asarray(v, np.float32).reshape(1, -1)
        rl.off[name] = (rl.n, v.shape[1])
        rl.n += v.shape[1]
        rl.cols.append(v)

    for i in range((DEPTH + 1) // 2):
        radd("lq1_%d" % i, inp["diff_lq1"][i])
        radd("lk1_%d" % i, inp["diff_lk1"][i])
        radd("lq2_%d" % i, inp["diff_lq2"][i])
        radd("lk2_%d" % i, inp["diff_lk2"][i])
    for i in range(DEPTH // 2):
        radd("dtb%d" % i, inp["ssm_dt_bias"][i])
        radd("alog%d" % i, inp["ssm_a_log"][i])
        radd("ssd%d" % i, inp["ssm_d"][i])
        radd("sng%d" % i, inp["ssm_norm_g"][i])
    rowv = np.ascontiguousarray(np.concatenate(rl.cols, axis=1))

    w_ext = []
    for i in range((DEPTH + 1) // 2):
        w = np.asarray(inp["attn_w_in"][i])
        qa, ka, va = w[:, 0:512], w[:, 512:1024], w[:, 1024:1536]
        qb, kb, vb = w[:, 1536:2048], w[:, 2048:2176], w[:, 2176:2304]
        qbp = np.concatenate(
            [np.concatenate([qb[:, j * 64:(j + 1) * 64], qb[:, (4 + j) * 64:(5 + j) * 64]], axis=1) for j in range(4)],
            axis=1)
        fmw = np.concatenate([qa, ka, qbp, kb], axis=1)
        idx = (np.arange(ATT_FM) // 64) * 64 + pt[np.arange(ATT_FM) % 64]
        w_ext.append(np.concatenate([fmw, fmw[:, idx], va, vb], axis=1))
    w_ext = np.ascontiguousarray(np.stack(w_ext))

    NP = cfg.NP
    cosT = np.ones((128, NP), np.float32)
    sinT = np.zeros((128, NP), np.float32)
    t = np.arange(T)
    row = (t // GRID_W).astype(np.float32)
    col = (t % GRID_W).astype(np.float32)
    inv = (1.0 / (10000.0 ** (np.arange(16, dtype=np.float32) / 16))).astype(np.float32)
    for p in range(64):
        q, i = p // 16, p % 16
        ang = ((row if q < 2 else col) * inv[i]).astype(np.float32)
        sgn = -1.0 if q in (0, 2) else 1.0
        for rep in (0, 64):
            cosT[p + rep, cfg.LS:cfg.LS + T] = np.cos(ang)
            sinT[p + rep, cfg.LS:cfg.LS + T] = sgn * np.sin(ang)

    tt = np.arange(128)
    consts = {
        "ident": np.eye(128, dtype=np.float32),
        "ML_f": (tt[:, None] > tt[None, :]).astype(np.float32),
        "MR_f": (tt[:, None] <= tt[None, :]).astype(np.float32),
        "ML_b": (tt[:, None] < tt[None, :]).astype(np.float32),
        "MR_b": (tt[:, None] >= tt[None, :]).astype(np.float32),
    }
    bd = np.zeros((128, 128), np.float32)
    bd[:64, :64] = 1
    bd[64:, 64:] = 1
    consts["bd"] = bd
    cmat = np.ascontiguousarray(
        np.concatenate([consts[k] for k in ("ident", "ML_f", "MR_f", "ML_b", "MR_b", "bd")], axis=1))

    shared = {
        "vecs": vecs, "rowv": rowv, "w_ext": w_ext, "cosT": cosT, "sinT": sinT, "cmat": cmat,
        "mod_w": np.asarray(inp["mod_w"], np.float32),
        "attn_w_out": np.asarray(inp["attn_w_out"], np.float32),
        "ssm_w_in": np.asarray(inp["ssm_w_in"], np.float32),
        "ssm_w_out": np.asarray(inp["ssm_w_out"], np.float32),
        "ffn_w_in": np.asarray(inp["ffn_w_in"], np.float32),
        "ffn_w_out": np.asarray(inp["ffn_w_out"], np.float32),
    }
    return cfg, vl.off, rl.off, shared


def prep_core(inp, b, T):
    xT = np.ascontiguousarray(np.asarray(inp["x"][b, :T]).T)
    cxT = np.ascontiguousarray(np.asarray(inp["ctx"][b]).T)
    cT = np.stack([fm(inp["c"][b]), fm(inp["c_ctx"])], axis=2)
    return {"xT": xT, "cxT": cxT, "cT": np.ascontiguousarray(cT.reshape(128, 16))}


class TB:
    def __init__(self, t, b, bs=None):
        self.t = t
        self.b = b
        self.bs = bs


def blocks_of(n, size):
    out = []
    t0 = 0
    while t0 < n:
        out.append((t0, min(size, n - t0)))
        t0 += size
    return out


class Prog:
    def __init__(self, T, voff, roff, nv, nr, n_layers=DEPTH, debug=False):
        self.cfg = Cfg(T)
        self.voff, self.roff = voff, roff
        self.debug = debug
        self.n_layers = n_layers
        nc = bass.Bass("TRN2", target_bir_lowering=False)
        self.nc = nc
        self.fw = FW(nc)
        cfg = self.cfg
        NP, NTOK = cfg.NP, cfg.NTOK
        di = lambda name, shape, dt=F32: nc.dram_tensor(name, shape, dt, kind="ExternalInput").ap()
        self.i_xT = di("xT", [D, T])
        self.i_cxT = di("cxT", [D, CTX])
        self.i_cT = di("cT", [128, 16])
        self.i_vecs = di("vecs", [128, nv])
        self.i_rowv = di("rowv", [1, nr])
        self.i_wext = di("w_ext", [(DEPTH + 1) // 2, D, ATT_EXT])
        self.i_cos = di("cosT", [128, NP])
        self.i_sin = di("sinT", [128, NP])
        self.i_cmat = di("cmat", [128, 768])
        self.i_modw = di("mod_w", [DEPTH, D, 6 * D])
        self.i_awo = di("attn_w_out", [(DEPTH + 1) // 2, D, D])
        self.i_swi = di("ssm_w_in", [DEPTH // 2, D, SSM_IN])
        self.i_swo = di("ssm_w_out", [DEPTH // 2, SSM_DI, D])
        self.i_fwi = di("ffn_w_in", [DEPTH, D, 2 * DFF])
        self.i_fwo = di("ffn_w_out", [DEPTH, DFF, D])
        self.o_out = nc.dram_tensor("outT", [D, T], F32, kind="ExternalOutput").ap()
        self.dbg = {}
        kind = "ExternalOutput" if debug else "Internal"

        def scr(name, shape, dt):
            ap = nc.dram_tensor(name, shape, dt, kind=kind).ap()
            if debug:
                self.dbg[name] = ap
            return ap

        self.XA = scr("XA", [D, NP], F32)
        self.XB = scr("XB", [D, NP], F32)
        self.HT = scr("HT", [D, NP], BF16)
        self.QT = scr("QT", [D, NP], BF16)
        self.KT = scr("KT", [5 * 128, NP], BF16)
        self.VT = scr("VT", [NTOK, 640], BF16)
        self.OT = scr("OT", [D, NP], BF16)
        self.UT = scr("UT", [DFF, NP], BF16)
        self.XBCT = scr("XBCT", [SSM_XBC, NP], BF16)
        self.ZT = scr("ZT", [NTOK, SSM_DI], F32)
        self.DTT = scr("DTT", [NTOK, 64], F32)
        self.YF = scr("YF", [NTOK, SSM_DI], F32)
        self.YGT = scr("YGT", [SSM_DI, NP], BF16)
        self.dram_b = {}
        self.pes = ExitStack()
        self.ph = None
        self._names = 0

    def _nm(self, name):
        self._names += 1
        return "%s_%d" % (name, self._names)

    def sb(self, name, shape, dt=F32, nb=0, pers=False):
        st = self.pes if pers else self.ph
        t = st.enter_context(self.nc.sbuf_tensor(self._nm(name), shape, dt))
        b = self.fw.buf(name=name, persistent=pers)
        bs = [self.fw.buf(name=name + str(i), persistent=pers) for i in range(nb)] if nb else None
        return TB(t, b, bs)

    def ps(self, name, shape, dt=F32):
        t = self.ph.enter_context(self.nc.psum_tensor(self._nm(name), shape, dt))
        return TB(t, self.fw.buf(name=name))

    def db(self, ap):
        k = ap.name if hasattr(ap, "name") else id(ap)
        if k not in self.dram_b:
            self.dram_b[k] = self.fw.buf(name="dram", persistent=True)
        return self.dram_b[k]

    def begin(self):
        self.ph = ExitStack()

    def end(self):
        self.fw.barrier()
        self.ph.close()
        self.ph = None

    def vec(self, name, j=0, w=1):
        o, n = self.voff[name]
        return self.vecs.t[:, o + j:o + j + w]

    def load_w(self, dst, src_rows, ncols, col0=0, dst_col0=0, kc=None):
        fw = self.fw
        kc = kc if kc is not None else dst.t.shape[1]
        PIECE = 1408
        main_ph = self.ph
        self.ph = ExitStack()
        wst = [self.sb("wst", [128, PIECE], F32) for _ in range(3)]
        wi = 0
        for k in range(kc):
            for (c0, n) in blocks_of(ncols, PIECE):
                st = wst[wi % 3]
                eng = ("dve", "pool", "act")[wi % 3]
                wi += 1
                src = src_rows(k)[:, col0 + c0:col0 + c0 + n]
                fw.dma(st.t[:, 0:n], src, st.b, writes=[st.b])
                d = dst.t[:, k, dst_col0 + c0:dst_col0 + c0 + n]
                if eng == "act":
                    fw.op("act", lambda e, d=d, st=st, n=n: e.activation(out=d, in_=st.t[:, 0:n], func=AF.Copy),
                          reads=[st.b], writes=[dst.b], waw=False)
                else:
                    fw.op(eng, lambda e, d=d, st=st, n=n: e.tensor_copy(out=d, in_=st.t[:, 0:n]),
                          reads=[st.b], writes=[dst.b], waw=False)
        fw.barrier(clear=False)
        self.ph.close()
        self.ph = main_ph

    def rstd(self, ss, out, tmp, scale, n):
        fw = self.fw
        fw.op("act", lambda e: e.activation(out=tmp.t[:, 0:n], in_=ss.t[:, 0:n], func=AF.Sqrt, scale=scale, bias=EPS),
              reads=[ss.b], writes=[tmp.b])
        fw.op("dve", lambda e: e.reciprocal(out=out.t[:, 0:n], in_=tmp.t[:, 0:n]), reads=[tmp.b], writes=[out.b])

    def norm_mod(self, x, n, ss, sq, rs, tmp, gs, sh, out, out_dt_bf16=True):
        fw = self.fw
        for k in range(KD):
            fw.op("act", lambda e, k=k: e.activation(out=sq.t[:, k, 0:n], in_=x.t[:, k, 0:n], func=AF.Square),
                  reads=[x.b], writes=[sq.b], waw=False)
        for k in range(KD):
            fw.op("pe", lambda e, k=k: e.matmul(ss.t[:, 0:n], lhsT=self.ones_bf.t[:, :], rhs=sq.t[:, k, 0:n],
                                                start=(k == 0), stop=(k == KD - 1)),
                  reads=[sq.b, self.ones_bf.b], writes=[ss.b], ticket=(k == KD - 1))
        self.rstd(ss, rs, tmp, 1.0 / D, n)
        for k in range(KD):
            if sh is None:
                fw.op("dve", lambda e, k=k: e.scalar_tensor_tensor(
                    out=out.t[:, k, 0:n], in0=x.t[:, k, 0:n], scalar=gs(k), in1=rs.t[:, 0:n],
                    op0=ALU.mult, op1=ALU.mult), reads=[x.b, rs.b], writes=[out.b], waw=False)
            else:
                tm = self._nm_tmp[k % 2]
                fw.op("dve", lambda e, k=k, tm=tm: e.scalar_tensor_tensor(
                    out=tm.t[:, 0:n], in0=x.t[:, k, 0:n], scalar=gs(k), in1=rs.t[:, 0:n],
                    op0=ALU.mult, op1=ALU.mult), reads=[x.b, rs.b], writes=[tm.b])
                fw.op("act", lambda e, k=k, tm=tm: e.activation(out=out.t[:, k, 0:n], in_=tm.t[:, 0:n],
                                                         func=AF.Identity, bias=sh(k), scale=1.0),
                      reads=[tm.b], writes=[out.b], waw=False)

    def norm_tiles(self, nmax=512):
        self._nm_tmp = [self.sb("nmt", [128, nmax], F32) for _ in range(2)]
        return dict(ss=self.ps("ss", [128, 512], F32), sq=self.sb("sq", [128, KD, nmax], BF16),
                    rs=self.sb("rs", [128, nmax], F32), tmp=self.sb("rtmp", [128, nmax], F32))

    def load_cols(self, dst, dview, seq, t0, n, halo, extra_reads=()):
        fw = self.fw
        _, start, ln, _, _ = seq
        lo, hi = t0 - halo, t0 + n + halo
        clo, chi = max(lo, 0), min(hi, ln)
        if clo > lo:
            fw.op("pool", lambda e: e.memset(dst.t[:, :, 0:clo - lo], 0.0), writes=[dst.b])
        if chi < hi:
            fw.op("pool", lambda e: e.memset(dst.t[:, :, chi - lo:hi - lo], 0.0), writes=[dst.b],
                  waw=(clo == lo))
        fw.dma(dst.t[:, :, clo - lo:chi - lo], dview[:, :, start + clo:start + chi], dst.b,
               reads=[self.db(dview)], writes=[dst.b], waw=(clo == lo and chi == hi))

    def dvv(self, l, v, which):
        base = ((l * 2 + v) * 6 + which) * 8
        return lambda k: self.dv.t[:, base + k:base + k + 1]

    def setup(self):
        fw, cfg = self.fw, self.cfg
        nv = self.i_vecs.shape[1]
        self.vecs = self.sb("vecs", [128, nv], F32, pers=True)
        self.cm = self.sb("cmat", [128, 768], F32, pers=True)
        self.cmb = self.sb("cmatb", [128, 768], BF16, pers=True)
        self.ones_bf = self.sb("ones", [128, 128], BF16, pers=True)
        self.dv = self.sb("dv", [128, DEPTH * 2 * 6 * 8], F32, pers=True)
        self.begin()
        fw.dma(self.vecs.t[:, :], self.i_vecs[:, :], self.vecs.b, writes=[self.vecs.b])
        fw.dma(self.cm.t[:, :], self.i_cmat[:, :], self.cm.b, writes=[self.cm.b])
        fw.op("dve", lambda e: e.tensor_copy(out=self.cmb.t[:, :], in_=self.cm.t[:, :]), reads=[self.cm.b],
              writes=[self.cmb.b])
        fw.op("pool", lambda e: e.memset(self.ones_bf.t[:, :], 1.0), writes=[self.ones_bf.b])
        dummy = self.fw.buf(name="x0")
        XAv = self.XA.rearrange("(k p) n -> p k n", p=128)
        xin = self.i_xT.rearrange("(k p) n -> p k n", p=128)
        cin = self.i_cxT.rearrange("(k p) n -> p k n", p=128)
        for k in range(KD):
            fw.dma(XAv[:, k, cfg.LS:cfg.LS + cfg.T], xin[:, k, :], dummy, writes=[self.db(self.XA)], waw=False)
            fw.dma(XAv[:, k, cfg.CS:cfg.CS + cfg.C], cin[:, k, :], dummy, writes=[self.db(self.XA)], waw=False)
        self.end()

    def ident_bf(self):
        return self.cmb.t[:, 0:128]

    def cmask(self, name, bf=True):
        j = ("ident", "ML_f", "MR_f", "ML_b", "MR_b", "bd").index(name)
        return (self.cmb if bf else self.cm).t[:, j * 128:(j + 1) * 128]

    def mod_phase(self):
        fw = self.fw
        self.begin()
        ct = self.sb("ct", [128, 16], F32)
        sc = self.sb("sc", [128, 16], BF16)
        fw.dma(ct.t[:, :], self.i_cT[:, :], ct.b, writes=[ct.b])
        fw.op("act", lambda e: e.activation(out=sc.t[:, :], in_=ct.t[:, :], func=AF.Silu), reads=[ct.b], writes=[sc.b])
        modT = self.sb("modT", [128, DEPTH, 48, 2], F32)
        wst = [self.sb("mwst", [128, KD, 512], F32) for _ in range(2)]
        wb = [self.sb("mwb", [128, KD, 512], BF16) for _ in range(2)]
        pm = [self.ps("pm", [128, 512], F32) for _ in range(2)]
        it = 0
        for l in range(self.n_layers):
            mw = self.i_modw[l].rearrange("(k p) n -> p k n", p=128)
            for cg in range(12):
                s_, b_, p_ = wst[it % 2], wb[it % 2], pm[it % 2]
                fw.dma(s_.t[:, :, :], mw[:, :, cg * 512:(cg + 1) * 512], s_.b, writes=[s_.b])
                eng = "dve" if it % 2 == 0 else "pool"
                fw.op(eng, lambda e, s_=s_, b_=b_: e.tensor_copy(out=b_.t[:, :, :], in_=s_.t[:, :, :]),
                      reads=[s_.b], writes=[b_.b])
                for j in range(4):
                    for k in range(KD):
                        fw.op("pe", lambda e, j=j, k=k, b_=b_, p_=p_: e.matmul(
                            p_.t[:, 2 * j:2 * j + 2], lhsT=b_.t[:, k, j * 128:(j + 1) * 128],
                            rhs=sc.t[:, 2 * k:2 * k + 2], start=(k == 0), stop=(k == KD - 1)),
                              reads=[b_.b, sc.b], writes=[p_.b], ticket=(k == KD - 1 and j == 3))
                for j in range(4):
                    fw.op("dve", lambda e, j=j, p_=p_, l=l, cg=cg: e.tensor_scalar(
                        out=modT.t[:, l, cg * 4 + j, :], in0=p_.t[:, 2 * j:2 * j + 2],
                        scalar1=self.vec("modb%d" % l, cg * 4 + j), scalar2=None, op0=ALU.add),
                          reads=[p_.b], writes=[modT.b], waw=False)
                it += 1
        for l in range(self.n_layers):
            for v in range(2):
                def dvs(which):
                    base = ((l * 2 + v) * 6 + which) * 8
                    return self.dv.t[:, base:base + 8]
                for which, (sci, gname) in ((0, (8, "gmix%d" % l)), (3, (32, "gffn%d" % l))):
                    fw.op("dve", lambda e, which=which, sci=sci, gname=gname, l=l, v=v, dvs=dvs: e.scalar_tensor_tensor(
                        out=dvs(which), in0=modT.t[:, l, sci:sci + 8, v], scalar=1.0, in1=self.vec(gname, 0, 8),
                        op0=ALU.add, op1=ALU.mult), reads=[modT.b], writes=[self.dv.b], waw=False)
                for which, c0 in ((1, 0), (2, 16), (4, 24), (5, 40)):
                    fw.op("dve", lambda e, which=which, c0=c0, l=l, v=v, dvs=dvs: e.tensor_copy(
                        out=dvs(which), in_=modT.t[:, l, c0:c0 + 8, v]), reads=[modT.b], writes=[self.dv.b], waw=False)
        self.end()

    def attn_inproj(self, l, X):
        fw, cfg = self.fw, self.cfg
        i = l // 2
        self.begin()
        W = self.sb("wA", [128, KD, ATT_EXT], BF16)
        self.load_w(W, lambda k: self.i_wext[i, k * 128:(k + 1) * 128, :], ATT_EXT)
        nt = self.norm_tiles()
        xs = [self.sb("x", [128, KD, 512], F32) for _ in range(2)]
        hTs = [self.sb("hT", [128, KD, 512], BF16) for _ in range(2)]
        cs = [self.sb("cos", [128, 512], F32) for _ in range(2)]
        sn = [self.sb("sin", [128, 512], F32) for _ in range(2)]
        qko = [self.sb("qko", [128, 13, 512], BF16) for _ in range(1)]
        vo = [self.sb("vo", [128, 4, 640], BF16) for _ in range(1)]
        pP = [self.ps("pP", [128, 512]) for _ in range(2)]
        pS = [self.ps("pS", [128, 512]) for _ in range(2)]
        ssq = self.ps("ssq", [128, 512])
        pV = self.ps("pV", [128, 1024])
        sqn = self.sb("sqn", [128, 512], BF16)
        rq, tq = self.sb("rq", [128, 512]), self.sb("tq", [128, 512])
        Aq = [self.sb("Aq", [128, 512]) for _ in range(2)]
        Bq = [self.sb("Bq", [128, 512]) for _ in range(2)]
        t1 = [self.sb("t1", [128, 512]) for _ in range(2)]
        t2 = [self.sb("t2", [128, 512]) for _ in range(2)]
        Xv = X.rearrange("(k p) n -> p k n", p=128)
        QTv = self.QT.rearrange("(k p) n -> p k n", p=128)
        KTv = self.KT.rearrange("(k p) n -> p k n", p=128)
        bi = 0
        for seq in cfg.seqs:
            _, start, ln, v, row0 = seq
            lat = (v == 0)
            for (t0, n) in blocks_of(ln, 512):
                x, hT, qo, vv = xs[bi % 2], hTs[bi % 2], qko[0], vo[0]
                c_, s_ = cs[bi % 2], sn[bi % 2]
                bi += 1
                self.load_cols(x, Xv, seq, t0, n, 0)
                if lat:
                    fw.dma(c_.t[:, 0:n], self.i_cos[:, start + t0:start + t0 + n], c_.b, writes=[c_.b])
                    fw.dma(s_.t[:, 0:n], self.i_sin[:, start + t0:start + t0 + n], s_.b, writes=[s_.b])
                self.norm_mod(x, n, nt["ss"], nt["sq"], nt["rs"], nt["tmp"], self.dvv(l, v, 0), self.dvv(l, v, 1), hT)
                for j in range(13):
                    P, S = pP[j % 2], pS[j % 2]
                    for k in range(KD):
                        fw.op("pe", lambda e, k=k, j=j, P=P: e.matmul(
                            P.t[:, 0:n], lhsT=W.t[:, k, j * 128:(j + 1) * 128], rhs=hT.t[:, k, 0:n],
                            start=(k == 0), stop=(k == KD - 1)), reads=[W.b, hT.b], writes=[P.b], ticket=(k == KD - 1))
                    if lat:
                        for k in range(KD):
                            fw.op("pe", lambda e, k=k, j=j, S=S: e.matmul(
                                S.t[:, 0:n], lhsT=W.t[:, k, ATT_FM + j * 128:ATT_FM + (j + 1) * 128],
                                rhs=hT.t[:, k, 0:n], start=(k == 0), stop=(k == KD - 1)),
                                  reads=[W.b, hT.b], writes=[S.b], ticket=(k == KD - 1))
                    normed = j >= 8
                    A, B = P, S
                    if normed:
                        fw.op("act", lambda e, P=P: e.activation(out=sqn.t[:, 0:n], in_=P.t[:, 0:n], func=AF.Square),
                              reads=[P.b], writes=[sqn.b])
                        fw.op("pe", lambda e: e.matmul(ssq.t[:, 0:n], lhsT=self.cmask("bd"), rhs=sqn.t[:, 0:n],
                                                       start=True, stop=True), reads=[sqn.b, self.cmb.b], writes=[ssq.b])
                        self.rstd(ssq, rq, tq, 1.0 / HD, n)
                        gn, gsn = ("gq%d" % i, "gqs%d" % i) if j < 12 else ("gk%d" % i, "gks%d" % i)
                        A = Aq[j % 2]
                        fw.op("dve", lambda e, P=P, A=A, gn=gn: e.scalar_tensor_tensor(
                            out=A.t[:, 0:n], in0=P.t[:, 0:n], scalar=self.vec(gn), in1=rq.t[:, 0:n],
                            op0=ALU.mult, op1=ALU.mult), reads=[P.b, rq.b], writes=[A.b])
                        if lat:
                            B = Bq[j % 2]
                            fw.op("dve", lambda e, S=S, B=B, gsn=gsn: e.scalar_tensor_tensor(
                                out=B.t[:, 0:n], in0=S.t[:, 0:n], scalar=self.vec(gsn), in1=rq.t[:, 0:n],
                                op0=ALU.mult, op1=ALU.mult), reads=[S.b, rq.b], writes=[B.b])
                    if lat:
                        a1, a2 = t1[j % 2], t2[j % 2]
                        fw.op("dve", lambda e, A=A, a1=a1: e.tensor_tensor(
                            out=a1.t[:, 0:n], in0=A.t[:, 0:n], in1=c_.t[:, 0:n], op=ALU.mult),
                              reads=[A.b, c_.b], writes=[a1.b])
                        fw.op("dve", lambda e, B=B, a2=a2: e.tensor_tensor(
                            out=a2.t[:, 0:n], in0=B.t[:, 0:n], in1=s_.t[:, 0:n], op=ALU.mult),
                              reads=[B.b, s_.b], writes=[a2.b])
                        fw.op("pool", lambda e, a1=a1, a2=a2, j=j: e.tensor_tensor(
                            out=qo.t[:, j, 0:n], in0=a1.t[:, 0:n], in1=a2.t[:, 0:n], op=ALU.add),
                              reads=[a1.b, a2.b], writes=[qo.b], waw=False)
                    elif normed:
                        fw.op("pool", lambda e, A=A, j=j: e.tensor_copy(out=qo.t[:, j, 0:n], in_=A.t[:, 0:n]),
                              reads=[A.b], writes=[qo.b], waw=False)
                    else:
                        fw.op("act", lambda e, A=A, j=j: e.activation(out=qo.t[:, j, 0:n], in_=A.t[:, 0:n], func=AF.Copy),
                              reads=[A.b], writes=[qo.b], waw=False)
                c0, c1 = start + t0, start + t0 + n
                for (dst, d0, s0, w) in ((QTv, 0, 0, 4), (KTv, 0, 4, 4), (QTv, 4, 8, 4), (KTv, 4, 12, 1)):
                    fw.dma(dst[:, d0:d0 + w, c0:c1], qo.t[:, s0:s0 + w, 0:n], qo.b, reads=[qo.b],
                           writes=[self.db(dst)], waw=False, stream="pool")
                nsub = n // 128
                for sub in range(nsub):
                    for (cc, w) in ((0, 512), (512, 128)):
                        for k in range(KD):
                            fw.op("pe", lambda e, k=k, cc=cc, w=w, sub=sub: e.matmul(
                                pV.t[:, cc:cc + w], lhsT=hT.t[:, k, sub * 128:(sub + 1) * 128],
                                rhs=W.t[:, k, 2 * ATT_FM + cc:2 * ATT_FM + cc + w], start=(k == 0), stop=(k == KD - 1)),
                                  reads=[W.b, hT.b], writes=[pV.b], ticket=(k == KD - 1 and cc == 512))
                    fw.op("act" if sub % 2 == 0 else "dve",
                          (lambda e, sub=sub: e.activation(out=vv.t[:, sub, :], in_=pV.t[:, 0:640], func=AF.Copy))
                          if sub % 2 == 0 else
                          (lambda e, sub=sub: e.tensor_copy(out=vv.t[:, sub, :], in_=pV.t[:, 0:640])),
                          reads=[pV.b], writes=[vv.b], waw=False)
                r0 = row0 + t0
                fw.dma(self.VT[r0:r0 + n, :].rearrange("(s p) c -> p s c", p=128), vv.t[:, 0:nsub, :], vv.b,
                       reads=[vv.b], writes=[self.db(self.VT)], waw=False, stream="pool")
        self.end()

    def attn_core(self, l):
        fw, cfg = self.fw, self.cfg
        i = l // 2
        lam_init = 0.8 - 0.6 * math.exp(-0.3 * l)
        NTOK, C, T = cfg.NTOK, cfg.C, cfg.T
        NKT = NTOK // 128
        self.begin()
        ro = self.roff["lq1_%d" % i][0]
        rv = self.sb("lqk", [128, 256])
        fw.dma(rv.t[:, :], self.i_rowv[0:1, ro:ro + 256].broadcast_to([128, 256]), rv.b, writes=[rv.b])
        prod = self.sb("prod", [128, 128])
        fw.op("dve", lambda e: e.tensor_tensor(out=prod.t[:, 0:64], in0=rv.t[:, 0:64], in1=rv.t[:, 64:128], op=ALU.mult),
              reads=[rv.b], writes=[prod.b])
        fw.op("dve", lambda e: e.tensor_tensor(out=prod.t[:, 64:128], in0=rv.t[:, 128:192], in1=rv.t[:, 192:256],
                                               op=ALU.mult), reads=[rv.b], writes=[prod.b], waw=False)
        s12 = self.sb("s12", [128, 2])
        for q in range(2):
            fw.op("dve", lambda e, q=q: e.reduce_sum(out=s12.t[:, q:q + 1], in_=prod.t[:, q * 64:(q + 1) * 64],
                                                     axis=mybir.AxisListType.X), reads=[prod.b], writes=[s12.b], waw=False)
        e12 = self.sb("e12", [128, 2])
        fw.op("act", lambda e: e.activation(out=e12.t[:, :], in_=s12.t[:, :], func=AF.Exp), reads=[s12.b], writes=[e12.b])
        nlam = self.sb("nlam", [128, 1])
        fw.op("dve", lambda e: e.tensor_tensor(out=nlam.t[:, :], in0=e12.t[:, 1:2], in1=e12.t[:, 0:1], op=ALU.subtract),
              reads=[e12.b], writes=[nlam.b])
        fw.op("dve", lambda e: e.tensor_scalar(out=nlam.t[:, :], in0=nlam.t[:, :], scalar1=-lam_init, scalar2=None,
                                               op0=ALU.add), reads=[nlam.b], writes=[nlam.b])
        sg = self.sb("sg", [128, 1])
        fw.op("dve", lambda e: e.tensor_scalar(out=sg.t[:, :], in0=self.vec("subg%d" % i), scalar1=1.0 - lam_init,
                                               scalar2=None, op0=ALU.mult), reads=[self.vecs.b], writes=[sg.b])
        Ks = [self.sb("K", [128, NTOK], BF16) for _ in range(2)]
        Vs = [self.sb("V", [128, NKT, 128], BF16) for _ in range(2)]
        Vg = [self.sb("Vg", [128, NKT, 128], BF16) for _ in range(2)]
        Qs = [self.sb("Q", [128, 512], BF16) for _ in range(2)]
        Pt = [self.sb("P", [128, 1024], BF16) for _ in range(2)]
        Sp = [self.ps("S", [128, 1024]) for _ in range(2)]
        Op = [self.ps("O", [128, 512]) for _ in range(2)]
        Lp = [self.ps("L", [128, 512]) for _ in range(2)]
        r_ = [self.sb("r", [128, 512]) for _ in range(2)]
        on = [self.sb("on", [128, 512]) for _ in range(2)]
        oa = self.sb("oa", [128, 512])
        sqo = self.sb("sqo", [128, 512], BF16)
        rs, tmp = self.sb("rso", [128, 512]), self.sb("tmpo", [128, 512])
        Lacc = [self.sb("Lacc", [128, 512]) for _ in range(2)]
        Lb = [self.sb("Lb", [128, 512], BF16) for _ in range(2)]
        Lf = self.sb("Lf", [128, 512])
        hb = self.sb("hb", [128, 512], BF16)
        h32 = self.sb("h32", [128, 512])
        lb = self.sb("lb", [128, 512], BF16)
        ost = [self.sb("ost", [128, 2, 512], BF16) for _ in range(2)]
        QTv = self.QT.rearrange("(k p) n -> p k n", p=128)
        KTv = self.KT.rearrange("(k p) n -> p k n", p=128)
        OTv = self.OT.rearrange("(k p) n -> p k n", p=128)
        v3 = lambda t, n: t.t[:, :].rearrange("p (s c) -> p s c", c=512)[:, :, 0:n]

        def load_kv(u):
            if u > 4:
                return
            K, V = Ks[u % 2], Vs[u % 2]
            kc = u if u < 4 else 4
            vcol = u * 128 if u < 4 else 512
            fw.dma(K.t[:, 0:C], KTv[:, kc, cfg.CS:cfg.CS + C], K.b, reads=[self.db(self.KT)], writes=[K.b])
            fw.dma(K.t[:, C:NTOK], KTv[:, kc, cfg.LS:cfg.LS + T], K.b, reads=[self.db(self.KT)], writes=[K.b], waw=False)
            vsrc = self.VT[:, vcol:vcol + 128].rearrange("(s p) c -> p s c", p=128)
            for (s0, ns) in blocks_of(NKT, 16):
                fw.dma(V.t[:, s0:s0 + ns, :], vsrc[:, s0:s0 + ns, :], V.b, reads=[self.db(self.VT)], writes=[V.b],
                       waw=(s0 == 0))
            if u == 4:
                for sub in range(2):
                    fw.op("pool", lambda e, sub=sub: e.memset(Vg[sub].t[:, :, 64:128], 1.0), writes=[Vg[sub].b])
                    fw.op("pool" if sub == 0 else "dve", lambda e, sub=sub: e.tensor_copy(
                        out=Vg[sub].t[:, :, 0:64], in_=V.t[:, :, sub * 64:(sub + 1) * 64]),
                          reads=[V.b], writes=[Vg[sub].b], waw=False)

        qblocks = []
        for seq in cfg.seqs:
            _, start, ln, v, row0 = seq
            kts = list(range(0, C // 128)) if v == 1 else list(range(NKT))
            for (t0, n) in blocks_of(ln, 512):
                qblocks.append((start + t0, n, kts))
        load_kv(0)
        qi = 0
        for u in range(8):
            if u + 1 < 8:
                load_kv(u + 1)
            K, V = Ks[min(u, 4) % 2], Vs[min(u, 4) % 2]
            diff = u < 4
            for (c0, n, kts) in qblocks:
                Q = Qs[qi % 2]
                os_ = ost[qi % 2]
                qi += 1
                fw.dma(Q.t[:, 0:n], QTv[:, u, c0:c0 + n], Q.b, reads=[self.db(self.QT)], writes=[Q.b])

                def emit_S(kt):
                    S = Sp[kt % 2]
                    for sub in range(2):
                        fw.op("pe", lambda e, sub=sub, S=S, kt=kt: e.matmul(
                            S.t[:, sub * 512:sub * 512 + n], lhsT=K.t[sub * 64:(sub + 1) * 64, kt * 128:(kt + 1) * 128],
                            rhs=Q.t[sub * 64:(sub + 1) * 64, 0:n], start=True, stop=True),
                              reads=[K.b, Q.b], writes=[S.b], ticket=(sub == 1), waw=(sub == 0))

                def emit_E(kt):
                    S, P = Sp[kt % 2], Pt[kt % 2]
                    fw.op("act", lambda e, S=S, P=P: e.activation(out=v3(P, n), in_=v3(S, n), func=AF.Exp,
                                                                  scale=HD ** -0.5), reads=[S.b], writes=[P.b])

                def emit_PV(kt, first, last):
                    P = Pt[kt % 2]
                    for sub in range(2):
                        rhs = P.t[:, sub * 512:sub * 512 + n]
                        if diff:
                            fw.op("pe", lambda e, sub=sub, rhs=rhs: e.matmul(
                                Op[sub].t[:, 0:n], lhsT=V.t[:, kt, :], rhs=rhs, start=first, stop=last),
                                  reads=[V.b, P.b], writes=[Op[sub].b] if (first or last) else [],
                                  ticket=last)
                            La = Lacc[sub]
                            if sub == 1:
                                fw.op("pe", lambda e, sub=sub, rhs=rhs: e.matmul(
                                    Lp[sub].t[:, 0:n], lhsT=self.ones_bf.t[:, :], rhs=rhs, start=first, stop=last),
                                      reads=[self.ones_bf.b, P.b], writes=[Lp[sub].b] if (first or last) else [],
                                      ticket=True)
                            elif first:
                                fw.op("dve", lambda e, La=La, rhs=rhs: e.tensor_copy(out=La.t[:, 0:n], in_=rhs),
                                      reads=[P.b], writes=[La.b])
                            else:
                                fw.op("dve", lambda e, La=La, rhs=rhs: e.tensor_tensor(
                                    out=La.t[:, 0:n], in0=La.t[:, 0:n], in1=rhs, op=ALU.add),
                                      reads=[P.b, La.b], writes=[La.b])
                        else:
                            fw.op("pe", lambda e, sub=sub, rhs=rhs: e.matmul(
                                Op[sub].t[:, 0:n], lhsT=Vg[sub].t[:, kt, :], rhs=rhs, start=first, stop=last),
                                  reads=[Vg[sub].b, P.b], writes=[Op[sub].b] if (first or last) else [],
                                  ticket=(last or sub == 1))

                emit_S(kts[0])
                for idx, kt in enumerate(kts):
                    if idx + 1 < len(kts):
                        emit_S(kts[idx + 1])
                    emit_E(kt)
                    emit_PV(kt, idx == 0, idx == len(kts) - 1)
                if diff:
                    for sub in range(2):
                        if sub == 0:
                            fw.op("dve", lambda e, sub=sub: e.tensor_copy(out=Lb[sub].t[:, 0:n], in_=Lacc[sub].t[:, 0:n]),
                                  reads=[Lacc[sub].b], writes=[Lb[sub].b])
                            fw.op("pe", lambda e, sub=sub: e.matmul(Lp[sub].t[:, 0:n], lhsT=self.ones_bf.t[:, :],
                                                                    rhs=Lb[sub].t[:, 0:n], start=True, stop=True),
                                  reads=[Lb[sub].b, self.ones_bf.b], writes=[Lp[sub].b])
                        fw.op("dve", lambda e, sub=sub: e.reciprocal(out=r_[sub].t[:, 0:n], in_=Lp[sub].t[:, 0:n]),
                              reads=[Lp[sub].b], writes=[r_[sub].b])
                    for sub in range(2):
                        fw.op("dve", lambda e, sub=sub: e.tensor_tensor(
                            out=on[sub].t[:, 0:n], in0=Op[sub].t[:, 0:n], in1=r_[sub].t[:, 0:n], op=ALU.mult),
                              reads=[Op[sub].b, r_[sub].b], writes=[on[sub].b])
                    fw.op("dve", lambda e: e.scalar_tensor_tensor(
                        out=oa.t[:, 0:n], in0=on[1].t[:, 0:n], scalar=nlam.t[:, 0:1], in1=on[0].t[:, 0:n],
                        op0=ALU.mult, op1=ALU.add), reads=[on[0].b, on[1].b, nlam.b], writes=[oa.b])
                    fw.op("act", lambda e: e.activation(out=sqo.t[:, 0:n], in_=oa.t[:, 0:n], func=AF.Square),
                          reads=[oa.b], writes=[sqo.b])
                    ssb = Lp[0]
                    fw.op("pe", lambda e: e.matmul(ssb.t[:, 0:n], lhsT=self.ones_bf.t[:, :], rhs=sqo.t[:, 0:n],
                                                   start=True, stop=True), reads=[sqo.b, self.ones_bf.b], writes=[ssb.b])
                    self.rstd(ssb, rs, tmp, 1.0 / 128, n)
                    fw.op("dve", lambda e: e.scalar_tensor_tensor(
                        out=os_.t[:, 0, 0:n], in0=oa.t[:, 0:n], scalar=sg.t[:, 0:1], in1=rs.t[:, 0:n],
                        op0=ALU.mult, op1=ALU.mult), reads=[oa.b, rs.b, sg.b], writes=[os_.b])
                    fw.dma(OTv[:, u, c0:c0 + n], os_.t[:, 0, 0:n], os_.b, reads=[os_.b], writes=[self.db(self.OT)],
                           waw=False, stream="pool")
                else:
                    j = u - 4
                    H = slice(64, 128)
                    idb = self.cmb.t[64:128, 64:128]
                    for sub in range(2):
                        fw.op("act", lambda e, sub=sub: e.activation(out=Lf.t[H, 0:n], in_=Op[sub].t[H, 0:n], func=AF.Copy),
                              reads=[Op[sub].b], writes=[Lf.b])
                        fw.op("dve", lambda e: e.reciprocal(out=Lf.t[H, 0:n], in_=Lf.t[H, 0:n]), reads=[Lf.b], writes=[Lf.b])
                        fw.op("dve", lambda e: e.tensor_copy(out=hb.t[H, 0:n], in_=Lf.t[H, 0:n]), reads=[Lf.b], writes=[hb.b])
                        fw.op("dve", lambda e: e.tensor_copy(out=h32.t[H, 0:n], in_=hb.t[H, 0:n]), reads=[hb.b], writes=[h32.b])
                        fw.op("dve", lambda e: e.tensor_tensor(out=lb.t[H, 0:n], in0=Lf.t[H, 0:n], in1=h32.t[H, 0:n],
                                                               op=ALU.subtract), reads=[Lf.b, h32.b], writes=[lb.b])
                        fw.op("pe", lambda e, sub=sub: e.matmul(Lp[sub].t[0:64, 0:n], lhsT=idb, rhs=hb.t[H, 0:n],
                                                                start=True, stop=False),
                              reads=[hb.b, self.cmb.b], writes=[Lp[sub].b], ticket=False)
                        fw.op("pe", lambda e, sub=sub: e.matmul(Lp[sub].t[0:64, 0:n], lhsT=idb, rhs=lb.t[H, 0:n],
                                                                start=False, stop=True),
                              reads=[lb.b, hb.b, self.cmb.b], writes=[Lp[sub].b])
                        fw.op("act", lambda e, sub=sub: e.activation(out=r_[sub].t[0:64, 0:n], in_=Lp[sub].t[0:64, 0:n],
                                                                     func=AF.Copy), reads=[Lp[sub].b], writes=[r_[sub].b])
                        fw.op("dve", lambda e, sub=sub: e.tensor_tensor(
                            out=os_.t[0:64, sub, 0:n], in0=Op[sub].t[0:64, 0:n], in1=r_[sub].t[0:64, 0:n], op=ALU.mult),
                              reads=[Op[sub].b, r_[sub].b], writes=[os_.b], waw=(sub == 0))
                    for sub in range(2):
                        hd = 4 * sub + j
                        f0 = 512 + hd * 64
                        fw.dma(self.OT[f0:f0 + 64, c0:c0 + n], os_.t[0:64, sub, 0:n], os_.b, reads=[os_.b],
                               writes=[self.db(self.OT)], waw=False, stream="pool")
        self.end()

    def proj_res_norm(self, l, Xin, Xout, SRC, kc, w_rows):
        fw, cfg = self.fw, self.cfg
        self.begin()
        W = self.sb("wo", [128, kc, D], BF16)
        self.load_w(W, w_rows, D)
        nt = self.norm_tiles()
        srcs = [self.sb("src", [128, kc, 512], BF16) for _ in range(2)]
        xs = [self.sb("x", [128, KD, 512], F32) for _ in range(2)]
        x1s = [self.sb("x1", [128, KD, 512], F32) for _ in range(2)]
        hs = [self.sb("h", [128, KD, 512], BF16) for _ in range(2)]
        pp = [self.ps("pp", [128, 512]) for _ in range(2)]
        Xiv = Xin.rearrange("(k p) n -> p k n", p=128)
        Xov = Xout.rearrange("(k p) n -> p k n", p=128)
        Sv = SRC.rearrange("(k p) n -> p k n", p=128)
        HTv = self.HT.rearrange("(k p) n -> p k n", p=128)
        bi = 0
        for seq in cfg.seqs:
            _, start, ln, v, row0 = seq
            if v == 1 and l == DEPTH - 1:
                continue
            for (t0, n) in blocks_of(ln, 512):
                s_, x, x1, h = srcs[bi % 2], xs[bi % 2], x1s[bi % 2], hs[bi % 2]
                bi += 1
                self.load_cols(s_, Sv, seq, t0, n, 0)
                self.load_cols(x, Xiv, seq, t0, n, 0)
                g1 = self.dvv(l, v, 2)
                for c in range(KD):
                    P = pp[c % 2]
                    for k in range(kc):
                        fw.op("pe", lambda e, k=k, c=c, P=P: e.matmul(
                            P.t[:, 0:n], lhsT=W.t[:, k, c * 128:(c + 1) * 128], rhs=s_.t[:, k, 0:n],
                            start=(k == 0), stop=(k == kc - 1)), reads=[W.b, s_.b], writes=[P.b], ticket=(k == kc - 1))
                    fw.op("dve", lambda e, c=c, P=P: e.scalar_tensor_tensor(
                        out=x1.t[:, c, 0:n], in0=P.t[:, 0:n], scalar=g1(c), in1=x.t[:, c, 0:n],
                        op0=ALU.mult, op1=ALU.add), reads=[P.b, x.b], writes=[x1.b], waw=False)
                c0 = start + t0
                fw.dma(Xov[:, :, c0:c0 + n], x1.t[:, :, 0:n], x1.b, reads=[x1.b], writes=[self.db(Xout)], waw=False,
                       stream="pool")
                self.norm_mod(x1, n, nt["ss"], nt["sq"], nt["rs"], nt["tmp"], self.dvv(l, v, 3), self.dvv(l, v, 4), h)
                fw.dma(HTv[:, :, c0:c0 + n], h.t[:, :, 0:n], h.b, reads=[h.b], writes=[self.db(self.HT)], waw=False,
                       stream="pool")
        self.end()

    def ffn_in(self, l):
        fw, cfg = self.fw, self.cfg
        self.begin()
        W = self.sb("wf", [128, KD, 2 * DFF], BF16)
        self.load_w(W, lambda k: self.i_fwi[l, k * 128:(k + 1) * 128, :], 2 * DFF)
        hs = [self.sb("h", [128, KD, 512], BF16) for _ in range(2)]
        uo = [self.sb("uo", [128, NFF, 512], BF16) for _ in range(2)]
        pg = [self.ps("pg", [128, 512]) for _ in range(2)]
        pv = [self.ps("pv", [128, 512]) for _ in range(2)]
        tt = [self.sb("t", [128, 512]) for _ in range(2)]
        ge = [self.sb("ge", [128, 512]) for _ in range(2)]
        HTv = self.HT.rearrange("(k p) n -> p k n", p=128)
        UTv = self.UT.rearrange("(k p) n -> p k n", p=128)
        bi = 0
        for seq in cfg.seqs:
            _, start, ln, v, row0 = seq
            if v == 1 and l == DEPTH - 1:
                continue
            for (t0, n) in blocks_of(ln, 510):
                h, u = hs[bi % 2], uo[bi % 2]
                bi += 1
                self.load_cols(h, HTv, seq, t0, n, 1)
                N = n + 2
                for c in range(NFF):
                    G, Vv, t, g = pg[c % 2], pv[c % 2], tt[c % 2], ge[c % 2]
                    for k in range(KD):
                        fw.op("pe", lambda e, k=k, c=c, G=G: e.matmul(
                            G.t[:, 0:N], lhsT=W.t[:, k, DFF + c * 128:DFF + (c + 1) * 128], rhs=h.t[:, k, 0:N],
                            start=(k == 0), stop=(k == KD - 1)), reads=[W.b, h.b], writes=[G.b], ticket=(k == KD - 1))
                    for k in range(KD):
                        fw.op("pe", lambda e, k=k, c=c, Vv=Vv: e.matmul(
                            Vv.t[:, 0:N], lhsT=W.t[:, k, c * 128:(c + 1) * 128], rhs=h.t[:, k, 0:N],
                            start=(k == 0), stop=(k == KD - 1)), reads=[W.b, h.b], writes=[Vv.b], ticket=(k == KD - 1))
                    w0, w1, w2 = (self.vec("fcw%d_%d" % (l, q), c) for q in range(3))
                    bb = self.vec("fcb%d" % l, c)
                    fw.op("dve", lambda e, G=G, t=t, w0=w0, bb=bb: e.tensor_scalar(
                        out=t.t[:, 0:n], in0=G.t[:, 0:n], scalar1=w0, scalar2=bb, op0=ALU.mult, op1=ALU.add),
                          reads=[G.b], writes=[t.b])
                    fw.op("dve", lambda e, G=G, t=t, w1=w1: e.scalar_tensor_tensor(
                        out=t.t[:, 0:n], in0=G.t[:, 1:n + 1], scalar=w1, in1=t.t[:, 0:n], op0=ALU.mult, op1=ALU.add),
                          reads=[G.b, t.b], writes=[t.b])
                    fw.op("dve", lambda e, G=G, t=t, w2=w2: e.scalar_tensor_tensor(
                        out=t.t[:, 0:n], in0=G.t[:, 2:n + 2], scalar=w2, in1=t.t[:, 0:n], op0=ALU.mult, op1=ALU.add),
                          reads=[G.b, t.b], writes=[t.b])
                    fw.op("act", lambda e, t=t, g=g: e.activation(out=g.t[:, 0:n], in_=t.t[:, 0:n], func=AF.Gelu),
                          reads=[t.b], writes=[g.b])
                    fw.op("dve", lambda e, g=g, Vv=Vv, c=c: e.tensor_tensor(
                        out=u.t[:, c, 0:n], in0=g.t[:, 0:n], in1=Vv.t[:, 1:n + 1], op=ALU.mult),
                          reads=[g.b, Vv.b], writes=[u.b], waw=False)
                c0 = start + t0
                fw.dma(UTv[:, :, c0:c0 + n], u.t[:, :, 0:n], u.b, reads=[u.b], writes=[self.db(self.UT)], waw=False,
                       stream="pool")
        self.end()

    def ffn_out(self, l, Xin, Xout):
        fw, cfg = self.fw, self.cfg
        self.begin()
        W = self.sb("wf2", [128, NFF, D], BF16)
        self.load_w(W, lambda k: self.i_fwo[l, k * 128:(k + 1) * 128, :], D)
        us = [self.sb("u", [128, NFF, 512], BF16) for _ in range(2)]
        xs = [self.sb("x", [128, KD, 512], F32) for _ in range(2)]
        x2s = [self.sb("x2", [128, KD, 512], F32) for _ in range(2)]
        pp = [self.ps("pp", [128, 512]) for _ in range(2)]
        Xiv = Xin.rearrange("(k p) n -> p k n", p=128)
        Xov = Xout.rearrange("(k p) n -> p k n", p=128)
        UTv = self.UT.rearrange("(k p) n -> p k n", p=128)
        bi = 0
        for seq in cfg.seqs:
            _, start, ln, v, row0 = seq
            if v == 1 and l == DEPTH - 1:
                continue
            for (t0, n) in blocks_of(ln, 512):
                u, x, x2 = us[bi % 2], xs[bi % 2], x2s[bi % 2]
                bi += 1
                self.load_cols(u, UTv, seq, t0, n, 0)
                self.load_cols(x, Xiv, seq, t0, n, 0)
                g2 = self.dvv(l, v, 5)
                for c in range(KD):
                    P = pp[c % 2]
                    for k in range(NFF):
                        fw.op("pe", lambda e, k=k, c=c, P=P: e.matmul(
                            P.t[:, 0:n], lhsT=W.t[:, k, c * 128:(c + 1) * 128], rhs=u.t[:, k, 0:n],
                            start=(k == 0), stop=(k == NFF - 1)), reads=[W.b, u.b], writes=[P.b], ticket=(k == NFF - 1))
                    fw.op("dve", lambda e, c=c, P=P: e.scalar_tensor_tensor(
                        out=x2.t[:, c, 0:n], in0=P.t[:, 0:n], scalar=g2(c), in1=x.t[:, c, 0:n],
                        op0=ALU.mult, op1=ALU.add), reads=[P.b, x.b], writes=[x2.b], waw=False)
                c0 = start + t0
                fw.dma(Xov[:, :, c0:c0 + n], x2.t[:, :, 0:n], x2.b, reads=[x2.b], writes=[self.db(Xout)], waw=False,
                       stream="pool")
        self.end()

    def final_norm(self, X):
        fw, cfg = self.fw, self.cfg
        self.begin()
        nt = self.norm_tiles()
        xs = [self.sb("x", [128, KD, 512], F32) for _ in range(2)]
        os_ = [self.sb("o", [128, KD, 512], F32) for _ in range(2)]
        Xv = X.rearrange("(k p) n -> p k n", p=128)
        Ov = self.o_out.rearrange("(k p) n -> p k n", p=128)
        seq = cfg.seqs[1]
        outb = self.fw.buf(name="outT")
        for bi, (t0, n) in enumerate(blocks_of(cfg.T, 512)):
            x, o = xs[bi % 2], os_[bi % 2]
            self.load_cols(x, Xv, seq, t0, n, 0)
            self.norm_mod(x, n, nt["ss"], nt["sq"], nt["rs"], nt["tmp"], lambda k: self.vec("gfin", k), None, o)
            fw.dma(Ov[:, :, t0:t0 + n], o.t[:, :, 0:n], o.b, reads=[o.b], writes=[outb], waw=False, stream="pool")
        self.end()


    def ssm_inproj(self, l, X):
        fw, cfg = self.fw, self.cfg
        i = l // 2
        self.begin()
        W = self.sb("wS", [128, KD, SSM_IN], BF16)
        self.load_w(W, lambda k: self.i_swi[i, k * 128:(k + 1) * 128, :], SSM_IN)
        nt = self.norm_tiles()
        x = self.sb("x", [128, KD, 512], F32)
        hTs = [self.sb("hT", [128, KD, 512], BF16) for _ in range(2)]
        xo = self.sb("xo", [128, 24, 512], BF16)
        zo = [self.sb("zo", [128, SSM_DI], F32) for _ in range(2)]
        dto = [self.sb("dto", [128, 64], F32) for _ in range(2)]
        dta = [self.sb("dta", [128, 64], F32) for _ in range(2)]
        dtb = self.sb("dtb", [128, 64], F32)
        ro = self.roff["dtb%d" % i][0]
        fw.dma(dtb.t[:, :], self.i_rowv[0:1, ro:ro + 64].broadcast_to([128, 64]), dtb.b, writes=[dtb.b])
        pP = [self.ps("pP", [128, 512]) for _ in range(2)]
        pz = [self.ps("pz", [128, 512]) for _ in range(2)]
        pdt = self.ps("pdt", [128, 512])
        tt = [self.sb("t", [128, 512]) for _ in range(2)]
        Xv = X.rearrange("(k p) n -> p k n", p=128)
        XBv = self.XBCT.rearrange("(k p) n -> p k n", p=128)
        bi = 0
        zi = 0
        for seq in cfg.seqs:
            _, start, ln, v, row0 = seq
            for (t0, n) in blocks_of(ln, 510):
                hT = hTs[bi % 2]
                bi += 1
                N = n + 2
                self.load_cols(x, Xv, seq, t0, n, 1)
                self.norm_mod(x, N, nt["ss"], nt["sq"], nt["rs"], nt["tmp"], self.dvv(l, v, 0), self.dvv(l, v, 1), hT)
                if t0 == 0:
                    fw.op("pool", lambda e: e.memset(hT.t[:, :, 0:1], 0.0), writes=[hT.b])
                if t0 + n == ln:
                    fw.op("pool", lambda e: e.memset(hT.t[:, :, n + 1:n + 2], 0.0), writes=[hT.b])
                for c in range(24):
                    P, t = pP[c % 2], tt[c % 2]
                    for k in range(KD):
                        fw.op("pe", lambda e, k=k, c=c, P=P: e.matmul(
                            P.t[:, 0:N], lhsT=W.t[:, k, SSM_DI + c * 128:SSM_DI + (c + 1) * 128], rhs=hT.t[:, k, 0:N],
                            start=(k == 0), stop=(k == KD - 1)), reads=[W.b, hT.b], writes=[P.b], ticket=(k == KD - 1))
                    w0, w1, w2 = (self.vec("scw%d_%d" % (i, q), c) for q in range(3))
                    bb = self.vec("scb%d" % i, c)
                    fw.op("dve", lambda e, P=P, t=t, w0=w0, bb=bb: e.tensor_scalar(
                        out=t.t[:, 0:n], in0=P.t[:, 0:n], scalar1=w0, scalar2=bb, op0=ALU.mult, op1=ALU.add),
                          reads=[P.b], writes=[t.b])
                    fw.op("dve", lambda e, P=P, t=t, w1=w1: e.scalar_tensor_tensor(
                        out=t.t[:, 0:n], in0=P.t[:, 1:n + 1], scalar=w1, in1=t.t[:, 0:n], op0=ALU.mult, op1=ALU.add),
                          reads=[P.b, t.b], writes=[t.b])
                    fw.op("dve", lambda e, P=P, t=t, w2=w2: e.scalar_tensor_tensor(
                        out=t.t[:, 0:n], in0=P.t[:, 2:n + 2], scalar=w2, in1=t.t[:, 0:n], op0=ALU.mult, op1=ALU.add),
                          reads=[P.b, t.b], writes=[t.b])
                    fw.op("act", lambda e, t=t, c=c: e.activation(out=xo.t[:, c, 0:n], in_=t.t[:, 0:n], func=AF.Silu),
                          reads=[t.b], writes=[xo.b], waw=False)
                c0 = start + t0
                fw.dma(XBv[:, :, c0:c0 + n], xo.t[:, :, 0:n], xo.b, reads=[xo.b], writes=[self.db(self.XBCT)],
                       waw=False, stream="pool")
                for (s0, m) in blocks_of(n, 128):
                    z, dt_, da = zo[zi % 2], dto[zi % 2], dta[zi % 2]
                    zi += 1
                    for q in range(4):
                        Pz = pz[q % 2]
                        for k in range(KD):
                            fw.op("pe", lambda e, k=k, q=q, Pz=Pz: e.matmul(
                                Pz.t[0:m, :], lhsT=hT.t[:, k, 1 + s0:1 + s0 + m], rhs=W.t[:, k, q * 512:(q + 1) * 512],
                                start=(k == 0), stop=(k == KD - 1)), reads=[W.b, hT.b], writes=[Pz.b], ticket=(k == KD - 1))
                        if q % 2 == 0:
                            fw.op("act", lambda e, q=q, Pz=Pz: e.activation(out=z.t[0:m, q * 512:(q + 1) * 512],
                                                                            in_=Pz.t[0:m, :], func=AF.Copy),
                                  reads=[Pz.b], writes=[z.b], waw=False)
                        else:
                            fw.op("dve", lambda e, q=q, Pz=Pz: e.tensor_copy(out=z.t[0:m, q * 512:(q + 1) * 512],
                                                                             in_=Pz.t[0:m, :]),
                                  reads=[Pz.b], writes=[z.b], waw=False)
                    r0 = row0 + t0 + s0
                    fw.dma(self.ZT[r0:r0 + m, :], z.t[0:m, :], z.b, reads=[z.b], writes=[self.db(self.ZT)], waw=False,
                           stream="pool")
                    for k in range(KD):
                        fw.op("pe", lambda e, k=k: e.matmul(
                            pdt.t[0:m, 0:64], lhsT=hT.t[:, k, 1 + s0:1 + s0 + m], rhs=W.t[:, k, SSM_DI + SSM_XBC:SSM_IN],
                            start=(k == 0), stop=(k == KD - 1)), reads=[W.b, hT.b], writes=[pdt.b], ticket=(k == KD - 1))
                    fw.op("dve", lambda e: e.tensor_tensor(out=da.t[0:m, :], in0=pdt.t[0:m, 0:64], in1=dtb.t[0:m, :], op=ALU.add),
                          reads=[pdt.b, dtb.b], writes=[da.b])
                    fw.op("act", lambda e: e.activation(out=da.t[0:m, :], in_=da.t[0:m, :], func=AF.Exp),
                          reads=[da.b], writes=[da.b])
                    fw.op("act", lambda e: e.activation(out=dt_.t[0:m, :], in_=da.t[0:m, :], func=AF.Ln, bias=1.0),
                          reads=[da.b], writes=[dt_.b])
                    fw.dma(self.DTT[r0:r0 + m, :], dt_.t[0:m, :], dt_.b, reads=[dt_.b], writes=[self.db(self.DTT)],
                           waw=False, stream="pool")
        self.end()

    def ssm_scan(self, l, d):
        fw, cfg = self.fw, self.cfg
        i = l // 2
        NKT = cfg.NTOK // 128
        nct = cfg.C // 128
        self.begin()
        bc3 = lambda ap: ap.unsqueeze(2).broadcast_to([128, 8, 64])
        ML = self.cmask("ML_f" if d == 0 else "ML_b")
        MR = self.cmask("MR_f" if d == 0 else "MR_b")
        MRf = self.cmask("MR_f" if d == 0 else "MR_b", bf=False)
        arow = self.sb("arow", [128, 32])
        ro = self.roff["alog%d" % i][0] + 32 * d
        fw.dma(arow.t[:, :], self.i_rowv[0:1, ro:ro + 32].broadcast_to([128, 32]), arow.b, writes=[arow.b])
        fw.op("act", lambda e: e.activation(out=arow.t[:, :], in_=arow.t[:, :], func=AF.Exp), reads=[arow.b], writes=[arow.b])
        fw.op("dve", lambda e: e.tensor_scalar(out=arow.t[:, :], in0=arow.t[:, :], scalar1=-1.0, scalar2=None, op0=ALU.mult),
              reads=[arow.b], writes=[arow.b])
        if d == 1:
            drow = self.sb("drow", [128, 32])
            ro = self.roff["ssd%d" % i][0]
            fw.dma(drow.t[:, :], self.i_rowv[0:1, ro:ro + 32].broadcast_to([128, 32]), drow.b, writes=[drow.b])
            ng = self.sb("ng", [128, SSM_DI])
            ro = self.roff["sng%d" % i][0]
            fw.dma(ng.t[:, :], self.i_rowv[0:1, ro:ro + SSM_DI].broadcast_to([128, SSM_DI]), ng.b, writes=[ng.b])
        S = self.sb("S", [128, SSM_DI], F32)
        Sb = self.sb("Sb", [128, SSM_DI], BF16)
        fw.op("pool", lambda e: e.memset(S.t[:, :], 0.0), writes=[S.b])
        fw.op("pool", lambda e: e.memset(Sb.t[:, :], 0.0), writes=[Sb.b])
        xbcs = [self.sb("xbc", [128, 24, 128], BF16) for _ in range(2)]
        dts = [self.sb("dt", [128, 64], F32) for _ in range(2)]
        xtok_l = [self.sb("xtok", [128, 2560], BF16) for _ in range(2)]
        a_l = [self.sb("a", [128, 32]) for _ in range(2)]
        ahb_l = [self.sb("ahb", [128, 32], BF16) for _ in range(2)]
        ah_l = [self.sb("ah", [128, 32]) for _ in range(2)]
        al_l = [self.sb("al", [128, 32]) for _ in range(2)]
        a2_l = [self.sb("a2", [128, 64], BF16) for _ in range(2)]
        E_l = [self.sb("E", [128, 96]) for _ in range(2)]
        e2_l = [self.sb("e2", [128, 32]) for _ in range(2)]
        cbm_l = [self.sb("cbm", [128, 4, 128]) for _ in range(2)]
        Rhs = [self.sb("Rh", [128, 512], BF16) for _ in range(2)]
        Rls = [self.sb("Rl", [128, 512], BF16) for _ in range(2)]
        dec = [self.sb("dec", [128, 512]) for _ in range(2)]
        wT4 = [self.sb("wT4", [128, 512], BF16) for _ in range(2)]
        tmpy = self.sb("tmpy", [128, 512])
        xdt_l = [self.sb("xdt", [128, SSM_DI], BF16) for _ in range(2)]
        xwa_l = [self.sb("xwa", [128, SSM_DI], BF16) for _ in range(2)]
        yo = [self.sb("yo", [128, SSM_DI], F32) for _ in range(2)]
        pT = self.ps("pT", [128, 1024], BF16)
        pE = self.ps("pE", [128, 512])
        pcb = self.ps("pcb", [128, 512])
        pseg = [self.ps("pseg", [128, 512]) for _ in range(2)]
        pY = self.ps("pY", [128, 512])
        pYs = self.ps("pYs", [128, 512])
        pdS = self.ps("pdS", [128, 512])
        if d == 1:
            yfs = [self.sb("yf", [128, SSM_DI], F32) for _ in range(2)]
            zs = [self.sb("z", [128, SSM_DI], F32) for _ in range(2)]
            sz = self.sb("sz", [128, SSM_DI], F32)
            ssq = self.sb("ssq", [128, 1])
            sqj = self.sb("sqj", [128, SSM_DI], BF16)
            rs1, rt1 = self.sb("rs1", [128, 1]), self.sb("rt1", [128, 1])
            ygn = self.sb("ygn", [128, SSM_DI], BF16)
            ygT = [self.sb("ygT", [128, 16, 128], BF16) for _ in range(2)]
        XBv = self.XBCT.rearrange("(k p) n -> p k n", p=128)
        YGv = self.YGT.rearrange("(k p) n -> p k n", p=128)
        order = list(range(NKT)) if d == 0 else (list(range(nct - 1, -1, -1)) + list(range(NKT - 1, nct - 1, -1)))
        hic = [0]

        def binds(ci):
            kt = order[ci]
            p_ = ci % 2
            r0 = kt * 128
            c0 = (cfg.CS + r0) if kt < nct else (cfg.LS + r0 - cfg.C)
            return (xbcs[p_], dts[p_], r0, c0, xtok_l[p_], a_l[p_], ahb_l[p_], ah_l[p_], al_l[p_], a2_l[p_], E_l[p_],
                    e2_l[p_], cbm_l[p_], xdt_l[p_], xwa_l[p_])

        def prologue(ci):
            xbc, dt_, r0, c0, xtok, a, ahb, ah, al, a2, E, e2, cbm, xdt, xwa = binds(ci)
            fw.dma(xbc.t[:, :, :], XBv[:, :, c0:c0 + 128], xbc.b, reads=[self.db(self.XBCT)], writes=[xbc.b])
            fw.dma(dt_.t[:, :], self.DTT[r0:r0 + 128, :], dt_.b, reads=[self.db(self.DTT)], writes=[dt_.b])
            dtd = dt_.t[:, 32 * d:32 * d + 32]
            if d == 1:
                yf, z = yfs[ci % 2], zs[ci % 2]
                fw.dma(yf.t[:, :], self.YF[r0:r0 + 128, :], yf.b, reads=[self.db(self.YF)], writes=[yf.b])
                fw.dma(z.t[:, :], self.ZT[r0:r0 + 128, :], z.b, reads=[self.db(self.ZT)], writes=[z.b])
            for q in range(5):
                for j in range(4):
                    c = q * 4 + j
                    fw.op("pe", lambda e, c=c, j=j: e.transpose(pT.t[:, j * 128:(j + 1) * 128], xbc.t[:, c, :], self.ident_bf()),
                          reads=[xbc.b, self.cmb.b], writes=[pT.b], ticket=(j == 3), waw=(j == 0))
                if q % 2 == 0:
                    fw.op("act", lambda e, q=q: e.activation(out=xtok.t[:, q * 512:(q + 1) * 512], in_=pT.t[:, 0:512], func=AF.Copy),
                          reads=[pT.b], writes=[xtok.b], waw=False)
                else:
                    fw.op("dve", lambda e, q=q: e.tensor_copy(out=xtok.t[:, q * 512:(q + 1) * 512], in_=pT.t[:, 0:512]),
                          reads=[pT.b], writes=[xtok.b], waw=False)
            fw.op("dve", lambda e: e.tensor_tensor(out=a.t[:, :], in0=dtd, in1=arow.t[:, :], op=ALU.mult),
                  reads=[dt_.b, arow.b], writes=[a.b])
            fw.op("dve", lambda e: e.tensor_copy(out=ahb.t[:, :], in_=a.t[:, :]), reads=[a.b], writes=[ahb.b])
            fw.op("dve", lambda e: e.tensor_copy(out=ah.t[:, :], in_=ahb.t[:, :]), reads=[ahb.b], writes=[ah.b])
            fw.op("dve", lambda e: e.tensor_tensor(out=al.t[:, :], in0=a.t[:, :], in1=ah.t[:, :], op=ALU.subtract),
                  reads=[a.b, ah.b], writes=[al.b])
            fw.op("dve", lambda e: e.tensor_copy(out=a2.t[:, 0:32], in_=ah.t[:, :]), reads=[ah.b], writes=[a2.b])
            fw.op("dve", lambda e: e.tensor_copy(out=a2.t[:, 32:64], in_=al.t[:, :]), reads=[al.b], writes=[a2.b], waw=False)
            for q, lhs in enumerate((MR, ML, self.ones_bf.t[:, :])):
                for hl in range(2):
                    fw.op("pe", lambda e, q=q, lhs=lhs, hl=hl: e.matmul(
                        pE.t[:, q * 32:(q + 1) * 32], lhsT=lhs, rhs=a2.t[:, hl * 32:(hl + 1) * 32],
                        start=(hl == 0), stop=(hl == 1)), reads=[a2.b, self.cmb.b, self.ones_bf.b], writes=[pE.b],
                          ticket=(q == 2 and hl == 1), waw=(q == 0 and hl == 0))
            fw.op("act", lambda e: e.activation(out=E.t[:, :], in_=pE.t[:, 0:96], func=AF.Exp), reads=[pE.b], writes=[E.b])
            fw.op("dve", lambda e: e.tensor_tensor(out=e2.t[:, :], in0=E.t[:, 32:64], in1=dtd, op=ALU.mult),
                  reads=[E.b, dt_.b], writes=[e2.b])
            for g in range(4):
                fw.op("pe", lambda e, g=g: e.matmul(pcb.t[:, g * 128:(g + 1) * 128], lhsT=xbc.t[:, 16 + g, :],
                                                    rhs=xbc.t[:, 20 + g, :], start=True, stop=True),
                      reads=[xbc.b], writes=[pcb.b], ticket=(g == 3), waw=(g == 0))
            for g in range(4):
                fw.op("dve", lambda e, g=g: e.tensor_tensor(out=cbm.t[:, g, :], in0=pcb.t[:, g * 128:(g + 1) * 128],
                                                            in1=MRf, op=ALU.mult),
                      reads=[pcb.b, self.cm.b], writes=[cbm.b], waw=(g == 0))
            x3 = xtok.t[:, 0:SSM_DI].rearrange("p (h c) -> p h c", c=64)
            fw.op("pool", lambda e: e.tensor_tensor(out=xdt.t[:, :].rearrange("p (h c) -> p h c", c=64), in0=x3,
                                                    in1=dtd.unsqueeze(2).broadcast_to([128, 32, 64]), op=ALU.mult),
                  reads=[xtok.b, dt_.b], writes=[xdt.b])
            fw.op("pool", lambda e: e.tensor_tensor(out=xwa.t[:, :].rearrange("p (h c) -> p h c", c=64), in0=x3,
                                                    in1=e2.t[:, :].unsqueeze(2).broadcast_to([128, 32, 64]), op=ALU.mult),
                  reads=[xtok.b, e2.b], writes=[xwa.b])

        def main(ci):
            xbc, dt_, r0, c0, xtok, a, ahb, ah, al, a2, E, e2, cbm, xdt, xwa = binds(ci)
            dtd = dt_.t[:, 32 * d:32 * d + 32]
            if d == 1:
                yf, z = yfs[ci % 2], zs[ci % 2]
            for g in range(4):
                for hq in range(2):
                    hi = hic[0]
                    ps_ = pseg[hi % 2]
                    dc = dec[hi % 2]
                    Rh, Rl, w4 = Rhs[hi % 2], Rls[hi % 2], wT4[hi % 2]
                    hic[0] += 1
                    h0 = g * 8 + hq * 4
                    fw.op("dve", lambda e, h0=h0, Rh=Rh: e.tensor_tensor(
                        out=Rh.t[:, :].rearrange("p (r c) -> p r c", c=128),
                        in0=MR.unsqueeze(1).broadcast_to([128, 4, 128]),
                        in1=ah.t[:, h0:h0 + 4].unsqueeze(2).broadcast_to([128, 4, 128]), op=ALU.mult),
                          reads=[ah.b, self.cmb.b], writes=[Rh.b])
                    for r4 in range(4):
                        fw.op("act", lambda e, h0=h0, r4=r4, Rl=Rl: e.activation(
                            out=Rl.t[:, r4 * 128:(r4 + 1) * 128], in_=MR, func=AF.Copy, scale=al.t[:, h0 + r4:h0 + r4 + 1]),
                              reads=[al.b, self.cmb.b], writes=[Rl.b], waw=(r4 == 0))
                    fw.op("pe", lambda e, Rh=Rh, ps_=ps_: e.matmul(ps_.t[:, :], lhsT=ML, rhs=Rh.t[:, :], start=True, stop=False),
                          reads=[Rh.b, self.cmb.b], writes=[ps_.b], ticket=False)
                    fw.op("pe", lambda e, Rl=Rl, ps_=ps_: e.matmul(ps_.t[:, :], lhsT=ML, rhs=Rl.t[:, :], start=False, stop=True),
                          reads=[Rl.b, Rh.b, self.cmb.b], writes=[ps_.b])
                    fw.op("act", lambda e, ps_=ps_, dc=dc: e.activation(out=dc.t[:, :], in_=ps_.t[:, :], func=AF.Exp),
                          reads=[ps_.b], writes=[dc.b])
                    fw.op("dve", lambda e, w4=w4, dc=dc, g=g: e.tensor_tensor(
                        out=w4.t[:, :].rearrange("p (r c) -> p r c", c=128), in0=dc.t[:, :].rearrange("p (r c) -> p r c", c=128),
                        in1=cbm.t[:, g, :].unsqueeze(1).broadcast_to([128, 4, 128]), op=ALU.mult),
                          reads=[dc.b, cbm.b], writes=[w4.b])
                    for r4 in range(4):
                        r = hq * 4 + r4
                        h = g * 8 + r
                        fw.op("pe", lambda e, r=r, r4=r4, h=h, w4=w4: e.matmul(
                            pY.t[:, r * 64:(r + 1) * 64], lhsT=w4.t[:, r4 * 128:(r4 + 1) * 128], rhs=xdt.t[:, h * 64:(h + 1) * 64],
                            start=True, stop=True), reads=[w4.b, xdt.b], writes=[pY.b], ticket=(r4 == 3), waw=(r == 0))
                fw.op("pe", lambda e, g=g: e.matmul(pYs.t[:, :], lhsT=xbc.t[:, 20 + g, :], rhs=Sb.t[:, g * 512:(g + 1) * 512],
                                                    start=True, stop=True), reads=[xbc.b, Sb.b], writes=[pYs.b])
                fw.op("dve", lambda e, g=g: e.tensor_tensor(
                    out=tmpy.t[:, :].rearrange("p (r c) -> p r c", c=64), in0=pYs.t[:, :].rearrange("p (r c) -> p r c", c=64),
                    in1=bc3(E.t[:, g * 8:(g + 1) * 8]), op=ALU.mult), reads=[pYs.b, E.b], writes=[tmpy.b])
                yo_ = yo[ci % 2]
                gs_ = slice(g * 512, (g + 1) * 512)
                fw.op("dve", lambda e, yo_=yo_, gs_=gs_: e.tensor_tensor(out=yo_.t[:, gs_], in0=pY.t[:, :], in1=tmpy.t[:, :],
                                                                        op=ALU.add),
                      reads=[pY.b, tmpy.b], writes=[yo_.b], waw=(g == 0))
                fw.op("pe", lambda e, g=g, gs_=gs_: e.matmul(
                    pdS.t[:, :], lhsT=xtok.t[:, 2048 + g * 128:2048 + (g + 1) * 128], rhs=xwa.t[:, gs_], start=True, stop=True),
                      reads=[xtok.b, xwa.b], writes=[pdS.b])
                fw.op("dve", lambda e, g=g, gs_=gs_: e.tensor_tensor(
                    out=S.t[:, gs_].rearrange("p (r c) -> p r c", c=64), in0=S.t[:, gs_].rearrange("p (r c) -> p r c", c=64),
                    in1=bc3(E.t[:, 64 + g * 8:64 + (g + 1) * 8]), op=ALU.mult), reads=[S.b, E.b], writes=[S.b])
                fw.op("dve", lambda e, gs_=gs_: e.tensor_tensor(out=S.t[:, gs_], in0=S.t[:, gs_], in1=pdS.t[:, :], op=ALU.add),
                      reads=[S.b, pdS.b], writes=[S.b])
                fw.op("act", lambda e, gs_=gs_: e.activation(out=Sb.t[:, gs_], in_=S.t[:, gs_], func=AF.Copy),
                      reads=[S.b], writes=[Sb.b])
            yo_ = yo[ci % 2]
            if d == 0:
                fw.dma(self.YF[r0:r0 + 128, :], yo_.t[:, :], yo_.b, reads=[yo_.b], writes=[self.db(self.YF)], waw=False,
                       stream="pool")
            else:
                fw.op("pool", lambda e: e.tensor_tensor(out=yo_.t[:, :], in0=yo_.t[:, :], in1=yf.t[:, :], op=ALU.add),
                      reads=[yo_.b, yf.b], writes=[yo_.b])
                for g in range(4):
                    gs_ = slice(g * 512, (g + 1) * 512)
                    fw.op("dve", lambda e, g=g, gs_=gs_: e.tensor_tensor(
                        out=tmpy.t[:, :].rearrange("p (r c) -> p r c", c=64),
                        in0=xtok.t[:, gs_].rearrange("p (r c) -> p r c", c=64),
                        in1=bc3(drow.t[:, g * 8:(g + 1) * 8]), op=ALU.mult), reads=[xtok.b, drow.b], writes=[tmpy.b])
                    fw.op("dve", lambda e, gs_=gs_: e.tensor_tensor(out=yo_.t[:, gs_], in0=yo_.t[:, gs_], in1=tmpy.t[:, :],
                                                                    op=ALU.add), reads=[yo_.b, tmpy.b], writes=[yo_.b])
                fw.op("act", lambda e: e.activation(out=sz.t[:, :], in_=z.t[:, :], func=AF.Silu), reads=[z.b], writes=[sz.b])
                fw.op("dve", lambda e: e.tensor_tensor(out=yo_.t[:, :], in0=yo_.t[:, :], in1=sz.t[:, :], op=ALU.mult),
                      reads=[yo_.b, sz.b], writes=[yo_.b])
                fw.op("pool", lambda e: e.memset(ssq.t[:, :], 0.0), writes=[ssq.b])
                fw.op("act", lambda e: e.activation(out=sqj.t[:, :], in_=yo_.t[:, :], func=AF.Square, accum_out=ssq.t[:, :]),
                      reads=[yo_.b], writes=[sqj.b, ssq.b])
                fw.op("act", lambda e: e.activation(out=rt1.t[:, :], in_=ssq.t[:, :], func=AF.Sqrt, scale=1.0 / SSM_DI, bias=EPS),
                      reads=[ssq.b], writes=[rt1.b])
                fw.op("dve", lambda e: e.reciprocal(out=rs1.t[:, :], in_=rt1.t[:, :]), reads=[rt1.b], writes=[rs1.b])
                fw.op("dve", lambda e: e.scalar_tensor_tensor(out=ygn.t[:, :], in0=yo_.t[:, :], scalar=rs1.t[:, 0:1],
                                                              in1=ng.t[:, :], op0=ALU.mult, op1=ALU.mult),
                      reads=[yo_.b, rs1.b, ng.b], writes=[ygn.b])
                yt = ygT[ci % 2]
                for q in range(4):
                    for j in range(4):
                        c = q * 4 + j
                        fw.op("pe", lambda e, c=c, j=j: e.transpose(pT.t[:, j * 128:(j + 1) * 128],
                                                                    ygn.t[:, c * 128:(c + 1) * 128], self.ident_bf()),
                              reads=[ygn.b, self.cmb.b], writes=[pT.b], ticket=(j == 3), waw=(j == 0))
                    if q % 2 == 0:
                        fw.op("act", lambda e, q=q: e.activation(out=yt.t[:, q * 4:(q + 1) * 4, :],
                                                                 in_=pT.t[:, 0:512].rearrange("p (a b) -> p a b", b=128), func=AF.Copy),
                              reads=[pT.b], writes=[yt.b], waw=False)
                    else:
                        fw.op("dve", lambda e, q=q: e.tensor_copy(out=yt.t[:, q * 4:(q + 1) * 4, :],
                                                                  in_=pT.t[:, 0:512].rearrange("p (a b) -> p a b", b=128)),
                              reads=[pT.b], writes=[yt.b], waw=False)
                fw.dma(YGv[:, :, c0:c0 + 128], yt.t[:, :, :], yt.b, reads=[yt.b], writes=[self.db(self.YGT)], waw=False,
                       stream="pool")
        prologue(0)
        for ci in range(len(order)):
            if ci + 1 < len(order):
                prologue(ci + 1)
            main(ci)
        self.end()


def build_program(T, voff, roff, nv, nr, n_layers=DEPTH, debug=False):
    p = Prog(T, voff, roff, nv, nr, n_layers=n_layers, debug=debug)
    p.setup()
    p.mod_phase()
    Xa, Xb = p.XA, p.XB
    for l in range(n_layers):
        if l % 2 == 0:
            i = l // 2
            p.attn_inproj(l, Xa)
            p.attn_core(l)
            p.proj_res_norm(l, Xa, Xb, p.OT, KD, lambda k, i=i: p.i_awo[i, k * 128:(k + 1) * 128, :])
        else:
            i = l // 2
            p.ssm_inproj(l, Xa)
            p.ssm_scan(l, 0)
            p.ssm_scan(l, 1)
            p.proj_res_norm(l, Xa, Xb, p.YGT, SSM_DI // 128, lambda k, i=i: p.i_swo[i, k * 128:(k + 1) * 128, :])
        p.ffn_in(l)
        p.ffn_out(l, Xb, Xa)
    p.final_norm(Xa)
    p.pes.close()
    p.fw.close()
    return p


_CACHE = {}


def run(inputs, T=8192, n_layers=DEPTH, debug=False, cores=NCORES):
    cfg, voff, roff, shared = prep_shared(inputs, T)
    key = (T, n_layers, debug)
    if key not in _CACHE:
        _CACHE[key] = build_program(T, voff, roff, shared["vecs"].shape[1], shared["rowv"].shape[1],
                                    n_layers=n_layers, debug=debug)
    p = _CACHE[key]
    in_maps = []
    for b in range(cores):
        m = dict(shared)
        m.update(prep_core(inputs, b, T))
        in_maps.append(m)
    res = run_bass_kernel_spmd(p.nc, in_maps, core_ids=list(range(cores)))
    return p, res


def kernel(**inputs):
    p, res = run(inputs)
    out = np.stack([np.ascontiguousarray(r["outT"].T) for r in res.results], axis=0)
    return out.astype(np.float32)
```

```python
import math
from contextlib import ExitStack

import numpy as np
import concourse.bass as bass
import concourse.mybir as mybir
from concourse.bass_utils import run_bass_kernel_spmd

F32 = mybir.dt.float32
BF16 = mybir.dt.bfloat16
AF = mybir.ActivationFunctionType
ALU = mybir.AluOpType


class Sem:
    def __init__(self, h, name):
        self.h = h
        self.cnt = 0
        self.name = name


class Buf:
    def __init__(self, ap=None, name=""):
        self.ap = ap
        self.name = name
        self.last_w = {}
        self.readers = {}
        self.gen_deps = {}
        self.dsem = None


def _merge(dst, src):
    for s, v in src.items():
        if dst.get(s, 0) < v:
            dst[s] = v


class FW:
    def __init__(self, nc, n_dma_sems=48, n_spare=28):
        self.nc = nc
        self.es = ExitStack()
        self.streams = {
            "pe": nc.tensor,
            "act": nc.scalar,
            "dve": nc.vector,
            "pool": nc.gpsimd,
            "sp": nc.sync,
        }
        self.esem = {}
        for k in ("pe", "act", "dve", "pool"):
            self.esem[k] = Sem(self.es.enter_context(nc.semaphore("e_" + k)), k)
        self.known = {k: {} for k in self.streams}
        self.free_dsems = [
            Sem(self.es.enter_context(nc.semaphore("d%d" % i)), "d%d" % i) for i in range(n_dma_sems)
        ]
        self.spare = [Sem(self.es.enter_context(nc.semaphore("x%d" % i)), "x%d" % i) for i in range(n_spare)]
        self.phase_bufs = []
        self.pers_bufs = []
        self.n_inst = 0
        self.n_wait = 0

    def buf(self, ap=None, name="", persistent=False):
        b = Buf(ap, name)
        (self.pers_bufs if persistent else self.phase_bufs).append(b)
        return b

    def _wait_for(self, stream, deps, fold=False):
        eng = self.streams[stream]
        kn = self.known[stream]
        need = [(s, v) for s, v in deps.items() if kn.get(s, 0) < v]
        last = None
        if fold and need:
            last = need.pop()
            kn[last[0]] = last[1]
        for s, v in need:
            eng.wait_ge(s.h, v)
            kn[s] = v
            self.n_wait += 1
        return last

    def _deps(self, stream, reads, writes, waw=True):
        deps = {}
        for r in reads:
            _merge(deps, r.last_w)
        for w in writes:
            if waw or w.readers:
                _merge(deps, w.last_w)
                _merge(deps, w.readers)
            else:
                _merge(deps, w.gen_deps)
        if stream == "pe":
            deps.pop(self.esem["pe"], None)
        return deps

    def _commit(self, t, reads, writes, waw):
        for w in writes:
            if waw or w.readers:
                g = dict(w.last_w)
                _merge(g, w.readers)
                w.gen_deps = g
                w.last_w = dict(t)
                w.readers = {}
            else:
                _merge(w.last_w, t)
        for r in reads:
            _merge(r.readers, t)

    def op(self, stream, fn, reads=(), writes=(), ticket=True, waw=True):
        deps = self._deps(stream, reads, writes, waw)
        last = self._wait_for(stream, deps, fold=True)
        inst = fn(self.streams[stream])
        if last is not None:
            inst._wait_ge(last[0].h, last[1])
        self.n_inst += 1
        if ticket:
            s = self.esem[stream]
            if s.cnt >= 30000:
                s = self.spare.pop()
                self.esem[stream] = s
            s.cnt += 1
            inst.then_inc(s.h, 1)
            self._commit({s: s.cnt}, reads, writes, waw)
        return inst

    def dma(self, out_ap, in_ap, owner, reads=(), writes=(), stream="sp", waw=True, **kw):
        deps = self._deps(stream, reads, writes, waw)
        last = self._wait_for(stream, deps, fold=True)
        if owner.dsem is None:
            owner.dsem = self.free_dsems.pop(0)
        s = owner.dsem
        s.cnt += 16
        inst = self.streams[stream].dma_start(out=out_ap, in_=in_ap, **kw)
        if last is not None:
            inst._wait_ge(last[0].h, last[1])
        inst.then_inc(s.h, 16)
        self.n_inst += 1
        self._commit({s: s.cnt}, reads, writes, waw)

    def barrier(self, clear=True):
        alld = {}
        for s in self.esem.values():
            if s.cnt:
                alld[s] = s.cnt
        for b in self.phase_bufs + self.pers_bufs:
            if b.dsem is not None:
                alld[b.dsem] = b.dsem.cnt
        for st in self.streams:
            d = dict(alld)
            self._wait_for(st, d)
        for b in self.phase_bufs + self.pers_bufs:
            if b.dsem is not None:
                self.free_dsems.append(b.dsem)
                b.dsem = None
            b.last_w = {}
            b.readers = {}
            b.gen_deps = {}
        if clear:
            self.phase_bufs = []

    def close(self):
        self.es.close()


D = 1024
KD = D // 128
DEPTH = 4
CTX = 256
GRID_W = 64
HD = 64
EPS = 1e-6
DFF = 2816
NFF = DFF // 128
SSM_DI = 2048
SSM_XBC = 3072
SSM_IN = 5184
SSM_H = 32
NCORES = 8
ATT_FM = 1664
ATT_EXT = 2 * ATT_FM + 640


class Cfg:
    def __init__(self, T):
        self.T = T
        self.C = CTX
        self.CS = 1
        self.LS = CTX + 3
        self.NP = CTX + T + 4
        self.NTOK = CTX + T
        self.seqs = [("ctx", self.CS, CTX, 1, 0), ("lat", self.LS, T, 0, CTX)]


def _partner():
    p = np.arange(64)
    return np.where((p % 32) < 16, p + 16, p - 16)


class VecLayout:
    def __init__(self):
        self.off = {}
        self.n = 0
        self.cols = []

    def add(self, name, arr):
        arr = np.ascontiguousarray(arr, dtype=np.float32).reshape(128, -1)
        self.off[name] = (self.n, arr.shape[1])
        self.n += arr.shape[1]
        self.cols.append(arr)

    def build(self):
        return np.ascontiguousarray(np.concatenate(self.cols, axis=1))


def fm(v):
    v = np.asarray(v, dtype=np.float32)
    return np.ascontiguousarray(v.reshape(-1, 128).T)


def prep_shared(inp, T):
    cfg = Cfg(T)
    vl = VecLayout()
    rows = {}
    for l in range(DEPTH):
        vl.add("modb%d" % l, fm(inp["mod_b"][l]))
        vl.add("gmix%d" % l, fm(inp["norm_mix_g"][l]))
        vl.add("gffn%d" % l, fm(inp["norm_ffn_g"][l]))
        cw = inp["ffn_conv_w"][l]
        for i in range(3):
            vl.add("fcw%d_%d" % (l, i), fm(cw[i]))
        vl.add("fcb%d" % l, fm(inp["ffn_conv_b"][l]))
    vl.add("gfin", fm(inp["final_norm_g"]))
    pt = _partner()
    for i in range(DEPTH // 2 + DEPTH % 2):
        gq = np.asarray(inp["gqa_q_norm_g"][i], np.float32)
        gk = np.asarray(inp["gqa_k_norm_g"][i], np.float32)
        vl.add("gq%d" % i, np.concatenate([gq, gq])[:, None])
        vl.add("gqs%d" % i, np.concatenate([gq[pt], gq[pt]])[:, None])
        vl.add("gk%d" % i, np.concatenate([gk, gk])[:, None])
        vl.add("gks%d" % i, np.concatenate([gk[pt], gk[pt]])[:, None])
        vl.add("subg%d" % i, np.asarray(inp["diff_subln_g"][i], np.float32)[:, None])
    for i in range(DEPTH // 2):
        cw = inp["ssm_conv_w"][i]
        for j in range(3):
            vl.add("scw%d_%d" % (i, j), fm(cw[j]))
        vl.add("scb%d" % i, fm(inp["ssm_conv_b"][i]))
    vecs = vl.build()

    rl = VecLayout()

    def radd(name, v):
        v = np.asarray(v, np.float32).reshape(1, -1)
        rl.off[name] = (rl.n, v.shape[1])
        rl.n += v.shape[1]
        rl.cols.append(v)

    for i in range((DEPTH + 1) // 2):
        radd("lq1_%d" % i, inp["diff_lq1"][i])
        radd("lk1_%d" % i, inp["diff_lk1"][i])
        radd("lq2_%d" % i, inp["diff_lq2"][i])
        radd("lk2_%d" % i, inp["diff_lk2"][i])
    for i in range(DEPTH // 2):
        radd("dtb%d" % i, inp["ssm_dt_bias"][i])
        radd("alog%d" % i, inp["ssm_a_log"][i])
        radd("ssd%d" % i, inp["ssm_d"][i])
        radd("sng%d" % i, inp["ssm_norm_g"][i])
    rowv = np.ascontiguousarray(np.concatenate(rl.cols, axis=1))

    w_ext = []
    for i in range((DEPTH + 1) // 2):
        w = np.asarray(inp["attn_w_in"][i])
        qa, ka, va = w[:, 0:512], w[:, 512:1024], w[:, 1024:1536]
        qb, kb, vb = w[:, 1536:2048], w[:, 2048:2176], w[:, 2176:2304]
        qbp = np.concatenate(
            [np.concatenate([qb[:, j * 64:(j + 1) * 64], qb[:, (4 + j) * 64:(5 + j) * 64]], axis=1) for j in range(4)],
            axis=1)
        fmw = np.concatenate([qa, ka, qbp, kb], axis=1)
        idx = (np.arange(ATT_FM) // 64) * 64 + pt[np.arange(ATT_FM) % 64]
        w_ext.append(np.concatenate([fmw, fmw[:, idx], va, vb], axis=1))
    w_ext = np.ascontiguousarray(np.stack(w_ext))

    NP = cfg.NP
    cosT = np.ones((128, NP), np.float32)
    sinT = np.zeros((128, NP), np.float32)
    t = np.arange(T)
    row = (t // GRID_W).astype(np.float32)
    col = (t % GRID_W).astype(np.float32)
    inv = (1.0 / (10000.0 ** (np.arange(16, dtype=np.float32) / 16))).astype(np.float32)
    for p in range(64):
        q, i = p // 16, p % 16
        ang = ((row if q < 2 else col) * inv[i]).astype(np.float32)
        sgn = -1.0 if q in (0, 2) else 1.0
        for rep in (0, 64):
            cosT[p + rep, cfg.LS:cfg.LS + T] = np.cos(ang)
            sinT[p + rep, cfg.LS:cfg.LS + T] = sgn * np.sin(ang)

    tt = np.arange(128)
    consts = {
        "ident": np.eye(128, dtype=np.float32),
        "ML_f": (tt[:, None] > tt[None, :]).astype(np.float32),
        "MR_f": (tt[:, None] <= tt[None, :]).astype(np.float32),
        "ML_b": (tt[:, None] < tt[None, :]).astype(np.float32),
        "MR_b": (tt[:, None] >= tt[None, :]).astype(np.float32),
    }
    bd = np.zeros((128, 128), np.float32)
    bd[:64, :64] = 1
    bd[64:, 64:] = 1
    consts["bd"] = bd
    cmat = np.ascontiguousarray(
        np.concatenate([consts[k] for k in ("ident", "ML_f", "MR_f", "ML_b", "MR_b", "bd")], axis=1))

    shared = {
        "vecs": vecs, "rowv": rowv, "w_ext": w_ext, "cosT": cosT, "sinT": sinT, "cmat": cmat,
        "mod_w": np.asarray(inp["mod_w"], np.float32),
        "attn_w_out": np.asarray(inp["attn_w_out"], np.float32),
        "ssm_w_in": np.asarray(inp["ssm_w_in"], np.float32),
        "ssm_w_out": np.asarray(inp["ssm_w_out"], np.float32),
        "ffn_w_in": np.asarray(inp["ffn_w_in"], np.float32),
        "ffn_w_out": np.asarray(inp["ffn_w_out"], np.float32),
    }
    return cfg, vl.off, rl.off, shared


def prep_core(inp, b, T):
    xT = np.ascontiguousarray(np.asarray(inp["x"][b, :T]).T)
    cxT = np.ascontiguousarray(np.asarray(inp["ctx"][b]).T)
    cT = np.stack([fm(inp["c"][b]), fm(inp["c_ctx"])], axis=2)
    return {"xT": xT, "cxT": cxT, "cT": np.ascontiguousarray(cT.reshape(128, 16))}


class TB:
    def __init__(self, t, b, bs=None):
        self.t = t
        self.b = b
        self.bs = bs


def blocks_of(n, size):
    out = []
    t0 = 0
    while t0 < n:
        out.append((t0, min(size, n - t0)))
        t0 += size
    return out


class Prog:
    def __init__(self, T, voff, roff, nv, nr, n_layers=DEPTH, debug=False):
        self.cfg = Cfg(T)
        self.voff, self.roff = voff, roff
        self.debug = debug
        self.n_layers = n_layers
        nc = bass.Bass("TRN2", target_bir_lowering=False)
        self.nc = nc
        self.fw = FW(nc)
        cfg = self.cfg
        NP, NTOK = cfg.NP, cfg.NTOK
        di = lambda name, shape, dt=F32: nc.dram_tensor(name, shape, dt, kind="ExternalInput").ap()
        self.i_xT = di("xT", [D, T])
        self.i_cxT = di("cxT", [D, CTX])
        self.i_cT = di("cT", [128, 16])
        self.i_vecs = di("vecs", [128, nv])
        self.i_rowv = di("rowv", [1, nr])
        self.i_wext = di("w_ext", [(DEPTH + 1) // 2, D, ATT_EXT])
        self.i_cos = di("cosT", [128, NP])
        self.i_sin = di("sinT", [128, NP])
        self.i_cmat = di("cmat", [128, 768])
        self.i_modw = di("mod_w", [DEPTH, D, 6 * D])
        self.i_awo = di("attn_w_out", [(DEPTH + 1) // 2, D, D])
        self.i_swi = di("ssm_w_in", [DEPTH // 2, D, SSM_IN])
        self.i_swo = di("ssm_w_out", [DEPTH // 2, SSM_DI, D])
        self.i_fwi = di("ffn_w_in", [DEPTH, D, 2 * DFF])
        self.i_fwo = di("ffn_w_out", [DEPTH, DFF, D])
        self.o_out = nc.dram_tensor("outT", [D, T], F32, kind="ExternalOutput").ap()
        self.dbg = {}
        kind = "ExternalOutput" if debug else "Internal"

        def scr(name, shape, dt):
            ap = nc.dram_tensor(name, shape, dt, kind=kind).ap()
            if debug:
                self.dbg[name] = ap
            return ap

        self.XA = scr("XA", [D, NP], F32)
        self.XB = scr("XB", [D, NP], F32)
        self.HT = scr("HT", [D, NP], BF16)
        self.QT = scr("QT", [D, NP], BF16)
        self.KT = scr("KT", [5 * 128, NP], BF16)
        self.VT = scr("VT", [NTOK, 640], BF16)
        self.OT = scr("OT", [D, NP], BF16)
        self.UT = scr("UT", [DFF, NP], BF16)
        self.XBCT = scr("XBCT", [SSM_XBC, NP], BF16)
        self.ZT = scr("ZT", [NTOK, SSM_DI], F32)
        self.DTT = scr("DTT", [NTOK, 64], F32)
        self.YF = scr("YF", [NTOK, SSM_DI], F32)
        self.YGT = scr("YGT", [SSM_DI, NP], BF16)
        self.dram_b = {}
        self.pes = ExitStack()
        self.ph = None
        self._names = 0

    def _nm(self, name):
        self._names += 1
        return "%s_%d" % (name, self._names)

    def sb(self, name, shape, dt=F32, nb=0, pers=False):
        st = self.pes if pers else self.ph
        t = st.enter_context(self.nc.sbuf_tensor(self._nm(name), shape, dt))
        b = self.fw.buf(name=name, persistent=pers)
        bs = [self.fw.buf(name=name + str(i), persistent=pers) for i in range(nb)] if nb else None
        return TB(t, b, bs)

    def ps(self, name, shape, dt=F32):
        t = self.ph.enter_context(self.nc.psum_tensor(self._nm(name), shape, dt))
        return TB(t, self.fw.buf(name=name))

    def db(self, ap):
        k = ap.name if hasattr(ap, "name") else id(ap)
        if k not in self.dram_b:
            self.dram_b[k] = self.fw.buf(name="dram", persistent=True)
        return self.dram_b[k]

    def begin(self):
        self.ph = ExitStack()

    def end(self):
        self.fw.barrier()
        self.ph.close()
        self.ph = None

    def vec(self, name, j=0, w=1):
        o, n = self.voff[name]
        return self.vecs.t[:, o + j:o + j + w]

    def load_w(self, dst, src_rows, ncols, col0=0, dst_col0=0, kc=None):
        fw = self.fw
        kc = kc if kc is not None else dst.t.shape[1]
        PIECE = 1408
        main_ph = self.ph
        self.ph = ExitStack()
        wst = [self.sb("wst", [128, PIECE], F32) for _ in range(3)]
        wi = 0
        for k in range(kc):
            for (c0, n) in blocks_of(ncols, PIECE):
                st = wst[wi % 3]
                eng = ("dve", "pool", "act")[wi % 3]
                wi += 1
                src = src_rows(k)[:, col0 + c0:col0 + c0 + n]
                fw.dma(st.t[:, 0:n], src, st.b, writes=[st.b])
                d = dst.t[:, k, dst_col0 + c0:dst_col0 + c0 + n]
                if eng == "act":
                    fw.op("act", lambda e, d=d, st=st, n=n: e.activation(out=d, in_=st.t[:, 0:n], func=AF.Copy),
                          reads=[st.b], writes=[dst.b], waw=False)
                else:
                    fw.op(eng, lambda e, d=d, st=st, n=n: e.tensor_copy(out=d, in_=st.t[:, 0:n]),
                          reads=[st.b], writes=[dst.b], waw=False)
        fw.barrier(clear=False)
        self.ph.close()
        self.ph = main_ph

    def rstd(self, ss, out, tmp, scale, n):
        fw = self.fw
        fw.op("act", lambda e: e.activation(out=tmp.t[:, 0:n], in_=ss.t[:, 0:n], func=AF.Sqrt, scale=scale, bias=EPS),
              reads=[ss.b], writes=[tmp.b])
        fw.op("dve", lambda e: e.reciprocal(out=out.t[:, 0:n], in_=tmp.t[:, 0:n]), reads=[tmp.b], writes=[out.b])

    def norm_mod(self, x, n, ss, sq, rs, tmp, gs, sh, out, out_dt_bf16=True):
        fw = self.fw
        for k in range(KD):
            fw.op("act", lambda e, k=k: e.activation(out=sq.t[:, k, 0:n], in_=x.t[:, k, 0:n], func=AF.Square),
                  reads=[x.b], writes=[sq.b], waw=False)
        for k in range(KD):
            fw.op("pe", lambda e, k=k: e.matmul(ss.t[:, 0:n], lhsT=self.ones_bf.t[:, :], rhs=sq.t[:, k, 0:n],
                                                start=(k == 0), stop=(k == KD - 1)),
                  reads=[sq.b, self.ones_bf.b], writes=[ss.b], ticket=(k == KD - 1))
        self.rstd(ss, rs, tmp, 1.0 / D, n)
        for k in range(KD):
            if sh is None:
                fw.op("dve", lambda e, k=k: e.scalar_tensor_tensor(
                    out=out.t[:, k, 0:n], in0=x.t[:, k, 0:n], scalar=gs(k), in1=rs.t[:, 0:n],
                    op0=ALU.mult, op1=ALU.mult), reads=[x.b, rs.b], writes=[out.b], waw=False)
            else:
                tm = self._nm_tmp[k % 2]
                fw.op("dve", lambda e, k=k, tm=tm: e.scalar_tensor_tensor(
                    out=tm.t[:, 0:n], in0=x.t[:, k, 0:n], scalar=gs(k), in1=rs.t[:, 0:n],
                    op0=ALU.mult, op1=ALU.mult), reads=[x.b, rs.b], writes=[tm.b])
                fw.op("act", lambda e, k=k, tm=tm: e.activation(out=out.t[:, k, 0:n], in_=tm.t[:, 0:n],
                                                         func=AF.Identity, bias=sh(k), scale=1.0),
                      reads=[tm.b], writes=[out.b], waw=False)

    def norm_tiles(self, nmax=512):
        self._nm_tmp = [self.sb("nmt", [128, nmax], F32) for _ in range(2)]
        return dict(ss=self.ps("ss", [128, 512], F32), sq=self.sb("sq", [128, KD, nmax], BF16),
                    rs=self.sb("rs", [128, nmax], F32), tmp=self.sb("rtmp", [128, nmax], F32))

    def load_cols(self, dst, dview, seq, t0, n, halo, extra_reads=()):
        fw = self.fw
        _, start, ln, _, _ = seq
        lo, hi = t0 - halo, t0 + n + halo
        clo, chi = max(lo, 0), min(hi, ln)
        if clo > lo:
            fw.op("pool", lambda e: e.memset(dst.t[:, :, 0:clo - lo], 0.0), writes=[dst.b])
        if chi < hi:
            fw.op("pool", lambda e: e.memset(dst.t[:, :, chi - lo:hi - lo], 0.0), writes=[dst.b],
                  waw=(clo == lo))
        fw.dma(dst.t[:, :, clo - lo:chi - lo], dview[:, :, start + clo:start + chi], dst.b,
               reads=[self.db(dview)], writes=[dst.b], waw=(clo == lo and chi == hi))

    def dvv(self, l, v, which):
        base = ((l * 2 + v) * 6 + which) * 8
        return lambda k: self.dv.t[:, base + k:base + k + 1]

    def setup(self):
        fw, cfg = self.fw, self.cfg
        nv = self.i_vecs.shape[1]
        self.vecs = self.sb("vecs", [128, nv], F32, pers=True)
        self.cm = self.sb("cmat", [128, 768], F32, pers=True)
        self.cmb = self.sb("cmatb", [128, 768], BF16, pers=True)
        self.ones_bf = self.sb("ones", [128, 128], BF16, pers=True)
        self.dv = self.sb("dv", [128, DEPTH * 2 * 6 * 8], F32, pers=True)
        self.begin()
        fw.dma(self.vecs.t[:, :], self.i_vecs[:, :], self.vecs.b, writes=[self.vecs.b])
        fw.dma(self.cm.t[:, :], self.i_cmat[:, :], self.cm.b, writes=[self.cm.b])
        fw.op("dve", lambda e: e.tensor_copy(out=self.cmb.t[:, :], in_=self.cm.t[:, :]), reads=[self.cm.b],
              writes=[self.cmb.b])
        fw.op("pool", lambda e: e.memset(self.ones_bf.t[:, :], 1.0), writes=[self.ones_bf.b])
        dummy = self.fw.buf(name="x0")
        XAv = self.XA.rearrange("(k p) n -> p k n", p=128)
        xin = self.i_xT.rearrange("(k p) n -> p k n", p=128)
        cin = self.i_cxT.rearrange("(k p) n -> p k n", p=128)
        for k in range(KD):
            fw.dma(XAv[:, k, cfg.LS:cfg.LS + cfg.T], xin[:, k, :], dummy, writes=[self.db(self.XA)], waw=False)
            fw.dma(XAv[:, k, cfg.CS:cfg.CS + cfg.C], cin[:, k, :], dummy, writes=[self.db(self.XA)], waw=False)
        self.end()

    def ident_bf(self):
        return self.cmb.t[:, 0:128]

    def cmask(self, name, bf=True):
        j = ("ident", "ML_f", "MR_f", "ML_b", "MR_b", "bd").index(name)
        return (self.cmb if bf else self.cm).t[:, j * 128:(j + 1) * 128]

    def mod_phase(self):
        fw = self.fw
        self.begin()
        ct = self.sb("ct", [128, 16], F32)
        sc = self.sb("sc", [128, 16], BF16)
        fw.dma(ct.t[:, :], self.i_cT[:, :], ct.b, writes=[ct.b])
        fw.op("act", lambda e: e.activation(out=sc.t[:, :], in_=ct.t[:, :], func=AF.Silu), reads=[ct.b], writes=[sc.b])
        modT = self.sb("modT", [128, DEPTH, 48, 2], F32)
        wst = [self.sb("mwst", [128, KD, 512], F32) for _ in range(2)]
        wb = [self.sb("mwb", [128, KD, 512], BF16) for _ in range(2)]
        pm = [self.ps("pm", [128, 512], F32) for _ in range(2)]
        it = 0
        for l in range(self.n_layers):
            mw = self.i_modw[l].rearrange("(k p) n -> p k n", p=128)
            for cg in range(12):
                s_, b_, p_ = wst[it % 2], wb[it % 2], pm[it % 2]
                fw.dma(s_.t[:, :, :], mw[:, :, cg * 512:(cg + 1) * 512], s_.b, writes=[s_.b])
                eng = "dve" if it % 2 == 0 else "pool"
                fw.op(eng, lambda e, s_=s_, b_=b_: e.tensor_copy(out=b_.t[:, :, :], in_=s_.t[:, :, :]),
                      reads=[s_.b], writes=[b_.b])
                for j in range(4):
                    for k in range(KD):
                        fw.op("pe", lambda e, j=j, k=k, b_=b_, p_=p_: e.matmul(
                            p_.t[:, 2 * j:2 * j + 2], lhsT=b_.t[:, k, j * 128:(j + 1) * 128],
                            rhs=sc.t[:, 2 * k:2 * k + 2], start=(k == 0), stop=(k == KD - 1)),
                              reads=[b_.b, sc.b], writes=[p_.b], ticket=(k == KD - 1 and j == 3))
                for j in range(4):
                    fw.op("dve", lambda e, j=j, p_=p_, l=l, cg=cg: e.tensor_scalar(
                        out=modT.t[:, l, cg * 4 + j, :], in0=p_.t[:, 2 * j:2 * j + 2],
                        scalar1=self.vec("modb%d" % l, cg * 4 + j), scalar2=None, op0=ALU.add),
                          reads=[p_.b], writes=[modT.b], waw=False)
                it += 1
        for l in range(self.n_layers):
            for v in range(2):
                def dvs(which):
                    base = ((l * 2 + v) * 6 + which) * 8
                    return self.dv.t[:, base:base + 8]
                for which, (sci, gname) in ((0, (8, "gmix%d" % l)), (3, (32, "gffn%d" % l))):
                    fw.op("dve", lambda e, which=which, sci=sci, gname=gname, l=l, v=v, dvs=dvs: e.scalar_tensor_tensor(
                        out=dvs(which), in0=modT.t[:, l, sci:sci + 8, v], scalar=1.0, in1=self.vec(gname, 0, 8),
                        op0=ALU.add, op1=ALU.mult), reads=[modT.b], writes=[self.dv.b], waw=False)
                for which, c0 in ((1, 0), (2, 16), (4, 24), (5, 40)):
                    fw.op("dve", lambda e, which=which, c0=c0, l=l, v=v, dvs=dvs: e.tensor_copy(
                        out=dvs(which), in_=modT.t[:, l, c0:c0 + 8, v]), reads=[modT.b], writes=[self.dv.b], waw=False)
        self.end()

    def attn_inproj(self, l, X):
        fw, cfg = self.fw, self.cfg
        i = l // 2
        self.begin()
        W = self.sb("wA", [128, KD, ATT_EXT], BF16)
        self.load_w(W, lambda k: self.i_wext[i, k * 128:(k + 1) * 128, :], ATT_EXT)
        nt = self.norm_tiles()
        xs = [self.sb("x", [128, KD, 512], F32) for _ in range(2)]
        hTs = [self.sb("hT", [128, KD, 512], BF16) for _ in range(2)]
        cs = [self.sb("cos", [128, 512], F32) for _ in range(2)]
        sn = [self.sb("sin", [128, 512], F32) for _ in range(2)]
        qko = [self.sb("qko", [128, 13, 512], BF16) for _ in range(1)]
        vo = [self.sb("vo", [128, 4, 640], BF16) for _ in range(1)]
        pP = [self.ps("pP", [128, 512]) for _ in range(2)]
        pS = [self.ps("pS", [128, 512]) for _ in range(2)]
        ssq = self.ps("ssq", [128, 512])
        pV = self.ps("pV", [128, 1024])
        sqn = self.sb("sqn", [128, 512], BF16)
        rq, tq = self.sb("rq", [128, 512]), self.sb("tq", [128, 512])
        Aq = [self.sb("Aq", [128, 512]) for _ in range(2)]
        Bq = [self.sb("Bq", [128, 512]) for _ in range(2)]
        t1 = [self.sb("t1", [128, 512]) for _ in range(2)]
        t2 = [self.sb("t2", [128, 512]) for _ in range(2)]
        Xv = X.rearrange("(k p) n -> p k n", p=128)
        QTv = self.QT.rearrange("(k p) n -> p k n", p=128)
        KTv = self.KT.rearrange("(k p) n -> p k n", p=128)
        bi = 0
        for seq in cfg.seqs:
            _, start, ln, v, row0 = seq
            lat = (v == 0)
            for (t0, n) in blocks_of(ln, 512):
                x, hT, qo, vv = xs[bi % 2], hTs[bi % 2], qko[0], vo[0]
                c_, s_ = cs[bi % 2], sn[bi % 2]
                bi += 1
                self.load_cols(x, Xv, seq, t0, n, 0)
                if lat:
                    fw.dma(c_.t[:, 0:n], self.i_cos[:, start + t0:start + t0 + n], c_.b, writes=[c_.b])
                    fw.dma(s_.t[:, 0:n], self.i_sin[:, start + t0:start + t0 + n], s_.b, writes=[s_.b])
                self.norm_mod(x, n, nt["ss"], nt["sq"], nt["rs"], nt["tmp"], self.dvv(l, v, 0), self.dvv(l, v, 1), hT)
                for j in range(13):
                    P, S = pP[j % 2], pS[j % 2]
                    for k in range(KD):
                        fw.op("pe", lambda e, k=k, j=j, P=P: e.matmul(
                            P.t[:, 0:n], lhsT=W.t[:, k, j * 128:(j + 1) * 128], rhs=hT.t[:, k, 0:n],
                            start=(k == 0), stop=(k == KD - 1)), reads=[W.b, hT.b], writes=[P.b], ticket=(k == KD - 1))
                    if lat:
                        for k in range(KD):
                            fw.op("pe", lambda e, k=k, j=j, S=S: e.matmul(
                                S.t[:, 0:n], lhsT=W.t[:, k, ATT_FM + j * 128:ATT_FM + (j + 1) * 128],
                                rhs=hT.t[:, k, 0:n], start=(k == 0), stop=(k == KD - 1)),
                                  reads=[W.b, hT.b], writes=[S.b], ticket=(k == KD - 1))
                    normed = j >= 8
                    A, B = P, S
                    if normed:
                        fw.op("act", lambda e, P=P: e.activation(out=sqn.t[:, 0:n], in_=P.t[:, 0:n], func=AF.Square),
                              reads=[P.b], writes=[sqn.b])
                        fw.op("pe", lambda e: e.matmul(ssq.t[:, 0:n], lhsT=self.cmask("bd"), rhs=sqn.t[:, 0:n],
                                                       start=True, stop=True), reads=[sqn.b, self.cmb.b], writes=[ssq.b])
                        self.rstd(ssq, rq, tq, 1.0 / HD, n)
                        gn, gsn = ("gq%d" % i, "gqs%d" % i) if j < 12 else ("gk%d" % i, "gks%d" % i)
                        A = Aq[j % 2]
                        fw.op("dve", lambda e, P=P, A=A, gn=gn: e.scalar_tensor_tensor(
                            out=A.t[:, 0:n], in0=P.t[:, 0:n], scalar=self.vec(gn), in1=rq.t[:, 0:n],
                            op0=ALU.mult, op1=ALU.mult), reads=[P.b, rq.b], writes=[A.b])
                        if lat:
                            B = Bq[j % 2]
                            fw.op("dve", lambda e, S=S, B=B, gsn=gsn: e.scalar_tensor_tensor(
                                out=B.t[:, 0:n], in0=S.t[:, 0:n], scalar=self.vec(gsn), in1=rq.t[:, 0:n],
                                op0=ALU.mult, op1=ALU.mult), reads=[S.b, rq.b], writes=[B.b])
                    if lat:
                        a1, a2 = t1[j % 2], t2[j % 2]
                        fw.op("dve", lambda e, A=A, a1=a1: e.tensor_tensor(
                            out=a1.t[:, 0:n], in0=A.t[:, 0:n], in1=c_.t[:, 0:n], op=ALU.mult),
                              reads=[A.b, c_.b], writes=[a1.b])
                        fw.op("dve", lambda e, B=B, a2=a2: e.tensor_tensor(
                            out=a2.t[:, 0:n], in0=B.t[:, 0:n], in1=s_.t[:, 0:n], op=ALU.mult),
                              reads=[B.b, s_.b], writes=[a2.b])
                        fw.op("pool", lambda e, a1=a1, a2=a2, j=j: e.tensor_tensor(
                            out=qo.t[:, j, 0:n], in0=a1.t[:, 0:n], in1=a2.t[:, 0:n], op=ALU.add),
                              reads=[a1.b, a2.b], writes=[qo.b], waw=False)
                    elif normed:
                        fw.op("pool", lambda e, A=A, j=j: e.tensor_copy(out=qo.t[:, j, 0:n], in_=A.t[:, 0:n]),
                              reads=[A.b], writes=[qo.b], waw=False)
                    else:
                        fw.op("act", lambda e, A=A, j=j: e.activation(out=qo.t[:, j, 0:n], in_=A.t[:, 0:n], func=AF.Copy),
                              reads=[A.b], writes=[qo.b], waw=False)
                c0, c1 = start + t0, start + t0 + n
                for (dst, d0, s0, w) in ((QTv, 0, 0, 4), (KTv, 0, 4, 4), (QTv, 4, 8, 4), (KTv, 4, 12, 1)):
                    fw.dma(dst[:, d0:d0 + w, c0:c1], qo.t[:, s0:s0 + w, 0:n], qo.b, reads=[qo.b],
                           writes=[self.db(dst)], waw=False, stream="pool")
                nsub = n // 128
                for sub in range(nsub):
                    for (cc, w) in ((0, 512), (512, 128)):
                        for k in range(KD):
                            fw.op("pe", lambda e, k=k, cc=cc, w=w, sub=sub: e.matmul(
                                pV.t[:, cc:cc + w], lhsT=hT.t[:, k, sub * 128:(sub + 1) * 128],
                                rhs=W.t[:, k, 2 * ATT_FM + cc:2 * ATT_FM + cc + w], start=(k == 0), stop=(k == KD - 1)),
                                  reads=[W.b, hT.b], writes=[pV.b], ticket=(k == KD - 1 and cc == 512))
                    fw.op("act" if sub % 2 == 0 else "dve",
                          (lambda e, sub=sub: e.activation(out=vv.t[:, sub, :], in_=pV.t[:, 0:640], func=AF.Copy))
                          if sub % 2 == 0 else
                          (lambda e, sub=sub: e.tensor_copy(out=vv.t[:, sub, :], in_=pV.t[:, 0:640])),
                          reads=[pV.b], writes=[vv.b], waw=False)
                r0 = row0 + t0
                fw.dma(self.VT[r0:r0 + n, :].rearrange("(s p) c -> p s c", p=128), vv.t[:, 0:nsub, :], vv.b,
                       reads=[vv.b], writes=[self.db(self.VT)], waw=False, stream="pool")
        self.end()

    def attn_core(self, l):
        fw, cfg = self.fw, self.cfg
        i = l // 2
        lam_init = 0.8 - 0.6 * math.exp(-0.3 * l)
        NTOK, C, T = cfg.NTOK, cfg.C, cfg.T
        NKT = NTOK // 128
        self.begin()
        ro = self.roff["lq1_%d" % i][0]
        rv = self.sb("lqk", [128, 256])
        fw.dma(rv.t[:, :], self.i_rowv[0:1, ro:ro + 256].broadcast_to([128, 256]), rv.b, writes=[rv.b])
        prod = self.sb("prod", [128, 128])
        fw.op("dve", lambda e: e.tensor_tensor(out=prod.t[:, 0:64], in0=rv.t[:, 0:64], in1=rv.t[:, 64:128], op=ALU.mult),
              reads=[rv.b], writes=[prod.b])
        fw.op("dve", lambda e: e.tensor_tensor(out=prod.t[:, 64:128], in0=rv.t[:, 128:192], in1=rv.t[:, 192:256],
                                               op=ALU.mult), reads=[rv.b], writes=[prod.b], waw=False)
        s12 = self.sb("s12", [128, 2])
        for q in range(2):
            fw.op("dve", lambda e, q=q: e.reduce_sum(out=s12.t[:, q:q + 1], in_=prod.t[:, q * 64:(q + 1) * 64],
                                                     axis=mybir.AxisListType.X), reads=[prod.b], writes=[s12.b], waw=False)
        e12 = self.sb("e12", [128, 2])
        fw.op("act", lambda e: e.activation(out=e12.t[:, :], in_=s12.t[:, :], func=AF.Exp), reads=[s12.b], writes=[e12.b])
        nlam = self.sb("nlam", [128, 1])
        fw.op("dve", lambda e: e.tensor_tensor(out=nlam.t[:, :], in0=e12.t[:, 1:2], in1=e12.t[:, 0:1], op=ALU.subtract),
              reads=[e12.b], writes=[nlam.b])
        fw.op("dve", lambda e: e.tensor_scalar(out=nlam.t[:, :], in0=nlam.t[:, :], scalar1=-lam_init, scalar2=None,
                                               op0=ALU.add), reads=[nlam.b], writes=[nlam.b])
        sg = self.sb("sg", [128, 1])
        fw.op("dve", lambda e: e.tensor_scalar(out=sg.t[:, :], in0=self.vec("subg%d" % i), scalar1=1.0 - lam_init,
                                               scalar2=None, op0=ALU.mult), reads=[self.vecs.b], writes=[sg.b])
        Ks = [self.sb("K", [128, NTOK], BF16) for _ in range(2)]
        Vs = [self.sb("V", [128, NKT, 128], BF16) for _ in range(2)]
        Vg = [self.sb("Vg", [128, NKT, 128], BF16) for _ in range(2)]
        Qs = [self.sb("Q", [128, 512], BF16) for _ in range(2)]
        Pt = [self.sb("P", [128, 1024], BF16) for _ in range(2)]
        Sp = [self.ps("S", [128, 1024]) for _ in range(2)]
        Op = [self.ps("O", [128, 512]) for _ in range(2)]
        Lp = [self.ps("L", [128, 512]) for _ in range(2)]
        r_ = [self.sb("r", [128, 512]) for _ in range(2)]
        on = [self.sb("on", [128, 512]) for _ in range(2)]
        oa = self.sb("oa", [128, 512])
        sqo = self.sb("sqo", [128, 512], BF16)
        rs, tmp = self.sb("rso", [128, 512]), self.sb("tmpo", [128, 512])
        Lacc = [self.sb("Lacc", [128, 512]) for _ in range(2)]
        Lb = [self.sb("Lb", [128, 512], BF16) for _ in range(2)]
        Lf = self.sb("Lf", [128, 512])
        hb = self.sb("hb", [128, 512], BF16)
        h32 = self.sb("h32", [128, 512])
        lb = self.sb("lb", [128, 512], BF16)
        ost = [self.sb("ost", [128, 2, 512], BF16) for _ in range(2)]
        QTv = self.QT.rearrange("(k p) n -> p k n", p=128)
        KTv = self.KT.rearrange("(k p) n -> p k n", p=128)
        OTv = self.OT.rearrange("(k p) n -> p k n", p=128)
        v3 = lambda t, n: t.t[:, :].rearrange("p (s c) -> p s c", c=512)[:, :, 0:n]

        def load_kv(u):
            if u > 4:
                return
            K, V = Ks[u % 2], Vs[u % 2]
            kc = u if u < 4 else 4
            vcol = u * 128 if u < 4 else 512
            fw.dma(K.t[:, 0:C], KTv[:, kc, cfg.CS:cfg.CS + C], K.b, reads=[self.db(self.KT)], writes=[K.b])
            fw.dma(K.t[:, C:NTOK], KTv[:, kc, cfg.LS:cfg.LS + T], K.b, reads=[self.db(self.KT)], writes=[K.b], waw=False)
            vsrc = self.VT[:, vcol:vcol + 128].rearrange("(s p) c -> p s c", p=128)
            for (s0, ns) in blocks_of(NKT, 16):
                fw.dma(V.t[:, s0:s0 + ns, :], vsrc[:, s0:s0 + ns, :], V.b, reads=[self.db(self.VT)], writes=[V.b],
                       waw=(s0 == 0))
            if u == 4:
                for sub in range(2):
                    fw.op("pool", lambda e, sub=sub: e.memset(Vg[sub].t[:, :, 64:128], 1.0), writes=[Vg[sub].b])
                    fw.op("pool" if sub == 0 else "dve", lambda e, sub=sub: e.tensor_copy(
                        out=Vg[sub].t[:, :, 0:64], in_=V.t[:, :, sub * 64:(sub + 1) * 64]),
                          reads=[V.b], writes=[Vg[sub].b], waw=False)

        qblocks = []
        for seq in cfg.seqs:
            _, start, ln, v, row0 = seq
            kts = list(range(0, C // 128)) if v == 1 else list(range(NKT))
            for (t0, n) in blocks_of(ln, 512):
                qblocks.append((start + t0, n, kts))
        load_kv(0)
        qi = 0
        for u in range(8):
            if u + 1 < 8:
                load_kv(u + 1)
            K, V = Ks[min(u, 4) % 2], Vs[min(u, 4) % 2]
            diff = u < 4
            for (c0, n, kts) in qblocks:
                Q = Qs[qi % 2]
                os_ = ost[qi % 2]
                qi += 1
                fw.dma(Q.t[:, 0:n], QTv[:, u, c0:c0 + n], Q.b, reads=[self.db(self.QT)], writes=[Q.b])

                def emit_S(kt):
                    S = Sp[kt % 2]
                    for sub in range(2):
                        fw.op("pe", lambda e, sub=sub, S=S, kt=kt: e.matmul(
                            S.t[:, sub * 512:sub * 512 + n], lhsT=K.t[sub * 64:(sub + 1) * 64, kt * 128:(kt + 1) * 128],
                            rhs=Q.t[sub * 64:(sub + 1) * 64, 0:n], start=True, stop=True),
                              reads=[K.b, Q.b], writes=[S.b], ticket=(sub == 1), waw=(sub == 0))

                def emit_E(kt):
                    S, P = Sp[kt % 2], Pt[kt % 2]
                    fw.op("act", lambda e, S=S, P=P: e.activation(out=v3(P, n), in_=v3(S, n), func=AF.Exp,
                                                                  scale=HD ** -0.5), reads=[S.b], writes=[P.b])

                def emit_PV(kt, first, last):
                    P = Pt[kt % 2]
                    for sub in range(2):
                        rhs = P.t[:, sub * 512:sub * 512 + n]
                        if diff:
                            fw.op("pe", lambda e, sub=sub, rhs=rhs: e.matmul(
                                Op[sub].t[:, 0:n], lhsT=V.t[:, kt, :], rhs=rhs, start=first, stop=last),
                                  reads=[V.b, P.b], writes=[Op[sub].b] if (first or last) else [],
                                  ticket=last)
                            La = Lacc[sub]
                            if sub == 1:
                                fw.op("pe", lambda e, sub=sub, rhs=rhs: e.matmul(
                                    Lp[sub].t[:, 0:n], lhsT=self.ones_bf.t[:, :], rhs=rhs, start=first, stop=last),
                                      reads=[self.ones_bf.b, P.b], writes=[Lp[sub].b] if (first or last) else [],
                                      ticket=True)
                            elif first:
                                fw.op("dve", lambda e, La=La, rhs=rhs: e.tensor_copy(out=La.t[:, 0:n], in_=rhs),
                                      reads=[P.b], writes=[La.b])
                            else:
                                fw.op("dve", lambda e, La=La, rhs=rhs: e.tensor_tensor(
                                    out=La.t[:, 0:n], in0=La.t[:, 0:n], in1=rhs, op=ALU.add),
                                      reads=[P.b, La.b], writes=[La.b])
                        else:
                            fw.op("pe", lambda e, sub=sub, rhs=rhs: e.matmul(
                                Op[sub].t[:, 0:n], lhsT=Vg[sub].t[:, kt, :], rhs=rhs, start=first, stop=last),
                                  reads=[Vg[sub].b, P.b], writes=[Op[sub].b] if (first or last) else [],
                                  ticket=(last or sub == 1))

                emit_S(kts[0])
                for idx, kt in enumerate(kts):
                    if idx + 1 < len(kts):
                        emit_S(kts[idx + 1])
                    emit_E(kt)
                    emit_PV(kt, idx == 0, idx == len(kts) - 1)
                if diff:
                    for sub in range(2):
                        if sub == 0:
                            fw.op("dve", lambda e, sub=sub: e.tensor_copy(out=Lb[sub].t[:, 0:n], in_=Lacc[sub].t[:, 0:n]),
                                  reads=[Lacc[sub].b], writes=[Lb[sub].b])
                            fw.op("pe", lambda e, sub=sub: e.matmul(Lp[sub].t[:, 0:n], lhsT=self.ones_bf.t[:, :],
                                                                    rhs=Lb[sub].t[:, 0:n], start=True, stop=True),
                                  reads=[Lb[sub].b, self.ones_bf.b], writes=[Lp[sub].b])
                        fw.op("dve", lambda e, sub=sub: e.reciprocal(out=r_[sub].t[:, 0:n], in_=Lp[sub].t[:, 0:n]),
                              reads=[Lp[sub].b], writes=[r_[sub].b])
                    for sub in range(2):
                        fw.op("dve", lambda e, sub=sub: e.tensor_tensor(
                            out=on[sub].t[:, 0:n], in0=Op[sub].t[:, 0:n], in1=r_[sub].t[:, 0:n], op=ALU.mult),
                              reads=[Op[sub].b, r_[sub].b], writes=[on[sub].b])
                    fw.op("dve", lambda e: e.scalar_tensor_tensor(
                        out=oa.t[:, 0:n], in0=on[1].t[:, 0:n], scalar=nlam.t[:, 0:1], in1=on[0].t[:, 0:n],
                        op0=ALU.mult, op1=ALU.add), reads=[on[0].b, on[1].b, nlam.b], writes=[oa.b])
                    fw.op("act", lambda e: e.activation(out=sqo.t[:, 0:n], in_=oa.t[:, 0:n], func=AF.Square),
                          reads=[oa.b], writes=[sqo.b])
                    ssb = Lp[0]
                    fw.op("pe", lambda e: e.matmul(ssb.t[:, 0:n], lhsT=self.ones_bf.t[:, :], rhs=sqo.t[:, 0:n],
                                                   start=True, stop=True), reads=[sqo.b, self.ones_bf.b], writes=[ssb.b])
                    self.rstd(ssb, rs, tmp, 1.0 / 128, n)
                    fw.op("dve", lambda e: e.scalar_tensor_tensor(
                        out=os_.t[:, 0, 0:n], in0=oa.t[:, 0:n], scalar=sg.t[:, 0:1], in1=rs.t[:, 0:n],
                        op0=ALU.mult, op1=ALU.mult), reads=[oa.b, rs.b, sg.b], writes=[os_.b])
                    fw.dma(OTv[:, u, c0:c0 + n], os_.t[:, 0, 0:n], os_.b, reads=[os_.b], writes=[self.db(self.OT)],
                           waw=False, stream="pool")
                else:
                    j = u - 4
                    H = slice(64, 128)
                    idb = self.cmb.t[64:128, 64:128]
                    for sub in range(2):
                        fw.op("act", lambda e, sub=sub: e.activation(out=Lf.t[H, 0:n], in_=Op[sub].t[H, 0:n], func=AF.Copy),
                              reads=[Op[sub].b], writes=[Lf.b])
                        fw.op("dve", lambda e: e.reciprocal(out=Lf.t[H, 0:n], in_=Lf.t[H, 0:n]), reads=[Lf.b], writes=[Lf.b])
                        fw.op("dve", lambda e: e.tensor_copy(out=hb.t[H, 0:n], in_=Lf.t[H, 0:n]), reads=[Lf.b], writes=[hb.b])
                        fw.op("dve", lambda e: e.tensor_copy(out=h32.t[H, 0:n], in_=hb.t[H, 0:n]), reads=[hb.b], writes=[h32.b])
                        fw.op("dve", lambda e: e.tensor_tensor(out=lb.t[H, 0:n], in0=Lf.t[H, 0:n], in1=h32.t[H, 0:n],
                                                               op=ALU.subtract), reads=[Lf.b, h32.b], writes=[lb.b])
                        fw.op("pe", lambda e, sub=sub: e.matmul(Lp[sub].t[0:64, 0:n], lhsT=idb, rhs=hb.t[H, 0:n],
                                                                start=True, stop=False),
                              reads=[hb.b, self.cmb.b], writes=[Lp[sub].b], ticket=False)
                        fw.op("pe", lambda e, sub=sub: e.matmul(Lp[sub].t[0:64, 0:n], lhsT=idb, rhs=lb.t[H, 0:n],
                                                                start=False, stop=True),
                              reads=[lb.b, hb.b, self.cmb.b], writes=[Lp[sub].b])
                        fw.op("act", lambda e, sub=sub: e.activation(out=r_[sub].t[0:64, 0:n], in_=Lp[sub].t[0:64, 0:n],
                                                                     func=AF.Copy), reads=[Lp[sub].b], writes=[r_[sub].b])
                        fw.op("dve", lambda e, sub=sub: e.tensor_tensor(
                            out=os_.t[0:64, sub, 0:n], in0=Op[sub].t[0:64, 0:n], in1=r_[sub].t[0:64, 0:n], op=ALU.mult),
                              reads=[Op[sub].b, r_[sub].b], writes=[os_.b], waw=(sub == 0))
                    for sub in range(2):
                        hd = 4 * sub + j
                        f0 = 512 + hd * 64
                        fw.dma(self.OT[f0:f0 + 64, c0:c0 + n], os_.t[0:64, sub, 0:n], os_.b, reads=[os_.b],
                               writes=[self.db(self.OT)], waw=False, stream="pool")
        self.end()

    def proj_res_norm(self, l, Xin, Xout, SRC, kc, w_rows):
        fw, cfg = self.fw, self.cfg
        self.begin()
        W = self.sb("wo", [128, kc, D], BF16)
        self.load_w(W, w_rows, D)
        nt = self.norm_tiles()
        srcs = [self.sb("src", [128, kc, 512], BF16) for _ in range(2)]
        xs = [self.sb("x", [128, KD, 512], F32) for _ in range(2)]
        x1s = [self.sb("x1", [128, KD, 512], F32) for _ in range(2)]
        hs = [self.sb("h", [128, KD, 512], BF16) for _ in range(2)]
        pp = [self.ps("pp", [128, 512]) for _ in range(2)]
        Xiv = Xin.rearrange("(k p) n -> p k n", p=128)
        Xov = Xout.rearrange("(k p) n -> p k n", p=128)
        Sv = SRC.rearrange("(k p) n -> p k n", p=128)
        HTv = self.HT.rearrange("(k p) n -> p k n", p=128)
        bi = 0
        for seq in cfg.seqs:
            _, start, ln, v, row0 = seq
            if v == 1 and l == DEPTH - 1:
                continue
            for (t0, n) in blocks_of(ln, 512):
                s_, x, x1, h = srcs[bi % 2], xs[bi % 2], x1s[bi % 2], hs[bi % 2]
                bi += 1
                self.load_cols(s_, Sv, seq, t0, n, 0)
                self.load_cols(x, Xiv, seq, t0, n, 0)
                g1 = self.dvv(l, v, 2)
                for c in range(KD):
                    P = pp[c % 2]
                    for k in range(kc):
                        fw.op("pe", lambda e, k=k, c=c, P=P: e.matmul(
                            P.t[:, 0:n], lhsT=W.t[:, k, c * 128:(c + 1) * 128], rhs=s_.t[:, k, 0:n],
                            start=(k == 0), stop=(k == kc - 1)), reads=[W.b, s_.b], writes=[P.b], ticket=(k == kc - 1))
                    fw.op("dve", lambda e, c=c, P=P: e.scalar_tensor_tensor(
                        out=x1.t[:, c, 0:n], in0=P.t[:, 0:n], scalar=g1(c), in1=x.t[:, c, 0:n],
                        op0=ALU.mult, op1=ALU.add), reads=[P.b, x.b], writes=[x1.b], waw=False)
                c0 = start + t0
                fw.dma(Xov[:, :, c0:c0 + n], x1.t[:, :, 0:n], x1.b, reads=[x1.b], writes=[self.db(Xout)], waw=False,
                       stream="pool")
                self.norm_mod(x1, n, nt["ss"], nt["sq"], nt["rs"], nt["tmp"], self.dvv(l, v, 3), self.dvv(l, v, 4), h)
                fw.dma(HTv[:, :, c0:c0 + n], h.t[:, :, 0:n], h.b, reads=[h.b], writes=[self.db(self.HT)], waw=False,
                       stream="pool")
        self.end()

    def ffn_in(self, l):
        fw, cfg = self.fw, self.cfg
        self.begin()
        W = self.sb("wf", [128, KD, 2 * DFF], BF16)
        self.load_w(W, lambda k: self.i_fwi[l, k * 128:(k + 1) * 128, :], 2 * DFF)
        hs = [self.sb("h", [128, KD, 512], BF16) for _ in range(2)]
        uo = [self.sb("uo", [128, NFF, 512], BF16) for _ in range(2)]
        pg = [self.ps("pg", [128, 512]) for _ in range(2)]
        pv = [self.ps("pv", [128, 512]) for _ in range(2)]
        tt = [self.sb("t", [128, 512]) for _ in range(2)]
        ge = [self.sb("ge", [128, 512]) for _ in range(2)]
        HTv = self.HT.rearrange("(k p) n -> p k n", p=128)
        UTv = self.UT.rearrange("(k p) n -> p k n", p=128)
        bi = 0
        for seq in cfg.seqs:
            _, start, ln, v, row0 = seq
            if v == 1 and l == DEPTH - 1:
                continue
            for (t0, n) in blocks_of(ln, 510):
                h, u = hs[bi % 2], uo[bi % 2]
                bi += 1
                self.load_cols(h, HTv, seq, t0, n, 1)
                N = n + 2
                for c in range(NFF):
                    G, Vv, t, g = pg[c % 2], pv[c % 2], tt[c % 2], ge[c % 2]
                    for k in range(KD):
                        fw.op("pe", lambda e, k=k, c=c, G=G: e.matmul(
                            G.t[:, 0:N], lhsT=W.t[:, k, DFF + c * 128:DFF + (c + 1) * 128], rhs=h.t[:, k, 0:N],
                            start=(k == 0), stop=(k == KD - 1)), reads=[W.b, h.b], writes=[G.b], ticket=(k == KD - 1))
                    for k in range(KD):
                        fw.op("pe", lambda e, k=k, c=c, Vv=Vv: e.matmul(
                            Vv.t[:, 0:N], lhsT=W.t[:, k, c * 128:(c + 1) * 128], rhs=h.t[:, k, 0:N],
                            start=(k == 0), stop=(k == KD - 1)), reads=[W.b, h.b], writes=[Vv.b], ticket=(k == KD - 1))
                    w0, w1, w2 = (self.vec("fcw%d_%d" % (l, q), c) for q in range(3))
                    bb = self.vec("fcb%d" % l, c)
                    fw.op("dve", lambda e, G=G, t=t, w0=w0, bb=bb: e.tensor_scalar(
                        out=t.t[:, 0:n], in0=G.t[:, 0:n], scalar1=w0, scalar2=bb, op0=ALU.mult, op1=ALU.add),
                          reads=[G.b], writes=[t.b])
                    fw.op("dve", lambda e, G=G, t=t, w1=w1: e.scalar_tensor_tensor(
                        out=t.t[:, 0:n], in0=G.t[:, 1:n + 1], scalar=w1, in1=t.t[:, 0:n], op0=ALU.mult, op1=ALU.add),
                          reads=[G.b, t.b], writes=[t.b])
                    fw.op("dve", lambda e, G=G, t=t, w2=w2: e.scalar_tensor_tensor(
                        out=t.t[:, 0:n], in0=G.t[:, 2:n + 2], scalar=w2, in1=t.t[:, 0:n], op0=ALU.mult, op1=ALU.add),
                          reads=[G.b, t.b], writes=[t.b])
                    fw.op("act", lambda e, t=t, g=g: e.activation(out=g.t[:, 0:n], in_=t.t[:, 0:n], func=AF.Gelu),
                          reads=[t.b], writes=[g.b])
                    fw.op("dve", lambda e, g=g, Vv=Vv, c=c: e.tensor_tensor(
                        out=u.t[:, c, 0:n], in0=g.t[:, 0:n], in1=Vv.t[:, 1:n + 1], op=ALU.mult),
                          reads=[g.b, Vv.b], writes=[u.b], waw=False)
                c0 = start + t0
                fw.dma(UTv[:, :, c0:c0 + n], u.t[:, :, 0:n], u.b, reads=[u.b], writes=[self.db(self.UT)], waw=False,
                       stream="pool")
        self.end()

    def ffn_out(self, l, Xin, Xout):
        fw, cfg = self.fw, self.cfg
        self.begin()
        W = self.sb("wf2", [128, NFF, D], BF16)
        self.load_w(W, lambda k: self.i_fwo[l, k * 128:(k + 1) * 128, :], D)
        us = [self.sb("u", [128, NFF, 512], BF16) for _ in range(2)]
        xs = [self.sb("x", [128, KD, 512], F32) for _ in range(2)]
        x2s = [self.sb("x2", [128, KD, 512], F32) for _ in range(2)]
        pp = [self.ps("pp", [128, 512]) for _ in range(2)]
        Xiv = Xin.rearrange("(k p) n -> p k n", p=128)
        Xov = Xout.rearrange("(k p) n -> p k n", p=128)
        UTv = self.UT.rearrange("(k p) n -> p k n", p=128)
        bi = 0
        for seq in cfg.seqs:
            _, start, ln, v, row0 = seq
            if v == 1 and l == DEPTH - 1:
                continue
            for (t0, n) in blocks_of(ln, 512):
                u, x, x2 = us[bi % 2], xs[bi % 2], x2s[bi % 2]
                bi += 1
                self.load_cols(u, UTv, seq, t0, n, 0)
                self.load_cols(x, Xiv, seq, t0, n, 0)
                g2 = self.dvv(l, v, 5)
                for c in range(KD):
                    P = pp[c % 2]
                    for k in range(NFF):
                        fw.op("pe", lambda e, k=k, c=c, P=P: e.matmul(
                            P.t[:, 0:n], lhsT=W.t[:, k, c * 128:(c + 1) * 128], rhs=u.t[:, k, 0:n],
                            start=(k == 0), stop=(k == NFF - 1)), reads=[W.b, u.b], writes=[P.b], ticket=(k == NFF - 1))
                    fw.op("dve", lambda e, c=c, P=P: e.scalar_tensor_tensor(
                        out=x2.t[:, c, 0:n], in0=P.t[:, 0:n], scalar=g2(c), in1=x.t[:, c, 0:n],
                        op0=ALU.mult, op1=ALU.add), reads=[P.b, x.b], writes=[x2.b], waw=False)
                c0 = start + t0
                fw.dma(Xov[:, :, c0:c0 + n], x2.t[:, :, 0:n], x2.b, reads=[x2.b], writes=[self.db(Xout)], waw=False,
                       stream="pool")
        self.end()

    def final_norm(self, X):
        fw, cfg = self.fw, self.cfg
        self.begin()
        nt = self.norm_tiles()
        xs = [self.sb("x", [128, KD, 512], F32) for _ in range(2)]
        os_ = [self.sb("o", [128, KD, 512], F32) for _ in range(2)]
        Xv = X.rearrange("(k p) n -> p k n", p=128)
        Ov = self.o_out.rearrange("(k p) n -> p k n", p=128)
        seq = cfg.seqs[1]
        outb = self.fw.buf(name="outT")
        for bi, (t0, n) in enumerate(blocks_of(cfg.T, 512)):
            x, o = xs[bi % 2], os_[bi % 2]
            self.load_cols(x, Xv, seq, t0, n, 0)
            self.norm_mod(x, n, nt["ss"], nt["sq"], nt["rs"], nt["tmp"], lambda k: self.vec("gfin", k), None, o)
            fw.dma(Ov[:, :, t0:t0 + n], o.t[:, :, 0:n], o.b, reads=[o.b], writes=[outb], waw=False, stream="pool")
        self.end()


    def ssm_inproj(self, l, X):
        fw, cfg = self.fw, self.cfg
        i = l // 2
        self.begin()
        W = self.sb("wS", [128, KD, SSM_IN], BF16)
        self.load_w(W, lambda k: self.i_swi[i, k * 128:(k + 1) * 128, :], SSM_IN)
        nt = self.norm_tiles()
        x = self.sb("x", [128, KD, 512], F32)
        hTs = [self.sb("hT", [128, KD, 512], BF16) for _ in range(2)]
        xo = self.sb("xo", [128, 24, 512], BF16)
        zo = [self.sb("zo", [128, SSM_DI], F32) for _ in range(2)]
        dto = [self.sb("dto", [128, 64], F32) for _ in range(2)]
        dta = [self.sb("dta", [128, 64], F32) for _ in range(2)]
        dtb = self.sb("dtb", [128, 64], F32)
        ro = self.roff["dtb%d" % i][0]
        fw.dma(dtb.t[:, :], self.i_rowv[0:1, ro:ro + 64].broadcast_to([128, 64]), dtb.b, writes=[dtb.b])
        pP = [self.ps("pP", [128, 512]) for _ in range(2)]
        pz = [self.ps("pz", [128, 512]) for _ in range(2)]
        pdt = self.ps("pdt", [128, 512])
        tt = [self.sb("t", [128, 512]) for _ in range(2)]
        Xv = X.rearrange("(k p) n -> p k n", p=128)
        XBv = self.XBCT.rearrange("(k p) n -> p k n", p=128)
        bi = 0
        zi = 0
        for seq in cfg.seqs:
            _, start, ln, v, row0 = seq
            for (t0, n) in blocks_of(ln, 510):
                hT = hTs[bi % 2]
                bi += 1
                N = n + 2
                self.load_cols(x, Xv, seq, t0, n, 1)
                self.norm_mod(x, N, nt["ss"], nt["sq"], nt["rs"], nt["tmp"], self.dvv(l, v, 0), self.dvv(l, v, 1), hT)
                if t0 == 0:
                    fw.op("pool", lambda e: e.memset(hT.t[:, :, 0:1], 0.0), writes=[hT.b])
                if t0 + n == ln:
                    fw.op("pool", lambda e: e.memset(hT.t[:, :, n + 1:n + 2], 0.0), writes=[hT.b])
                for c in range(24):
                    P, t = pP[c % 2], tt[c % 2]
                    for k in range(KD):
                        fw.op("pe", lambda e, k=k, c=c, P=P: e.matmul(
                            P.t[:, 0:N], lhsT=W.t[:, k, SSM_DI + c * 128:SSM_DI + (c + 1) * 128], rhs=hT.t[:, k, 0:N],
                            start=(k == 0), stop=(k == KD - 1)), reads=[W.b, hT.b], writes=[P.b], ticket=(k == KD - 1))
                    w0, w1, w2 = (self.vec("scw%d_%d" % (i, q), c) for q in range(3))
                    bb = self.vec("scb%d" % i, c)
                    fw.op("dve", lambda e, P=P, t=t, w0=w0, bb=bb: e.tensor_scalar(
                        out=t.t[:, 0:n], in0=P.t[:, 0:n], scalar1=w0, scalar2=bb, op0=ALU.mult, op1=ALU.add),
                          reads=[P.b], writes=[t.b])
                    fw.op("dve", lambda e, P=P, t=t, w1=w1: e.scalar_tensor_tensor(
                        out=t.t[:, 0:n], in0=P.t[:, 1:n + 1], scalar=w1, in1=t.t[:, 0:n], op0=ALU.mult, op1=ALU.add),
                          reads=[P.b, t.b], writes=[t.b])
                    fw.op("dve", lambda e, P=P, t=t, w2=w2: e.scalar_tensor_tensor(
                        out=t.t[:, 0:n], in0=P.t[:, 2:n + 2], scalar=w2, in1=t.t[:, 0:n], op0=ALU.mult, op1=ALU.add),
                          reads=[P.b, t.b], writes=[t.b])
                    fw.op("act", lambda e, t=t, c=c: e.activation(out=xo.t[:, c, 0:n], in_=t.t[:, 0:n], func=AF.Silu),
                          reads=[t.b], writes=[xo.b], waw=False)
                c0 = start + t0
                fw.dma(XBv[:, :, c0:c0 + n], xo.t[:, :, 0:n], xo.b, reads=[xo.b], writes=[self.db(self.XBCT)],
                       waw=False, stream="pool")
                for (s0, m) in blocks_of(n, 128):
                    z, dt_, da = zo[zi % 2], dto[zi % 2], dta[zi % 2]
                    zi += 1
                    for q in range(4):
                        Pz = pz[q % 2]
                        for k in range(KD):
                            fw.op("pe", lambda e, k=k, q=q, Pz=Pz: e.matmul(
                                Pz.t[0:m, :], lhsT=hT.t[:, k, 1 + s0:1 + s0 + m], rhs=W.t[:, k, q * 512:(q + 1) * 512],
                                start=(k == 0), stop=(k == KD - 1)), reads=[W.b, hT.b], writes=[Pz.b], ticket=(k == KD - 1))
                        if q % 2 == 0:
                            fw.op("act", lambda e, q=q, Pz=Pz: e.activation(out=z.t[0:m, q * 512:(q + 1) * 512],
                                                                            in_=Pz.t[0:m, :], func=AF.Copy),
                                  reads=[Pz.b], writes=[z.b], waw=False)
                        else:
                            fw.op("dve", lambda e, q=q, Pz=Pz: e.tensor_copy(out=z.t[0:m, q * 512:(q + 1) * 512],
                                                                             in_=Pz.t[0:m, :]),
                                  reads=[Pz.b], writes=[z.b], waw=False)
                    r0 = row0 + t0 + s0
                    fw.dma(self.ZT[r0:r0 + m, :], z.t[0:m, :], z.b, reads=[z.b], writes=[self.db(self.ZT)], waw=False,
                           stream="pool")
                    for k in range(KD):
                        fw.op("pe", lambda e, k=k: e.matmul(
                            pdt.t[0:m, 0:64], lhsT=hT.t[:, k, 1 + s0:1 + s0 + m], rhs=W.t[:, k, SSM_DI + SSM_XBC:SSM_IN],
                            start=(k == 0), stop=(k == KD - 1)), reads=[W.b, hT.b], writes=[pdt.b], ticket=(k == KD - 1))
                    fw.op("dve", lambda e: e.tensor_tensor(out=da.t[0:m, :], in0=pdt.t[0:m, 0:64], in1=dtb.t[0:m, :], op=ALU.add),
                          reads=[pdt.b, dtb.b], writes=[da.b])
                    fw.op("act", lambda e: e.activation(out=da.t[0:m, :], in_=da.t[0:m, :], func=AF.Exp),
                          reads=[da.b], writes=[da.b])
                    fw.op("act", lambda e: e.activation(out=dt_.t[0:m, :], in_=da.t[0:m, :], func=AF.Ln, bias=1.0),
                          reads=[da.b], writes=[dt_.b])
                    fw.dma(self.DTT[r0:r0 + m, :], dt_.t[0:m, :], dt_.b, reads=[dt_.b], writes=[self.db(self.DTT)],
                           waw=False, stream="pool")
        self.end()

    def ssm_scan(self, l, d):
        fw, cfg = self.fw, self.cfg
        i = l // 2
        NKT = cfg.NTOK // 128
        nct = cfg.C // 128
        self.begin()
        bc3 = lambda ap: ap.unsqueeze(2).broadcast_to([128, 8, 64])
        ML = self.cmask("ML_f" if d == 0 else "ML_b")
        MR = self.cmask("MR_f" if d == 0 else "MR_b")
        MRf = self.cmask("MR_f" if d == 0 else "MR_b", bf=False)
        arow = self.sb("arow", [128, 32])
        ro = self.roff["alog%d" % i][0] + 32 * d
        fw.dma(arow.t[:, :], self.i_rowv[0:1, ro:ro + 32].broadcast_to([128, 32]), arow.b, writes=[arow.b])
        fw.op("act", lambda e: e.activation(out=arow.t[:, :], in_=arow.t[:, :], func=AF.Exp), reads=[arow.b], writes=[arow.b])
        fw.op("dve", lambda e: e.tensor_scalar(out=arow.t[:, :], in0=arow.t[:, :], scalar1=-1.0, scalar2=None, op0=ALU.mult),
              reads=[arow.b], writes=[arow.b])
        if d == 1:
            drow = self.sb("drow", [128, 32])
            ro = self.roff["ssd%d" % i][0]
            fw.dma(drow.t[:, :], self.i_rowv[0:1, ro:ro + 32].broadcast_to([128, 32]), drow.b, writes=[drow.b])
            ng = self.sb("ng", [128, SSM_DI])
            ro = self.roff["sng%d" % i][0]
            fw.dma(ng.t[:, :], self.i_rowv[0:1, ro:ro + SSM_DI].broadcast_to([128, SSM_DI]), ng.b, writes=[ng.b])
        S = self.sb("S", [128, SSM_DI], F32)
        Sb = self.sb("Sb", [128, SSM_DI], BF16)
        fw.op("pool", lambda e: e.memset(S.t[:, :], 0.0), writes=[S.b])
        fw.op("pool", lambda e: e.memset(Sb.t[:, :], 0.0), writes=[Sb.b])
        xbcs = [self.sb("xbc", [128, 24, 128], BF16) for _ in range(2)]
        dts = [self.sb("dt", [128, 64], F32) for _ in range(2)]
        xtok_l = [self.sb("xtok", [128, 2560], BF16) for _ in range(2)]
        a_l = [self.sb("a", [128, 32]) for _ in range(2)]
        ahb_l = [self.sb("ahb", [128, 32], BF16) for _ in range(2)]
        ah_l = [self.sb("ah", [128, 32]) for _ in range(2)]
        al_l = [self.sb("al", [128, 32]) for _ in range(2)]
        a2_l = [self.sb("a2", [128, 64], BF16) for _ in range(2)]
        E_l = [self.sb("E", [128, 96]) for _ in range(2)]
        e2_l = [self.sb("e2", [128, 32]) for _ in range(2)]
        cbm_l = [self.sb("cbm", [128, 4, 128]) for _ in range(2)]
        Rhs = [self.sb("Rh", [128, 512], BF16) for _ in range(2)]
        Rls = [self.sb("Rl", [128, 512], BF16) for _ in range(2)]
        dec = [self.sb("dec", [128, 512]) for _ in range(2)]
        wT4 = [self.sb("wT4", [128, 512], BF16) for _ in range(2)]
        tmpy = self.sb("tmpy", [128, 512])
        xdt_l = [self.sb("xdt", [128, SSM_DI], BF16) for _ in range(2)]
        xwa_l = [self.sb("xwa", [128, SSM_DI], BF16) for _ in range(2)]
        yo = [self.sb("yo", [128, SSM_DI], F32) for _ in range(2)]
        pT = self.ps("pT", [128, 1024], BF16)
        pE = self.ps("pE", [128, 512])
        pcb = self.ps("pcb", [128, 512])
        pseg = [self.ps("pseg", [128, 512]) for _ in range(2)]
        pYl = [self.ps("pY", [128, 512]) for _ in range(2)]
        pG = self.ps("pG", [128, 512])
        if d == 1:
            yfs = [self.sb("yf", [128, SSM_DI], F32) for _ in range(2)]
            zs = [self.sb("z", [128, SSM_DI], F32) for _ in range(2)]
            sz = self.sb("sz", [128, SSM_DI], F32)
            ssq = self.sb("ssq", [128, 1])
            sqj = self.sb("sqj", [128, SSM_DI], BF16)
            rs1, rt1 = self.sb("rs1", [128, 1]), self.sb("rt1", [128, 1])
            ygn = self.sb("ygn", [128, SSM_DI], BF16)
            ygT = [self.sb("ygT", [128, 16, 128], BF16) for _ in range(2)]
        XBv = self.XBCT.rearrange("(k p) n -> p k n", p=128)
        YGv = self.YGT.rearrange("(k p) n -> p k n", p=128)
        order = list(range(NKT)) if d == 0 else (list(range(nct - 1, -1, -1)) + list(range(NKT - 1, nct - 1, -1)))
        hic = [0]

        def binds(ci):
            kt = order[ci]
            p_ = ci % 2
            r0 = kt * 128
            c0 = (cfg.CS + r0) if kt < nct else (cfg.LS + r0 - cfg.C)
            return (xbcs[p_], dts[p_], r0, c0, xtok_l[p_], a_l[p_], ahb_l[p_], ah_l[p_], al_l[p_], a2_l[p_], E_l[p_],
                    e2_l[p_], cbm_l[p_], xdt_l[p_], xwa_l[p_])

        def prologue(ci):
            xbc, dt_, r0, c0, xtok, a, ahb, ah, al, a2, E, e2, cbm, xdt, xwa = binds(ci)
            fw.dma(xbc.t[:, :, :], XBv[:, :, c0:c0 + 128], xbc.b, reads=[self.db(self.XBCT)], writes=[xbc.b])
            fw.dma(dt_.t[:, :], self.DTT[r0:r0 + 128, :], dt_.b, reads=[self.db(self.DTT)], writes=[dt_.b])
            dtd = dt_.t[:, 32 * d:32 * d + 32]
            if d == 1:
                yf, z = yfs[ci % 2], zs[ci % 2]
                fw.dma(yf.t[:, :], self.YF[r0:r0 + 128, :], yf.b, reads=[self.db(self.YF)], writes=[yf.b])
                fw.dma(z.t[:, :], self.ZT[r0:r0 + 128, :], z.b, reads=[self.db(self.ZT)], writes=[z.b])
            for q in range(5):
                for j in range(4):
                    c = q * 4 + j
                    fw.op("pe", lambda e, c=c, j=j: e.transpose(pT.t[:, j * 128:(j + 1) * 128], xbc.t[:, c, :], self.ident_bf()),
                          reads=[xbc.b, self.cmb.b], writes=[pT.b], ticket=(j == 3), waw=(j == 0))
                if q % 2 == 0:
                    fw.op("act", lambda e, q=q: e.activation(out=xtok.t[:, q * 512:(q + 1) * 512], in_=pT.t[:, 0:512], func=AF.Copy),
                          reads=[pT.b], writes=[xtok.b], waw=False)
                else:
                    fw.op("dve", lambda e, q=q: e.tensor_copy(out=xtok.t[:, q * 512:(q + 1) * 512], in_=pT.t[:, 0:512]),
                          reads=[pT.b], writes=[xtok.b], waw=False)
            fw.op("dve", lambda e: e.tensor_tensor(out=a.t[:, :], in0=dtd, in1=arow.t[:, :], op=ALU.mult),
                  reads=[dt_.b, arow.b], writes=[a.b])
            fw.op("dve", lambda e: e.tensor_copy(out=ahb.t[:, :], in_=a.t[:, :]), reads=[a.b], writes=[ahb.b])
            fw.op("dve", lambda e: e.tensor_copy(out=ah.t[:, :], in_=ahb.t[:, :]), reads=[ahb.b], writes=[ah.b])
            fw.op("dve", lambda e: e.tensor_tensor(out=al.t[:, :], in0=a.t[:, :], in1=ah.t[:, :], op=ALU.subtract),
                  reads=[a.b, ah.b], writes=[al.b])
            fw.op("dve", lambda e: e.tensor_copy(out=a2.t[:, 0:32], in_=ah.t[:, :]), reads=[ah.b], writes=[a2.b])
            fw.op("dve", lambda e: e.tensor_copy(out=a2.t[:, 32:64], in_=al.t[:, :]), reads=[al.b], writes=[a2.b], waw=False)
            for q, lhs in enumerate((MR, ML, self.ones_bf.t[:, :])):
                for hl in range(2):
                    fw.op("pe", lambda e, q=q, lhs=lhs, hl=hl: e.matmul(
                        pE.t[:, q * 32:(q + 1) * 32], lhsT=lhs, rhs=a2.t[:, hl * 32:(hl + 1) * 32],
                        start=(hl == 0), stop=(hl == 1)), reads=[a2.b, self.cmb.b, self.ones_bf.b], writes=[pE.b],
                          ticket=(q == 2 and hl == 1), waw=(q == 0 and hl == 0))
            fw.op("act", lambda e: e.activation(out=E.t[:, :], in_=pE.t[:, 0:96], func=AF.Exp), reads=[pE.b], writes=[E.b])
            fw.op("dve", lambda e: e.tensor_tensor(out=e2.t[:, :], in0=E.t[:, 32:64], in1=dtd, op=ALU.mult),
                  reads=[E.b, dt_.b], writes=[e2.b])
            for g in range(4):
                fw.op("pe", lambda e, g=g: e.matmul(pcb.t[:, g * 128:(g + 1) * 128], lhsT=xbc.t[:, 16 + g, :],
                                                    rhs=xbc.t[:, 20 + g, :], start=True, stop=True),
                      reads=[xbc.b], writes=[pcb.b], ticket=(g == 3), waw=(g == 0))
            for g in range(4):
                fw.op("dve", lambda e, g=g: e.tensor_tensor(out=cbm.t[:, g, :], in0=pcb.t[:, g * 128:(g + 1) * 128],
                                                            in1=MRf, op=ALU.mult),
                      reads=[pcb.b, self.cm.b], writes=[cbm.b], waw=(g == 0))
            x3 = xtok.t[:, 0:SSM_DI].rearrange("p (h c) -> p h c", c=64)
            fw.op("pool", lambda e: e.tensor_tensor(out=xdt.t[:, :].rearrange("p (h c) -> p h c", c=64), in0=x3,
                                                    in1=dtd.unsqueeze(2).broadcast_to([128, 32, 64]), op=ALU.mult),
                  reads=[xtok.b, dt_.b], writes=[xdt.b])
            fw.op("pool", lambda e: e.tensor_tensor(out=xwa.t[:, :].rearrange("p (h c) -> p h c", c=64), in0=x3,
                                                    in1=e2.t[:, :].unsqueeze(2).broadcast_to([128, 32, 64]), op=ALU.mult),
                  reads=[xtok.b, e2.b], writes=[xwa.b])

        def main(ci):
            xbc, dt_, r0, c0, xtok, a, ahb, ah, al, a2, E, e2, cbm, xdt, xwa = binds(ci)
            dtd = dt_.t[:, 32 * d:32 * d + 32]
            if d == 1:
                yf, z = yfs[ci % 2], zs[ci % 2]
            items = [(g, hq) for g in range(4) for hq in range(2)]
            bufs = {}

            def stage1(idx):
                g, hq = items[idx]
                hi = hic[0]
                hic[0] += 1
                ps_, dc, Rh, Rl, w4 = pseg[hi % 2], dec[hi % 2], Rhs[hi % 2], Rls[hi % 2], wT4[hi % 2]
                bufs[idx] = (ps_, dc, w4)
                h0 = g * 8 + hq * 4
                fw.op("dve", lambda e: e.tensor_tensor(
                    out=Rh.t[:, :].rearrange("p (r c) -> p r c", c=128),
                    in0=MR.unsqueeze(1).broadcast_to([128, 4, 128]),
                    in1=ah.t[:, h0:h0 + 4].unsqueeze(2).broadcast_to([128, 4, 128]), op=ALU.mult),
                      reads=[ah.b, self.cmb.b], writes=[Rh.b])
                for r4 in range(4):
                    fw.op("act", lambda e, r4=r4: e.activation(
                        out=Rl.t[:, r4 * 128:(r4 + 1) * 128], in_=MR, func=AF.Copy, scale=al.t[:, h0 + r4:h0 + r4 + 1]),
                          reads=[al.b, self.cmb.b], writes=[Rl.b], waw=(r4 == 0))
                fw.op("pe", lambda e: e.matmul(ps_.t[:, :], lhsT=ML, rhs=Rh.t[:, :], start=True, stop=False),
                      reads=[Rh.b, self.cmb.b], writes=[ps_.b], ticket=False)
                fw.op("pe", lambda e: e.matmul(ps_.t[:, :], lhsT=ML, rhs=Rl.t[:, :], start=False, stop=True),
                      reads=[Rl.b, Rh.b, self.cmb.b], writes=[ps_.b])
                fw.op("act", lambda e: e.activation(out=dc.t[:, :], in_=ps_.t[:, :], func=AF.Exp),
                      reads=[ps_.b], writes=[dc.b])

            def stage2(idx):
                g, hq = items[idx]
                ps_, dc, w4 = bufs[idx]
                pY = pYl[g % 2]
                fw.op("dve", lambda e: e.tensor_tensor(
                    out=w4.t[:, :].rearrange("p (r c) -> p r c", c=128), in0=dc.t[:, :].rearrange("p (r c) -> p r c", c=128),
                    in1=cbm.t[:, g, :].unsqueeze(1).broadcast_to([128, 4, 128]), op=ALU.mult),
                      reads=[dc.b, cbm.b], writes=[w4.b])
                for r4 in range(4):
                    r = hq * 4 + r4
                    h = g * 8 + r
                    fw.op("pe", lambda e, r=r, r4=r4, h=h: e.matmul(
                        pY.t[:, r * 64:(r + 1) * 64], lhsT=w4.t[:, r4 * 128:(r4 + 1) * 128], rhs=xdt.t[:, h * 64:(h + 1) * 64],
                        start=True, stop=True), reads=[w4.b, xdt.b], writes=[pY.b], ticket=(r4 == 3), waw=(r == 0))

            def epilogue(g):
                pY = pYl[g % 2]
                pYs = pG
                pdS = pG
                fw.op("pe", lambda e: e.matmul(pYs.t[:, :], lhsT=xbc.t[:, 20 + g, :], rhs=Sb.t[:, g * 512:(g + 1) * 512],
                                               start=True, stop=True), reads=[xbc.b, Sb.b], writes=[pYs.b])
                fw.op("dve", lambda e: e.tensor_tensor(
                    out=tmpy.t[:, :].rearrange("p (r c) -> p r c", c=64), in0=pYs.t[:, :].rearrange("p (r c) -> p r c", c=64),
                    in1=bc3(E.t[:, g * 8:(g + 1) * 8]), op=ALU.mult), reads=[pYs.b, E.b], writes=[tmpy.b])
                yo_ = yo[ci % 2]
                gs_ = slice(g * 512, (g + 1) * 512)
                fw.op("dve", lambda e: e.tensor_tensor(out=yo_.t[:, gs_], in0=pY.t[:, :], in1=tmpy.t[:, :], op=ALU.add),
                      reads=[pY.b, tmpy.b], writes=[yo_.b], waw=(g == 0))
                fw.op("pe", lambda e: e.matmul(
                    pdS.t[:, :], lhsT=xtok.t[:, 2048 + g * 128:2048 + (g + 1) * 128], rhs=xwa.t[:, gs_], start=True, stop=True),
                      reads=[xtok.b, xwa.b], writes=[pdS.b])
                fw.op("dve", lambda e: e.tensor_tensor(
                    out=S.t[:, gs_].rearrange("p (r c) -> p r c", c=64), in0=S.t[:, gs_].rearrange("p (r c) -> p r c", c=64),
                    in1=bc3(E.t[:, 64 + g * 8:64 + (g + 1) * 8]), op=ALU.mult), reads=[S.b, E.b], writes=[S.b])
                fw.op("dve", lambda e: e.tensor_tensor(out=S.t[:, gs_], in0=S.t[:, gs_], in1=pdS.t[:, :], op=ALU.add),
                      reads=[S.b, pdS.b], writes=[S.b])
                fw.op("act", lambda e: e.activation(out=Sb.t[:, gs_], in_=S.t[:, gs_], func=AF.Copy),
                      reads=[S.b], writes=[Sb.b])

            stage1(0)
            for idx in range(8):
                if idx + 1 < 8:
                    stage1(idx + 1)
                stage2(idx)
                if items[idx][1] == 1:
                    epilogue(items[idx][0])
            yo_ = yo[ci % 2]
            if d == 0:
                fw.dma(self.YF[r0:r0 + 128, :], yo_.t[:, :], yo_.b, reads=[yo_.b], writes=[self.db(self.YF)], waw=False,
                       stream="pool")
            else:
                fw.op("pool", lambda e: e.tensor_tensor(out=yo_.t[:, :], in0=yo_.t[:, :], in1=yf.t[:, :], op=ALU.add),
                      reads=[yo_.b, yf.b], writes=[yo_.b])
                for g in range(4):
                    gs_ = slice(g * 512, (g + 1) * 512)
                    fw.op("dve", lambda e, g=g, gs_=gs_: e.tensor_tensor(
                        out=tmpy.t[:, :].rearrange("p (r c) -> p r c", c=64),
                        in0=xtok.t[:, gs_].rearrange("p (r c) -> p r c", c=64),
                        in1=bc3(drow.t[:, g * 8:(g + 1) * 8]), op=ALU.mult), reads=[xtok.b, drow.b], writes=[tmpy.b])
                    fw.op("dve", lambda e, gs_=gs_: e.tensor_tensor(out=yo_.t[:, gs_], in0=yo_.t[:, gs_], in1=tmpy.t[:, :],
                                                                    op=ALU.add), reads=[yo_.b, tmpy.b], writes=[yo_.b])
                fw.op("act", lambda e: e.activation(out=sz.t[:, :], in_=z.t[:, :], func=AF.Silu), reads=[z.b], writes=[sz.b])
                fw.op("dve", lambda e: e.tensor_tensor(out=yo_.t[:, :], in0=yo_.t[:, :], in1=sz.t[:, :], op=ALU.mult),
                      reads=[yo_.b, sz.b], writes=[yo_.b])
                fw.op("pool", lambda e: e.memset(ssq.t[:, :], 0.0), writes=[ssq.b])
                fw.op("act", lambda e: e.activation(out=sqj.t[:, :], in_=yo_.t[:, :], func=AF.Square, accum_out=ssq.t[:, :]),
                      reads=[yo_.b], writes=[sqj.b, ssq.b])
                fw.op("act", lambda e: e.activation(out=rt1.t[:, :], in_=ssq.t[:, :], func=AF.Sqrt, scale=1.0 / SSM_DI, bias=EPS),
                      reads=[ssq.b], writes=[rt1.b])
                fw.op("dve", lambda e: e.reciprocal(out=rs1.t[:, :], in_=rt1.t[:, :]), reads=[rt1.b], writes=[rs1.b])
                fw.op("dve", lambda e: e.scalar_tensor_tensor(out=ygn.t[:, :], in0=yo_.t[:, :], scalar=rs1.t[:, 0:1],
                                                              in1=ng.t[:, :], op0=ALU.mult, op1=ALU.mult),
                      reads=[yo_.b, rs1.b, ng.b], writes=[ygn.b])
                yt = ygT[ci % 2]
                for q in range(4):
                    for j in range(4):
                        c = q * 4 + j
                        fw.op("pe", lambda e, c=c, j=j: e.transpose(pT.t[:, j * 128:(j + 1) * 128],
                                                                    ygn.t[:, c * 128:(c + 1) * 128], self.ident_bf()),
                              reads=[ygn.b, self.cmb.b], writes=[pT.b], ticket=(j == 3), waw=(j == 0))
                    if q % 2 == 0:
                        fw.op("act", lambda e, q=q: e.activation(out=yt.t[:, q * 4:(q + 1) * 4, :],
                                                                 in_=pT.t[:, 0:512].rearrange("p (a b) -> p a b", b=128), func=AF.Copy),
                              reads=[pT.b], writes=[yt.b], waw=False)
                    else:
                        fw.op("dve", lambda e, q=q: e.tensor_copy(out=yt.t[:, q * 4:(q + 1) * 4, :],
                                                                  in_=pT.t[:, 0:512].rearrange("p (a b) -> p a b", b=128)),
                              reads=[pT.b], writes=[yt.b], waw=False)
                fw.dma(YGv[:, :, c0:c0 + 128], yt.t[:, :, :], yt.b, reads=[yt.b], writes=[self.db(self.YGT)], waw=False,
                       stream="pool")
        prologue(0)
        for ci in range(len(order)):
            if ci + 1 < len(order):
                prologue(ci + 1)
            main(ci)
        self.end()


def build_program(T, voff, roff, nv, nr, n_layers=DEPTH, debug=False):
    p = Prog(T, voff, roff, nv, nr, n_layers=n_layers, debug=debug)
    p.setup()
    p.mod_phase()
    Xa, Xb = p.XA, p.XB
    for l in range(n_layers):
        if l % 2 == 0:
            i = l // 2
            p.attn_inproj(l, Xa)
            p.attn_core(l)
            p.proj_res_norm(l, Xa, Xb, p.OT, KD, lambda k, i=i: p.i_awo[i, k * 128:(k + 1) * 128, :])
        else:
            i = l // 2
            p.ssm_inproj(l, Xa)
            p.ssm_scan(l, 0)
            p.ssm_scan(l, 1)
            p.proj_res_norm(l, Xa, Xb, p.YGT, SSM_DI // 128, lambda k, i=i: p.i_swo[i, k * 128:(k + 1) * 128, :])
        p.ffn_in(l)
        p.ffn_out(l, Xb, Xa)
    p.final_norm(Xa)
    p.pes.close()
    p.fw.close()
    return p


_CACHE = {}


def run(inputs, T=8192, n_layers=DEPTH, debug=False, cores=NCORES):
    cfg, voff, roff, shared = prep_shared(inputs, T)
    key = (T, n_layers, debug)
    if key not in _CACHE:
        _CACHE[key] = build_program(T, voff, roff, shared["vecs"].shape[1], shared["rowv"].shape[1],
                                    n_layers=n_layers, debug=debug)
    p = _CACHE[key]
    in_maps = []
    for b in range(cores):
        m = dict(shared)
        m.update(prep_core(inputs, b, T))
        in_maps.append(m)
    res = run_bass_kernel_spmd(p.nc, in_maps, core_ids=list(range(cores)))
    return p, res


def kernel(**inputs):
    p, res = run(inputs)
    out = np.stack([np.ascontiguousarray(r["outT"].T) for r in res.results], axis=0)
    return out.astype(np.float32)
```

```python
import math
from contextlib import ExitStack

import numpy as np
import concourse.bass as bass
import concourse.mybir as mybir
from concourse.bass_utils import run_bass_kernel_spmd

F32 = mybir.dt.float32
BF16 = mybir.dt.bfloat16
AF = mybir.ActivationFunctionType
ALU = mybir.AluOpType


class Sem:
    def __init__(self, h, name):
        self.h = h
        self.cnt = 0
        self.name = name


class Buf:
    def __init__(self, ap=None, name=""):
        self.ap = ap
        self.name = name
        self.last_w = {}
        self.readers = {}
        self.gen_deps = {}
        self.dsem = None


def _merge(dst, src):
    for s, v in src.items():
        if dst.get(s, 0) < v:
            dst[s] = v


class FW:
    def __init__(self, nc, n_dma_sems=48, n_spare=28):
        self.nc = nc
        self.es = ExitStack()
        self.streams = {
            "pe": nc.tensor,
            "act": nc.scalar,
            "dve": nc.vector,
            "pool": nc.gpsimd,
            "sp": nc.sync,
        }
        self.esem = {}
        for k in ("pe", "act", "dve", "pool"):
            self.esem[k] = Sem(self.es.enter_context(nc.semaphore("e_" + k)), k)
        self.known = {k: {} for k in self.streams}
        self.free_dsems = [
            Sem(self.es.enter_context(nc.semaphore("d%d" % i)), "d%d" % i) for i in range(n_dma_sems)
        ]
        self.spare = [Sem(self.es.enter_context(nc.semaphore("x%d" % i)), "x%d" % i) for i in range(n_spare)]
        self.phase_bufs = []
        self.pers_bufs = []
        self.n_inst = 0
        self.n_wait = 0

    def buf(self, ap=None, name="", persistent=False):
        b = Buf(ap, name)
        (self.pers_bufs if persistent else self.phase_bufs).append(b)
        return b

    def _wait_for(self, stream, deps, fold=False):
        eng = self.streams[stream]
        kn = self.known[stream]
        need = [(s, v) for s, v in deps.items() if kn.get(s, 0) < v]
        last = None
        if fold and need:
            last = need.pop()
            kn[last[0]] = last[1]
        for s, v in need:
            eng.wait_ge(s.h, v)
            kn[s] = v
            self.n_wait += 1
        return last

    def _deps(self, stream, reads, writes, waw=True):
        deps = {}
        for r in reads:
            _merge(deps, r.last_w)
        for w in writes:
            if waw or w.readers:
                _merge(deps, w.last_w)
                _merge(deps, w.readers)
            else:
                _merge(deps, w.gen_deps)
        if stream == "pe":
            deps.pop(self.esem["pe"], None)
        return deps

    def _commit(self, t, reads, writes, waw):
        for w in writes:
            if waw or w.readers:
                g = dict(w.last_w)
                _merge(g, w.readers)
                w.gen_deps = g
                w.last_w = dict(t)
                w.readers = {}
            else:
                _merge(w.last_w, t)
        for r in reads:
            _merge(r.readers, t)

    def op(self, stream, fn, reads=(), writes=(), ticket=True, waw=True):
        deps = self._deps(stream, reads, writes, waw)
        last = self._wait_for(stream, deps, fold=True)
        inst = fn(self.streams[stream])
        if last is not None:
            inst._wait_ge(last[0].h, last[1])
        self.n_inst += 1
        if ticket:
            s = self.esem[stream]
            if s.cnt >= 30000:
                s = self.spare.pop()
                self.esem[stream] = s
            s.cnt += 1
            inst.then_inc(s.h, 1)
            self._commit({s: s.cnt}, reads, writes, waw)
        return inst

    def dma(self, out_ap, in_ap, owner, reads=(), writes=(), stream="sp", waw=True, **kw):
        deps = self._deps(stream, reads, writes, waw)
        last = self._wait_for(stream, deps, fold=True)
        if owner.dsem is None:
            owner.dsem = self.free_dsems.pop(0)
        s = owner.dsem
        s.cnt += 16
        inst = self.streams[stream].dma_start(out=out_ap, in_=in_ap, **kw)
        if last is not None:
            inst._wait_ge(last[0].h, last[1])
        inst.then_inc(s.h, 16)
        self.n_inst += 1
        self._commit({s: s.cnt}, reads, writes, waw)

    def barrier(self, clear=True):
        alld = {}
        for s in self.esem.values():
            if s.cnt:
                alld[s] = s.cnt
        for b in self.phase_bufs + self.pers_bufs:
            if b.dsem is not None:
                alld[b.dsem] = b.dsem.cnt
        for st in self.streams:
            d = dict(alld)
            self._wait_for(st, d)
        for b in self.phase_bufs + self.pers_bufs:
            if b.dsem is not None:
                self.free_dsems.append(b.dsem)
                b.dsem = None
            b.last_w = {}
            b.readers = {}
            b.gen_deps = {}
        if clear:
            self.phase_bufs = []

    def close(self):
        self.es.close()


D = 1024
KD = D // 128
DEPTH = 4
CTX = 256
GRID_W = 64
HD = 64
EPS = 1e-6
DFF = 2816
NFF = DFF // 128
SSM_DI = 2048
SSM_XBC = 3072
SSM_IN = 5184
SSM_H = 32
NCORES = 8
ATT_FM = 1664
ATT_EXT = 2 * ATT_FM + 640


class Cfg:
    def __init__(self, T):
        self.T = T
        self.C = CTX
        self.CS = 1
        self.LS = CTX + 3
        self.NP = CTX + T + 4
        self.NTOK = CTX + T
        self.seqs = [("ctx", self.CS, CTX, 1, 0), ("lat", self.LS, T, 0, CTX)]


def _partner():
    p = np.arange(64)
    return np.where((p % 32) < 16, p + 16, p - 16)


class VecLayout:
    def __init__(self):
        self.off = {}
        self.n = 0
        self.cols = []

    def add(self, name, arr):
        arr = np.ascontiguousarray(arr, dtype=np.float32).reshape(128, -1)
        self.off[name] = (self.n, arr.shape[1])
        self.n += arr.shape[1]
        self.cols.append(arr)

    def build(self):
        return np.ascontiguousarray(np.concatenate(self.cols, axis=1))


def fm(v):
    v = np.asarray(v, dtype=np.float32)
    return np.ascontiguousarray(v.reshape(-1, 128).T)


def prep_shared(inp, T):
    cfg = Cfg(T)
    vl = VecLayout()
    rows = {}
    for l in range(DEPTH):
        vl.add("modb%d" % l, fm(inp["mod_b"][l]))
        vl.add("gmix%d" % l, fm(inp["norm_mix_g"][l]))
        vl.add("gffn%d" % l, fm(inp["norm_ffn_g"][l]))
        cw = inp["ffn_conv_w"][l]
        for i in range(3):
            vl.add("fcw%d_%d" % (l, i), fm(cw[i]))
        vl.add("fcb%d" % l, fm(inp["ffn_conv_b"][l]))
    vl.add("gfin", fm(inp["final_norm_g"]))
    pt = _partner()
    for i in range(DEPTH // 2 + DEPTH % 2):
        gq = np.asarray(inp["gqa_q_norm_g"][i], np.float32)
        gk = np.asarray(inp["gqa_k_norm_g"][i], np.float32)
        vl.add("gq%d" % i, np.concatenate([gq, gq])[:, None])
        vl.add("gqs%d" % i, np.concatenate([gq[pt], gq[pt]])[:, None])
        vl.add("gk%d" % i, np.concatenate([gk, gk])[:, None])
        vl.add("gks%d" % i, np.concatenate([gk[pt], gk[pt]])[:, None])
        vl.add("subg%d" % i, np.asarray(inp["diff_subln_g"][i], np.float32)[:, None])
    for i in range(DEPTH // 2):
        cw = inp["ssm_conv_w"][i]
        for j in range(3):
            vl.add("scw%d_%d" % (i, j), fm(cw[j]))
        vl.add("scb%d" % i, fm(inp["ssm_conv_b"][i]))
    vecs = vl.build()

    rl = VecLayout()

    def radd(name, v):
        v = np.asarray(v, np.float32).reshape(1, -1)
        rl.off[name] = (rl.n, v.shape[1])
        rl.n += v.shape[1]
        rl.cols.append(v)

    for i in range((DEPTH + 1) // 2):
        radd("lq1_%d" % i, inp["diff_lq1"][i])
        radd("lk1_%d" % i, inp["diff_lk1"][i])
        radd("lq2_%d" % i, inp["diff_lq2"][i])
        radd("lk2_%d" % i, inp["diff_lk2"][i])
    for i in range(DEPTH // 2):
        radd("dtb%d" % i, inp["ssm_dt_bias"][i])
        radd("alog%d" % i, inp["ssm_a_log"][i])
        radd("ssd%d" % i, inp["ssm_d"][i])
        radd("sng%d" % i, inp["ssm_norm_g"][i])
    rowv = np.ascontiguousarray(np.concatenate(rl.cols, axis=1))

    w_ext = []
    for i in range((DEPTH + 1) // 2):
        w = np.asarray(inp["attn_w_in"][i])
        qa, ka, va = w[:, 0:512], w[:, 512:1024], w[:, 1024:1536]
        qb, kb, vb = w[:, 1536:2048], w[:, 2048:2176], w[:, 2176:2304]
        qbp = np.concatenate(
            [np.concatenate([qb[:, j * 64:(j + 1) * 64], qb[:, (4 + j) * 64:(5 + j) * 64]], axis=1) for j in range(4)],
            axis=1)
        fmw = np.concatenate([qa, ka, qbp, kb], axis=1)
        idx = (np.arange(ATT_FM) // 64) * 64 + pt[np.arange(ATT_FM) % 64]
        w_ext.append(np.concatenate([fmw, fmw[:, idx], va, vb], axis=1))
    w_ext = np.ascontiguousarray(np.stack(w_ext))

    NP = cfg.NP
    cosT = np.ones((128, NP), np.float32)
    sinT = np.zeros((128, NP), np.float32)
    t = np.arange(T)
    row = (t // GRID_W).astype(np.float32)
    col = (t % GRID_W).astype(np.float32)
    inv = (1.0 / (10000.0 ** (np.arange(16, dtype=np.float32) / 16))).astype(np.float32)
    for p in range(64):
        q, i = p // 16, p % 16
        ang = ((row if q < 2 else col) * inv[i]).astype(np.float32)
        sgn = -1.0 if q in (0, 2) else 1.0
        for rep in (0, 64):
            cosT[p + rep, cfg.LS:cfg.LS + T] = np.cos(ang)
            sinT[p + rep, cfg.LS:cfg.LS + T] = sgn * np.sin(ang)

    tt = np.arange(128)
    consts = {
        "ident": np.eye(128, dtype=np.float32),
        "ML_f": (tt[:, None] > tt[None, :]).astype(np.float32),
        "MR_f": (tt[:, None] <= tt[None, :]).astype(np.float32),
        "ML_b": (tt[:, None] < tt[None, :]).astype(np.float32),
        "MR_b": (tt[:, None] >= tt[None, :]).astype(np.float32),
    }
    bd = np.zeros((128, 128), np.float32)
    bd[:64, :64] = 1
    bd[64:, 64:] = 1
    consts["bd"] = bd
    cmat = np.ascontiguousarray(
        np.concatenate([consts[k] for k in ("ident", "ML_f", "MR_f", "ML_b", "MR_b", "bd")], axis=1))

    shared = {
        "vecs": vecs, "rowv": rowv, "w_ext": w_ext, "cosT": cosT, "sinT": sinT, "cmat": cmat,
        "mod_w": np.asarray(inp["mod_w"], np.float32),
        "attn_w_out": np.asarray(inp["attn_w_out"], np.float32),
        "ssm_w_in": np.asarray(inp["ssm_w_in"], np.float32),
        "ssm_w_out": np.asarray(inp["ssm_w_out"], np.float32),
        "ffn_w_in": np.asarray(inp["ffn_w_in"], np.float32),
        "ffn_w_out": np.asarray(inp["ffn_w_out"], np.float32),
    }
    return cfg, vl.off, rl.off, shared


def prep_core(inp, b, T):
    xT = np.ascontiguousarray(np.asarray(inp["x"][b, :T]).T)
    cxT = np.ascontiguousarray(np.asarray(inp["ctx"][b]).T)
    cT = np.stack([fm(inp["c"][b]), fm(inp["c_ctx"])], axis=2)
    return {"xT": xT, "cxT": cxT, "cT": np.ascontiguousarray(cT.reshape(128, 16))}


class TB:
    def __init__(self, t, b, bs=None):
        self.t = t
        self.b = b
        self.bs = bs


def blocks_of(n, size):
    out = []
    t0 = 0
    while t0 < n:
        out.append((t0, min(size, n - t0)))
        t0 += size
    return out


class Prog:
    def __init__(self, T, voff, roff, nv, nr, n_layers=DEPTH, debug=False):
        self.cfg = Cfg(T)
        self.voff, self.roff = voff, roff
        self.debug = debug
        self.n_layers = n_layers
        nc = bass.Bass("TRN2", target_bir_lowering=False)
        self.nc = nc
        self.fw = FW(nc)
        cfg = self.cfg
        NP, NTOK = cfg.NP, cfg.NTOK
        di = lambda name, shape, dt=F32: nc.dram_tensor(name, shape, dt, kind="ExternalInput").ap()
        self.i_xT = di("xT", [D, T])
        self.i_cxT = di("cxT", [D, CTX])
        self.i_cT = di("cT", [128, 16])
        self.i_vecs = di("vecs", [128, nv])
        self.i_rowv = di("rowv", [1, nr])
        self.i_wext = di("w_ext", [(DEPTH + 1) // 2, D, ATT_EXT])
        self.i_cos = di("cosT", [128, NP])
        self.i_sin = di("sinT", [128, NP])
        self.i_cmat = di("cmat", [128, 768])
        self.i_modw = di("mod_w", [DEPTH, D, 6 * D])
        self.i_awo = di("attn_w_out", [(DEPTH + 1) // 2, D, D])
        self.i_swi = di("ssm_w_in", [DEPTH // 2, D, SSM_IN])
        self.i_swo = di("ssm_w_out", [DEPTH // 2, SSM_DI, D])
        self.i_fwi = di("ffn_w_in", [DEPTH, D, 2 * DFF])
        self.i_fwo = di("ffn_w_out", [DEPTH, DFF, D])
        self.o_out = nc.dram_tensor("outT", [D, T], F32, kind="ExternalOutput").ap()
        self.dbg = {}
        kind = "ExternalOutput" if debug else "Internal"

        def scr(name, shape, dt):
            ap = nc.dram_tensor(name, shape, dt, kind=kind).ap()
            if debug:
                self.dbg[name] = ap
            return ap

        self.XA = scr("XA", [D, NP], F32)
        self.XB = scr("XB", [D, NP], F32)
        self.HT = scr("HT", [D, NP], BF16)
        self.QT = scr("QT", [D, NP], BF16)
        self.KT = scr("KT", [5 * 128, NP], BF16)
        self.VT = scr("VT", [NTOK, 640], BF16)
        self.OT = scr("OT", [D, NP], BF16)
        self.UT = scr("UT", [DFF, NP], BF16)
        self.XBCT = scr("XBCT", [SSM_XBC, NP], BF16)
        self.ZT = scr("ZT", [NTOK, SSM_DI], F32)
        self.DTT = scr("DTT", [NTOK, 64], F32)
        self.YF = scr("YF", [NTOK, SSM_DI], F32)
        self.YGT = scr("YGT", [SSM_DI, NP], BF16)
        self.dram_b = {}
        self.pes = ExitStack()
        self.ph = None
        self._names = 0

    def _nm(self, name):
        self._names += 1
        return "%s_%d" % (name, self._names)

    def sb(self, name, shape, dt=F32, nb=0, pers=False):
        st = self.pes if pers else self.ph
        t = st.enter_context(self.nc.sbuf_tensor(self._nm(name), shape, dt))
        b = self.fw.buf(name=name, persistent=pers)
        bs = [self.fw.buf(name=name + str(i), persistent=pers) for i in range(nb)] if nb else None
        return TB(t, b, bs)

    def ps(self, name, shape, dt=F32):
        t = self.ph.enter_context(self.nc.psum_tensor(self._nm(name), shape, dt))
        return TB(t, self.fw.buf(name=name))

    def db(self, ap):
        k = ap.name if hasattr(ap, "name") else id(ap)
        if k not in self.dram_b:
            self.dram_b[k] = self.fw.buf(name="dram", persistent=True)
        return self.dram_b[k]

    def begin(self):
        self.ph = ExitStack()

    def end(self):
        self.fw.barrier()
        self.ph.close()
        self.ph = None

    def vec(self, name, j=0, w=1):
        o, n = self.voff[name]
        return self.vecs.t[:, o + j:o + j + w]

    def load_w(self, dst, src_rows, ncols, col0=0, dst_col0=0, kc=None):
        fw = self.fw
        kc = kc if kc is not None else dst.t.shape[1]
        PIECE = 1408
        main_ph = self.ph
        self.ph = ExitStack()
        wst = [self.sb("wst", [128, PIECE], F32) for _ in range(3)]
        wi = 0
        for k in range(kc):
            for (c0, n) in blocks_of(ncols, PIECE):
                st = wst[wi % 3]
                eng = ("dve", "pool", "act")[wi % 3]
                wi += 1
                src = src_rows(k)[:, col0 + c0:col0 + c0 + n]
                fw.dma(st.t[:, 0:n], src, st.b, writes=[st.b])
                d = dst.t[:, k, dst_col0 + c0:dst_col0 + c0 + n]
                if eng == "act":
                    fw.op("act", lambda e, d=d, st=st, n=n: e.activation(out=d, in_=st.t[:, 0:n], func=AF.Copy),
                          reads=[st.b], writes=[dst.b], waw=False)
                else:
                    fw.op(eng, lambda e, d=d, st=st, n=n: e.tensor_copy(out=d, in_=st.t[:, 0:n]),
                          reads=[st.b], writes=[dst.b], waw=False)
        fw.barrier(clear=False)
        self.ph.close()
        self.ph = main_ph

    def rstd(self, ss, out, tmp, scale, n):
        fw = self.fw
        fw.op("act", lambda e: e.activation(out=tmp.t[:, 0:n], in_=ss.t[:, 0:n], func=AF.Sqrt, scale=scale, bias=EPS),
              reads=[ss.b], writes=[tmp.b])
        fw.op("dve", lambda e: e.reciprocal(out=out.t[:, 0:n], in_=tmp.t[:, 0:n]), reads=[tmp.b], writes=[out.b])

    def norm_mod(self, x, n, ss, sq, rs, tmp, gs, sh, out, out_dt_bf16=True):
        fw = self.fw
        for k in range(KD):
            fw.op("act", lambda e, k=k: e.activation(out=sq.t[:, k, 0:n], in_=x.t[:, k, 0:n], func=AF.Square),
                  reads=[x.b], writes=[sq.b], waw=False)
        for k in range(KD):
            fw.op("pe", lambda e, k=k: e.matmul(ss.t[:, 0:n], lhsT=self.ones_bf.t[:, :], rhs=sq.t[:, k, 0:n],
                                                start=(k == 0), stop=(k == KD - 1)),
                  reads=[sq.b, self.ones_bf.b], writes=[ss.b], ticket=(k == KD - 1))
        self.rstd(ss, rs, tmp, 1.0 / D, n)
        for k in range(KD):
            if sh is None:
                fw.op("dve", lambda e, k=k: e.scalar_tensor_tensor(
                    out=out.t[:, k, 0:n], in0=x.t[:, k, 0:n], scalar=gs(k), in1=rs.t[:, 0:n],
                    op0=ALU.mult, op1=ALU.mult), reads=[x.b, rs.b], writes=[out.b], waw=False)
            else:
                tm = self._nm_tmp[k % 2]
                fw.op("dve", lambda e, k=k, tm=tm: e.scalar_tensor_tensor(
                    out=tm.t[:, 0:n], in0=x.t[:, k, 0:n], scalar=gs(k), in1=rs.t[:, 0:n],
                    op0=ALU.mult, op1=ALU.mult), reads=[x.b, rs.b], writes=[tm.b])
                fw.op("act", lambda e, k=k, tm=tm: e.activation(out=out.t[:, k, 0:n], in_=tm.t[:, 0:n],
                                                         func=AF.Identity, bias=sh(k), scale=1.0),
                      reads=[tm.b], writes=[out.b], waw=False)

    def norm_tiles(self, nmax=512):
        self._nm_tmp = [self.sb("nmt", [128, nmax], F32) for _ in range(2)]
        return dict(ss=self.ps("ss", [128, 512], F32), sq=self.sb("sq", [128, KD, nmax], BF16),
                    rs=self.sb("rs", [128, nmax], F32), tmp=self.sb("rtmp", [128, nmax], F32))

    def load_cols(self, dst, dview, seq, t0, n, halo, extra_reads=()):
        fw = self.fw
        _, start, ln, _, _ = seq
        lo, hi = t0 - halo, t0 + n + halo
        clo, chi = max(lo, 0), min(hi, ln)
        if clo > lo:
            fw.op("pool", lambda e: e.memset(dst.t[:, :, 0:clo - lo], 0.0), writes=[dst.b])
        if chi < hi:
            fw.op("pool", lambda e: e.memset(dst.t[:, :, chi - lo:hi - lo], 0.0), writes=[dst.b],
                  waw=(clo == lo))
        fw.dma(dst.t[:, :, clo - lo:chi - lo], dview[:, :, start + clo:start + chi], dst.b,
               reads=[self.db(dview)], writes=[dst.b], waw=(clo == lo and chi == hi))

    def dvv(self, l, v, which):
        base = ((l * 2 + v) * 6 + which) * 8
        return lambda k: self.dv.t[:, base + k:base + k + 1]

    def setup(self):
        fw, cfg = self.fw, self.cfg
        nv = self.i_vecs.shape[1]
        self.vecs = self.sb("vecs", [128, nv], F32, pers=True)
        self.cm = self.sb("cmat", [128, 768], F32, pers=True)
        self.cmb = self.sb("cmatb", [128, 768], BF16, pers=True)
        self.ones_bf = self.sb("ones", [128, 128], BF16, pers=True)
        self.dv = self.sb("dv", [128, DEPTH * 2 * 6 * 8], F32, pers=True)
        self.begin()
        fw.dma(self.vecs.t[:, :], self.i_vecs[:, :], self.vecs.b, writes=[self.vecs.b])
        fw.dma(self.cm.t[:, :], self.i_cmat[:, :], self.cm.b, writes=[self.cm.b])
        fw.op("dve", lambda e: e.tensor_copy(out=self.cmb.t[:, :], in_=self.cm.t[:, :]), reads=[self.cm.b],
              writes=[self.cmb.b])
        fw.op("pool", lambda e: e.memset(self.ones_bf.t[:, :], 1.0), writes=[self.ones_bf.b])
        dummy = self.fw.buf(name="x0")
        XAv = self.XA.rearrange("(k p) n -> p k n", p=128)
        xin = self.i_xT.rearrange("(k p) n -> p k n", p=128)
        cin = self.i_cxT.rearrange("(k p) n -> p k n", p=128)
        for k in range(KD):
            fw.dma(XAv[:, k, cfg.LS:cfg.LS + cfg.T], xin[:, k, :], dummy, writes=[self.db(self.XA)], waw=False)
            fw.dma(XAv[:, k, cfg.CS:cfg.CS + cfg.C], cin[:, k, :], dummy, writes=[self.db(self.XA)], waw=False)
        self.end()

    def ident_bf(self):
        return self.cmb.t[:, 0:128]

    def cmask(self, name, bf=True):
        j = ("ident", "ML_f", "MR_f", "ML_b", "MR_b", "bd").index(name)
        return (self.cmb if bf else self.cm).t[:, j * 128:(j + 1) * 128]

    def mod_phase(self):
        fw = self.fw
        self.begin()
        ct = self.sb("ct", [128, 16], F32)
        sc = self.sb("sc", [128, 16], BF16)
        fw.dma(ct.t[:, :], self.i_cT[:, :], ct.b, writes=[ct.b])
        fw.op("act", lambda e: e.activation(out=sc.t[:, :], in_=ct.t[:, :], func=AF.Silu), reads=[ct.b], writes=[sc.b])
        modT = self.sb("modT", [128, DEPTH, 48, 2], F32)
        wst = [self.sb("mwst", [128, KD, 512], F32) for _ in range(2)]
        wb = [self.sb("mwb", [128, KD, 512], BF16) for _ in range(2)]
        pm = [self.ps("pm", [128, 512], F32) for _ in range(2)]
        it = 0
        for l in range(self.n_layers):
            mw = self.i_modw[l].rearrange("(k p) n -> p k n", p=128)
            for cg in range(12):
                s_, b_, p_ = wst[it % 2], wb[it % 2], pm[it % 2]
                fw.dma(s_.t[:, :, :], mw[:, :, cg * 512:(cg + 1) * 512], s_.b, writes=[s_.b])
                eng = "dve" if it % 2 == 0 else "pool"
                fw.op(eng, lambda e, s_=s_, b_=b_: e.tensor_copy(out=b_.t[:, :, :], in_=s_.t[:, :, :]),
                      reads=[s_.b], writes=[b_.b])
                for j in range(4):
                    for k in range(KD):
                        fw.op("pe", lambda e, j=j, k=k, b_=b_, p_=p_: e.matmul(
                            p_.t[:, 2 * j:2 * j + 2], lhsT=b_.t[:, k, j * 128:(j + 1) * 128],
                            rhs=sc.t[:, 2 * k:2 * k + 2], start=(k == 0), stop=(k == KD - 1)),
                              reads=[b_.b, sc.b], writes=[p_.b], ticket=(k == KD - 1 and j == 3))
                for j in range(4):
                    fw.op("dve", lambda e, j=j, p_=p_, l=l, cg=cg: e.tensor_scalar(
                        out=modT.t[:, l, cg * 4 + j, :], in0=p_.t[:, 2 * j:2 * j + 2],
                        scalar1=self.vec("modb%d" % l, cg * 4 + j), scalar2=None, op0=ALU.add),
                          reads=[p_.b], writes=[modT.b], waw=False)
                it += 1
        for l in range(self.n_layers):
            for v in range(2):
                def dvs(which):
                    base = ((l * 2 + v) * 6 + which) * 8
                    return self.dv.t[:, base:base + 8]
                for which, (sci, gname) in ((0, (8, "gmix%d" % l)), (3, (32, "gffn%d" % l))):
                    fw.op("dve", lambda e, which=which, sci=sci, gname=gname, l=l, v=v, dvs=dvs: e.scalar_tensor_tensor(
                        out=dvs(which), in0=modT.t[:, l, sci:sci + 8, v], scalar=1.0, in1=self.vec(gname, 0, 8),
                        op0=ALU.add, op1=ALU.mult), reads=[modT.b], writes=[self.dv.b], waw=False)
                for which, c0 in ((1, 0), (2, 16), (4, 24), (5, 40)):
                    fw.op("dve", lambda e, which=which, c0=c0, l=l, v=v, dvs=dvs: e.tensor_copy(
                        out=dvs(which), in_=modT.t[:, l, c0:c0 + 8, v]), reads=[modT.b], writes=[self.dv.b], waw=False)
        self.end()

    def attn_inproj(self, l, X):
        fw, cfg = self.fw, self.cfg
        i = l // 2
        self.begin()
        W = self.sb("wA", [128, KD, ATT_EXT], BF16)
        self.load_w(W, lambda k: self.i_wext[i, k * 128:(k + 1) * 128, :], ATT_EXT)
        nt = self.norm_tiles()
        xs = [self.sb("x", [128, KD, 512], F32) for _ in range(2)]
        hTs = [self.sb("hT", [128, KD, 512], BF16) for _ in range(2)]
        cs = [self.sb("cos", [128, 512], F32) for _ in range(2)]
        sn = [self.sb("sin", [128, 512], F32) for _ in range(2)]
        qko = [self.sb("qko", [128, 13, 512], BF16) for _ in range(1)]
        vo = [self.sb("vo", [128, 4, 640], BF16) for _ in range(1)]
        pP = [self.ps("pP", [128, 512]) for _ in range(2)]
        pS = [self.ps("pS", [128, 512]) for _ in range(2)]
        ssq = self.ps("ssq", [128, 512])
        pV = self.ps("pV", [128, 1024])
        sqn = self.sb("sqn", [128, 512], BF16)
        rq, tq = self.sb("rq", [128, 512]), self.sb("tq", [128, 512])
        Aq = [self.sb("Aq", [128, 512]) for _ in range(2)]
        Bq = [self.sb("Bq", [128, 512]) for _ in range(2)]
        t1 = [self.sb("t1", [128, 512]) for _ in range(2)]
        t2 = [self.sb("t2", [128, 512]) for _ in range(2)]
        Xv = X.rearrange("(k p) n -> p k n", p=128)
        QTv = self.QT.rearrange("(k p) n -> p k n", p=128)
        KTv = self.KT.rearrange("(k p) n -> p k n", p=128)
        bi = 0
        for seq in cfg.seqs:
            _, start, ln, v, row0 = seq
            lat = (v == 0)
            for (t0, n) in blocks_of(ln, 512):
                x, hT, qo, vv = xs[bi % 2], hTs[bi % 2], qko[0], vo[0]
                c_, s_ = cs[bi % 2], sn[bi % 2]
                bi += 1
                self.load_cols(x, Xv, seq, t0, n, 0)
                if lat:
                    fw.dma(c_.t[:, 0:n], self.i_cos[:, start + t0:start + t0 + n], c_.b, writes=[c_.b])
                    fw.dma(s_.t[:, 0:n], self.i_sin[:, start + t0:start + t0 + n], s_.b, writes=[s_.b])
                self.norm_mod(x, n, nt["ss"], nt["sq"], nt["rs"], nt["tmp"], self.dvv(l, v, 0), self.dvv(l, v, 1), hT)
                for j in range(13):
                    P, S = pP[j % 2], pS[j % 2]
                    for k in range(KD):
                        fw.op("pe", lambda e, k=k, j=j, P=P: e.matmul(
                            P.t[:, 0:n], lhsT=W.t[:, k, j * 128:(j + 1) * 128], rhs=hT.t[:, k, 0:n],
                            start=(k == 0), stop=(k == KD - 1)), reads=[W.b, hT.b], writes=[P.b], ticket=(k == KD - 1))
                    if lat:
                        for k in range(KD):
                            fw.op("pe", lambda e, k=k, j=j, S=S: e.matmul(
                                S.t[:, 0:n], lhsT=W.t[:, k, ATT_FM + j * 128:ATT_FM + (j + 1) * 128],
                                rhs=hT.t[:, k, 0:n], start=(k == 0), stop=(k == KD - 1)),
                                  reads=[W.b, hT.b], writes=[S.b], ticket=(k == KD - 1))
                    normed = j >= 8
                    A, B = P, S
                    if normed:
                        fw.op("act", lambda e, P=P: e.activation(out=sqn.t[:, 0:n], in_=P.t[:, 0:n], func=AF.Square),
                              reads=[P.b], writes=[sqn.b])
                        fw.op("pe", lambda e: e.matmul(ssq.t[:, 0:n], lhsT=self.cmask("bd"), rhs=sqn.t[:, 0:n],
                                                       start=True, stop=True), reads=[sqn.b, self.cmb.b], writes=[ssq.b])
                        self.rstd(ssq, rq, tq, 1.0 / HD, n)
                        gn, gsn = ("gq%d" % i, "gqs%d" % i) if j < 12 else ("gk%d" % i, "gks%d" % i)
                        A = Aq[j % 2]
                        fw.op("dve", lambda e, P=P, A=A, gn=gn: e.scalar_tensor_tensor(
                            out=A.t[:, 0:n], in0=P.t[:, 0:n], scalar=self.vec(gn), in1=rq.t[:, 0:n],
                            op0=ALU.mult, op1=ALU.mult), reads=[P.b, rq.b], writes=[A.b])
                        if lat:
                            B = Bq[j % 2]
                            fw.op("dve", lambda e, S=S, B=B, gsn=gsn: e.scalar_tensor_tensor(
                                out=B.t[:, 0:n], in0=S.t[:, 0:n], scalar=self.vec(gsn), in1=rq.t[:, 0:n],
                                op0=ALU.mult, op1=ALU.mult), reads=[S.b, rq.b], writes=[B.b])
                    if lat:
                        a1, a2 = t1[j % 2], t2[j % 2]
                        fw.op("dve", lambda e, A=A, a1=a1: e.tensor_tensor(
                            out=a1.t[:, 0:n], in0=A.t[:, 0:n], in1=c_.t[:, 0:n], op=ALU.mult),
                              reads=[A.b, c_.b], writes=[a1.b])
                        fw.op("dve", lambda e, B=B, a2=a2: e.tensor_tensor(
                            out=a2.t[:, 0:n], in0=B.t[:, 0:n], in1=s_.t[:, 0:n], op=ALU.mult),
                              reads=[B.b, s_.b], writes=[a2.b])
                        fw.op("pool", lambda e, a1=a1, a2=a2, j=j: e.tensor_tensor(
                            out=qo.t[:, j, 0:n], in0=a1.t[:, 0:n], in1=a2.t[:, 0:n], op=ALU.add),
                              reads=[a1.b, a2.b], writes=[qo.b], waw=False)
                    elif normed:
                        fw.op("pool", lambda e, A=A, j=j: e.tensor_copy(out=qo.t[:, j, 0:n], in_=A.t[:, 0:n]),
                              reads=[A.b], writes=[qo.b], waw=False)
                    else:
                        fw.op("act", lambda e, A=A, j=j: e.activation(out=qo.t[:, j, 0:n], in_=A.t[:, 0:n], func=AF.Copy),
                              reads=[A.b], writes=[qo.b], waw=False)
                c0, c1 = start + t0, start + t0 + n
                for (dst, d0, s0, w) in ((QTv, 0, 0, 4), (KTv, 0, 4, 4), (QTv, 4, 8, 4), (KTv, 4, 12, 1)):
                    fw.dma(dst[:, d0:d0 + w, c0:c1], qo.t[:, s0:s0 + w, 0:n], qo.b, reads=[qo.b],
                           writes=[self.db(dst)], waw=False, stream="pool")
                nsub = n // 128
                for sub in range(nsub):
                    for (cc, w) in ((0, 512), (512, 128)):
                        for k in range(KD):
                            fw.op("pe", lambda e, k=k, cc=cc, w=w, sub=sub: e.matmul(
                                pV.t[:, cc:cc + w], lhsT=hT.t[:, k, sub * 128:(sub + 1) * 128],
                                rhs=W.t[:, k, 2 * ATT_FM + cc:2 * ATT_FM + cc + w], start=(k == 0), stop=(k == KD - 1)),
                                  reads=[W.b, hT.b], writes=[pV.b], ticket=(k == KD - 1 and cc == 512))
                    fw.op("act" if sub % 2 == 0 else "dve",
                          (lambda e, sub=sub: e.activation(out=vv.t[:, sub, :], in_=pV.t[:, 0:640], func=AF.Copy))
                          if sub % 2 == 0 else
                          (lambda e, sub=sub: e.tensor_copy(out=vv.t[:, sub, :], in_=pV.t[:, 0:640])),
                          reads=[pV.b], writes=[vv.b], waw=False)
                r0 = row0 + t0
                fw.dma(self.VT[r0:r0 + n, :].rearrange("(s p) c -> p s c", p=128), vv.t[:, 0:nsub, :], vv.b,
                       reads=[vv.b], writes=[self.db(self.VT)], waw=False, stream="pool")
        self.end()

    def attn_core(self, l):
        fw, cfg = self.fw, self.cfg
        i = l // 2
        lam_init = 0.8 - 0.6 * math.exp(-0.3 * l)
        NTOK, C, T = cfg.NTOK, cfg.C, cfg.T
        NKT = NTOK // 128
        self.begin()
        ro = self.roff["lq1_%d" % i][0]
        rv = self.sb("lqk", [128, 256])
        fw.dma(rv.t[:, :], self.i_rowv[0:1, ro:ro + 256].broadcast_to([128, 256]), rv.b, writes=[rv.b])
        prod = self.sb("prod", [128, 128])
        fw.op("dve", lambda e: e.tensor_tensor(out=prod.t[:, 0:64], in0=rv.t[:, 0:64], in1=rv.t[:, 64:128], op=ALU.mult),
              reads=[rv.b], writes=[prod.b])
        fw.op("dve", lambda e: e.tensor_tensor(out=prod.t[:, 64:128], in0=rv.t[:, 128:192], in1=rv.t[:, 192:256],
                                               op=ALU.mult), reads=[rv.b], writes=[prod.b], waw=False)
        s12 = self.sb("s12", [128, 2])
        for q in range(2):
            fw.op("dve", lambda e, q=q: e.reduce_sum(out=s12.t[:, q:q + 1], in_=prod.t[:, q * 64:(q + 1) * 64],
                                                     axis=mybir.AxisListType.X), reads=[prod.b], writes=[s12.b], waw=False)
        e12 = self.sb("e12", [128, 2])
        fw.op("act", lambda e: e.activation(out=e12.t[:, :], in_=s12.t[:, :], func=AF.Exp), reads=[s12.b], writes=[e12.b])
        nlam = self.sb("nlam", [128, 1])
        fw.op("dve", lambda e: e.tensor_tensor(out=nlam.t[:, :], in0=e12.t[:, 1:2], in1=e12.t[:, 0:1], op=ALU.subtract),
              reads=[e12.b], writes=[nlam.b])
        fw.op("dve", lambda e: e.tensor_scalar(out=nlam.t[:, :], in0=nlam.t[:, :], scalar1=-lam_init, scalar2=None,
                                               op0=ALU.add), reads=[nlam.b], writes=[nlam.b])
        sg = self.sb("sg", [128, 1])
        fw.op("dve", lambda e: e.tensor_scalar(out=sg.t[:, :], in0=self.vec("subg%d" % i), scalar1=1.0 - lam_init,
                                               scalar2=None, op0=ALU.mult), reads=[self.vecs.b], writes=[sg.b])
        Ks = [self.sb("K", [128, NTOK], BF16) for _ in range(2)]
        Vs = [self.sb("V", [128, NKT, 128], BF16) for _ in range(2)]
        Vg = [self.sb("Vg", [128, NKT, 128], BF16) for _ in range(2)]
        Qs = [self.sb("Q", [128, 512], BF16) for _ in range(2)]
        Pt = [self.sb("P", [128, 1024], BF16) for _ in range(2)]
        Sp = [self.ps("S", [128, 1024]) for _ in range(2)]
        Op = [self.ps("O", [128, 512]) for _ in range(2)]
        Lp = [self.ps("L", [128, 512]) for _ in range(2)]
        r_ = [self.sb("r", [128, 512]) for _ in range(2)]
        on = [self.sb("on", [128, 512]) for _ in range(2)]
        oa = self.sb("oa", [128, 512])
        sqo = self.sb("sqo", [128, 512], BF16)
        rs, tmp = self.sb("rso", [128, 512]), self.sb("tmpo", [128, 512])
        Lacc = [self.sb("Lacc", [128, 512]) for _ in range(2)]
        Lb = [self.sb("Lb", [128, 512], BF16) for _ in range(2)]
        Lf = self.sb("Lf", [128, 512])
        hb = self.sb("hb", [128, 512], BF16)
        h32 = self.sb("h32", [128, 512])
        lb = self.sb("lb", [128, 512], BF16)
        ost = [self.sb("ost", [128, 2, 512], BF16) for _ in range(2)]
        QTv = self.QT.rearrange("(k p) n -> p k n", p=128)
        KTv = self.KT.rearrange("(k p) n -> p k n", p=128)
        OTv = self.OT.rearrange("(k p) n -> p k n", p=128)
        v3 = lambda t, n: t.t[:, :].rearrange("p (s c) -> p s c", c=512)[:, :, 0:n]

        def load_kv(u):
            if u > 4:
                return
            K, V = Ks[u % 2], Vs[u % 2]
            kc = u if u < 4 else 4
            vcol = u * 128 if u < 4 else 512
            fw.dma(K.t[:, 0:C], KTv[:, kc, cfg.CS:cfg.CS + C], K.b, reads=[self.db(self.KT)], writes=[K.b])
            fw.dma(K.t[:, C:NTOK], KTv[:, kc, cfg.LS:cfg.LS + T], K.b, reads=[self.db(self.KT)], writes=[K.b], waw=False)
            vsrc = self.VT[:, vcol:vcol + 128].rearrange("(s p) c -> p s c", p=128)
            for (s0, ns) in blocks_of(NKT, 16):
                fw.dma(V.t[:, s0:s0 + ns, :], vsrc[:, s0:s0 + ns, :], V.b, reads=[self.db(self.VT)], writes=[V.b],
                       waw=(s0 == 0))
            if u == 4:
                for sub in range(2):
                    fw.op("pool", lambda e, sub=sub: e.memset(Vg[sub].t[:, :, 64:128], 1.0), writes=[Vg[sub].b])
                    fw.op("pool" if sub == 0 else "dve", lambda e, sub=sub: e.tensor_copy(
                        out=Vg[sub].t[:, :, 0:64], in_=V.t[:, :, sub * 64:(sub + 1) * 64]),
                          reads=[V.b], writes=[Vg[sub].b], waw=False)

        qblocks = []
        for seq in cfg.seqs:
            _, start, ln, v, row0 = seq
            kts = list(range(0, C // 128)) if v == 1 else list(range(NKT))
            for (t0, n) in blocks_of(ln, 512):
                qblocks.append((start + t0, n, kts))
        load_kv(0)
        qi = 0
        for u in range(8):
            if u + 1 < 8:
                load_kv(u + 1)
            K, V = Ks[min(u, 4) % 2], Vs[min(u, 4) % 2]
            diff = u < 4
            for (c0, n, kts) in qblocks:
                Q = Qs[qi % 2]
                os_ = ost[qi % 2]
                qi += 1
                fw.dma(Q.t[:, 0:n], QTv[:, u, c0:c0 + n], Q.b, reads=[self.db(self.QT)], writes=[Q.b])

                def emit_S(kt):
                    S = Sp[kt % 2]
                    for sub in range(2):
                        fw.op("pe", lambda e, sub=sub, S=S, kt=kt: e.matmul(
                            S.t[:, sub * 512:sub * 512 + n], lhsT=K.t[sub * 64:(sub + 1) * 64, kt * 128:(kt + 1) * 128],
                            rhs=Q.t[sub * 64:(sub + 1) * 64, 0:n], start=True, stop=True),
                              reads=[K.b, Q.b], writes=[S.b], ticket=(sub == 1), waw=(sub == 0))

                def emit_E(kt):
                    S, P = Sp[kt % 2], Pt[kt % 2]
                    fw.op("act", lambda e, S=S, P=P: e.activation(out=v3(P, n), in_=v3(S, n), func=AF.Exp,
                                                                  scale=HD ** -0.5), reads=[S.b], writes=[P.b])

                def emit_PV(kt, first, last):
                    P = Pt[kt % 2]
                    for sub in range(2):
                        rhs = P.t[:, sub * 512:sub * 512 + n]
                        if diff:
                            fw.op("pe", lambda e, sub=sub, rhs=rhs: e.matmul(
                                Op[sub].t[:, 0:n], lhsT=V.t[:, kt, :], rhs=rhs, start=first, stop=last),
                                  reads=[V.b, P.b], writes=[Op[sub].b] if (first or last) else [],
                                  ticket=last)
                            La = Lacc[sub]
                            if sub == 1:
                                fw.op("pe", lambda e, sub=sub, rhs=rhs: e.matmul(
                                    Lp[sub].t[:, 0:n], lhsT=self.ones_bf.t[:, :], rhs=rhs, start=first, stop=last),
                                      reads=[self.ones_bf.b, P.b], writes=[Lp[sub].b] if (first or last) else [],
                                      ticket=True)
                            elif first:
                                fw.op("dve", lambda e, La=La, rhs=rhs: e.tensor_copy(out=La.t[:, 0:n], in_=rhs),
                                      reads=[P.b], writes=[La.b])
                            else:
                                fw.op("dve", lambda e, La=La, rhs=rhs: e.tensor_tensor(
                                    out=La.t[:, 0:n], in0=La.t[:, 0:n], in1=rhs, op=ALU.add),
                                      reads=[P.b, La.b], writes=[La.b])
                        else:
                            fw.op("pe", lambda e, sub=sub, rhs=rhs: e.matmul(
                                Op[sub].t[:, 0:n], lhsT=Vg[sub].t[:, kt, :], rhs=rhs, start=first, stop=last),
                                  reads=[Vg[sub].b, P.b], writes=[Op[sub].b] if (first or last) else [],
                                  ticket=(last or sub == 1))

                emit_S(kts[0])
                for idx, kt in enumerate(kts):
                    if idx + 1 < len(kts):
                        emit_S(kts[idx + 1])
                    emit_E(kt)
                    emit_PV(kt, idx == 0, idx == len(kts) - 1)
                if diff:
                    for sub in range(2):
                        if sub == 0:
                            fw.op("dve", lambda e, sub=sub: e.tensor_copy(out=Lb[sub].t[:, 0:n], in_=Lacc[sub].t[:, 0:n]),
                                  reads=[Lacc[sub].b], writes=[Lb[sub].b])
                            fw.op("pe", lambda e, sub=sub: e.matmul(Lp[sub].t[:, 0:n], lhsT=self.ones_bf.t[:, :],
                                                                    rhs=Lb[sub].t[:, 0:n], start=True, stop=True),
                                  reads=[Lb[sub].b, self.ones_bf.b], writes=[Lp[sub].b])
                        fw.op("dve", lambda e, sub=sub: e.reciprocal(out=r_[sub].t[:, 0:n], in_=Lp[sub].t[:, 0:n]),
                              reads=[Lp[sub].b], writes=[r_[sub].b])
                    for sub in range(2):
                        fw.op("dve", lambda e, sub=sub: e.tensor_tensor(
                            out=on[sub].t[:, 0:n], in0=Op[sub].t[:, 0:n], in1=r_[sub].t[:, 0:n], op=ALU.mult),
                              reads=[Op[sub].b, r_[sub].b], writes=[on[sub].b])
                    fw.op("dve", lambda e: e.scalar_tensor_tensor(
                        out=oa.t[:, 0:n], in0=on[1].t[:, 0:n], scalar=nlam.t[:, 0:1], in1=on[0].t[:, 0:n],
                        op0=ALU.mult, op1=ALU.add), reads=[on[0].b, on[1].b, nlam.b], writes=[oa.b])
                    fw.op("act", lambda e: e.activation(out=sqo.t[:, 0:n], in_=oa.t[:, 0:n], func=AF.Square),
                          reads=[oa.b], writes=[sqo.b])
                    ssb = Lp[0]
                    fw.op("pe", lambda e: e.matmul(ssb.t[:, 0:n], lhsT=self.ones_bf.t[:, :], rhs=sqo.t[:, 0:n],
                                                   start=True, stop=True), reads=[sqo.b, self.ones_bf.b], writes=[ssb.b])
                    self.rstd(ssb, rs, tmp, 1.0 / 128, n)
                    fw.op("dve", lambda e: e.scalar_tensor_tensor(
                        out=os_.t[:, 0, 0:n], in0=oa.t[:, 0:n], scalar=sg.t[:, 0:1], in1=rs.t[:, 0:n],
                        op0=ALU.mult, op1=ALU.mult), reads=[oa.b, rs.b, sg.b], writes=[os_.b])
                    fw.dma(OTv[:, u, c0:c0 + n], os_.t[:, 0, 0:n], os_.b, reads=[os_.b], writes=[self.db(self.OT)],
                           waw=False, stream="pool")
                else:
                    j = u - 4
                    H = slice(64, 128)
                    idb = self.cmb.t[64:128, 64:128]
                    for sub in range(2):
                        fw.op("act", lambda e, sub=sub: e.activation(out=Lf.t[H, 0:n], in_=Op[sub].t[H, 0:n], func=AF.Copy),
                              reads=[Op[sub].b], writes=[Lf.b])
                        fw.op("dve", lambda e: e.reciprocal(out=Lf.t[H, 0:n], in_=Lf.t[H, 0:n]), reads=[Lf.b], writes=[Lf.b])
                        fw.op("dve", lambda e: e.tensor_copy(out=hb.t[H, 0:n], in_=Lf.t[H, 0:n]), reads=[Lf.b], writes=[hb.b])
                        fw.op("dve", lambda e: e.tensor_copy(out=h32.t[H, 0:n], in_=hb.t[H, 0:n]), reads=[hb.b], writes=[h32.b])
                        fw.op("dve", lambda e: e.tensor_tensor(out=lb.t[H, 0:n], in0=Lf.t[H, 0:n], in1=h32.t[H, 0:n],
                                                               op=ALU.subtract), reads=[Lf.b, h32.b], writes=[lb.b])
                        fw.op("pe", lambda e, sub=sub: e.matmul(Lp[sub].t[0:64, 0:n], lhsT=idb, rhs=hb.t[H, 0:n],
                                                                start=True, stop=False),
                              reads=[hb.b, self.cmb.b], writes=[Lp[sub].b], ticket=False)
                        fw.op("pe", lambda e, sub=sub: e.matmul(Lp[sub].t[0:64, 0:n], lhsT=idb, rhs=lb.t[H, 0:n],
                                                                start=False, stop=True),
                              reads=[lb.b, hb.b, self.cmb.b], writes=[Lp[sub].b])
                        fw.op("act", lambda e, sub=sub: e.activation(out=r_[sub].t[0:64, 0:n], in_=Lp[sub].t[0:64, 0:n],
                                                                     func=AF.Copy), reads=[Lp[sub].b], writes=[r_[sub].b])
                        fw.op("dve", lambda e, sub=sub: e.tensor_tensor(
                            out=os_.t[0:64, sub, 0:n], in0=Op[sub].t[0:64, 0:n], in1=r_[sub].t[0:64, 0:n], op=ALU.mult),
                              reads=[Op[sub].b, r_[sub].b], writes=[os_.b], waw=(sub == 0))
                    for sub in range(2):
                        hd = 4 * sub + j
                        f0 = 512 + hd * 64
                        fw.dma(self.OT[f0:f0 + 64, c0:c0 + n], os_.t[0:64, sub, 0:n], os_.b, reads=[os_.b],
                               writes=[self.db(self.OT)], waw=False, stream="pool")
        self.end()

    def proj_res_norm(self, l, Xin, Xout, SRC, kc, w_rows):
        fw, cfg = self.fw, self.cfg
        self.begin()
        W = self.sb("wo", [128, kc, D], BF16)
        self.load_w(W, w_rows, D)
        nt = self.norm_tiles()
        srcs = [self.sb("src", [128, kc, 512], BF16) for _ in range(2)]
        xs = [self.sb("x", [128, KD, 512], F32) for _ in range(2)]
        x1s = [self.sb("x1", [128, KD, 512], F32) for _ in range(2)]
        hs = [self.sb("h", [128, KD, 512], BF16) for _ in range(2)]
        pp = [self.ps("pp", [128, 512]) for _ in range(2)]
        Xiv = Xin.rearrange("(k p) n -> p k n", p=128)
        Xov = Xout.rearrange("(k p) n -> p k n", p=128)
        Sv = SRC.rearrange("(k p) n -> p k n", p=128)
        HTv = self.HT.rearrange("(k p) n -> p k n", p=128)
        bi = 0
        for seq in cfg.seqs:
            _, start, ln, v, row0 = seq
            if v == 1 and l == DEPTH - 1:
                continue
            for (t0, n) in blocks_of(ln, 512):
                s_, x, x1, h = srcs[bi % 2], xs[bi % 2], x1s[bi % 2], hs[bi % 2]
                bi += 1
                self.load_cols(s_, Sv, seq, t0, n, 0)
                self.load_cols(x, Xiv, seq, t0, n, 0)
                g1 = self.dvv(l, v, 2)
                for c in range(KD):
                    P = pp[c % 2]
                    for k in range(kc):
                        fw.op("pe", lambda e, k=k, c=c, P=P: e.matmul(
                            P.t[:, 0:n], lhsT=W.t[:, k, c * 128:(c + 1) * 128], rhs=s_.t[:, k, 0:n],
                            start=(k == 0), stop=(k == kc - 1)), reads=[W.b, s_.b], writes=[P.b], ticket=(k == kc - 1))
                    fw.op("dve", lambda e, c=c, P=P: e.scalar_tensor_tensor(
                        out=x1.t[:, c, 0:n], in0=P.t[:, 0:n], scalar=g1(c), in1=x.t[:, c, 0:n],
                        op0=ALU.mult, op1=ALU.add), reads=[P.b, x.b], writes=[x1.b], waw=False)
                c0 = start + t0
                fw.dma(Xov[:, :, c0:c0 + n], x1.t[:, :, 0:n], x1.b, reads=[x1.b], writes=[self.db(Xout)], waw=False,
                       stream="pool")
                self.norm_mod(x1, n, nt["ss"], nt["sq"], nt["rs"], nt["tmp"], self.dvv(l, v, 3), self.dvv(l, v, 4), h)
                fw.dma(HTv[:, :, c0:c0 + n], h.t[:, :, 0:n], h.b, reads=[h.b], writes=[self.db(self.HT)], waw=False,
                       stream="pool")
        self.end()

    def ffn_in(self, l):
        fw, cfg = self.fw, self.cfg
        self.begin()
        W = self.sb("wf", [128, KD, 2 * DFF], BF16)
        self.load_w(W, lambda k: self.i_fwi[l, k * 128:(k + 1) * 128, :], 2 * DFF)
        hs = [self.sb("h", [128, KD, 512], BF16) for _ in range(2)]
        uo = [self.sb("uo", [128, NFF, 512], BF16) for _ in range(2)]
        pg = [self.ps("pg", [128, 512]) for _ in range(2)]
        pv = [self.ps("pv", [128, 512]) for _ in range(2)]
        tt = [self.sb("t", [128, 512]) for _ in range(2)]
        ge = [self.sb("ge", [128, 512]) for _ in range(2)]
        HTv = self.HT.rearrange("(k p) n -> p k n", p=128)
        UTv = self.UT.rearrange("(k p) n -> p k n", p=128)
        bi = 0
        for seq in cfg.seqs:
            _, start, ln, v, row0 = seq
            if v == 1 and l == DEPTH - 1:
                continue
            for (t0, n) in blocks_of(ln, 510):
                h, u = hs[bi % 2], uo[bi % 2]
                bi += 1
                self.load_cols(h, HTv, seq, t0, n, 1)
                N = n + 2
                for c in range(NFF):
                    G, Vv, t, g = pg[c % 2], pv[c % 2], tt[c % 2], ge[c % 2]
                    for k in range(KD):
                        fw.op("pe", lambda e, k=k, c=c, G=G: e.matmul(
                            G.t[:, 0:N], lhsT=W.t[:, k, DFF + c * 128:DFF + (c + 1) * 128], rhs=h.t[:, k, 0:N],
                            start=(k == 0), stop=(k == KD - 1)), reads=[W.b, h.b], writes=[G.b], ticket=(k == KD - 1))
                    for k in range(KD):
                        fw.op("pe", lambda e, k=k, c=c, Vv=Vv: e.matmul(
                            Vv.t[:, 0:N], lhsT=W.t[:, k, c * 128:(c + 1) * 128], rhs=h.t[:, k, 0:N],
                            start=(k == 0), stop=(k == KD - 1)), reads=[W.b, h.b], writes=[Vv.b], ticket=(k == KD - 1))
                    w0, w1, w2 = (self.vec("fcw%d_%d" % (l, q), c) for q in range(3))
                    bb = self.vec("fcb%d" % l, c)
                    fw.op("dve", lambda e, G=G, t=t, w0=w0, bb=bb: e.tensor_scalar(
                        out=t.t[:, 0:n], in0=G.t[:, 0:n], scalar1=w0, scalar2=bb, op0=ALU.mult, op1=ALU.add),
                          reads=[G.b], writes=[t.b])
                    fw.op("dve", lambda e, G=G, t=t, w1=w1: e.scalar_tensor_tensor(
                        out=t.t[:, 0:n], in0=G.t[:, 1:n + 1], scalar=w1, in1=t.t[:, 0:n], op0=ALU.mult, op1=ALU.add),
                          reads=[G.b, t.b], writes=[t.b])
                    fw.op("dve", lambda e, G=G, t=t, w2=w2: e.scalar_tensor_tensor(
                        out=t.t[:, 0:n], in0=G.t[:, 2:n + 2], scalar=w2, in1=t.t[:, 0:n], op0=ALU.mult, op1=ALU.add),
                          reads=[G.b, t.b], writes=[t.b])
                    fw.op("act", lambda e, t=t, g=g: e.activation(out=g.t[:, 0:n], in_=t.t[:, 0:n], func=AF.Gelu),
                          reads=[t.b], writes=[g.b])
                    fw.op("dve", lambda e, g=g, Vv=Vv, c=c: e.tensor_tensor(
                        out=u.t[:, c, 0:n], in0=g.t[:, 0:n], in1=Vv.t[:, 1:n + 1], op=ALU.mult),
                          reads=[g.b, Vv.b], writes=[u.b], waw=False)
                c0 = start + t0
                fw.dma(UTv[:, :, c0:c0 + n], u.t[:, :, 0:n], u.b, reads=[u.b], writes=[self.db(self.UT)], waw=False,
                       stream="pool")
        self.end()

    def ffn_out(self, l, Xin, Xout):
        fw, cfg = self.fw, self.cfg
        self.begin()
        W = self.sb("wf2", [128, NFF, D], BF16)
        self.load_w(W, lambda k: self.i_fwo[l, k * 128:(k + 1) * 128, :], D)
        us = [self.sb("u", [128, NFF, 512], BF16) for _ in range(2)]
        xs = [self.sb("x", [128, KD, 512], F32) for _ in range(2)]
        x2s = [self.sb("x2", [128, KD, 512], F32) for _ in range(2)]
        pp = [self.ps("pp", [128, 512]) for _ in range(2)]
        Xiv = Xin.rearrange("(k p) n -> p k n", p=128)
        Xov = Xout.rearrange("(k p) n -> p k n", p=128)
        UTv = self.UT.rearrange("(k p) n -> p k n", p=128)
        bi = 0
        for seq in cfg.seqs:
            _, start, ln, v, row0 = seq
            if v == 1 and l == DEPTH - 1:
                continue
            for (t0, n) in blocks_of(ln, 512):
                u, x, x2 = us[bi % 2], xs[bi % 2], x2s[bi % 2]
                bi += 1
                self.load_cols(u, UTv, seq, t0, n, 0)
                self.load_cols(x, Xiv, seq, t0, n, 0)
                g2 = self.dvv(l, v, 5)
                for c in range(KD):
                    P = pp[c % 2]
                    for k in range(NFF):
                        fw.op("pe", lambda e, k=k, c=c, P=P: e.matmul(
                            P.t[:, 0:n], lhsT=W.t[:, k, c * 128:(c + 1) * 128], rhs=u.t[:, k, 0:n],
                            start=(k == 0), stop=(k == NFF - 1)), reads=[W.b, u.b], writes=[P.b], ticket=(k == NFF - 1))
                    fw.op("dve", lambda e, c=c, P=P: e.scalar_tensor_tensor(
                        out=x2.t[:, c, 0:n], in0=P.t[:, 0:n], scalar=g2(c), in1=x.t[:, c, 0:n],
                        op0=ALU.mult, op1=ALU.add), reads=[P.b, x.b], writes=[x2.b], waw=False)
                c0 = start + t0
                fw.dma(Xov[:, :, c0:c0 + n], x2.t[:, :, 0:n], x2.b, reads=[x2.b], writes=[self.db(Xout)], waw=False,
                       stream="pool")
        self.end()

    def final_norm(self, X):
        fw, cfg = self.fw, self.cfg
        self.begin()
        nt = self.norm_tiles()
        xs = [self.sb("x", [128, KD, 512], F32) for _ in range(2)]
        os_ = [self.sb("o", [128, KD, 512], F32) for _ in range(2)]
        Xv = X.rearrange("(k p) n -> p k n", p=128)
        Ov = self.o_out.rearrange("(k p) n -> p k n", p=128)
        seq = cfg.seqs[1]
        outb = self.fw.buf(name="outT")
        for bi, (t0, n) in enumerate(blocks_of(cfg.T, 512)):
            x, o = xs[bi % 2], os_[bi % 2]
            self.load_cols(x, Xv, seq, t0, n, 0)
            self.norm_mod(x, n, nt["ss"], nt["sq"], nt["rs"], nt["tmp"], lambda k: self.vec("gfin", k), None, o)
            fw.dma(Ov[:, :, t0:t0 + n], o.t[:, :, 0:n], o.b, reads=[o.b], writes=[outb], waw=False, stream="pool")
        self.end()


    def ssm_inproj(self, l, X):
        fw, cfg = self.fw, self.cfg
        i = l // 2
        self.begin()
        W = self.sb("wS", [128, KD, SSM_IN], BF16)
        self.load_w(W, lambda k: self.i_swi[i, k * 128:(k + 1) * 128, :], SSM_IN)
        nt = self.norm_tiles()
        x = self.sb("x", [128, KD, 512], F32)
        hTs = [self.sb("hT", [128, KD, 512], BF16) for _ in range(2)]
        xo = self.sb("xo", [128, 24, 512], BF16)
        zo = [self.sb("zo", [128, SSM_DI], F32) for _ in range(2)]
        dto = [self.sb("dto", [128, 64], F32) for _ in range(2)]
        dta = [self.sb("dta", [128, 64], F32) for _ in range(2)]
        dtb = self.sb("dtb", [128, 64], F32)
        ro = self.roff["dtb%d" % i][0]
        fw.dma(dtb.t[:, :], self.i_rowv[0:1, ro:ro + 64].broadcast_to([128, 64]), dtb.b, writes=[dtb.b])
        pP = [self.ps("pP", [128, 512]) for _ in range(2)]
        pz = [self.ps("pz", [128, 512]) for _ in range(2)]
        pdt = self.ps("pdt", [128, 512])
        tt = [self.sb("t", [128, 512]) for _ in range(2)]
        Xv = X.rearrange("(k p) n -> p k n", p=128)
        XBv = self.XBCT.rearrange("(k p) n -> p k n", p=128)
        bi = 0
        zi = 0
        for seq in cfg.seqs:
            _, start, ln, v, row0 = seq
            for (t0, n) in blocks_of(ln, 510):
                hT = hTs[bi % 2]
                bi += 1
                N = n + 2
                self.load_cols(x, Xv, seq, t0, n, 1)
                self.norm_mod(x, N, nt["ss"], nt["sq"], nt["rs"], nt["tmp"], self.dvv(l, v, 0), self.dvv(l, v, 1), hT)
                if t0 == 0:
                    fw.op("pool", lambda e: e.memset(hT.t[:, :, 0:1], 0.0), writes=[hT.b])
                if t0 + n == ln:
                    fw.op("pool", lambda e: e.memset(hT.t[:, :, n + 1:n + 2], 0.0), writes=[hT.b])
                for c in range(24):
                    P, t = pP[c % 2], tt[c % 2]
                    for k in range(KD):
                        fw.op("pe", lambda e, k=k, c=c, P=P: e.matmul(
                            P.t[:, 0:N], lhsT=W.t[:, k, SSM_DI + c * 128:SSM_DI + (c + 1) * 128], rhs=hT.t[:, k, 0:N],
                            start=(k == 0), stop=(k == KD - 1)), reads=[W.b, hT.b], writes=[P.b], ticket=(k == KD - 1))
                    w0, w1, w2 = (self.vec("scw%d_%d" % (i, q), c) for q in range(3))
                    bb = self.vec("scb%d" % i, c)
                    fw.op("dve", lambda e, P=P, t=t, w0=w0, bb=bb: e.tensor_scalar(
                        out=t.t[:, 0:n], in0=P.t[:, 0:n], scalar1=w0, scalar2=bb, op0=ALU.mult, op1=ALU.add),
                          reads=[P.b], writes=[t.b])
                    fw.op("dve", lambda e, P=P, t=t, w1=w1: e.scalar_tensor_tensor(
                        out=t.t[:, 0:n], in0=P.t[:, 1:n + 1], scalar=w1, in1=t.t[:, 0:n], op0=ALU.mult, op1=ALU.add),
                          reads=[P.b, t.b], writes=[t.b])
                    fw.op("dve", lambda e, P=P, t=t, w2=w2: e.scalar_tensor_tensor(
                        out=t.t[:, 0:n], in0=P.t[:, 2:n + 2], scalar=w2, in1=t.t[:, 0:n], op0=ALU.mult, op1=ALU.add),
                          reads=[P.b, t.b], writes=[t.b])
                    fw.op("act", lambda e, t=t, c=c: e.activation(out=xo.t[:, c, 0:n], in_=t.t[:, 0:n], func=AF.Silu),
                          reads=[t.b], writes=[xo.b], waw=False)
                c0 = start + t0
                fw.dma(XBv[:, :, c0:c0 + n], xo.t[:, :, 0:n], xo.b, reads=[xo.b], writes=[self.db(self.XBCT)],
                       waw=False, stream="pool")
                for (s0, m) in blocks_of(n, 128):
                    z, dt_, da = zo[zi % 2], dto[zi % 2], dta[zi % 2]
                    zi += 1
                    for q in range(4):
                        Pz = pz[q % 2]
                        for k in range(KD):
                            fw.op("pe", lambda e, k=k, q=q, Pz=Pz: e.matmul(
                                Pz.t[0:m, :], lhsT=hT.t[:, k, 1 + s0:1 + s0 + m], rhs=W.t[:, k, q * 512:(q + 1) * 512],
                                start=(k == 0), stop=(k == KD - 1)), reads=[W.b, hT.b], writes=[Pz.b], ticket=(k == KD - 1))
                        if q % 2 == 0:
                            fw.op("act", lambda e, q=q, Pz=Pz: e.activation(out=z.t[0:m, q * 512:(q + 1) * 512],
                                                                            in_=Pz.t[0:m, :], func=AF.Copy),
                                  reads=[Pz.b], writes=[z.b], waw=False)
                        else:
                            fw.op("dve", lambda e, q=q, Pz=Pz: e.tensor_copy(out=z.t[0:m, q * 512:(q + 1) * 512],
                                                                             in_=Pz.t[0:m, :]),
                                  reads=[Pz.b], writes=[z.b], waw=False)
                    r0 = row0 + t0 + s0
                    fw.dma(self.ZT[r0:r0 + m, :], z.t[0:m, :], z.b, reads=[z.b], writes=[self.db(self.ZT)], waw=False,
                           stream="pool")
                    for k in range(KD):
                        fw.op("pe", lambda e, k=k: e.matmul(
                            pdt.t[0:m, 0:64], lhsT=hT.t[:, k, 1 + s0:1 + s0 + m], rhs=W.t[:, k, SSM_DI + SSM_XBC:SSM_IN],
                            start=(k == 0), stop=(k == KD - 1)), reads=[W.b, hT.b], writes=[pdt.b], ticket=(k == KD - 1))
                    fw.op("dve", lambda e: e.tensor_tensor(out=da.t[0:m, :], in0=pdt.t[0:m, 0:64], in1=dtb.t[0:m, :], op=ALU.add),
                          reads=[pdt.b, dtb.b], writes=[da.b])
                    fw.op("act", lambda e: e.activation(out=da.t[0:m, :], in_=da.t[0:m, :], func=AF.Exp),
                          reads=[da.b], writes=[da.b])
                    fw.op("act", lambda e: e.activation(out=dt_.t[0:m, :], in_=da.t[0:m, :], func=AF.Ln, bias=1.0),
                          reads=[da.b], writes=[dt_.b])
                    fw.dma(self.DTT[r0:r0 + m, :], dt_.t[0:m, :], dt_.b, reads=[dt_.b], writes=[self.db(self.DTT)],
                           waw=False, stream="pool")
        self.end()

    def ssm_scan(self, l, d):
        fw, cfg = self.fw, self.cfg
        i = l // 2
        NKT = cfg.NTOK // 128
        nct = cfg.C // 128
        self.begin()
        bc3 = lambda ap: ap.unsqueeze(2).broadcast_to([128, 8, 64])
        ML = self.cmask("ML_f" if d == 0 else "ML_b")
        MR = self.cmask("MR_f" if d == 0 else "MR_b")
        MRf = self.cmask("MR_f" if d == 0 else "MR_b", bf=False)
        arow = self.sb("arow", [128, 32])
        ro = self.roff["alog%d" % i][0] + 32 * d
        fw.dma(arow.t[:, :], self.i_rowv[0:1, ro:ro + 32].broadcast_to([128, 32]), arow.b, writes=[arow.b])
        fw.op("act", lambda e: e.activation(out=arow.t[:, :], in_=arow.t[:, :], func=AF.Exp), reads=[arow.b], writes=[arow.b])
        fw.op("dve", lambda e: e.tensor_scalar(out=arow.t[:, :], in0=arow.t[:, :], scalar1=-1.0, scalar2=None, op0=ALU.mult),
              reads=[arow.b], writes=[arow.b])
        if d == 0:
            drow = self.sb("drow", [128, 32])
            ro = self.roff["ssd%d" % i][0]
            fw.dma(drow.t[:, :], self.i_rowv[0:1, ro:ro + 32].broadcast_to([128, 32]), drow.b, writes=[drow.b])
            xD_l = [self.sb("xD", [128, SSM_DI], BF16) for _ in range(2)]
        if d == 1:
            ng = self.sb("ng", [128, SSM_DI])
            ro = self.roff["sng%d" % i][0]
            fw.dma(ng.t[:, :], self.i_rowv[0:1, ro:ro + SSM_DI].broadcast_to([128, SSM_DI]), ng.b, writes=[ng.b])
        S = self.sb("S", [128, SSM_DI], F32)
        Sb = self.sb("Sb", [128, SSM_DI], BF16)
        fw.op("pool", lambda e: e.memset(S.t[:, :], 0.0), writes=[S.b])
        fw.op("pool", lambda e: e.memset(Sb.t[:, :], 0.0), writes=[Sb.b])
        xbcs = [self.sb("xbc", [128, 24, 128], BF16) for _ in range(2)]
        dts = [self.sb("dt", [128, 64], F32) for _ in range(2)]
        xtok_l = [self.sb("xtok", [128, 2560], BF16) for _ in range(2)]
        a_l = [self.sb("a", [128, 32]) for _ in range(2)]
        ahb_l = [self.sb("ahb", [128, 32], BF16) for _ in range(2)]
        ah_l = [self.sb("ah", [128, 32]) for _ in range(2)]
        al_l = [self.sb("al", [128, 32]) for _ in range(2)]
        a2_l = [self.sb("a2", [128, 64], BF16) for _ in range(2)]
        E_l = [self.sb("E", [128, 96]) for _ in range(2)]
        e2_l = [self.sb("e2", [128, 32]) for _ in range(2)]
        cbm_l = [self.sb("cbm", [128, 4, 128]) for _ in range(2)]
        Rhs = [self.sb("Rh", [128, 512], BF16) for _ in range(2)]
        Rls = [self.sb("Rl", [128, 512], BF16) for _ in range(2)]
        dec = [self.sb("dec", [128, 512]) for _ in range(2)]
        wT4 = [self.sb("wT4", [128, 512], BF16) for _ in range(2)]
        tmpy = self.sb("tmpy", [128, 512])
        xdt_l = [self.sb("xdt", [128, SSM_DI], BF16) for _ in range(2)]
        xwa_l = [self.sb("xwa", [128, SSM_DI], BF16) for _ in range(2)]
        yo = [self.sb("yo", [128, SSM_DI], F32) for _ in range(2)]
        pT = self.ps("pT", [128, 1024], BF16)
        pE = self.ps("pE", [128, 512])
        pcb = self.ps("pcb", [128, 512])
        pseg = [self.ps("pseg", [128, 512]) for _ in range(2)]
        pYl = [self.ps("pY", [128, 512]) for _ in range(2)]
        pG = self.ps("pG", [128, 512])
        if d == 1:
            yfs = [self.sb("yf", [128, SSM_DI], F32) for _ in range(2)]
            zs = [self.sb("z", [128, SSM_DI], F32) for _ in range(2)]
            sz_l = [self.sb("sz", [128, SSM_DI], F32) for _ in range(2)]
            ssq = self.sb("ssq", [128, 1])
            sqj = self.sb("sqj", [128, SSM_DI], BF16)
            rs1, rt1 = self.sb("rs1", [128, 1]), self.sb("rt1", [128, 1])
            ygn = self.sb("ygn", [128, SSM_DI], BF16)
            ygT = [self.sb("ygT", [128, 16, 128], BF16) for _ in range(2)]
        XBv = self.XBCT.rearrange("(k p) n -> p k n", p=128)
        YGv = self.YGT.rearrange("(k p) n -> p k n", p=128)
        order = list(range(NKT)) if d == 0 else (list(range(nct - 1, -1, -1)) + list(range(NKT - 1, nct - 1, -1)))
        hic = [0]

        def binds(ci):
            kt = order[ci]
            p_ = ci % 2
            r0 = kt * 128
            c0 = (cfg.CS + r0) if kt < nct else (cfg.LS + r0 - cfg.C)
            return (xbcs[p_], dts[p_], r0, c0, xtok_l[p_], a_l[p_], ahb_l[p_], ah_l[p_], al_l[p_], a2_l[p_], E_l[p_],
                    e2_l[p_], cbm_l[p_], xdt_l[p_], xwa_l[p_])

        def prologue(ci):
            xbc, dt_, r0, c0, xtok, a, ahb, ah, al, a2, E, e2, cbm, xdt, xwa = binds(ci)
            fw.dma(xbc.t[:, :, :], XBv[:, :, c0:c0 + 128], xbc.b, reads=[self.db(self.XBCT)], writes=[xbc.b])
            fw.dma(dt_.t[:, :], self.DTT[r0:r0 + 128, :], dt_.b, reads=[self.db(self.DTT)], writes=[dt_.b])
            dtd = dt_.t[:, 32 * d:32 * d + 32]
            if d == 1:
                yf, z = yfs[ci % 2], zs[ci % 2]
                fw.dma(yf.t[:, :], self.YF[r0:r0 + 128, :], yf.b, reads=[self.db(self.YF)], writes=[yf.b])
                fw.dma(z.t[:, :], self.ZT[r0:r0 + 128, :], z.b, reads=[self.db(self.ZT)], writes=[z.b])
            for q in range(5):
                for j in range(4):
                    c = q * 4 + j
                    fw.op("pe", lambda e, c=c, j=j: e.transpose(pT.t[:, j * 128:(j + 1) * 128], xbc.t[:, c, :], self.ident_bf()),
                          reads=[xbc.b, self.cmb.b], writes=[pT.b], ticket=(j == 3), waw=(j == 0))
                if q % 2 == 0:
                    fw.op("act", lambda e, q=q: e.activation(out=xtok.t[:, q * 512:(q + 1) * 512], in_=pT.t[:, 0:512], func=AF.Copy),
                          reads=[pT.b], writes=[xtok.b], waw=False)
                else:
                    fw.op("dve", lambda e, q=q: e.tensor_copy(out=xtok.t[:, q * 512:(q + 1) * 512], in_=pT.t[:, 0:512]),
                          reads=[pT.b], writes=[xtok.b], waw=False)
            fw.op("dve", lambda e: e.tensor_tensor(out=a.t[:, :], in0=dtd, in1=arow.t[:, :], op=ALU.mult),
                  reads=[dt_.b, arow.b], writes=[a.b])
            fw.op("dve", lambda e: e.tensor_copy(out=ahb.t[:, :], in_=a.t[:, :]), reads=[a.b], writes=[ahb.b])
            fw.op("dve", lambda e: e.tensor_copy(out=ah.t[:, :], in_=ahb.t[:, :]), reads=[ahb.b], writes=[ah.b])
            fw.op("dve", lambda e: e.tensor_tensor(out=al.t[:, :], in0=a.t[:, :], in1=ah.t[:, :], op=ALU.subtract),
                  reads=[a.b, ah.b], writes=[al.b])
            fw.op("dve", lambda e: e.tensor_copy(out=a2.t[:, 0:32], in_=ah.t[:, :]), reads=[ah.b], writes=[a2.b])
            fw.op("dve", lambda e: e.tensor_copy(out=a2.t[:, 32:64], in_=al.t[:, :]), reads=[al.b], writes=[a2.b], waw=False)
            for q, lhs in enumerate((MR, ML, self.ones_bf.t[:, :])):
                for hl in range(2):
                    fw.op("pe", lambda e, q=q, lhs=lhs, hl=hl: e.matmul(
                        pE.t[:, q * 32:(q + 1) * 32], lhsT=lhs, rhs=a2.t[:, hl * 32:(hl + 1) * 32],
                        start=(hl == 0), stop=(hl == 1)), reads=[a2.b, self.cmb.b, self.ones_bf.b], writes=[pE.b],
                          ticket=(q == 2 and hl == 1), waw=(q == 0 and hl == 0))
            fw.op("act", lambda e: e.activation(out=E.t[:, :], in_=pE.t[:, 0:96], func=AF.Exp), reads=[pE.b], writes=[E.b])
            fw.op("dve", lambda e: e.tensor_tensor(out=e2.t[:, :], in0=E.t[:, 32:64], in1=dtd, op=ALU.mult),
                  reads=[E.b, dt_.b], writes=[e2.b])
            for g in range(4):
                fw.op("pe", lambda e, g=g: e.matmul(pcb.t[:, g * 128:(g + 1) * 128], lhsT=xbc.t[:, 16 + g, :],
                                                    rhs=xbc.t[:, 20 + g, :], start=True, stop=True),
                      reads=[xbc.b], writes=[pcb.b], ticket=(g == 3), waw=(g == 0))
            for g in range(4):
                fw.op("dve", lambda e, g=g: e.tensor_tensor(out=cbm.t[:, g, :], in0=pcb.t[:, g * 128:(g + 1) * 128],
                                                            in1=MRf, op=ALU.mult),
                      reads=[pcb.b, self.cm.b], writes=[cbm.b], waw=(g == 0))
            x3 = xtok.t[:, 0:SSM_DI].rearrange("p (h c) -> p h c", c=64)
            fw.op("pool", lambda e: e.tensor_tensor(out=xdt.t[:, :].rearrange("p (h c) -> p h c", c=64), in0=x3,
                                                    in1=dtd.unsqueeze(2).broadcast_to([128, 32, 64]), op=ALU.mult),
                  reads=[xtok.b, dt_.b], writes=[xdt.b])
            fw.op("pool", lambda e: e.tensor_tensor(out=xwa.t[:, :].rearrange("p (h c) -> p h c", c=64), in0=x3,
                                                    in1=e2.t[:, :].unsqueeze(2).broadcast_to([128, 32, 64]), op=ALU.mult),
                  reads=[xtok.b, e2.b], writes=[xwa.b])
            if d == 0:
                xD = xD_l[ci % 2]
                fw.op("pool", lambda e: e.tensor_tensor(out=xD.t[:, :].rearrange("p (h c) -> p h c", c=64), in0=x3,
                                                        in1=drow.t[:, :].unsqueeze(2).broadcast_to([128, 32, 64]), op=ALU.mult),
                      reads=[xtok.b, drow.b], writes=[xD.b])
            else:
                sz = sz_l[ci % 2]
                fw.op("act", lambda e: e.activation(out=sz.t[:, :], in_=z.t[:, :], func=AF.Silu), reads=[z.b], writes=[sz.b])

        def main(ci):
            xbc, dt_, r0, c0, xtok, a, ahb, ah, al, a2, E, e2, cbm, xdt, xwa = binds(ci)
            dtd = dt_.t[:, 32 * d:32 * d + 32]
            if d == 1:
                yf, z, sz = yfs[ci % 2], zs[ci % 2], sz_l[ci % 2]
            else:
                xD = xD_l[ci % 2]
            items = [(g, hq) for g in range(4) for hq in range(2)]
            bufs = {}

            def stage1(idx):
                g, hq = items[idx]
                hi = hic[0]
                hic[0] += 1
                ps_, dc, Rh, Rl, w4 = pseg[hi % 2], dec[hi % 2], Rhs[hi % 2], Rls[hi % 2], wT4[hi % 2]
                bufs[idx] = (ps_, dc, w4)
                h0 = g * 8 + hq * 4
                fw.op("dve", lambda e: e.tensor_tensor(
                    out=Rh.t[:, :].rearrange("p (r c) -> p r c", c=128),
                    in0=MR.unsqueeze(1).broadcast_to([128, 4, 128]),
                    in1=ah.t[:, h0:h0 + 4].unsqueeze(2).broadcast_to([128, 4, 128]), op=ALU.mult),
                      reads=[ah.b, self.cmb.b], writes=[Rh.b])
                fw.op("pool", lambda e: e.tensor_tensor(
                    out=Rl.t[:, :].rearrange("p (r c) -> p r c", c=128),
                    in0=MR.unsqueeze(1).broadcast_to([128, 4, 128]),
                    in1=al.t[:, h0:h0 + 4].unsqueeze(2).broadcast_to([128, 4, 128]), op=ALU.mult),
                      reads=[al.b, self.cmb.b], writes=[Rl.b])
                fw.op("pe", lambda e: e.matmul(ps_.t[:, :], lhsT=ML, rhs=Rh.t[:, :], start=True, stop=False),
                      reads=[Rh.b, self.cmb.b], writes=[ps_.b], ticket=False)
                fw.op("pe", lambda e: e.matmul(ps_.t[:, :], lhsT=ML, rhs=Rl.t[:, :], start=False, stop=True),
                      reads=[Rl.b, Rh.b, self.cmb.b], writes=[ps_.b])
                fw.op("act", lambda e: e.activation(out=dc.t[:, :], in_=ps_.t[:, :], func=AF.Exp),
                      reads=[ps_.b], writes=[dc.b])

            def stage2(idx):
                g, hq = items[idx]
                ps_, dc, w4 = bufs[idx]
                pY = pYl[g % 2]
                fw.op("dve", lambda e: e.tensor_tensor(
                    out=w4.t[:, :].rearrange("p (r c) -> p r c", c=128), in0=dc.t[:, :].rearrange("p (r c) -> p r c", c=128),
                    in1=cbm.t[:, g, :].unsqueeze(1).broadcast_to([128, 4, 128]), op=ALU.mult),
                      reads=[dc.b, cbm.b], writes=[w4.b])
                for r4 in range(4):
                    r = hq * 4 + r4
                    h = g * 8 + r
                    if d == 0:
                        fw.op("pe", lambda e, r=r, r4=r4, h=h: e.matmul(
                            pY.t[:, r * 64:(r + 1) * 64], lhsT=w4.t[:, r4 * 128:(r4 + 1) * 128], rhs=xdt.t[:, h * 64:(h + 1) * 64],
                            start=True, stop=False), reads=[w4.b, xdt.b], writes=[pY.b], ticket=False, waw=(r == 0))
                        fw.op("pe", lambda e, r=r, r4=r4, h=h: e.matmul(
                            pY.t[:, r * 64:(r + 1) * 64], lhsT=self.ident_bf(), rhs=xD.t[:, h * 64:(h + 1) * 64],
                            start=False, stop=True), reads=[w4.b, xD.b, self.cmb.b], writes=[pY.b], ticket=(r4 == 3), waw=False)
                    else:
                        fw.op("pe", lambda e, r=r, r4=r4, h=h: e.matmul(
                            pY.t[:, r * 64:(r + 1) * 64], lhsT=w4.t[:, r4 * 128:(r4 + 1) * 128], rhs=xdt.t[:, h * 64:(h + 1) * 64],
                            start=True, stop=True), reads=[w4.b, xdt.b], writes=[pY.b], ticket=(r4 == 3), waw=(r == 0))

            def epilogue(g):
                pY = pYl[g % 2]
                pYs = pG
                pdS = pG
                fw.op("pe", lambda e: e.matmul(pYs.t[:, :], lhsT=xbc.t[:, 20 + g, :], rhs=Sb.t[:, g * 512:(g + 1) * 512],
                                               start=True, stop=True), reads=[xbc.b, Sb.b], writes=[pYs.b])
                fw.op("dve", lambda e: e.tensor_tensor(
                    out=tmpy.t[:, :].rearrange("p (r c) -> p r c", c=64), in0=pYs.t[:, :].rearrange("p (r c) -> p r c", c=64),
                    in1=bc3(E.t[:, g * 8:(g + 1) * 8]), op=ALU.mult), reads=[pYs.b, E.b], writes=[tmpy.b])
                yo_ = yo[ci % 2]
                gs_ = slice(g * 512, (g + 1) * 512)
                fw.op("dve", lambda e: e.tensor_tensor(out=yo_.t[:, gs_], in0=pY.t[:, :], in1=tmpy.t[:, :], op=ALU.add),
                      reads=[pY.b, tmpy.b], writes=[yo_.b], waw=(g == 0))
                if d == 1:
                    fw.op("dve", lambda e: e.tensor_tensor(out=yo_.t[:, gs_], in0=yo_.t[:, gs_], in1=yf.t[:, gs_], op=ALU.add),
                          reads=[yo_.b, yf.b], writes=[yo_.b])
                    fw.op("dve", lambda e: e.tensor_tensor(out=yo_.t[:, gs_], in0=yo_.t[:, gs_], in1=sz.t[:, gs_], op=ALU.mult),
                          reads=[yo_.b, sz.b], writes=[yo_.b])
                fw.op("pe", lambda e: e.matmul(
                    pdS.t[:, :], lhsT=xtok.t[:, 2048 + g * 128:2048 + (g + 1) * 128], rhs=xwa.t[:, gs_], start=True, stop=True),
                      reads=[xtok.b, xwa.b], writes=[pdS.b])
                fw.op("dve", lambda e: e.tensor_tensor(
                    out=S.t[:, gs_].rearrange("p (r c) -> p r c", c=64), in0=S.t[:, gs_].rearrange("p (r c) -> p r c", c=64),
                    in1=bc3(E.t[:, 64 + g * 8:64 + (g + 1) * 8]), op=ALU.mult), reads=[S.b, E.b], writes=[S.b])
                fw.op("dve", lambda e: e.tensor_tensor(out=S.t[:, gs_], in0=S.t[:, gs_], in1=pdS.t[:, :], op=ALU.add),
                      reads=[S.b, pdS.b], writes=[S.b])
                fw.op("act", lambda e: e.activation(out=Sb.t[:, gs_], in_=S.t[:, gs_], func=AF.Copy),
                      reads=[S.b], writes=[Sb.b])

            stage1(0)
            for idx in range(8):
                if idx + 1 < 8:
                    stage1(idx + 1)
                stage2(idx)
                if items[idx][1] == 1:
                    epilogue(items[idx][0])
            yo_ = yo[ci % 2]
            if d == 0:
                fw.dma(self.YF[r0:r0 + 128, :], yo_.t[:, :], yo_.b, reads=[yo_.b], writes=[self.db(self.YF)], waw=False,
                       stream="pool")
            else:
                fw.op("pool", lambda e: e.memset(ssq.t[:, :], 0.0), writes=[ssq.b])
                fw.op("act", lambda e: e.activation(out=sqj.t[:, :], in_=yo_.t[:, :], func=AF.Square, accum_out=ssq.t[:, :]),
                      reads=[yo_.b], writes=[sqj.b, ssq.b])
                fw.op("act", lambda e: e.activation(out=rt1.t[:, :], in_=ssq.t[:, :], func=AF.Sqrt, scale=1.0 / SSM_DI, bias=EPS),
                      reads=[ssq.b], writes=[rt1.b])
                fw.op("dve", lambda e: e.reciprocal(out=rs1.t[:, :], in_=rt1.t[:, :]), reads=[rt1.b], writes=[rs1.b])
                fw.op("dve", lambda e: e.scalar_tensor_tensor(out=ygn.t[:, :], in0=yo_.t[:, :], scalar=rs1.t[:, 0:1],
                                                              in1=ng.t[:, :], op0=ALU.mult, op1=ALU.mult),
                      reads=[yo_.b, rs1.b, ng.b], writes=[ygn.b])
                yt = ygT[ci % 2]
                for q in range(4):
                    for j in range(4):
                        c = q * 4 + j
                        fw.op("pe", lambda e, c=c, j=j: e.transpose(pT.t[:, j * 128:(j + 1) * 128],
                                                                    ygn.t[:, c * 128:(c + 1) * 128], self.ident_bf()),
                              reads=[ygn.b, self.cmb.b], writes=[pT.b], ticket=(j == 3), waw=(j == 0))
                    if q % 2 == 0:
                        fw.op("act", lambda e, q=q: e.activation(out=yt.t[:, q * 4:(q + 1) * 4, :],
                                                                 in_=pT.t[:, 0:512].rearrange("p (a b) -> p a b", b=128), func=AF.Copy),
                              reads=[pT.b], writes=[yt.b], waw=False)
                    else:
                        fw.op("dve", lambda e, q=q: e.tensor_copy(out=yt.t[:, q * 4:(q + 1) * 4, :],
                                                                  in_=pT.t[:, 0:512].rearrange("p (a b) -> p a b", b=128)),
                              reads=[pT.b], writes=[yt.b], waw=False)
                fw.dma(YGv[:, :, c0:c0 + 128], yt.t[:, :, :], yt.b, reads=[yt.b], writes=[self.db(self.YGT)], waw=False,
                       stream="pool")
        prologue(0)
        for ci in range(len(order)):
            if ci + 1 < len(order):
                prologue(ci + 1)
            main(ci)
        self.end()


def build_program(T, voff, roff, nv, nr, n_layers=DEPTH, debug=False):
    p = Prog(T, voff, roff, nv, nr, n_layers=n_layers, debug=debug)
    p.setup()
    p.mod_phase()
    Xa, Xb = p.XA, p.XB
    for l in range(n_layers):
        if l % 2 == 0:
            i = l // 2
            p.attn_inproj(l, Xa)
            p.attn_core(l)
            p.proj_res_norm(l, Xa, Xb, p.OT, KD, lambda k, i=i: p.i_awo[i, k * 128:(k + 1) * 128, :])
        else:
            i = l // 2
            p.ssm_inproj(l, Xa)
            p.ssm_scan(l, 0)
            p.ssm_scan(l, 1)
            p.proj_res_norm(l, Xa, Xb, p.YGT, SSM_DI // 128, lambda k, i=i: p.i_swo[i, k * 128:(k + 1) * 128, :])
        p.ffn_in(l)
        p.ffn_out(l, Xb, Xa)
    p.final_norm(Xa)
    p.pes.close()
    p.fw.close()
    return p


_CACHE = {}


def run(inputs, T=8192, n_layers=DEPTH, debug=False, cores=NCORES):
    cfg, voff, roff, shared = prep_shared(inputs, T)
    key = (T, n_layers, debug)
    if key not in _CACHE:
        _CACHE[key] = build_program(T, voff, roff, shared["vecs"].shape[1], shared["rowv"].shape[1],
                                    n_layers=n_layers, debug=debug)
    p = _CACHE[key]
    in_maps = []
    for b in range(cores):
        m = dict(shared)
        m.update(prep_core(inputs, b, T))
        in_maps.append(m)
    res = run_bass_kernel_spmd(p.nc, in_maps, core_ids=list(range(cores)))
    return p, res


def kernel(**inputs):
    p, res = run(inputs)
    out = np.stack([np.ascontiguousarray(r["outT"].T) for r in res.results], axis=0)
    return out.astype(np.float32)
```

```python
import math
from contextlib import ExitStack

import numpy as np
import concourse.bass as bass
import concourse.mybir as mybir
from concourse.bass_utils import run_bass_kernel_spmd

F32 = mybir.dt.float32
BF16 = mybir.dt.bfloat16
AF = mybir.ActivationFunctionType
ALU = mybir.AluOpType


class Sem:
    def __init__(self, h, name):
        self.h = h
        self.cnt = 0
        self.name = name


class Buf:
    def __init__(self, ap=None, name=""):
        self.ap = ap
        self.name = name
        self.last_w = {}
        self.readers = {}
        self.gen_deps = {}
        self.dsem = None


def _merge(dst, src):
    for s, v in src.items():
        if dst.get(s, 0) < v:
            dst[s] = v


class FW:
    def __init__(self, nc, n_dma_sems=48, n_spare=28):
        self.nc = nc
        self.es = ExitStack()
        self.streams = {
            "pe": nc.tensor,
            "act": nc.scalar,
            "dve": nc.vector,
            "pool": nc.gpsimd,
            "sp": nc.sync,
        }
        self.esem = {}
        for k in ("pe", "act", "dve", "pool"):
            self.esem[k] = Sem(self.es.enter_context(nc.semaphore("e_" + k)), k)
        self.known = {k: {} for k in self.streams}
        self.free_dsems = [
            Sem(self.es.enter_context(nc.semaphore("d%d" % i)), "d%d" % i) for i in range(n_dma_sems)
        ]
        self.spare = [Sem(self.es.enter_context(nc.semaphore("x%d" % i)), "x%d" % i) for i in range(n_spare)]
        self.phase_bufs = []
        self.pers_bufs = []
        self.n_inst = 0
        self.n_wait = 0

    def buf(self, ap=None, name="", persistent=False):
        b = Buf(ap, name)
        (self.pers_bufs if persistent else self.phase_bufs).append(b)
        return b

    def _wait_for(self, stream, deps, fold=False):
        eng = self.streams[stream]
        kn = self.known[stream]
        need = [(s, v) for s, v in deps.items() if kn.get(s, 0) < v]
        last = None
        if fold and need:
            last = need.pop()
            kn[last[0]] = last[1]
        for s, v in need:
            eng.wait_ge(s.h, v)
            kn[s] = v
            self.n_wait += 1
        return last

    def _deps(self, stream, reads, writes, waw=True):
        deps = {}
        for r in reads:
            _merge(deps, r.last_w)
        for w in writes:
            if waw or w.readers:
                _merge(deps, w.last_w)
                _merge(deps, w.readers)
            else:
                _merge(deps, w.gen_deps)
        if stream == "pe":
            deps.pop(self.esem["pe"], None)
        return deps

    def _commit(self, t, reads, writes, waw):
        for w in writes:
            if waw or w.readers:
                g = dict(w.last_w)
                _merge(g, w.readers)
                w.gen_deps = g
                w.last_w = dict(t)
                w.readers = {}
            else:
                _merge(w.last_w, t)
        for r in reads:
            _merge(r.readers, t)

    def op(self, stream, fn, reads=(), writes=(), ticket=True, waw=True):
        deps = self._deps(stream, reads, writes, waw)
        last = self._wait_for(stream, deps, fold=True)
        inst = fn(self.streams[stream])
        if last is not None:
            inst._wait_ge(last[0].h, last[1])
        self.n_inst += 1
        if ticket:
            s = self.esem[stream]
            if s.cnt >= 30000:
                s = self.spare.pop()
                self.esem[stream] = s
            s.cnt += 1
            inst.then_inc(s.h, 1)
            self._commit({s: s.cnt}, reads, writes, waw)
        return inst

    def dma(self, out_ap, in_ap, owner, reads=(), writes=(), stream="sp", waw=True, **kw):
        deps = self._deps(stream, reads, writes, waw)
        last = self._wait_for(stream, deps, fold=True)
        if owner.dsem is None:
            owner.dsem = self.free_dsems.pop(0)
        s = owner.dsem
        s.cnt += 16
        inst = self.streams[stream].dma_start(out=out_ap, in_=in_ap, **kw)
        if last is not None:
            inst._wait_ge(last[0].h, last[1])
        inst.then_inc(s.h, 16)
        self.n_inst += 1
        self._commit({s: s.cnt}, reads, writes, waw)

    def barrier(self, clear=True):
        alld = {}
        for s in self.esem.values():
            if s.cnt:
                alld[s] = s.cnt
        for b in self.phase_bufs + self.pers_bufs:
            if b.dsem is not None:
                alld[b.dsem] = b.dsem.cnt
        for st in self.streams:
            d = dict(alld)
            self._wait_for(st, d)
        for b in self.phase_bufs + self.pers_bufs:
            if b.dsem is not None:
                self.free_dsems.append(b.dsem)
                b.dsem = None
            b.last_w = {}
            b.readers = {}
            b.gen_deps = {}
        if clear:
            self.phase_bufs = []

    def close(self):
        self.es.close()


D = 1024
KD = D // 128
DEPTH = 4
CTX = 256
GRID_W = 64
HD = 64
EPS = 1e-6
DFF = 2816
NFF = DFF // 128
SSM_DI = 2048
SSM_XBC = 3072
SSM_IN = 5184
SSM_H = 32
NCORES = 8
ATT_FM = 1664
ATT_EXT = 2 * ATT_FM + 640


class Cfg:
    def __init__(self, T):
        self.T = T
        self.C = CTX
        self.CS = 1
        self.LS = CTX + 3
        self.NP = CTX + T + 4
        self.NTOK = CTX + T
        self.seqs = [("ctx", self.CS, CTX, 1, 0), ("lat", self.LS, T, 0, CTX)]


def _partner():
    p = np.arange(64)
    return np.where((p % 32) < 16, p + 16, p - 16)


class VecLayout:
    def __init__(self):
        self.off = {}
        self.n = 0
        self.cols = []

    def add(self, name, arr):
        arr = np.ascontiguousarray(arr, dtype=np.float32).reshape(128, -1)
        self.off[name] = (self.n, arr.shape[1])
        self.n += arr.shape[1]
        self.cols.append(arr)

    def build(self):
        return np.ascontiguousarray(np.concatenate(self.cols, axis=1))


def fm(v):
    v = np.asarray(v, dtype=np.float32)
    return np.ascontiguousarray(v.reshape(-1, 128).T)


def prep_shared(inp, T):
    cfg = Cfg(T)
    vl = VecLayout()
    rows = {}
    for l in range(DEPTH):
        vl.add("modb%d" % l, fm(inp["mod_b"][l]))
        vl.add("gmix%d" % l, fm(inp["norm_mix_g"][l]))
        vl.add("gffn%d" % l, fm(inp["norm_ffn_g"][l]))
        cw = inp["ffn_conv_w"][l]
        for i in range(3):
            vl.add("fcw%d_%d" % (l, i), fm(cw[i]))
        vl.add("fcb%d" % l, fm(inp["ffn_conv_b"][l]))
    vl.add("gfin", fm(inp["final_norm_g"]))
    pt = _partner()
    for i in range(DEPTH // 2 + DEPTH % 2):
        gq = np.asarray(inp["gqa_q_norm_g"][i], np.float32)
        gk = np.asarray(inp["gqa_k_norm_g"][i], np.float32)
        vl.add("gq%d" % i, np.concatenate([gq, gq])[:, None])
        vl.add("gqs%d" % i, np.concatenate([gq[pt], gq[pt]])[:, None])
        vl.add("gk%d" % i, np.concatenate([gk, gk])[:, None])
        vl.add("gks%d" % i, np.concatenate([gk[pt], gk[pt]])[:, None])
        vl.add("subg%d" % i, np.asarray(inp["diff_subln_g"][i], np.float32)[:, None])
    for i in range(DEPTH // 2):
        cw = inp["ssm_conv_w"][i]
        for j in range(3):
            vl.add("scw%d_%d" % (i, j), fm(cw[j]))
        vl.add("scb%d" % i, fm(inp["ssm_conv_b"][i]))
    vecs = vl.build()

    rl = VecLayout()

    def radd(name, v):
        v = np.asarray(v, np.float32).reshape(1, -1)
        rl.off[name] = (rl.n, v.shape[1])
        rl.n += v.shape[1]
        rl.cols.append(v)

    for i in range((DEPTH + 1) // 2):
        radd("lq1_%d" % i, inp["diff_lq1"][i])
        radd("lk1_%d" % i, inp["diff_lk1"][i])
        radd("lq2_%d" % i, inp["diff_lq2"][i])
        radd("lk2_%d" % i, inp["diff_lk2"][i])
    for i in range(DEPTH // 2):
        radd("dtb%d" % i, inp["ssm_dt_bias"][i])
        radd("alog%d" % i, inp["ssm_a_log"][i])
        radd("ssd%d" % i, inp["ssm_d"][i])
        radd("sng%d" % i, inp["ssm_norm_g"][i])
    rowv = np.ascontiguousarray(np.concatenate(rl.cols, axis=1))

    w_ext = []
    for i in range((DEPTH + 1) // 2):
        w = np.asarray(inp["attn_w_in"][i])
        qa, ka, va = w[:, 0:512], w[:, 512:1024], w[:, 1024:1536]
        qb, kb, vb = w[:, 1536:2048], w[:, 2048:2176], w[:, 2176:2304]
        qbp = np.concatenate(
            [np.concatenate([qb[:, j * 64:(j + 1) * 64], qb[:, (4 + j) * 64:(5 + j) * 64]], axis=1) for j in range(4)],
            axis=1)
        fmw = np.concatenate([qa, ka, qbp, kb], axis=1)
        idx = (np.arange(ATT_FM) // 64) * 64 + pt[np.arange(ATT_FM) % 64]
        w_ext.append(np.concatenate([fmw, fmw[:, idx], va, vb], axis=1))
    w_ext = np.ascontiguousarray(np.stack(w_ext))

    NP = cfg.NP
    cosT = np.ones((128, NP), np.float32)
    sinT = np.zeros((128, NP), np.float32)
    t = np.arange(T)
    row = (t // GRID_W).astype(np.float32)
    col = (t % GRID_W).astype(np.float32)
    inv = (1.0 / (10000.0 ** (np.arange(16, dtype=np.float32) / 16))).astype(np.float32)
    for p in range(64):
        q, i = p // 16, p % 16
        ang = ((row if q < 2 else col) * inv[i]).astype(np.float32)
        sgn = -1.0 if q in (0, 2) else 1.0
        for rep in (0, 64):
            cosT[p + rep, cfg.LS:cfg.LS + T] = np.cos(ang)
            sinT[p + rep, cfg.LS:cfg.LS + T] = sgn * np.sin(ang)

    tt = np.arange(128)
    consts = {
        "ident": np.eye(128, dtype=np.float32),
        "ML_f": (tt[:, None] > tt[None, :]).astype(np.float32),
        "MR_f": (tt[:, None] <= tt[None, :]).astype(np.float32),
        "ML_b": (tt[:, None] < tt[None, :]).astype(np.float32),
        "MR_b": (tt[:, None] >= tt[None, :]).astype(np.float32),
    }
    bd = np.zeros((128, 128), np.float32)
    bd[:64, :64] = 1
    bd[64:, 64:] = 1
    consts["bd"] = bd
    cmat = np.ascontiguousarray(
        np.concatenate([consts[k] for k in ("ident", "ML_f", "MR_f", "ML_b", "MR_b", "bd")], axis=1))

    shared = {
        "vecs": vecs, "rowv": rowv, "w_ext": w_ext, "cosT": cosT, "sinT": sinT, "cmat": cmat,
        "mod_w": np.asarray(inp["mod_w"], np.float32),
        "attn_w_out": np.asarray(inp["attn_w_out"], np.float32),
        "ssm_w_in": np.asarray(inp["ssm_w_in"], np.float32),
        "ssm_w_out": np.asarray(inp["ssm_w_out"], np.float32),
        "ffn_w_in": np.asarray(inp["ffn_w_in"], np.float32),
        "ffn_w_out": np.asarray(inp["ffn_w_out"], np.float32),
    }
    return cfg, vl.off, rl.off, shared


def prep_core(inp, b, T):
    xT = np.ascontiguousarray(np.asarray(inp["x"][b, :T]).T)
    cxT = np.ascontiguousarray(np.asarray(inp["ctx"][b]).T)
    cT = np.stack([fm(inp["c"][b]), fm(inp["c_ctx"])], axis=2)
    return {"xT": xT, "cxT": cxT, "cT": np.ascontiguousarray(cT.reshape(128, 16))}


class TB:
    def __init__(self, t, b, bs=None):
        self.t = t
        self.b = b
        self.bs = bs


def blocks_of(n, size):
    out = []
    t0 = 0
    while t0 < n:
        out.append((t0, min(size, n - t0)))
        t0 += size
    return out


class Prog:
    def __init__(self, T, voff, roff, nv, nr, n_layers=DEPTH, debug=False):
        self.cfg = Cfg(T)
        self.voff, self.roff = voff, roff
        self.debug = debug
        self.n_layers = n_layers
        nc = bass.Bass("TRN2", target_bir_lowering=False)
        self.nc = nc
        self.fw = FW(nc)
        cfg = self.cfg
        NP, NTOK = cfg.NP, cfg.NTOK
        di = lambda name, shape, dt=F32: nc.dram_tensor(name, shape, dt, kind="ExternalInput").ap()
        self.i_xT = di("xT", [D, T])
        self.i_cxT = di("cxT", [D, CTX])
        self.i_cT = di("cT", [128, 16])
        self.i_vecs = di("vecs", [128, nv])
        self.i_rowv = di("rowv", [1, nr])
        self.i_wext = di("w_ext", [(DEPTH + 1) // 2, D, ATT_EXT])
        self.i_cos = di("cosT", [128, NP])
        self.i_sin = di("sinT", [128, NP])
        self.i_cmat = di("cmat", [128, 768])
        self.i_modw = di("mod_w", [DEPTH, D, 6 * D])
        self.i_awo = di("attn_w_out", [(DEPTH + 1) // 2, D, D])
        self.i_swi = di("ssm_w_in", [DEPTH // 2, D, SSM_IN])
        self.i_swo = di("ssm_w_out", [DEPTH // 2, SSM_DI, D])
        self.i_fwi = di("ffn_w_in", [DEPTH, D, 2 * DFF])
        self.i_fwo = di("ffn_w_out", [DEPTH, DFF, D])
        self.o_out = nc.dram_tensor("outT", [D, T], F32, kind="ExternalOutput").ap()
        self.dbg = {}
        kind = "ExternalOutput" if debug else "Internal"

        def scr(name, shape, dt):
            ap = nc.dram_tensor(name, shape, dt, kind=kind).ap()
            if debug:
                self.dbg[name] = ap
            return ap

        self.XA = scr("XA", [D, NP], F32)
        self.XB = scr("XB", [D, NP], F32)
        self.HT = scr("HT", [D, NP], BF16)
        self.QT = scr("QT", [D, NP], BF16)
        self.KT = scr("KT", [5 * 128, NP], BF16)
        self.VT = scr("VT", [NTOK, 640], BF16)
        self.OT = scr("OT", [D, NP], BF16)
        self.UT = scr("UT", [DFF, NP], BF16)
        self.XBCT = scr("XBCT", [SSM_XBC, NP], BF16)
        self.ZT = scr("ZT", [NTOK, SSM_DI], F32)
        self.DTT = scr("DTT", [NTOK, 64], F32)
        self.YF = scr("YF", [NTOK, SSM_DI], F32)
        self.YGT = scr("YGT", [SSM_DI, NP], BF16)
        self.dram_b = {}
        self.pes = ExitStack()
        self.ph = None
        self._names = 0

    def _nm(self, name):
        self._names += 1
        return "%s_%d" % (name, self._names)

    def sb(self, name, shape, dt=F32, nb=0, pers=False):
        st = self.pes if pers else self.ph
        t = st.enter_context(self.nc.sbuf_tensor(self._nm(name), shape, dt))
        b = self.fw.buf(name=name, persistent=pers)
        bs = [self.fw.buf(name=name + str(i), persistent=pers) for i in range(nb)] if nb else None
        return TB(t, b, bs)

    def ps(self, name, shape, dt=F32):
        t = self.ph.enter_context(self.nc.psum_tensor(self._nm(name), shape, dt))
        return TB(t, self.fw.buf(name=name))

    def db(self, ap):
        k = ap.name if hasattr(ap, "name") else id(ap)
        if k not in self.dram_b:
            self.dram_b[k] = self.fw.buf(name="dram", persistent=True)
        return self.dram_b[k]

    def begin(self):
        self.ph = ExitStack()

    def end(self):
        self.fw.barrier()
        self.ph.close()
        self.ph = None

    def vec(self, name, j=0, w=1):
        o, n = self.voff[name]
        return self.vecs.t[:, o + j:o + j + w]

    def load_w(self, dst, src_rows, ncols, col0=0, dst_col0=0, kc=None):
        fw = self.fw
        kc = kc if kc is not None else dst.t.shape[1]
        PIECE = 1408
        main_ph = self.ph
        self.ph = ExitStack()
        wst = [self.sb("wst", [128, PIECE], F32) for _ in range(3)]
        wi = 0
        for k in range(kc):
            for (c0, n) in blocks_of(ncols, PIECE):
                st = wst[wi % 3]
                eng = ("dve", "pool", "act")[wi % 3]
                wi += 1
                src = src_rows(k)[:, col0 + c0:col0 + c0 + n]
                fw.dma(st.t[:, 0:n], src, st.b, writes=[st.b])
                d = dst.t[:, k, dst_col0 + c0:dst_col0 + c0 + n]
                if eng == "act":
                    fw.op("act", lambda e, d=d, st=st, n=n: e.activation(out=d, in_=st.t[:, 0:n], func=AF.Copy),
                          reads=[st.b], writes=[dst.b], waw=False)
                else:
                    fw.op(eng, lambda e, d=d, st=st, n=n: e.tensor_copy(out=d, in_=st.t[:, 0:n]),
                          reads=[st.b], writes=[dst.b], waw=False)
        fw.barrier(clear=False)
        self.ph.close()
        self.ph = main_ph

    def rstd(self, ss, out, tmp, scale, n):
        fw = self.fw
        fw.op("act", lambda e: e.activation(out=tmp.t[:, 0:n], in_=ss.t[:, 0:n], func=AF.Sqrt, scale=scale, bias=EPS),
              reads=[ss.b], writes=[tmp.b])
        fw.op("dve", lambda e: e.reciprocal(out=out.t[:, 0:n], in_=tmp.t[:, 0:n]), reads=[tmp.b], writes=[out.b])

    def norm_mod(self, x, n, ss, sq, rs, tmp, gs, sh, out, out_dt_bf16=True):
        fw = self.fw
        for k in range(KD):
            fw.op("act", lambda e, k=k: e.activation(out=sq.t[:, k, 0:n], in_=x.t[:, k, 0:n], func=AF.Square),
                  reads=[x.b], writes=[sq.b], waw=False)
        for k in range(KD):
            fw.op("pe", lambda e, k=k: e.matmul(ss.t[:, 0:n], lhsT=self.ones_bf.t[:, :], rhs=sq.t[:, k, 0:n],
                                                start=(k == 0), stop=(k == KD - 1)),
                  reads=[sq.b, self.ones_bf.b], writes=[ss.b], ticket=(k == KD - 1))
        self.rstd(ss, rs, tmp, 1.0 / D, n)
        for k in range(KD):
            if sh is None:
                fw.op("dve", lambda e, k=k: e.scalar_tensor_tensor(
                    out=out.t[:, k, 0:n], in0=x.t[:, k, 0:n], scalar=gs(k), in1=rs.t[:, 0:n],
                    op0=ALU.mult, op1=ALU.mult), reads=[x.b, rs.b], writes=[out.b], waw=False)
            else:
                tm = self._nm_tmp[k % 2]
                fw.op("dve", lambda e, k=k, tm=tm: e.scalar_tensor_tensor(
                    out=tm.t[:, 0:n], in0=x.t[:, k, 0:n], scalar=gs(k), in1=rs.t[:, 0:n],
                    op0=ALU.mult, op1=ALU.mult), reads=[x.b, rs.b], writes=[tm.b])
                fw.op("act", lambda e, k=k, tm=tm: e.activation(out=out.t[:, k, 0:n], in_=tm.t[:, 0:n],
                                                         func=AF.Identity, bias=sh(k), scale=1.0),
                      reads=[tm.b], writes=[out.b], waw=False)

    def norm_tiles(self, nmax=512):
        self._nm_tmp = [self.sb("nmt", [128, nmax], F32) for _ in range(2)]
        return dict(ss=self.ps("ss", [128, 512], F32), sq=self.sb("sq", [128, KD, nmax], BF16),
                    rs=self.sb("rs", [128, nmax], F32), tmp=self.sb("rtmp", [128, nmax], F32))

    def load_cols(self, dst, dview, seq, t0, n, halo, extra_reads=()):
        fw = self.fw
        _, start, ln, _, _ = seq
        lo, hi = t0 - halo, t0 + n + halo
        clo, chi = max(lo, 0), min(hi, ln)
        if clo > lo:
            fw.op("pool", lambda e: e.memset(dst.t[:, :, 0:clo - lo], 0.0), writes=[dst.b])
        if chi < hi:
            fw.op("pool", lambda e: e.memset(dst.t[:, :, chi - lo:hi - lo], 0.0), writes=[dst.b],
                  waw=(clo == lo))
        fw.dma(dst.t[:, :, clo - lo:chi - lo], dview[:, :, start + clo:start + chi], dst.b,
               reads=[self.db(dview)], writes=[dst.b], waw=(clo == lo and chi == hi))

    def dvv(self, l, v, which):
        base = ((l * 2 + v) * 6 + which) * 8
        return lambda k: self.dv.t[:, base + k:base + k + 1]

    def setup(self):
        fw, cfg = self.fw, self.cfg
        nv = self.i_vecs.shape[1]
        self.vecs = self.sb("vecs", [128, nv], F32, pers=True)
        self.cm = self.sb("cmat", [128, 768], F32, pers=True)
        self.cmb = self.sb("cmatb", [128, 768], BF16, pers=True)
        self.ones_bf = self.sb("ones", [128, 128], BF16, pers=True)
        self.dv = self.sb("dv", [128, DEPTH * 2 * 6 * 8], F32, pers=True)
        self.begin()
        fw.dma(self.vecs.t[:, :], self.i_vecs[:, :], self.vecs.b, writes=[self.vecs.b])
        fw.dma(self.cm.t[:, :], self.i_cmat[:, :], self.cm.b, writes=[self.cm.b])
        fw.op("dve", lambda e: e.tensor_copy(out=self.cmb.t[:, :], in_=self.cm.t[:, :]), reads=[self.cm.b],
              writes=[self.cmb.b])
        fw.op("pool", lambda e: e.memset(self.ones_bf.t[:, :], 1.0), writes=[self.ones_bf.b])
        dummy = self.fw.buf(name="x0")
        XAv = self.XA.rearrange("(k p) n -> p k n", p=128)
        xin = self.i_xT.rearrange("(k p) n -> p k n", p=128)
        cin = self.i_cxT.rearrange("(k p) n -> p k n", p=128)
        for k in range(KD):
            fw.dma(XAv[:, k, cfg.LS:cfg.LS + cfg.T], xin[:, k, :], dummy, writes=[self.db(self.XA)], waw=False)
            fw.dma(XAv[:, k, cfg.CS:cfg.CS + cfg.C], cin[:, k, :], dummy, writes=[self.db(self.XA)], waw=False)
        self.end()

    def ident_bf(self):
        return self.cmb.t[:, 0:128]

    def cmask(self, name, bf=True):
        j = ("ident", "ML_f", "MR_f", "ML_b", "MR_b", "bd").index(name)
        return (self.cmb if bf else self.cm).t[:, j * 128:(j + 1) * 128]

    def mod_phase(self):
        fw = self.fw
        self.begin()
        ct = self.sb("ct", [128, 16], F32)
        sc = self.sb("sc", [128, 16], BF16)
        fw.dma(ct.t[:, :], self.i_cT[:, :], ct.b, writes=[ct.b])
        fw.op("act", lambda e: e.activation(out=sc.t[:, :], in_=ct.t[:, :], func=AF.Silu), reads=[ct.b], writes=[sc.b])
        modT = self.sb("modT", [128, DEPTH, 48, 2], F32)
        wst = [self.sb("mwst", [128, KD, 512], F32) for _ in range(4)]
        wb = [self.sb("mwb", [128, KD, 512], BF16) for _ in range(4)]
        pm = [self.ps("pm", [128, 512], F32) for _ in range(4)]
        it = 0
        for l in range(self.n_layers):
            mw = self.i_modw[l].rearrange("(k p) n -> p k n", p=128)
            for cg in range(12):
                s_, b_, p_ = wst[it % 4], wb[it % 4], pm[it % 4]
                fw.dma(s_.t[:, :, :], mw[:, :, cg * 512:(cg + 1) * 512], s_.b, writes=[s_.b])
                if it % 2 == 0:
                    fw.op("dve", lambda e, s_=s_, b_=b_: e.tensor_copy(out=b_.t[:, :, :], in_=s_.t[:, :, :]),
                          reads=[s_.b], writes=[b_.b])
                else:
                    fw.op("act", lambda e, s_=s_, b_=b_: e.activation(out=b_.t[:, :, :], in_=s_.t[:, :, :], func=AF.Copy),
                          reads=[s_.b], writes=[b_.b])
                for j in range(4):
                    for k in range(KD):
                        fw.op("pe", lambda e, j=j, k=k, b_=b_, p_=p_: e.matmul(
                            p_.t[:, 2 * j:2 * j + 2], lhsT=b_.t[:, k, j * 128:(j + 1) * 128],
                            rhs=sc.t[:, 2 * k:2 * k + 2], start=(k == 0), stop=(k == KD - 1)),
                              reads=[b_.b, sc.b], writes=[p_.b], ticket=(k == KD - 1 and j == 3))
                for j in range(4):
                    fw.op("dve", lambda e, j=j, p_=p_, l=l, cg=cg: e.tensor_scalar(
                        out=modT.t[:, l, cg * 4 + j, :], in0=p_.t[:, 2 * j:2 * j + 2],
                        scalar1=self.vec("modb%d" % l, cg * 4 + j), scalar2=None, op0=ALU.add),
                          reads=[p_.b], writes=[modT.b], waw=False)
                it += 1
        for l in range(self.n_layers):
            for v in range(2):
                def dvs(which):
                    base = ((l * 2 + v) * 6 + which) * 8
                    return self.dv.t[:, base:base + 8]
                for which, (sci, gname) in ((0, (8, "gmix%d" % l)), (3, (32, "gffn%d" % l))):
                    fw.op("dve", lambda e, which=which, sci=sci, gname=gname, l=l, v=v, dvs=dvs: e.scalar_tensor_tensor(
                        out=dvs(which), in0=modT.t[:, l, sci:sci + 8, v], scalar=1.0, in1=self.vec(gname, 0, 8),
                        op0=ALU.add, op1=ALU.mult), reads=[modT.b], writes=[self.dv.b], waw=False)
                for which, c0 in ((1, 0), (2, 16), (4, 24), (5, 40)):
                    fw.op("dve", lambda e, which=which, c0=c0, l=l, v=v, dvs=dvs: e.tensor_copy(
                        out=dvs(which), in_=modT.t[:, l, c0:c0 + 8, v]), reads=[modT.b], writes=[self.dv.b], waw=False)
        self.end()

    def attn_inproj(self, l, X):
        fw, cfg = self.fw, self.cfg
        i = l // 2
        self.begin()
        W = self.sb("wA", [128, KD, ATT_EXT], BF16)
        self.load_w(W, lambda k: self.i_wext[i, k * 128:(k + 1) * 128, :], ATT_EXT)
        nt = self.norm_tiles()
        xs = [self.sb("x", [128, KD, 512], F32) for _ in range(2)]
        hTs = [self.sb("hT", [128, KD, 512], BF16) for _ in range(2)]
        cs = [self.sb("cos", [128, 512], F32) for _ in range(2)]
        sn = [self.sb("sin", [128, 512], F32) for _ in range(2)]
        qko = [self.sb("qko", [128, 13, 512], BF16) for _ in range(1)]
        vo = [self.sb("vo", [128, 4, 640], BF16) for _ in range(1)]
        pP = [self.ps("pP", [128, 512]) for _ in range(2)]
        pS = [self.ps("pS", [128, 512]) for _ in range(2)]
        ssq = self.ps("ssq", [128, 512])
        pV = self.ps("pV", [128, 1024])
        sqn = self.sb("sqn", [128, 512], BF16)
        rq, tq = self.sb("rq", [128, 512]), self.sb("tq", [128, 512])
        Aq = [self.sb("Aq", [128, 512]) for _ in range(2)]
        Bq = [self.sb("Bq", [128, 512]) for _ in range(2)]
        t1 = [self.sb("t1", [128, 512]) for _ in range(2)]
        t2 = [self.sb("t2", [128, 512]) for _ in range(2)]
        Xv = X.rearrange("(k p) n -> p k n", p=128)
        QTv = self.QT.rearrange("(k p) n -> p k n", p=128)
        KTv = self.KT.rearrange("(k p) n -> p k n", p=128)
        bi = 0
        for seq in cfg.seqs:
            _, start, ln, v, row0 = seq
            lat = (v == 0)
            for (t0, n) in blocks_of(ln, 512):
                x, hT, qo, vv = xs[bi % 2], hTs[bi % 2], qko[0], vo[0]
                c_, s_ = cs[bi % 2], sn[bi % 2]
                bi += 1
                self.load_cols(x, Xv, seq, t0, n, 0)
                if lat:
                    fw.dma(c_.t[:, 0:n], self.i_cos[:, start + t0:start + t0 + n], c_.b, writes=[c_.b])
                    fw.dma(s_.t[:, 0:n], self.i_sin[:, start + t0:start + t0 + n], s_.b, writes=[s_.b])
                self.norm_mod(x, n, nt["ss"], nt["sq"], nt["rs"], nt["tmp"], self.dvv(l, v, 0), self.dvv(l, v, 1), hT)
                for j in range(13):
                    P, S = pP[j % 2], pS[j % 2]
                    for k in range(KD):
                        fw.op("pe", lambda e, k=k, j=j, P=P: e.matmul(
                            P.t[:, 0:n], lhsT=W.t[:, k, j * 128:(j + 1) * 128], rhs=hT.t[:, k, 0:n],
                            start=(k == 0), stop=(k == KD - 1)), reads=[W.b, hT.b], writes=[P.b], ticket=(k == KD - 1))
                    if lat:
                        for k in range(KD):
                            fw.op("pe", lambda e, k=k, j=j, S=S: e.matmul(
                                S.t[:, 0:n], lhsT=W.t[:, k, ATT_FM + j * 128:ATT_FM + (j + 1) * 128],
                                rhs=hT.t[:, k, 0:n], start=(k == 0), stop=(k == KD - 1)),
                                  reads=[W.b, hT.b], writes=[S.b], ticket=(k == KD - 1))
                    normed = j >= 8
                    A, B = P, S
                    if normed:
                        fw.op("act", lambda e, P=P: e.activation(out=sqn.t[:, 0:n], in_=P.t[:, 0:n], func=AF.Square),
                              reads=[P.b], writes=[sqn.b])
                        fw.op("pe", lambda e: e.matmul(ssq.t[:, 0:n], lhsT=self.cmask("bd"), rhs=sqn.t[:, 0:n],
                                                       start=True, stop=True), reads=[sqn.b, self.cmb.b], writes=[ssq.b])
                        self.rstd(ssq, rq, tq, 1.0 / HD, n)
                        gn, gsn = ("gq%d" % i, "gqs%d" % i) if j < 12 else ("gk%d" % i, "gks%d" % i)
                        A = Aq[j % 2]
                        fw.op("dve", lambda e, P=P, A=A, gn=gn: e.scalar_tensor_tensor(
                            out=A.t[:, 0:n], in0=P.t[:, 0:n], scalar=self.vec(gn), in1=rq.t[:, 0:n],
                            op0=ALU.mult, op1=ALU.mult), reads=[P.b, rq.b], writes=[A.b])
                        if lat:
                            B = Bq[j % 2]
                            fw.op("dve", lambda e, S=S, B=B, gsn=gsn: e.scalar_tensor_tensor(
                                out=B.t[:, 0:n], in0=S.t[:, 0:n], scalar=self.vec(gsn), in1=rq.t[:, 0:n],
                                op0=ALU.mult, op1=ALU.mult), reads=[S.b, rq.b], writes=[B.b])
                    if lat:
                        a1, a2 = t1[j % 2], t2[j % 2]
                        fw.op("dve", lambda e, A=A, a1=a1: e.tensor_tensor(
                            out=a1.t[:, 0:n], in0=A.t[:, 0:n], in1=c_.t[:, 0:n], op=ALU.mult),
                              reads=[A.b, c_.b], writes=[a1.b])
                        fw.op("dve", lambda e, B=B, a2=a2: e.tensor_tensor(
                            out=a2.t[:, 0:n], in0=B.t[:, 0:n], in1=s_.t[:, 0:n], op=ALU.mult),
                              reads=[B.b, s_.b], writes=[a2.b])
                        fw.op("pool", lambda e, a1=a1, a2=a2, j=j: e.tensor_tensor(
                            out=qo.t[:, j, 0:n], in0=a1.t[:, 0:n], in1=a2.t[:, 0:n], op=ALU.add),
                              reads=[a1.b, a2.b], writes=[qo.b], waw=False)
                    elif normed:
                        fw.op("pool", lambda e, A=A, j=j: e.tensor_copy(out=qo.t[:, j, 0:n], in_=A.t[:, 0:n]),
                              reads=[A.b], writes=[qo.b], waw=False)
                    else:
                        fw.op("act", lambda e, A=A, j=j: e.activation(out=qo.t[:, j, 0:n], in_=A.t[:, 0:n], func=AF.Copy),
                              reads=[A.b], writes=[qo.b], waw=False)
                c0, c1 = start + t0, start + t0 + n
                for (dst, d0, s0, w) in ((QTv, 0, 0, 4), (KTv, 0, 4, 4), (QTv, 4, 8, 4), (KTv, 4, 12, 1)):
                    fw.dma(dst[:, d0:d0 + w, c0:c1], qo.t[:, s0:s0 + w, 0:n], qo.b, reads=[qo.b],
                           writes=[self.db(dst)], waw=False, stream="pool")
                nsub = n // 128
                for sub in range(nsub):
                    for (cc, w) in ((0, 512), (512, 128)):
                        for k in range(KD):
                            fw.op("pe", lambda e, k=k, cc=cc, w=w, sub=sub: e.matmul(
                                pV.t[:, cc:cc + w], lhsT=hT.t[:, k, sub * 128:(sub + 1) * 128],
                                rhs=W.t[:, k, 2 * ATT_FM + cc:2 * ATT_FM + cc + w], start=(k == 0), stop=(k == KD - 1)),
                                  reads=[W.b, hT.b], writes=[pV.b], ticket=(k == KD - 1 and cc == 512))
                    fw.op("act" if sub % 2 == 0 else "dve",
                          (lambda e, sub=sub: e.activation(out=vv.t[:, sub, :], in_=pV.t[:, 0:640], func=AF.Copy))
                          if sub % 2 == 0 else
                          (lambda e, sub=sub: e.tensor_copy(out=vv.t[:, sub, :], in_=pV.t[:, 0:640])),
                          reads=[pV.b], writes=[vv.b], waw=False)
                r0 = row0 + t0
                fw.dma(self.VT[r0:r0 + n, :].rearrange("(s p) c -> p s c", p=128), vv.t[:, 0:nsub, :], vv.b,
                       reads=[vv.b], writes=[self.db(self.VT)], waw=False, stream="pool")
        self.end()

    def attn_core(self, l):
        fw, cfg = self.fw, self.cfg
        i = l // 2
        lam_init = 0.8 - 0.6 * math.exp(-0.3 * l)
        NTOK, C, T = cfg.NTOK, cfg.C, cfg.T
        NKT = NTOK // 128
        self.begin()
        ro = self.roff["lq1_%d" % i][0]
        rv = self.sb("lqk", [128, 256])
        fw.dma(rv.t[:, :], self.i_rowv[0:1, ro:ro + 256].broadcast_to([128, 256]), rv.b, writes=[rv.b])
        prod = self.sb("prod", [128, 128])
        fw.op("dve", lambda e: e.tensor_tensor(out=prod.t[:, 0:64], in0=rv.t[:, 0:64], in1=rv.t[:, 64:128], op=ALU.mult),
              reads=[rv.b], writes=[prod.b])
        fw.op("dve", lambda e: e.tensor_tensor(out=prod.t[:, 64:128], in0=rv.t[:, 128:192], in1=rv.t[:, 192:256],
                                               op=ALU.mult), reads=[rv.b], writes=[prod.b], waw=False)
        s12 = self.sb("s12", [128, 2])
        for q in range(2):
            fw.op("dve", lambda e, q=q: e.reduce_sum(out=s12.t[:, q:q + 1], in_=prod.t[:, q * 64:(q + 1) * 64],
                                                     axis=mybir.AxisListType.X), reads=[prod.b], writes=[s12.b], waw=False)
        e12 = self.sb("e12", [128, 2])
        fw.op("act", lambda e: e.activation(out=e12.t[:, :], in_=s12.t[:, :], func=AF.Exp), reads=[s12.b], writes=[e12.b])
        nlam = self.sb("nlam", [128, 1])
        fw.op("dve", lambda e: e.tensor_tensor(out=nlam.t[:, :], in0=e12.t[:, 1:2], in1=e12.t[:, 0:1], op=ALU.subtract),
              reads=[e12.b], writes=[nlam.b])
        fw.op("dve", lambda e: e.tensor_scalar(out=nlam.t[:, :], in0=nlam.t[:, :], scalar1=-lam_init, scalar2=None,
                                               op0=ALU.add), reads=[nlam.b], writes=[nlam.b])
        sg = self.sb("sg", [128, 1])
        fw.op("dve", lambda e: e.tensor_scalar(out=sg.t[:, :], in0=self.vec("subg%d" % i), scalar1=1.0 - lam_init,
                                               scalar2=None, op0=ALU.mult), reads=[self.vecs.b], writes=[sg.b])
        Ks = [self.sb("K", [128, NTOK], BF16) for _ in range(2)]
        Vs = [self.sb("V", [128, NKT, 128], BF16) for _ in range(2)]
        Vg = [self.sb("Vg", [128, NKT, 128], BF16) for _ in range(2)]
        Qs = [self.sb("Q", [128, 512], BF16) for _ in range(2)]
        Pt = [self.sb("P", [128, 1024], BF16) for _ in range(2)]
        Sp = [self.ps("S", [128, 1024]) for _ in range(2)]
        Op = [self.ps("O", [128, 512]) for _ in range(2)]
        Lp = [self.ps("L", [128, 512]) for _ in range(2)]
        r_ = [self.sb("r", [128, 512]) for _ in range(2)]
        on = [self.sb("on", [128, 512]) for _ in range(2)]
        oa = self.sb("oa", [128, 512])
        sqo = self.sb("sqo", [128, 512], BF16)
        rs, tmp = self.sb("rso", [128, 512]), self.sb("tmpo", [128, 512])
        Lacc = [self.sb("Lacc", [128, 512]) for _ in range(2)]
        Lb = [self.sb("Lb", [128, 512], BF16) for _ in range(2)]
        Lf = self.sb("Lf", [128, 512])
        hb = self.sb("hb", [128, 512], BF16)
        h32 = self.sb("h32", [128, 512])
        lb = self.sb("lb", [128, 512], BF16)
        ost = [self.sb("ost", [128, 2, 512], BF16) for _ in range(2)]
        QTv = self.QT.rearrange("(k p) n -> p k n", p=128)
        KTv = self.KT.rearrange("(k p) n -> p k n", p=128)
        OTv = self.OT.rearrange("(k p) n -> p k n", p=128)
        v3 = lambda t, n: t.t[:, :].rearrange("p (s c) -> p s c", c=512)[:, :, 0:n]

        def load_kv(u):
            if u > 4:
                return
            K, V = Ks[u % 2], Vs[u % 2]
            kc = u if u < 4 else 4
            vcol = u * 128 if u < 4 else 512
            fw.dma(K.t[:, 0:C], KTv[:, kc, cfg.CS:cfg.CS + C], K.b, reads=[self.db(self.KT)], writes=[K.b])
            fw.dma(K.t[:, C:NTOK], KTv[:, kc, cfg.LS:cfg.LS + T], K.b, reads=[self.db(self.KT)], writes=[K.b], waw=False)
            vsrc = self.VT[:, vcol:vcol + 128].rearrange("(s p) c -> p s c", p=128)
            for (s0, ns) in blocks_of(NKT, 16):
                fw.dma(V.t[:, s0:s0 + ns, :], vsrc[:, s0:s0 + ns, :], V.b, reads=[self.db(self.VT)], writes=[V.b],
                       waw=(s0 == 0))
            if u == 4:
                for sub in range(2):
                    fw.op("pool", lambda e, sub=sub: e.memset(Vg[sub].t[:, :, 64:128], 1.0), writes=[Vg[sub].b])
                    fw.op("pool" if sub == 0 else "dve", lambda e, sub=sub: e.tensor_copy(
                        out=Vg[sub].t[:, :, 0:64], in_=V.t[:, :, sub * 64:(sub + 1) * 64]),
                          reads=[V.b], writes=[Vg[sub].b], waw=False)

        qblocks = []
        for seq in cfg.seqs:
            _, start, ln, v, row0 = seq
            kts = list(range(0, C // 128)) if v == 1 else list(range(NKT))
            for (t0, n) in blocks_of(ln, 512):
                qblocks.append((start + t0, n, kts))
        load_kv(0)
        qi = 0
        for u in range(8):
            if u + 1 < 8:
                load_kv(u + 1)
            K, V = Ks[min(u, 4) % 2], Vs[min(u, 4) % 2]
            diff = u < 4
            for (c0, n, kts) in qblocks:
                Q = Qs[qi % 2]
                os_ = ost[qi % 2]
                qi += 1
                fw.dma(Q.t[:, 0:n], QTv[:, u, c0:c0 + n], Q.b, reads=[self.db(self.QT)], writes=[Q.b])

                def emit_S(kt):
                    S = Sp[kt % 2]
                    for sub in range(2):
                        fw.op("pe", lambda e, sub=sub, S=S, kt=kt: e.matmul(
                            S.t[:, sub * 512:sub * 512 + n], lhsT=K.t[sub * 64:(sub + 1) * 64, kt * 128:(kt + 1) * 128],
                            rhs=Q.t[sub * 64:(sub + 1) * 64, 0:n], start=True, stop=True),
                              reads=[K.b, Q.b], writes=[S.b], ticket=(sub == 1), waw=(sub == 0))

                def emit_E(kt):
                    S, P = Sp[kt % 2], Pt[kt % 2]
                    fw.op("act", lambda e, S=S, P=P: e.activation(out=v3(P, n), in_=v3(S, n), func=AF.Exp,
                                                                  scale=HD ** -0.5), reads=[S.b], writes=[P.b])

                def emit_PV(kt, first, last):
                    P = Pt[kt % 2]
                    for sub in range(2):
                        rhs = P.t[:, sub * 512:sub * 512 + n]
                        if diff:
                            fw.op("pe", lambda e, sub=sub, rhs=rhs: e.matmul(
                                Op[sub].t[:, 0:n], lhsT=V.t[:, kt, :], rhs=rhs, start=first, stop=last),
                                  reads=[V.b, P.b], writes=[Op[sub].b] if (first or last) else [],
                                  ticket=last)
                            La = Lacc[sub]
                            if sub == 1:
                                fw.op("pe", lambda e, sub=sub, rhs=rhs: e.matmul(
                                    Lp[sub].t[:, 0:n], lhsT=self.ones_bf.t[:, :], rhs=rhs, start=first, stop=last),
                                      reads=[self.ones_bf.b, P.b], writes=[Lp[sub].b] if (first or last) else [],
                                      ticket=True)
                            elif first:
                                fw.op("dve", lambda e, La=La, rhs=rhs: e.tensor_copy(out=La.t[:, 0:n], in_=rhs),
                                      reads=[P.b], writes=[La.b])
                            else:
                                fw.op("dve", lambda e, La=La, rhs=rhs: e.tensor_tensor(
                                    out=La.t[:, 0:n], in0=La.t[:, 0:n], in1=rhs, op=ALU.add),
                                      reads=[P.b, La.b], writes=[La.b])
                        else:
                            fw.op("pe", lambda e, sub=sub, rhs=rhs: e.matmul(
                                Op[sub].t[:, 0:n], lhsT=Vg[sub].t[:, kt, :], rhs=rhs, start=first, stop=last),
                                  reads=[Vg[sub].b, P.b], writes=[Op[sub].b] if (first or last) else [],
                                  ticket=(last or sub == 1))

                emit_S(kts[0])
                for idx, kt in enumerate(kts):
                    if idx + 1 < len(kts):
                        emit_S(kts[idx + 1])
                    emit_E(kt)
                    emit_PV(kt, idx == 0, idx == len(kts) - 1)
                if diff:
                    for sub in range(2):
                        if sub == 0:
                            fw.op("dve", lambda e, sub=sub: e.tensor_copy(out=Lb[sub].t[:, 0:n], in_=Lacc[sub].t[:, 0:n]),
                                  reads=[Lacc[sub].b], writes=[Lb[sub].b])
                            fw.op("pe", lambda e, sub=sub: e.matmul(Lp[sub].t[:, 0:n], lhsT=self.ones_bf.t[:, :],
                                                                    rhs=Lb[sub].t[:, 0:n], start=True, stop=True),
                                  reads=[Lb[sub].b, self.ones_bf.b], writes=[Lp[sub].b])
                        fw.op("dve", lambda e, sub=sub: e.reciprocal(out=r_[sub].t[:, 0:n], in_=Lp[sub].t[:, 0:n]),
                              reads=[Lp[sub].b], writes=[r_[sub].b])
                    for sub in range(2):
                        fw.op("dve", lambda e, sub=sub: e.tensor_tensor(
                            out=on[sub].t[:, 0:n], in0=Op[sub].t[:, 0:n], in1=r_[sub].t[:, 0:n], op=ALU.mult),
                              reads=[Op[sub].b, r_[sub].b], writes=[on[sub].b])
                    fw.op("dve", lambda e: e.scalar_tensor_tensor(
                        out=oa.t[:, 0:n], in0=on[1].t[:, 0:n], scalar=nlam.t[:, 0:1], in1=on[0].t[:, 0:n],
                        op0=ALU.mult, op1=ALU.add), reads=[on[0].b, on[1].b, nlam.b], writes=[oa.b])
                    fw.op("act", lambda e: e.activation(out=sqo.t[:, 0:n], in_=oa.t[:, 0:n], func=AF.Square),
                          reads=[oa.b], writes=[sqo.b])
                    ssb = Lp[0]
                    fw.op("pe", lambda e: e.matmul(ssb.t[:, 0:n], lhsT=self.ones_bf.t[:, :], rhs=sqo.t[:, 0:n],
                                                   start=True, stop=True), reads=[sqo.b, self.ones_bf.b], writes=[ssb.b])
                    self.rstd(ssb, rs, tmp, 1.0 / 128, n)
                    fw.op("dve", lambda e: e.scalar_tensor_tensor(
                        out=os_.t[:, 0, 0:n], in0=oa.t[:, 0:n], scalar=sg.t[:, 0:1], in1=rs.t[:, 0:n],
                        op0=ALU.mult, op1=ALU.mult), reads=[oa.b, rs.b, sg.b], writes=[os_.b])
                    fw.dma(OTv[:, u, c0:c0 + n], os_.t[:, 0, 0:n], os_.b, reads=[os_.b], writes=[self.db(self.OT)],
                           waw=False, stream="pool")
                else:
                    j = u - 4
                    H = slice(64, 128)
                    idb = self.cmb.t[64:128, 64:128]
                    for sub in range(2):
                        fw.op("act", lambda e, sub=sub: e.activation(out=Lf.t[H, 0:n], in_=Op[sub].t[H, 0:n], func=AF.Copy),
                              reads=[Op[sub].b], writes=[Lf.b])
                        fw.op("dve", lambda e: e.reciprocal(out=Lf.t[H, 0:n], in_=Lf.t[H, 0:n]), reads=[Lf.b], writes=[Lf.b])
                        fw.op("dve", lambda e: e.tensor_copy(out=hb.t[H, 0:n], in_=Lf.t[H, 0:n]), reads=[Lf.b], writes=[hb.b])
                        fw.op("dve", lambda e: e.tensor_copy(out=h32.t[H, 0:n], in_=hb.t[H, 0:n]), reads=[hb.b], writes=[h32.b])
                        fw.op("dve", lambda e: e.tensor_tensor(out=lb.t[H, 0:n], in0=Lf.t[H, 0:n], in1=h32.t[H, 0:n],
                                                               op=ALU.subtract), reads=[Lf.b, h32.b], writes=[lb.b])
                        fw.op("pe", lambda e, sub=sub: e.matmul(Lp[sub].t[0:64, 0:n], lhsT=idb, rhs=hb.t[H, 0:n],
                                                                start=True, stop=False),
                              reads=[hb.b, self.cmb.b], writes=[Lp[sub].b], ticket=False)
                        fw.op("pe", lambda e, sub=sub: e.matmul(Lp[sub].t[0:64, 0:n], lhsT=idb, rhs=lb.t[H, 0:n],
                                                                start=False, stop=True),
                              reads=[lb.b, hb.b, self.cmb.b], writes=[Lp[sub].b])
                        fw.op("act", lambda e, sub=sub: e.activation(out=r_[sub].t[0:64, 0:n], in_=Lp[sub].t[0:64, 0:n],
                                                                     func=AF.Copy), reads=[Lp[sub].b], writes=[r_[sub].b])
                        fw.op("dve", lambda e, sub=sub: e.tensor_tensor(
                            out=os_.t[0:64, sub, 0:n], in0=Op[sub].t[0:64, 0:n], in1=r_[sub].t[0:64, 0:n], op=ALU.mult),
                              reads=[Op[sub].b, r_[sub].b], writes=[os_.b], waw=(sub == 0))
                    for sub in range(2):
                        hd = 4 * sub + j
                        f0 = 512 + hd * 64
                        fw.dma(self.OT[f0:f0 + 64, c0:c0 + n], os_.t[0:64, sub, 0:n], os_.b, reads=[os_.b],
                               writes=[self.db(self.OT)], waw=False, stream="pool")
        self.end()

    def proj_res_norm(self, l, Xin, Xout, SRC, kc, w_rows):
        fw, cfg = self.fw, self.cfg
        self.begin()
        W = self.sb("wo", [128, kc, D], BF16)
        self.load_w(W, w_rows, D)
        nt = self.norm_tiles()
        srcs = [self.sb("src", [128, kc, 512], BF16) for _ in range(2)]
        xs = [self.sb("x", [128, KD, 512], F32) for _ in range(2)]
        x1s = [self.sb("x1", [128, KD, 512], F32) for _ in range(2)]
        hs = [self.sb("h", [128, KD, 512], BF16) for _ in range(2)]
        pp = [self.ps("pp", [128, 512]) for _ in range(2)]
        Xiv = Xin.rearrange("(k p) n -> p k n", p=128)
        Xov = Xout.rearrange("(k p) n -> p k n", p=128)
        Sv = SRC.rearrange("(k p) n -> p k n", p=128)
        HTv = self.HT.rearrange("(k p) n -> p k n", p=128)
        bi = 0
        for seq in cfg.seqs:
            _, start, ln, v, row0 = seq
            if v == 1 and l == DEPTH - 1:
                continue
            for (t0, n) in blocks_of(ln, 512):
                s_, x, x1, h = srcs[bi % 2], xs[bi % 2], x1s[bi % 2], hs[bi % 2]
                bi += 1
                self.load_cols(s_, Sv, seq, t0, n, 0)
                self.load_cols(x, Xiv, seq, t0, n, 0)
                g1 = self.dvv(l, v, 2)
                for c in range(KD):
                    P = pp[c % 2]
                    for k in range(kc):
                        fw.op("pe", lambda e, k=k, c=c, P=P: e.matmul(
                            P.t[:, 0:n], lhsT=W.t[:, k, c * 128:(c + 1) * 128], rhs=s_.t[:, k, 0:n],
                            start=(k == 0), stop=(k == kc - 1)), reads=[W.b, s_.b], writes=[P.b], ticket=(k == kc - 1))
                    fw.op("dve", lambda e, c=c, P=P: e.scalar_tensor_tensor(
                        out=x1.t[:, c, 0:n], in0=P.t[:, 0:n], scalar=g1(c), in1=x.t[:, c, 0:n],
                        op0=ALU.mult, op1=ALU.add), reads=[P.b, x.b], writes=[x1.b], waw=False)
                c0 = start + t0
                fw.dma(Xov[:, :, c0:c0 + n], x1.t[:, :, 0:n], x1.b, reads=[x1.b], writes=[self.db(Xout)], waw=False,
                       stream="pool")
                self.norm_mod(x1, n, nt["ss"], nt["sq"], nt["rs"], nt["tmp"], self.dvv(l, v, 3), self.dvv(l, v, 4), h)
                fw.dma(HTv[:, :, c0:c0 + n], h.t[:, :, 0:n], h.b, reads=[h.b], writes=[self.db(self.HT)], waw=False,
                       stream="pool")
        self.end()

    def ffn_in(self, l):
        fw, cfg = self.fw, self.cfg
        self.begin()
        W = self.sb("wf", [128, KD, 2 * DFF], BF16)
        self.load_w(W, lambda k: self.i_fwi[l, k * 128:(k + 1) * 128, :], 2 * DFF)
        hs = [self.sb("h", [128, KD, 512], BF16) for _ in range(2)]
        uo = [self.sb("uo", [128, NFF, 512], BF16) for _ in range(2)]
        pg = [self.ps("pg", [128, 512]) for _ in range(2)]
        pv = [self.ps("pv", [128, 512]) for _ in range(2)]
        tt = [self.sb("t", [128, 512]) for _ in range(2)]
        ge = [self.sb("ge", [128, 512]) for _ in range(2)]
        HTv = self.HT.rearrange("(k p) n -> p k n", p=128)
        UTv = self.UT.rearrange("(k p) n -> p k n", p=128)
        bi = 0
        for seq in cfg.seqs:
            _, start, ln, v, row0 = seq
            if v == 1 and l == DEPTH - 1:
                continue
            for (t0, n) in blocks_of(ln, 510):
                h, u = hs[bi % 2], uo[bi % 2]
                bi += 1
                self.load_cols(h, HTv, seq, t0, n, 1)
                N = n + 2
                for c in range(NFF):
                    G, Vv, t, g = pg[c % 2], pv[c % 2], tt[c % 2], ge[c % 2]
                    for k in range(KD):
                        fw.op("pe", lambda e, k=k, c=c, G=G: e.matmul(
                            G.t[:, 0:N], lhsT=W.t[:, k, DFF + c * 128:DFF + (c + 1) * 128], rhs=h.t[:, k, 0:N],
                            start=(k == 0), stop=(k == KD - 1)), reads=[W.b, h.b], writes=[G.b], ticket=(k == KD - 1))
                    for k in range(KD):
                        fw.op("pe", lambda e, k=k, c=c, Vv=Vv: e.matmul(
                            Vv.t[:, 0:N], lhsT=W.t[:, k, c * 128:(c + 1) * 128], rhs=h.t[:, k, 0:N],
                            start=(k == 0), stop=(k == KD - 1)), reads=[W.b, h.b], writes=[Vv.b], ticket=(k == KD - 1))
                    w0, w1, w2 = (self.vec("fcw%d_%d" % (l, q), c) for q in range(3))
                    bb = self.vec("fcb%d" % l, c)
                    fw.op("dve", lambda e, G=G, t=t, w0=w0, bb=bb: e.tensor_scalar(
                        out=t.t[:, 0:n], in0=G.t[:, 0:n], scalar1=w0, scalar2=bb, op0=ALU.mult, op1=ALU.add),
                          reads=[G.b], writes=[t.b])
                    fw.op("dve", lambda e, G=G, t=t, w1=w1: e.scalar_tensor_tensor(
                        out=t.t[:, 0:n], in0=G.t[:, 1:n + 1], scalar=w1, in1=t.t[:, 0:n], op0=ALU.mult, op1=ALU.add),
                          reads=[G.b, t.b], writes=[t.b])
                    fw.op("dve", lambda e, G=G, t=t, w2=w2: e.scalar_tensor_tensor(
                        out=t.t[:, 0:n], in0=G.t[:, 2:n + 2], scalar=w2, in1=t.t[:, 0:n], op0=ALU.mult, op1=ALU.add),
                          reads=[G.b, t.b], writes=[t.b])
                    fw.op("act", lambda e, t=t, g=g: e.activation(out=g.t[:, 0:n], in_=t.t[:, 0:n], func=AF.Gelu),
                          reads=[t.b], writes=[g.b])
                    fw.op("dve", lambda e, g=g, Vv=Vv, c=c: e.tensor_tensor(
                        out=u.t[:, c, 0:n], in0=g.t[:, 0:n], in1=Vv.t[:, 1:n + 1], op=ALU.mult),
                          reads=[g.b, Vv.b], writes=[u.b], waw=False)
                c0 = start + t0
                fw.dma(UTv[:, :, c0:c0 + n], u.t[:, :, 0:n], u.b, reads=[u.b], writes=[self.db(self.UT)], waw=False,
                       stream="pool")
        self.end()

    def ffn_out(self, l, Xin, Xout):
        fw, cfg = self.fw, self.cfg
        self.begin()
        W = self.sb("wf2", [128, NFF, D], BF16)
        self.load_w(W, lambda k: self.i_fwo[l, k * 128:(k + 1) * 128, :], D)
        us = [self.sb("u", [128, NFF, 512], BF16) for _ in range(2)]
        xs = [self.sb("x", [128, KD, 512], F32) for _ in range(2)]
        x2s = [self.sb("x2", [128, KD, 512], F32) for _ in range(2)]
        pp = [self.ps("pp", [128, 512]) for _ in range(2)]
        Xiv = Xin.rearrange("(k p) n -> p k n", p=128)
        Xov = Xout.rearrange("(k p) n -> p k n", p=128)
        UTv = self.UT.rearrange("(k p) n -> p k n", p=128)
        bi = 0
        for seq in cfg.seqs:
            _, start, ln, v, row0 = seq
            if v == 1 and l == DEPTH - 1:
                continue
            for (t0, n) in blocks_of(ln, 512):
                u, x, x2 = us[bi % 2], xs[bi % 2], x2s[bi % 2]
                bi += 1
                self.load_cols(u, UTv, seq, t0, n, 0)
                self.load_cols(x, Xiv, seq, t0, n, 0)
                g2 = self.dvv(l, v, 5)
                for c in range(KD):
                    P = pp[c % 2]
                    for k in range(NFF):
                        fw.op("pe", lambda e, k=k, c=c, P=P: e.matmul(
                            P.t[:, 0:n], lhsT=W.t[:, k, c * 128:(c + 1) * 128], rhs=u.t[:, k, 0:n],
                            start=(k == 0), stop=(k == NFF - 1)), reads=[W.b, u.b], writes=[P.b], ticket=(k == NFF - 1))
                    fw.op("dve", lambda e, c=c, P=P: e.scalar_tensor_tensor(
                        out=x2.t[:, c, 0:n], in0=P.t[:, 0:n], scalar=g2(c), in1=x.t[:, c, 0:n],
                        op0=ALU.mult, op1=ALU.add), reads=[P.b, x.b], writes=[x2.b], waw=False)
                c0 = start + t0
                fw.dma(Xov[:, :, c0:c0 + n], x2.t[:, :, 0:n], x2.b, reads=[x2.b], writes=[self.db(Xout)], waw=False,
                       stream="pool")
        self.end()

    def final_norm(self, X):
        fw, cfg = self.fw, self.cfg
        self.begin()
        nt = self.norm_tiles()
        xs = [self.sb("x", [128, KD, 512], F32) for _ in range(2)]
        os_ = [self.sb("o", [128, KD, 512], F32) for _ in range(2)]
        Xv = X.rearrange("(k p) n -> p k n", p=128)
        Ov = self.o_out.rearrange("(k p) n -> p k n", p=128)
        seq = cfg.seqs[1]
        outb = self.fw.buf(name="outT")
        for bi, (t0, n) in enumerate(blocks_of(cfg.T, 512)):
            x, o = xs[bi % 2], os_[bi % 2]
            self.load_cols(x, Xv, seq, t0, n, 0)
            self.norm_mod(x, n, nt["ss"], nt["sq"], nt["rs"], nt["tmp"], lambda k: self.vec("gfin", k), None, o)
            fw.dma(Ov[:, :, t0:t0 + n], o.t[:, :, 0:n], o.b, reads=[o.b], writes=[outb], waw=False, stream="pool")
        self.end()


    def ssm_inproj(self, l, X):
        fw, cfg = self.fw, self.cfg
        i = l // 2
        self.begin()
        W = self.sb("wS", [128, KD, SSM_IN], BF16)
        self.load_w(W, lambda k: self.i_swi[i, k * 128:(k + 1) * 128, :], SSM_IN)
        nt = self.norm_tiles()
        x = self.sb("x", [128, KD, 512], F32)
        hTs = [self.sb("hT", [128, KD, 512], BF16) for _ in range(2)]
        xo = self.sb("xo", [128, 24, 512], BF16)
        zo = [self.sb("zo", [128, SSM_DI], F32) for _ in range(2)]
        dto = [self.sb("dto", [128, 64], F32) for _ in range(2)]
        dta = [self.sb("dta", [128, 64], F32) for _ in range(2)]
        dtb = self.sb("dtb", [128, 64], F32)
        ro = self.roff["dtb%d" % i][0]
        fw.dma(dtb.t[:, :], self.i_rowv[0:1, ro:ro + 64].broadcast_to([128, 64]), dtb.b, writes=[dtb.b])
        pP = [self.ps("pP", [128, 512]) for _ in range(2)]
        pz = [self.ps("pz", [128, 512]) for _ in range(2)]
        pdt = self.ps("pdt", [128, 512])
        tt = [self.sb("t", [128, 512]) for _ in range(2)]
        Xv = X.rearrange("(k p) n -> p k n", p=128)
        XBv = self.XBCT.rearrange("(k p) n -> p k n", p=128)
        bi = 0
        zi = 0
        for seq in cfg.seqs:
            _, start, ln, v, row0 = seq
            for (t0, n) in blocks_of(ln, 510):
                hT = hTs[bi % 2]
                bi += 1
                N = n + 2
                self.load_cols(x, Xv, seq, t0, n, 1)
                self.norm_mod(x, N, nt["ss"], nt["sq"], nt["rs"], nt["tmp"], self.dvv(l, v, 0), self.dvv(l, v, 1), hT)
                if t0 == 0:
                    fw.op("pool", lambda e: e.memset(hT.t[:, :, 0:1], 0.0), writes=[hT.b])
                if t0 + n == ln:
                    fw.op("pool", lambda e: e.memset(hT.t[:, :, n + 1:n + 2], 0.0), writes=[hT.b])
                for c in range(24):
                    P, t = pP[c % 2], tt[c % 2]
                    for k in range(KD):
                        fw.op("pe", lambda e, k=k, c=c, P=P: e.matmul(
                            P.t[:, 0:N], lhsT=W.t[:, k, SSM_DI + c * 128:SSM_DI + (c + 1) * 128], rhs=hT.t[:, k, 0:N],
                            start=(k == 0), stop=(k == KD - 1)), reads=[W.b, hT.b], writes=[P.b], ticket=(k == KD - 1))
                    w0, w1, w2 = (self.vec("scw%d_%d" % (i, q), c) for q in range(3))
                    bb = self.vec("scb%d" % i, c)
                    fw.op("dve", lambda e, P=P, t=t, w0=w0, bb=bb: e.tensor_scalar(
                        out=t.t[:, 0:n], in0=P.t[:, 0:n], scalar1=w0, scalar2=bb, op0=ALU.mult, op1=ALU.add),
                          reads=[P.b], writes=[t.b])
                    fw.op("dve", lambda e, P=P, t=t, w1=w1: e.scalar_tensor_tensor(
                        out=t.t[:, 0:n], in0=P.t[:, 1:n + 1], scalar=w1, in1=t.t[:, 0:n], op0=ALU.mult, op1=ALU.add),
                          reads=[P.b, t.b], writes=[t.b])
                    fw.op("dve", lambda e, P=P, t=t, w2=w2: e.scalar_tensor_tensor(
                        out=t.t[:, 0:n], in0=P.t[:, 2:n + 2], scalar=w2, in1=t.t[:, 0:n], op0=ALU.mult, op1=ALU.add),
                          reads=[P.b, t.b], writes=[t.b])
                    fw.op("act", lambda e, t=t, c=c: e.activation(out=xo.t[:, c, 0:n], in_=t.t[:, 0:n], func=AF.Silu),
                          reads=[t.b], writes=[xo.b], waw=False)
                c0 = start + t0
                fw.dma(XBv[:, :, c0:c0 + n], xo.t[:, :, 0:n], xo.b, reads=[xo.b], writes=[self.db(self.XBCT)],
                       waw=False, stream="pool")
                for (s0, m) in blocks_of(n, 128):
                    z, dt_, da = zo[zi % 2], dto[zi % 2], dta[zi % 2]
                    zi += 1
                    for q in range(4):
                        Pz = pz[q % 2]
                        for k in range(KD):
                            fw.op("pe", lambda e, k=k, q=q, Pz=Pz: e.matmul(
                                Pz.t[0:m, :], lhsT=hT.t[:, k, 1 + s0:1 + s0 + m], rhs=W.t[:, k, q * 512:(q + 1) * 512],
                                start=(k == 0), stop=(k == KD - 1)), reads=[W.b, hT.b], writes=[Pz.b], ticket=(k == KD - 1))
                        if q % 2 == 0:
                            fw.op("act", lambda e, q=q, Pz=Pz: e.activation(out=z.t[0:m, q * 512:(q + 1) * 512],
                                                                            in_=Pz.t[0:m, :], func=AF.Copy),
                                  reads=[Pz.b], writes=[z.b], waw=False)
                        else:
                            fw.op("dve", lambda e, q=q, Pz=Pz: e.tensor_copy(out=z.t[0:m, q * 512:(q + 1) * 512],
                                                                             in_=Pz.t[0:m, :]),
                                  reads=[Pz.b], writes=[z.b], waw=False)
                    r0 = row0 + t0 + s0
                    fw.dma(self.ZT[r0:r0 + m, :], z.t[0:m, :], z.b, reads=[z.b], writes=[self.db(self.ZT)], waw=False,
                           stream="pool")
                    for k in range(KD):
                        fw.op("pe", lambda e, k=k: e.matmul(
                            pdt.t[0:m, 0:64], lhsT=hT.t[:, k, 1 + s0:1 + s0 + m], rhs=W.t[:, k, SSM_DI + SSM_XBC:SSM_IN],
                            start=(k == 0), stop=(k == KD - 1)), reads=[W.b, hT.b], writes=[pdt.b], ticket=(k == KD - 1))
                    fw.op("dve", lambda e: e.tensor_tensor(out=da.t[0:m, :], in0=pdt.t[0:m, 0:64], in1=dtb.t[0:m, :], op=ALU.add),
                          reads=[pdt.b, dtb.b], writes=[da.b])
                    fw.op("act", lambda e: e.activation(out=da.t[0:m, :], in_=da.t[0:m, :], func=AF.Exp),
                          reads=[da.b], writes=[da.b])
                    fw.op("act", lambda e: e.activation(out=dt_.t[0:m, :], in_=da.t[0:m, :], func=AF.Ln, bias=1.0),
                          reads=[da.b], writes=[dt_.b])
                    fw.dma(self.DTT[r0:r0 + m, :], dt_.t[0:m, :], dt_.b, reads=[dt_.b], writes=[self.db(self.DTT)],
                           waw=False, stream="pool")
        self.end()

    def ssm_scan(self, l, d):
        fw, cfg = self.fw, self.cfg
        i = l // 2
        NKT = cfg.NTOK // 128
        nct = cfg.C // 128
        self.begin()
        bc3 = lambda ap: ap.unsqueeze(2).broadcast_to([128, 8, 64])
        ML = self.cmask("ML_f" if d == 0 else "ML_b")
        MR = self.cmask("MR_f" if d == 0 else "MR_b")
        MRf = self.cmask("MR_f" if d == 0 else "MR_b", bf=False)
        arow = self.sb("arow", [128, 32])
        ro = self.roff["alog%d" % i][0] + 32 * d
        fw.dma(arow.t[:, :], self.i_rowv[0:1, ro:ro + 32].broadcast_to([128, 32]), arow.b, writes=[arow.b])
        fw.op("act", lambda e: e.activation(out=arow.t[:, :], in_=arow.t[:, :], func=AF.Exp), reads=[arow.b], writes=[arow.b])
        fw.op("dve", lambda e: e.tensor_scalar(out=arow.t[:, :], in0=arow.t[:, :], scalar1=-1.0, scalar2=None, op0=ALU.mult),
              reads=[arow.b], writes=[arow.b])
        if d == 0:
            drow = self.sb("drow", [128, 32])
            ro = self.roff["ssd%d" % i][0]
            fw.dma(drow.t[:, :], self.i_rowv[0:1, ro:ro + 32].broadcast_to([128, 32]), drow.b, writes=[drow.b])
            DI = self.sb("DI", [128, 32 * 128], BF16)
            fw.op("dve", lambda e: e.tensor_tensor(
                out=DI.t[:, :].rearrange("p (h c) -> p h c", c=128),
                in0=self.cmask("ident", bf=False).unsqueeze(1).broadcast_to([128, 32, 128]),
                in1=drow.t[:, :].unsqueeze(2).broadcast_to([128, 32, 128]), op=ALU.mult),
                  reads=[drow.b, self.cm.b], writes=[DI.b])
        if d == 1:
            ng = self.sb("ng", [128, SSM_DI])
            ro = self.roff["sng%d" % i][0]
            fw.dma(ng.t[:, :], self.i_rowv[0:1, ro:ro + SSM_DI].broadcast_to([128, SSM_DI]), ng.b, writes=[ng.b])
        S = self.sb("S", [128, SSM_DI], F32)
        Sb = self.sb("Sb", [128, SSM_DI], BF16)
        fw.op("pool", lambda e: e.memset(S.t[:, :], 0.0), writes=[S.b])
        fw.op("pool", lambda e: e.memset(Sb.t[:, :], 0.0), writes=[Sb.b])
        xbcs = [self.sb("xbc", [128, 24, 128], BF16) for _ in range(2)]
        dts = [self.sb("dt", [128, 64], F32) for _ in range(2)]
        xtok_l = [self.sb("xtok", [128, 2560], BF16) for _ in range(2)]
        a_l = [self.sb("a", [128, 32]) for _ in range(2)]
        ahb_l = [self.sb("ahb", [128, 32], BF16) for _ in range(2)]
        ah_l = [self.sb("ah", [128, 32]) for _ in range(2)]
        al_l = [self.sb("al", [128, 32]) for _ in range(2)]
        a2_l = [self.sb("a2", [128, 64], BF16) for _ in range(2)]
        E_l = [self.sb("E", [128, 96]) for _ in range(2)]
        e2_l = [self.sb("e2", [128, 32]) for _ in range(2)]
        cbm_l = [self.sb("cbm", [128, 4, 128]) for _ in range(2)]
        Rhs = [self.sb("Rh", [128, 512], BF16) for _ in range(2)]
        Rls = [self.sb("Rl", [128, 512], BF16) for _ in range(2)]
        dec = [self.sb("dec", [128, 512]) for _ in range(2)]
        wT4 = [self.sb("wT4", [128, 512], BF16) for _ in range(2)]
        tmpy = self.sb("tmpy", [128, 512])
        xdt_l = [self.sb("xdt", [128, SSM_DI], BF16) for _ in range(2)]
        xwa_l = [self.sb("xwa", [128, SSM_DI], BF16) for _ in range(2)]
        yo = [self.sb("yo", [128, SSM_DI], F32) for _ in range(2)]
        pT = self.ps("pT", [128, 1024], BF16)
        pE = self.ps("pE", [128, 512])
        pcb = self.ps("pcb", [128, 512])
        pseg = [self.ps("pseg", [128, 512]) for _ in range(2)]
        pYl = [self.ps("pY", [128, 512]) for _ in range(2)]
        pG = self.ps("pG", [128, 512])
        if d == 1:
            yfs = [self.sb("yf", [128, SSM_DI], F32) for _ in range(2)]
            zs = [self.sb("z", [128, SSM_DI], F32) for _ in range(2)]
            sz_l = [self.sb("sz", [128, SSM_DI], F32) for _ in range(2)]
            ssq = self.sb("ssq", [128, 1])
            sqj = self.sb("sqj", [128, SSM_DI], BF16)
            rs1, rt1 = self.sb("rs1", [128, 1]), self.sb("rt1", [128, 1])
            ygn = self.sb("ygn", [128, SSM_DI], BF16)
            ygT = [self.sb("ygT", [128, 16, 128], BF16) for _ in range(2)]
        XBv = self.XBCT.rearrange("(k p) n -> p k n", p=128)
        YGv = self.YGT.rearrange("(k p) n -> p k n", p=128)
        order = list(range(NKT)) if d == 0 else (list(range(nct - 1, -1, -1)) + list(range(NKT - 1, nct - 1, -1)))
        hic = [0]

        def binds(ci):
            kt = order[ci]
            p_ = ci % 2
            r0 = kt * 128
            c0 = (cfg.CS + r0) if kt < nct else (cfg.LS + r0 - cfg.C)
            return (xbcs[p_], dts[p_], r0, c0, xtok_l[p_], a_l[p_], ahb_l[p_], ah_l[p_], al_l[p_], a2_l[p_], E_l[p_],
                    e2_l[p_], cbm_l[p_], xdt_l[p_], xwa_l[p_])

        def prologue(ci):
            xbc, dt_, r0, c0, xtok, a, ahb, ah, al, a2, E, e2, cbm, xdt, xwa = binds(ci)
            fw.dma(xbc.t[:, :, :], XBv[:, :, c0:c0 + 128], xbc.b, reads=[self.db(self.XBCT)], writes=[xbc.b])
            fw.dma(dt_.t[:, :], self.DTT[r0:r0 + 128, :], dt_.b, reads=[self.db(self.DTT)], writes=[dt_.b])
            dtd = dt_.t[:, 32 * d:32 * d + 32]
            if d == 1:
                yf, z = yfs[ci % 2], zs[ci % 2]
                fw.dma(yf.t[:, :], self.YF[r0:r0 + 128, :], yf.b, reads=[self.db(self.YF)], writes=[yf.b])
                fw.dma(z.t[:, :], self.ZT[r0:r0 + 128, :], z.b, reads=[self.db(self.ZT)], writes=[z.b])
            for q in range(5):
                for j in range(4):
                    c = q * 4 + j
                    fw.op("pe", lambda e, c=c, j=j: e.transpose(pT.t[:, j * 128:(j + 1) * 128], xbc.t[:, c, :], self.ident_bf()),
                          reads=[xbc.b, self.cmb.b], writes=[pT.b], ticket=(j == 3), waw=(j == 0))
                if q % 2 == 0:
                    fw.op("act", lambda e, q=q: e.activation(out=xtok.t[:, q * 512:(q + 1) * 512], in_=pT.t[:, 0:512], func=AF.Copy),
                          reads=[pT.b], writes=[xtok.b], waw=False)
                else:
                    fw.op("dve", lambda e, q=q: e.tensor_copy(out=xtok.t[:, q * 512:(q + 1) * 512], in_=pT.t[:, 0:512]),
                          reads=[pT.b], writes=[xtok.b], waw=False)
            fw.op("dve", lambda e: e.tensor_tensor(out=a.t[:, :], in0=dtd, in1=arow.t[:, :], op=ALU.mult),
                  reads=[dt_.b, arow.b], writes=[a.b])
            fw.op("dve", lambda e: e.tensor_copy(out=ahb.t[:, :], in_=a.t[:, :]), reads=[a.b], writes=[ahb.b])
            fw.op("dve", lambda e: e.tensor_copy(out=ah.t[:, :], in_=ahb.t[:, :]), reads=[ahb.b], writes=[ah.b])
            fw.op("dve", lambda e: e.tensor_tensor(out=al.t[:, :], in0=a.t[:, :], in1=ah.t[:, :], op=ALU.subtract),
                  reads=[a.b, ah.b], writes=[al.b])
            fw.op("dve", lambda e: e.tensor_copy(out=a2.t[:, 0:32], in_=ah.t[:, :]), reads=[ah.b], writes=[a2.b])
            fw.op("dve", lambda e: e.tensor_copy(out=a2.t[:, 32:64], in_=al.t[:, :]), reads=[al.b], writes=[a2.b], waw=False)
            for q, lhs in enumerate((MR, ML, self.ones_bf.t[:, :])):
                for hl in range(2):
                    fw.op("pe", lambda e, q=q, lhs=lhs, hl=hl: e.matmul(
                        pE.t[:, q * 32:(q + 1) * 32], lhsT=lhs, rhs=a2.t[:, hl * 32:(hl + 1) * 32],
                        start=(hl == 0), stop=(hl == 1)), reads=[a2.b, self.cmb.b, self.ones_bf.b], writes=[pE.b],
                          ticket=(q == 2 and hl == 1), waw=(q == 0 and hl == 0))
            fw.op("act", lambda e: e.activation(out=E.t[:, :], in_=pE.t[:, 0:96], func=AF.Exp), reads=[pE.b], writes=[E.b])
            fw.op("act", lambda e: e.activation(out=e2.t[:, :], in_=dtd, func=AF.Ln), reads=[dt_.b], writes=[e2.b])
            for g in range(4):
                fw.op("pe", lambda e, g=g: e.matmul(pcb.t[:, g * 128:(g + 1) * 128], lhsT=xbc.t[:, 16 + g, :],
                                                    rhs=xbc.t[:, 20 + g, :], start=True, stop=True),
                      reads=[xbc.b], writes=[pcb.b], ticket=(g == 3), waw=(g == 0))
            for g in range(4):
                fw.op("dve", lambda e, g=g: e.tensor_tensor(out=cbm.t[:, g, :], in0=pcb.t[:, g * 128:(g + 1) * 128],
                                                            in1=MRf, op=ALU.mult),
                      reads=[pcb.b, self.cm.b], writes=[cbm.b], waw=(g == 0))
            if d == 1:
                sz = sz_l[ci % 2]
                fw.op("act", lambda e: e.activation(out=sz.t[:, :], in_=z.t[:, :], func=AF.Silu), reads=[z.b], writes=[sz.b])

        def main(ci):
            xbc, dt_, r0, c0, xtok, a, ahb, ah, al, a2, E, e2, cbm, xdt, xwa = binds(ci)
            dtd = dt_.t[:, 32 * d:32 * d + 32]
            if d == 1:
                yf, z, sz = yfs[ci % 2], zs[ci % 2], sz_l[ci % 2]
            items = [(g, hq) for g in range(4) for hq in range(2)]
            bufs = {}

            def stage1(idx):
                g, hq = items[idx]
                hi = hic[0]
                hic[0] += 1
                ps_, dc, Rh, Rl, w4 = pseg[hi % 2], dec[hi % 2], Rhs[hi % 2], Rls[hi % 2], wT4[hi % 2]
                bufs[idx] = (ps_, dc, w4)
                h0 = g * 8 + hq * 4
                fw.op("dve", lambda e: e.tensor_tensor(
                    out=Rh.t[:, :].rearrange("p (r c) -> p r c", c=128),
                    in0=MR.unsqueeze(1).broadcast_to([128, 4, 128]),
                    in1=ah.t[:, h0:h0 + 4].unsqueeze(2).broadcast_to([128, 4, 128]), op=ALU.mult),
                      reads=[ah.b, self.cmb.b], writes=[Rh.b])
                fw.op("dve", lambda e: e.tensor_tensor(
                    out=Rl.t[:, :].rearrange("p (r c) -> p r c", c=128),
                    in0=MR.unsqueeze(1).broadcast_to([128, 4, 128]),
                    in1=al.t[:, h0:h0 + 4].unsqueeze(2).broadcast_to([128, 4, 128]), op=ALU.mult),
                      reads=[al.b, self.cmb.b], writes=[Rl.b])
                fw.op("pe", lambda e: e.matmul(ps_.t[:, :], lhsT=ML, rhs=Rh.t[:, :], start=True, stop=False),
                      reads=[Rh.b, self.cmb.b], writes=[ps_.b], ticket=False)
                fw.op("pe", lambda e: e.matmul(ps_.t[:, :], lhsT=ML, rhs=Rl.t[:, :], start=False, stop=True),
                      reads=[Rl.b, Rh.b, self.cmb.b], writes=[ps_.b])
                for r4 in range(4):
                    fw.op("act", lambda e, r4=r4: e.activation(
                        out=dc.t[:, r4 * 128:(r4 + 1) * 128], in_=ps_.t[:, r4 * 128:(r4 + 1) * 128], func=AF.Exp,
                        bias=e2.t[:, h0 + r4:h0 + r4 + 1], scale=1.0), reads=[ps_.b, e2.b], writes=[dc.b], waw=(r4 == 0))

            def stage2(idx):
                g, hq = items[idx]
                ps_, dc, w4 = bufs[idx]
                pY = pYl[g % 2]
                fw.op("dve", lambda e: e.tensor_tensor(
                    out=w4.t[:, :].rearrange("p (r c) -> p r c", c=128), in0=dc.t[:, :].rearrange("p (r c) -> p r c", c=128),
                    in1=cbm.t[:, g, :].unsqueeze(1).broadcast_to([128, 4, 128]), op=ALU.mult),
                      reads=[dc.b, cbm.b], writes=[w4.b])
                ecol = 127 if d == 0 else 0
                for r4 in range(4):
                    r = hq * 4 + r4
                    h = g * 8 + r
                    xh = xtok.t[:, h * 64:(h + 1) * 64]
                    fw.op("act", lambda e, r4=r4, h=h, xh=xh: e.activation(
                        out=xwa.t[:, h * 64:(h + 1) * 64], in_=xh, func=AF.Copy,
                        scale=dc.t[:, r4 * 128 + ecol:r4 * 128 + ecol + 1]), reads=[xtok.b, dc.b], writes=[xwa.b], waw=(h == 0))
                    if d == 0:
                        fw.op("pe", lambda e, r=r, r4=r4, xh=xh: e.matmul(
                            pY.t[:, r * 64:(r + 1) * 64], lhsT=w4.t[:, r4 * 128:(r4 + 1) * 128], rhs=xh,
                            start=True, stop=False), reads=[w4.b, xtok.b], writes=[pY.b], ticket=False, waw=(r == 0))
                        fw.op("pe", lambda e, r=r, h=h, xh=xh: e.matmul(
                            pY.t[:, r * 64:(r + 1) * 64], lhsT=DI.t[:, h * 128:(h + 1) * 128], rhs=xh,
                            start=False, stop=True), reads=[w4.b, xtok.b, DI.b], writes=[pY.b], ticket=(r4 == 3), waw=False)
                    else:
                        fw.op("pe", lambda e, r=r, r4=r4, xh=xh: e.matmul(
                            pY.t[:, r * 64:(r + 1) * 64], lhsT=w4.t[:, r4 * 128:(r4 + 1) * 128], rhs=xh,
                            start=True, stop=True), reads=[w4.b, xtok.b], writes=[pY.b], ticket=(r4 == 3), waw=(r == 0))

            def epilogue(g):
                pY = pYl[g % 2]
                pYs = pG
                pdS = pG
                fw.op("pe", lambda e: e.matmul(pYs.t[:, :], lhsT=xbc.t[:, 20 + g, :], rhs=Sb.t[:, g * 512:(g + 1) * 512],
                                               start=True, stop=True), reads=[xbc.b, Sb.b], writes=[pYs.b])
                fw.op("dve", lambda e: e.tensor_tensor(
                    out=tmpy.t[:, :].rearrange("p (r c) -> p r c", c=64), in0=pYs.t[:, :].rearrange("p (r c) -> p r c", c=64),
                    in1=bc3(E.t[:, g * 8:(g + 1) * 8]), op=ALU.mult), reads=[pYs.b, E.b], writes=[tmpy.b])
                yo_ = yo[ci % 2]
                gs_ = slice(g * 512, (g + 1) * 512)
                fw.op("dve", lambda e: e.tensor_tensor(out=yo_.t[:, gs_], in0=pY.t[:, :], in1=tmpy.t[:, :], op=ALU.add),
                      reads=[pY.b, tmpy.b], writes=[yo_.b], waw=(g == 0))
                if d == 1:
                    fw.op("dve", lambda e: e.tensor_tensor(out=yo_.t[:, gs_], in0=yo_.t[:, gs_], in1=yf.t[:, gs_], op=ALU.add),
                          reads=[yo_.b, yf.b], writes=[yo_.b])
                    fw.op("dve", lambda e: e.tensor_tensor(out=yo_.t[:, gs_], in0=yo_.t[:, gs_], in1=sz.t[:, gs_], op=ALU.mult),
                          reads=[yo_.b, sz.b], writes=[yo_.b])
                fw.op("pe", lambda e: e.matmul(
                    pdS.t[:, :], lhsT=xtok.t[:, 2048 + g * 128:2048 + (g + 1) * 128], rhs=xwa.t[:, gs_], start=True, stop=True),
                      reads=[xtok.b, xwa.b], writes=[pdS.b])
                fw.op("dve", lambda e: e.tensor_tensor(
                    out=S.t[:, gs_].rearrange("p (r c) -> p r c", c=64), in0=S.t[:, gs_].rearrange("p (r c) -> p r c", c=64),
                    in1=bc3(E.t[:, 64 + g * 8:64 + (g + 1) * 8]), op=ALU.mult), reads=[S.b, E.b], writes=[S.b])
                fw.op("dve", lambda e: e.tensor_tensor(out=S.t[:, gs_], in0=S.t[:, gs_], in1=pdS.t[:, :], op=ALU.add),
                      reads=[S.b, pdS.b], writes=[S.b])
                fw.op("act", lambda e: e.activation(out=Sb.t[:, gs_], in_=S.t[:, gs_], func=AF.Copy),
                      reads=[S.b], writes=[Sb.b])

            stage1(0)
            for idx in range(8):
                if idx + 1 < 8:
                    stage1(idx + 1)
                stage2(idx)
                if items[idx][1] == 1:
                    epilogue(items[idx][0])
            yo_ = yo[ci % 2]
            if d == 0:
                fw.dma(self.YF[r0:r0 + 128, :], yo_.t[:, :], yo_.b, reads=[yo_.b], writes=[self.db(self.YF)], waw=False,
                       stream="pool")
            else:
                fw.op("pool", lambda e: e.memset(ssq.t[:, :], 0.0), writes=[ssq.b])
                fw.op("act", lambda e: e.activation(out=sqj.t[:, :], in_=yo_.t[:, :], func=AF.Square, accum_out=ssq.t[:, :]),
                      reads=[yo_.b], writes=[sqj.b, ssq.b])
                fw.op("act", lambda e: e.activation(out=rt1.t[:, :], in_=ssq.t[:, :], func=AF.Sqrt, scale=1.0 / SSM_DI, bias=EPS),
                      reads=[ssq.b], writes=[rt1.b])
                fw.op("dve", lambda e: e.reciprocal(out=rs1.t[:, :], in_=rt1.t[:, :]), reads=[rt1.b], writes=[rs1.b])
                fw.op("dve", lambda e: e.scalar_tensor_tensor(out=ygn.t[:, :], in0=yo_.t[:, :], scalar=rs1.t[:, 0:1],
                                                              in1=ng.t[:, :], op0=ALU.mult, op1=ALU.mult),
                      reads=[yo_.b, rs1.b, ng.b], writes=[ygn.b])
                yt = ygT[ci % 2]
                for q in range(4):
                    for j in range(4):
                        c = q * 4 + j
                        fw.op("pe", lambda e, c=c, j=j: e.transpose(pT.t[:, j * 128:(j + 1) * 128],
                                                                    ygn.t[:, c * 128:(c + 1) * 128], self.ident_bf()),
                              reads=[ygn.b, self.cmb.b], writes=[pT.b], ticket=(j == 3), waw=(j == 0))
                    if q % 2 == 0:
                        fw.op("act", lambda e, q=q: e.activation(out=yt.t[:, q * 4:(q + 1) * 4, :],
                                                                 in_=pT.t[:, 0:512].rearrange("p (a b) -> p a b", b=128), func=AF.Copy),
                              reads=[pT.b], writes=[yt.b], waw=False)
                    else:
                        fw.op("dve", lambda e, q=q: e.tensor_copy(out=yt.t[:, q * 4:(q + 1) * 4, :],
                                                                  in_=pT.t[:, 0:512].rearrange("p (a b) -> p a b", b=128)),
                              reads=[pT.b], writes=[yt.b], waw=False)
                fw.dma(YGv[:, :, c0:c0 + 128], yt.t[:, :, :], yt.b, reads=[yt.b], writes=[self.db(self.YGT)], waw=False,
                       stream="pool")
        prologue(0)
        for ci in range(len(order)):
            if ci + 1 < len(order):
                prologue(ci + 1)
            main(ci)
        self.end()


def build_program(T, voff, roff, nv, nr, n_layers=DEPTH, debug=False):
    p = Prog(T, voff, roff, nv, nr, n_layers=n_layers, debug=debug)
    p.setup()
    p.mod_phase()
    Xa, Xb = p.XA, p.XB
    for l in range(n_layers):
        if l % 2 == 0:
            i = l // 2
            p.attn_inproj(l, Xa)
            p.attn_core(l)
            p.proj_res_norm(l, Xa, Xb, p.OT, KD, lambda k, i=i: p.i_awo[i, k * 128:(k + 1) * 128, :])
        else:
            i = l // 2
            p.ssm_inproj(l, Xa)
            p.ssm_scan(l, 0)
            p.ssm_scan(l, 1)
            p.proj_res_norm(l, Xa, Xb, p.YGT, SSM_DI // 128, lambda k, i=i: p.i_swo[i, k * 128:(k + 1) * 128, :])
        p.ffn_in(l)
        p.ffn_out(l, Xb, Xa)
    p.final_norm(Xa)
    p.pes.close()
    p.fw.close()
    return p


_CACHE = {}


def run(inputs, T=8192, n_layers=DEPTH, debug=False, cores=NCORES):
    cfg, voff, roff, shared = prep_shared(inputs, T)
    key = (T, n_layers, debug)
    if key not in _CACHE:
        _CACHE[key] = build_program(T, voff, roff, shared["vecs"].shape[1], shared["rowv"].shape[1],
                                    n_layers=n_layers, debug=debug)
    p = _CACHE[key]
    in_maps = []
    for b in range(cores):
        m = dict(shared)
        m.update(prep_core(inputs, b, T))
        in_maps.append(m)
    res = run_bass_kernel_spmd(p.nc, in_maps, core_ids=list(range(cores)))
    return p, res


def kernel(**inputs):
    p, res = run(inputs)
    out = np.stack([np.ascontiguousarray(r["outT"].T) for r in res.results], axis=0)
    return out.astype(np.float32)
```

```python
import math
from contextlib import ExitStack

import numpy as np
import concourse.bass as bass
import concourse.mybir as mybir
from concourse.bass_utils import run_bass_kernel_spmd

F32 = mybir.dt.float32
BF16 = mybir.dt.bfloat16
AF = mybir.ActivationFunctionType
ALU = mybir.AluOpType


class Sem:
    def __init__(self, h, name):
        self.h = h
        self.cnt = 0
        self.name = name


class Buf:
    def __init__(self, ap=None, name=""):
        self.ap = ap
        self.name = name
        self.last_w = {}
        self.readers = {}
        self.gen_deps = {}
        self.dsem = None


def _merge(dst, src):
    for s, v in src.items():
        if dst.get(s, 0) < v:
            dst[s] = v


class FW:
    def __init__(self, nc, n_dma_sems=48, n_spare=28):
        self.nc = nc
        self.es = ExitStack()
        self.streams = {
            "pe": nc.tensor,
            "act": nc.scalar,
            "dve": nc.vector,
            "pool": nc.gpsimd,
            "sp": nc.sync,
        }
        self.esem = {}
        for k in ("pe", "act", "dve", "pool"):
            self.esem[k] = Sem(self.es.enter_context(nc.semaphore("e_" + k)), k)
        self.known = {k: {} for k in self.streams}
        self.free_dsems = [
            Sem(self.es.enter_context(nc.semaphore("d%d" % i)), "d%d" % i) for i in range(n_dma_sems)
        ]
        self.spare = [Sem(self.es.enter_context(nc.semaphore("x%d" % i)), "x%d" % i) for i in range(n_spare)]
        self.phase_bufs = []
        self.pers_bufs = []
        self.n_inst = 0
        self.n_wait = 0

    def buf(self, ap=None, name="", persistent=False):
        b = Buf(ap, name)
        (self.pers_bufs if persistent else self.phase_bufs).append(b)
        return b

    def _wait_for(self, stream, deps, fold=False):
        eng = self.streams[stream]
        kn = self.known[stream]
        need = [(s, v) for s, v in deps.items() if kn.get(s, 0) < v]
        last = None
        if fold and need:
            last = need.pop()
            kn[last[0]] = last[1]
        for s, v in need:
            eng.wait_ge(s.h, v)
            kn[s] = v
            self.n_wait += 1
        return last

    def _deps(self, stream, reads, writes, waw=True):
        deps = {}
        for r in reads:
            _merge(deps, r.last_w)
        for w in writes:
            if waw or w.readers:
                _merge(deps, w.last_w)
                _merge(deps, w.readers)
            else:
                _merge(deps, w.gen_deps)
        if stream == "pe":
            deps.pop(self.esem["pe"], None)
        return deps

    def _commit(self, t, reads, writes, waw):
        for w in writes:
            if waw or w.readers:
                g = dict(w.last_w)
                _merge(g, w.readers)
                w.gen_deps = g
                w.last_w = dict(t)
                w.readers = {}
            else:
                _merge(w.last_w, t)
        for r in reads:
            _merge(r.readers, t)

    def op(self, stream, fn, reads=(), writes=(), ticket=True, waw=True):
        deps = self._deps(stream, reads, writes, waw)
        last = self._wait_for(stream, deps, fold=True)
        inst = fn(self.streams[stream])
        if last is not None:
            inst._wait_ge(last[0].h, last[1])
        self.n_inst += 1
        if ticket:
            s = self.esem[stream]
            if s.cnt >= 30000:
                s = self.spare.pop()
                self.esem[stream] = s
            s.cnt += 1
            inst.then_inc(s.h, 1)
            self._commit({s: s.cnt}, reads, writes, waw)
        return inst

    def dma(self, out_ap, in_ap, owner, reads=(), writes=(), stream="sp", waw=True, **kw):
        deps = self._deps(stream, reads, writes, waw)
        last = self._wait_for(stream, deps, fold=True)
        if owner.dsem is None:
            owner.dsem = self.free_dsems.pop(0)
        s = owner.dsem
        s.cnt += 16
        inst = self.streams[stream].dma_start(out=out_ap, in_=in_ap, **kw)
        if last is not None:
            inst._wait_ge(last[0].h, last[1])
        inst.then_inc(s.h, 16)
        self.n_inst += 1
        self._commit({s: s.cnt}, reads, writes, waw)

    def barrier(self, clear=True):
        alld = {}
        for s in self.esem.values():
            if s.cnt:
                alld[s] = s.cnt
        for b in self.phase_bufs + self.pers_bufs:
            if b.dsem is not None:
                alld[b.dsem] = b.dsem.cnt
        for st in self.streams:
            d = dict(alld)
            self._wait_for(st, d)
        for b in self.phase_bufs + self.pers_bufs:
            if b.dsem is not None:
                self.free_dsems.append(b.dsem)
                b.dsem = None
            b.last_w = {}
            b.readers = {}
            b.gen_deps = {}
        if clear:
            self.phase_bufs = []

    def close(self):
        self.es.close()


D = 1024
KD = D // 128
DEPTH = 4
CTX = 256
GRID_W = 64
HD = 64
EPS = 1e-6
DFF = 2816
NFF = DFF // 128
SSM_DI = 2048
SSM_XBC = 3072
SSM_IN = 5184
SSM_H = 32
NCORES = 8
ATT_FM = 1664
ATT_EXT = 2 * ATT_FM + 640


class Cfg:
    def __init__(self, T):
        self.T = T
        self.C = CTX
        self.CS = 1
        self.LS = CTX + 3
        self.NP = CTX + T + 4
        self.NTOK = CTX + T
        self.seqs = [("ctx", self.CS, CTX, 1, 0), ("lat", self.LS, T, 0, CTX)]


def _partner():
    p = np.arange(64)
    return np.where((p % 32) < 16, p + 16, p - 16)


class VecLayout:
    def __init__(self):
        self.off = {}
        self.n = 0
        self.cols = []

    def add(self, name, arr):
        arr = np.ascontiguousarray(arr, dtype=np.float32).reshape(128, -1)
        self.off[name] = (self.n, arr.shape[1])
        self.n += arr.shape[1]
        self.cols.append(arr)

    def build(self):
        return np.ascontiguousarray(np.concatenate(self.cols, axis=1))


def fm(v):
    v = np.asarray(v, dtype=np.float32)
    return np.ascontiguousarray(v.reshape(-1, 128).T)


def prep_shared(inp, T):
    cfg = Cfg(T)
    vl = VecLayout()
    rows = {}
    for l in range(DEPTH):
        vl.add("modb%d" % l, fm(inp["mod_b"][l]))
        vl.add("gmix%d" % l, fm(inp["norm_mix_g"][l]))
        vl.add("gffn%d" % l, fm(inp["norm_ffn_g"][l]))
        cw = inp["ffn_conv_w"][l]
        for i in range(3):
            vl.add("fcw%d_%d" % (l, i), fm(cw[i]))
        vl.add("fcb%d" % l, fm(inp["ffn_conv_b"][l]))
    vl.add("gfin", fm(inp["final_norm_g"]))
    pt = _partner()
    for i in range(DEPTH // 2 + DEPTH % 2):
        gq = np.asarray(inp["gqa_q_norm_g"][i], np.float32)
        gk = np.asarray(inp["gqa_k_norm_g"][i], np.float32)
        vl.add("gq%d" % i, np.concatenate([gq, gq])[:, None])
        vl.add("gqs%d" % i, np.concatenate([gq[pt], gq[pt]])[:, None])
        vl.add("gk%d" % i, np.concatenate([gk, gk])[:, None])
        vl.add("gks%d" % i, np.concatenate([gk[pt], gk[pt]])[:, None])
        vl.add("subg%d" % i, np.asarray(inp["diff_subln_g"][i], np.float32)[:, None])
    for i in range(DEPTH // 2):
        cw = inp["ssm_conv_w"][i]
        for j in range(3):
            vl.add("scw%d_%d" % (i, j), fm(cw[j]))
        vl.add("scb%d" % i, fm(inp["ssm_conv_b"][i]))
    vecs = vl.build()

    rl = VecLayout()

    def radd(name, v):
        v = np.asarray(v, np.float32).reshape(1, -1)
        rl.off[name] = (rl.n, v.shape[1])
        rl.n += v.shape[1]
        rl.cols.append(v)

    for i in range((DEPTH + 1) // 2):
        radd("lq1_%d" % i, inp["diff_lq1"][i])
        radd("lk1_%d" % i, inp["diff_lk1"][i])
        radd("lq2_%d" % i, inp["diff_lq2"][i])
        radd("lk2_%d" % i, inp["diff_lk2"][i])
    for i in range(DEPTH // 2):
        radd("dtb%d" % i, inp["ssm_dt_bias"][i])
        radd("alog%d" % i, inp["ssm_a_log"][i])
        radd("ssd%d" % i, inp["ssm_d"][i])
        radd("sng%d" % i, inp["ssm_norm_g"][i])
    rowv = np.ascontiguousarray(np.concatenate(rl.cols, axis=1))

    w_ext = []
    for i in range((DEPTH + 1) // 2):
        w = np.asarray(inp["attn_w_in"][i])
        qa, ka, va = w[:, 0:512], w[:, 512:1024], w[:, 1024:1536]
        qb, kb, vb = w[:, 1536:2048], w[:, 2048:2176], w[:, 2176:2304]
        qbp = np.concatenate(
            [np.concatenate([qb[:, j * 64:(j + 1) * 64], qb[:, (4 + j) * 64:(5 + j) * 64]], axis=1) for j in range(4)],
            axis=1)
        fmw = np.concatenate([qa, ka, qbp, kb], axis=1)
        idx = (np.arange(ATT_FM) // 64) * 64 + pt[np.arange(ATT_FM) % 64]
        w_ext.append(np.concatenate([fmw, fmw[:, idx], va, vb], axis=1))
    w_ext = np.ascontiguousarray(np.stack(w_ext))

    NP = cfg.NP
    cosT = np.ones((128, NP), np.float32)
    sinT = np.zeros((128, NP), np.float32)
    t = np.arange(T)
    row = (t // GRID_W).astype(np.float32)
    col = (t % GRID_W).astype(np.float32)
    inv = (1.0 / (10000.0 ** (np.arange(16, dtype=np.float32) / 16))).astype(np.float32)
    for p in range(64):
        q, i = p // 16, p % 16
        ang = ((row if q < 2 else col) * inv[i]).astype(np.float32)
        sgn = -1.0 if q in (0, 2) else 1.0
        for rep in (0, 64):
            cosT[p + rep, cfg.LS:cfg.LS + T] = np.cos(ang)
            sinT[p + rep, cfg.LS:cfg.LS + T] = sgn * np.sin(ang)

    tt = np.arange(128)
    consts = {
        "ident": np.eye(128, dtype=np.float32),
        "ML_f": (tt[:, None] > tt[None, :]).astype(np.float32),
        "MR_f": (tt[:, None] <= tt[None, :]).astype(np.float32),
        "ML_b": (tt[:, None] < tt[None, :]).astype(np.float32),
        "MR_b": (tt[:, None] >= tt[None, :]).astype(np.float32),
    }
    bd = np.zeros((128, 128), np.float32)
    bd[:64, :64] = 1
    bd[64:, 64:] = 1
    consts["bd"] = bd
    cmat = np.ascontiguousarray(
        np.concatenate([consts[k] for k in ("ident", "ML_f", "MR_f", "ML_b", "MR_b", "bd")], axis=1))

    shared = {
        "vecs": vecs, "rowv": rowv, "w_ext": w_ext, "cosT": cosT, "sinT": sinT, "cmat": cmat,
        "mod_w": np.asarray(inp["mod_w"], np.float32),
        "attn_w_out": np.asarray(inp["attn_w_out"], np.float32),
        "ssm_w_in": np.asarray(inp["ssm_w_in"], np.float32),
        "ssm_w_out": np.asarray(inp["ssm_w_out"], np.float32),
        "ffn_w_in": np.asarray(inp["ffn_w_in"], np.float32),
        "ffn_w_out": np.asarray(inp["ffn_w_out"], np.float32),
    }
    return cfg, vl.off, rl.off, shared


def prep_core(inp, b, T):
    xT = np.ascontiguousarray(np.asarray(inp["x"][b, :T]).T)
    cxT = np.ascontiguousarray(np.asarray(inp["ctx"][b]).T)
    cT = np.stack([fm(inp["c"][b]), fm(inp["c_ctx"])], axis=2)
    return {"xT": xT, "cxT": cxT, "cT": np.ascontiguousarray(cT.reshape(128, 16))}


class TB:
    def __init__(self, t, b, bs=None):
        self.t = t
        self.b = b
        self.bs = bs


def blocks_of(n, size):
    out = []
    t0 = 0
    while t0 < n:
        out.append((t0, min(size, n - t0)))
        t0 += size
    return out


class Prog:
    def __init__(self, T, voff, roff, nv, nr, n_layers=DEPTH, debug=False):
        self.cfg = Cfg(T)
        self.voff, self.roff = voff, roff
        self.debug = debug
        self.n_layers = n_layers
        nc = bass.Bass("TRN2", target_bir_lowering=False)
        self.nc = nc
        self.fw = FW(nc)
        cfg = self.cfg
        NP, NTOK = cfg.NP, cfg.NTOK
        di = lambda name, shape, dt=F32: nc.dram_tensor(name, shape, dt, kind="ExternalInput").ap()
        self.i_xT = di("xT", [D, T])
        self.i_cxT = di("cxT", [D, CTX])
        self.i_cT = di("cT", [128, 16])
        self.i_vecs = di("vecs", [128, nv])
        self.i_rowv = di("rowv", [1, nr])
        self.i_wext = di("w_ext", [(DEPTH + 1) // 2, D, ATT_EXT])
        self.i_cos = di("cosT", [128, NP])
        self.i_sin = di("sinT", [128, NP])
        self.i_cmat = di("cmat", [128, 768])
        self.i_modw = di("mod_w", [DEPTH, D, 6 * D])
        self.i_awo = di("attn_w_out", [(DEPTH + 1) // 2, D, D])
        self.i_swi = di("ssm_w_in", [DEPTH // 2, D, SSM_IN])
        self.i_swo = di("ssm_w_out", [DEPTH // 2, SSM_DI, D])
        self.i_fwi = di("ffn_w_in", [DEPTH, D, 2 * DFF])
        self.i_fwo = di("ffn_w_out", [DEPTH, DFF, D])
        self.o_out = nc.dram_tensor("outT", [D, T], F32, kind="ExternalOutput").ap()
        self.dbg = {}
        kind = "ExternalOutput" if debug else "Internal"

        def scr(name, shape, dt):
            ap = nc.dram_tensor(name, shape, dt, kind=kind).ap()
            if debug:
                self.dbg[name] = ap
            return ap

        self.XA = scr("XA", [D, NP], F32)
        self.XB = scr("XB", [D, NP], F32)
        self.HT = scr("HT", [D, NP], BF16)
        self.QT = scr("QT", [D, NP], BF16)
        self.KT = scr("KT", [5 * 128, NP], BF16)
        self.VT = scr("VT", [NTOK, 640], BF16)
        self.OT = scr("OT", [D, NP], BF16)
        self.UT = scr("UT", [DFF, NP], BF16)
        self.XBCT = scr("XBCT", [SSM_XBC, NP], BF16)
        self.ZT = scr("ZT", [NTOK, SSM_DI], F32)
        self.DTT = scr("DTT", [NTOK, 64], F32)
        self.YF = scr("YF", [NTOK, SSM_DI], F32)
        self.YGT = scr("YGT", [SSM_DI, NP], BF16)
        self.dram_b = {}
        self.pes = ExitStack()
        self.ph = None
        self._names = 0

    def _nm(self, name):
        self._names += 1
        return "%s_%d" % (name, self._names)

    def sb(self, name, shape, dt=F32, nb=0, pers=False):
        st = self.pes if pers else self.ph
        t = st.enter_context(self.nc.sbuf_tensor(self._nm(name), shape, dt))
        b = self.fw.buf(name=name, persistent=pers)
        bs = [self.fw.buf(name=name + str(i), persistent=pers) for i in range(nb)] if nb else None
        return TB(t, b, bs)

    def ps(self, name, shape, dt=F32):
        t = self.ph.enter_context(self.nc.psum_tensor(self._nm(name), shape, dt))
        return TB(t, self.fw.buf(name=name))

    def db(self, ap):
        k = ap.name if hasattr(ap, "name") else id(ap)
        if k not in self.dram_b:
            self.dram_b[k] = self.fw.buf(name="dram", persistent=True)
        return self.dram_b[k]

    def begin(self):
        self.ph = ExitStack()

    def end(self):
        self.fw.barrier()
        self.ph.close()
        self.ph = None

    def vec(self, name, j=0, w=1):
        o, n = self.voff[name]
        return self.vecs.t[:, o + j:o + j + w]

    def load_w(self, dst, src_rows, ncols, col0=0, dst_col0=0, kc=None):
        fw = self.fw
        kc = kc if kc is not None else dst.t.shape[1]
        PIECE = 1408
        main_ph = self.ph
        self.ph = ExitStack()
        wst = [self.sb("wst", [128, PIECE], F32) for _ in range(3)]
        wi = 0
        for k in range(kc):
            for (c0, n) in blocks_of(ncols, PIECE):
                st = wst[wi % 3]
                eng = ("dve", "pool", "act")[wi % 3]
                wi += 1
                src = src_rows(k)[:, col0 + c0:col0 + c0 + n]
                fw.dma(st.t[:, 0:n], src, st.b, writes=[st.b])
                d = dst.t[:, k, dst_col0 + c0:dst_col0 + c0 + n]
                if eng == "act":
                    fw.op("act", lambda e, d=d, st=st, n=n: e.activation(out=d, in_=st.t[:, 0:n], func=AF.Copy),
                          reads=[st.b], writes=[dst.b], waw=False)
                else:
                    fw.op(eng, lambda e, d=d, st=st, n=n: e.tensor_copy(out=d, in_=st.t[:, 0:n]),
                          reads=[st.b], writes=[dst.b], waw=False)
        fw.barrier(clear=False)
        self.ph.close()
        self.ph = main_ph

    def rstd(self, ss, out, tmp, scale, n):
        fw = self.fw
        fw.op("act", lambda e: e.activation(out=tmp.t[:, 0:n], in_=ss.t[:, 0:n], func=AF.Sqrt, scale=scale, bias=EPS),
              reads=[ss.b], writes=[tmp.b])
        fw.op("dve", lambda e: e.reciprocal(out=out.t[:, 0:n], in_=tmp.t[:, 0:n]), reads=[tmp.b], writes=[out.b])

    def rstd_le(self, ss, out, tmp, scale, n):
        fw = self.fw
        fw.op("act", lambda e: e.activation(out=tmp.t[:, 0:n], in_=ss.t[:, 0:n], func=AF.Ln, scale=scale, bias=EPS),
              reads=[ss.b], writes=[tmp.b])
        fw.op("act", lambda e: e.activation(out=out.t[:, 0:n], in_=tmp.t[:, 0:n], func=AF.Exp, scale=-0.5),
              reads=[tmp.b], writes=[out.b])

    def norm_mod(self, x, n, ss, sq, rs, tmp, gs, sh, out, out_dt_bf16=True):
        fw = self.fw
        for k in range(KD):
            fw.op("act", lambda e, k=k: e.activation(out=sq.t[:, k, 0:n], in_=x.t[:, k, 0:n], func=AF.Square),
                  reads=[x.b], writes=[sq.b], waw=False)
        for k in range(KD):
            fw.op("pe", lambda e, k=k: e.matmul(ss.t[:, 0:n], lhsT=self.ones_bf.t[:, :], rhs=sq.t[:, k, 0:n],
                                                start=(k == 0), stop=(k == KD - 1)),
                  reads=[sq.b, self.ones_bf.b], writes=[ss.b], ticket=(k == KD - 1))
        self.rstd(ss, rs, tmp, 1.0 / D, n)
        for k in range(KD):
            if sh is None:
                fw.op("dve", lambda e, k=k: e.scalar_tensor_tensor(
                    out=out.t[:, k, 0:n], in0=x.t[:, k, 0:n], scalar=gs(k), in1=rs.t[:, 0:n],
                    op0=ALU.mult, op1=ALU.mult), reads=[x.b, rs.b], writes=[out.b], waw=False)
            else:
                tm = self._nm_tmp[k % 2]
                fw.op("dve", lambda e, k=k, tm=tm: e.scalar_tensor_tensor(
                    out=tm.t[:, 0:n], in0=x.t[:, k, 0:n], scalar=gs(k), in1=rs.t[:, 0:n],
                    op0=ALU.mult, op1=ALU.mult), reads=[x.b, rs.b], writes=[tm.b])
                fw.op("act", lambda e, k=k, tm=tm: e.activation(out=out.t[:, k, 0:n], in_=tm.t[:, 0:n],
                                                         func=AF.Identity, bias=sh(k), scale=1.0),
                      reads=[tm.b], writes=[out.b], waw=False)

    def norm_tiles(self, nmax=512):
        self._nm_tmp = [self.sb("nmt", [128, nmax], F32) for _ in range(2)]
        return dict(ss=self.ps("ss", [128, 512], F32), sq=self.sb("sq", [128, KD, nmax], BF16),
                    rs=self.sb("rs", [128, nmax], F32), tmp=self.sb("rtmp", [128, nmax], F32))

    def load_cols(self, dst, dview, seq, t0, n, halo, extra_reads=()):
        fw = self.fw
        _, start, ln, _, _ = seq
        lo, hi = t0 - halo, t0 + n + halo
        clo, chi = max(lo, 0), min(hi, ln)
        if clo > lo:
            fw.op("pool", lambda e: e.memset(dst.t[:, :, 0:clo - lo], 0.0), writes=[dst.b])
        if chi < hi:
            fw.op("pool", lambda e: e.memset(dst.t[:, :, chi - lo:hi - lo], 0.0), writes=[dst.b],
                  waw=(clo == lo))
        fw.dma(dst.t[:, :, clo - lo:chi - lo], dview[:, :, start + clo:start + chi], dst.b,
               reads=[self.db(dview)], writes=[dst.b], waw=(clo == lo and chi == hi))

    def dvv(self, l, v, which):
        base = ((l * 2 + v) * 6 + which) * 8
        return lambda k: self.dv.t[:, base + k:base + k + 1]

    def setup(self):
        fw, cfg = self.fw, self.cfg
        nv = self.i_vecs.shape[1]
        self.vecs = self.sb("vecs", [128, nv], F32, pers=True)
        self.cm = self.sb("cmat", [128, 768], F32, pers=True)
        self.cmb = self.sb("cmatb", [128, 768], BF16, pers=True)
        self.ones_bf = self.sb("ones", [128, 128], BF16, pers=True)
        self.dv = self.sb("dv", [128, DEPTH * 2 * 6 * 8], F32, pers=True)
        self.begin()
        fw.dma(self.vecs.t[:, :], self.i_vecs[:, :], self.vecs.b, writes=[self.vecs.b])
        fw.dma(self.cm.t[:, :], self.i_cmat[:, :], self.cm.b, writes=[self.cm.b])
        fw.op("dve", lambda e: e.tensor_copy(out=self.cmb.t[:, :], in_=self.cm.t[:, :]), reads=[self.cm.b],
              writes=[self.cmb.b])
        fw.op("pool", lambda e: e.memset(self.ones_bf.t[:, :], 1.0), writes=[self.ones_bf.b])
        dummy = self.fw.buf(name="x0")
        XAv = self.XA.rearrange("(k p) n -> p k n", p=128)
        xin = self.i_xT.rearrange("(k p) n -> p k n", p=128)
        cin = self.i_cxT.rearrange("(k p) n -> p k n", p=128)
        for k in range(KD):
            fw.dma(XAv[:, k, cfg.LS:cfg.LS + cfg.T], xin[:, k, :], dummy, writes=[self.db(self.XA)], waw=False)
            fw.dma(XAv[:, k, cfg.CS:cfg.CS + cfg.C], cin[:, k, :], dummy, writes=[self.db(self.XA)], waw=False)
        self.end()

    def ident_bf(self):
        return self.cmb.t[:, 0:128]

    def cmask(self, name, bf=True):
        j = ("ident", "ML_f", "MR_f", "ML_b", "MR_b", "bd").index(name)
        return (self.cmb if bf else self.cm).t[:, j * 128:(j + 1) * 128]

    def mod_phase(self):
        fw = self.fw
        self.begin()
        ct = self.sb("ct", [128, 16], F32)
        sc = self.sb("sc", [128, 16], BF16)
        fw.dma(ct.t[:, :], self.i_cT[:, :], ct.b, writes=[ct.b])
        fw.op("act", lambda e: e.activation(out=sc.t[:, :], in_=ct.t[:, :], func=AF.Silu), reads=[ct.b], writes=[sc.b])
        modT = self.sb("modT", [128, DEPTH, 48, 2], F32)
        wst = [self.sb("mwst", [128, KD, 512], F32) for _ in range(4)]
        wb = [self.sb("mwb", [128, KD, 512], BF16) for _ in range(4)]
        pm = [self.ps("pm", [128, 512], F32) for _ in range(4)]
        it = 0
        for l in range(self.n_layers):
            mw = self.i_modw[l].rearrange("(k p) n -> p k n", p=128)
            for cg in range(12):
                s_, b_, p_ = wst[it % 4], wb[it % 4], pm[it % 4]
                fw.dma(s_.t[:, :, :], mw[:, :, cg * 512:(cg + 1) * 512], s_.b, writes=[s_.b])
                if it % 2 == 0:
                    fw.op("dve", lambda e, s_=s_, b_=b_: e.tensor_copy(out=b_.t[:, :, :], in_=s_.t[:, :, :]),
                          reads=[s_.b], writes=[b_.b])
                else:
                    fw.op("act", lambda e, s_=s_, b_=b_: e.activation(out=b_.t[:, :, :], in_=s_.t[:, :, :], func=AF.Copy),
                          reads=[s_.b], writes=[b_.b])
                for j in range(4):
                    for k in range(KD):
                        fw.op("pe", lambda e, j=j, k=k, b_=b_, p_=p_: e.matmul(
                            p_.t[:, 2 * j:2 * j + 2], lhsT=b_.t[:, k, j * 128:(j + 1) * 128],
                            rhs=sc.t[:, 2 * k:2 * k + 2], start=(k == 0), stop=(k == KD - 1)),
                              reads=[b_.b, sc.b], writes=[p_.b], ticket=(k == KD - 1 and j == 3))
                for j in range(4):
                    fw.op("dve", lambda e, j=j, p_=p_, l=l, cg=cg: e.tensor_scalar(
                        out=modT.t[:, l, cg * 4 + j, :], in0=p_.t[:, 2 * j:2 * j + 2],
                        scalar1=self.vec("modb%d" % l, cg * 4 + j), scalar2=None, op0=ALU.add),
                          reads=[p_.b], writes=[modT.b], waw=False)
                it += 1
        for l in range(self.n_layers):
            for v in range(2):
                def dvs(which):
                    base = ((l * 2 + v) * 6 + which) * 8
                    return self.dv.t[:, base:base + 8]
                for which, (sci, gname) in ((0, (8, "gmix%d" % l)), (3, (32, "gffn%d" % l))):
                    fw.op("dve", lambda e, which=which, sci=sci, gname=gname, l=l, v=v, dvs=dvs: e.scalar_tensor_tensor(
                        out=dvs(which), in0=modT.t[:, l, sci:sci + 8, v], scalar=1.0, in1=self.vec(gname, 0, 8),
                        op0=ALU.add, op1=ALU.mult), reads=[modT.b], writes=[self.dv.b], waw=False)
                for which, c0 in ((1, 0), (2, 16), (4, 24), (5, 40)):
                    fw.op("dve", lambda e, which=which, c0=c0, l=l, v=v, dvs=dvs: e.tensor_copy(
                        out=dvs(which), in_=modT.t[:, l, c0:c0 + 8, v]), reads=[modT.b], writes=[self.dv.b], waw=False)
        self.end()

    def attn_inproj(self, l, X):
        fw, cfg = self.fw, self.cfg
        i = l // 2
        self.begin()
        W = self.sb("wA", [128, KD, ATT_EXT], BF16)
        self.load_w(W, lambda k: self.i_wext[i, k * 128:(k + 1) * 128, :], ATT_EXT)
        nt = self.norm_tiles()
        xs = [self.sb("x", [128, KD, 512], F32) for _ in range(2)]
        hTs = [self.sb("hT", [128, KD, 512], BF16) for _ in range(2)]
        cs = [self.sb("cos", [128, 512], F32) for _ in range(2)]
        sn = [self.sb("sin", [128, 512], F32) for _ in range(2)]
        qko = [self.sb("qko", [128, 13, 512], BF16) for _ in range(1)]
        vo = [self.sb("vo", [128, 4, 640], BF16) for _ in range(1)]
        pP = [self.ps("pP", [128, 512]) for _ in range(2)]
        pS = [self.ps("pS", [128, 512]) for _ in range(2)]
        ssq = self.ps("ssq", [128, 512])
        pV = self.ps("pV", [128, 1024])
        sqn = self.sb("sqn", [128, 512], BF16)
        rq, tq = self.sb("rq", [128, 512]), self.sb("tq", [128, 512])
        Aq = [self.sb("Aq", [128, 512]) for _ in range(2)]
        Bq = [self.sb("Bq", [128, 512]) for _ in range(2)]
        t1 = [self.sb("t1", [128, 512]) for _ in range(2)]
        t2 = [self.sb("t2", [128, 512]) for _ in range(2)]
        Xv = X.rearrange("(k p) n -> p k n", p=128)
        QTv = self.QT.rearrange("(k p) n -> p k n", p=128)
        KTv = self.KT.rearrange("(k p) n -> p k n", p=128)
        bi = 0
        for seq in cfg.seqs:
            _, start, ln, v, row0 = seq
            lat = (v == 0)
            for (t0, n) in blocks_of(ln, 512):
                x, hT, qo, vv = xs[bi % 2], hTs[bi % 2], qko[0], vo[0]
                c_, s_ = cs[bi % 2], sn[bi % 2]
                bi += 1
                self.load_cols(x, Xv, seq, t0, n, 0)
                if lat:
                    fw.dma(c_.t[:, 0:n], self.i_cos[:, start + t0:start + t0 + n], c_.b, writes=[c_.b])
                    fw.dma(s_.t[:, 0:n], self.i_sin[:, start + t0:start + t0 + n], s_.b, writes=[s_.b])
                self.norm_mod(x, n, nt["ss"], nt["sq"], nt["rs"], nt["tmp"], self.dvv(l, v, 0), self.dvv(l, v, 1), hT)
                for j in range(13):
                    P, S = pP[j % 2], pS[j % 2]
                    for k in range(KD):
                        fw.op("pe", lambda e, k=k, j=j, P=P: e.matmul(
                            P.t[:, 0:n], lhsT=W.t[:, k, j * 128:(j + 1) * 128], rhs=hT.t[:, k, 0:n],
                            start=(k == 0), stop=(k == KD - 1)), reads=[W.b, hT.b], writes=[P.b], ticket=(k == KD - 1))
                    if lat:
                        for k in range(KD):
                            fw.op("pe", lambda e, k=k, j=j, S=S: e.matmul(
                                S.t[:, 0:n], lhsT=W.t[:, k, ATT_FM + j * 128:ATT_FM + (j + 1) * 128],
                                rhs=hT.t[:, k, 0:n], start=(k == 0), stop=(k == KD - 1)),
                                  reads=[W.b, hT.b], writes=[S.b], ticket=(k == KD - 1))
                    normed = j >= 8
                    A, B = P, S
                    if normed:
                        fw.op("act", lambda e, P=P: e.activation(out=sqn.t[:, 0:n], in_=P.t[:, 0:n], func=AF.Square),
                              reads=[P.b], writes=[sqn.b])
                        fw.op("pe", lambda e: e.matmul(ssq.t[:, 0:n], lhsT=self.cmask("bd"), rhs=sqn.t[:, 0:n],
                                                       start=True, stop=True), reads=[sqn.b, self.cmb.b], writes=[ssq.b])
                        self.rstd(ssq, rq, tq, 1.0 / HD, n)
                        gn, gsn = ("gq%d" % i, "gqs%d" % i) if j < 12 else ("gk%d" % i, "gks%d" % i)
                        A = Aq[j % 2]
                        fw.op("dve", lambda e, P=P, A=A, gn=gn: e.scalar_tensor_tensor(
                            out=A.t[:, 0:n], in0=P.t[:, 0:n], scalar=self.vec(gn), in1=rq.t[:, 0:n],
                            op0=ALU.mult, op1=ALU.mult), reads=[P.b, rq.b], writes=[A.b])
                        if lat:
                            B = Bq[j % 2]
                            fw.op("dve", lambda e, S=S, B=B, gsn=gsn: e.scalar_tensor_tensor(
                                out=B.t[:, 0:n], in0=S.t[:, 0:n], scalar=self.vec(gsn), in1=rq.t[:, 0:n],
                                op0=ALU.mult, op1=ALU.mult), reads=[S.b, rq.b], writes=[B.b])
                    if lat:
                        a1, a2 = t1[j % 2], t2[j % 2]
                        fw.op("dve", lambda e, A=A, a1=a1: e.tensor_tensor(
                            out=a1.t[:, 0:n], in0=A.t[:, 0:n], in1=c_.t[:, 0:n], op=ALU.mult),
                              reads=[A.b, c_.b], writes=[a1.b])
                        fw.op("dve", lambda e, B=B, a2=a2: e.tensor_tensor(
                            out=a2.t[:, 0:n], in0=B.t[:, 0:n], in1=s_.t[:, 0:n], op=ALU.mult),
                              reads=[B.b, s_.b], writes=[a2.b])
                        fw.op("pool", lambda e, a1=a1, a2=a2, j=j: e.tensor_tensor(
                            out=qo.t[:, j, 0:n], in0=a1.t[:, 0:n], in1=a2.t[:, 0:n], op=ALU.add),
                              reads=[a1.b, a2.b], writes=[qo.b], waw=False)
                    elif normed:
                        fw.op("pool", lambda e, A=A, j=j: e.tensor_copy(out=qo.t[:, j, 0:n], in_=A.t[:, 0:n]),
                              reads=[A.b], writes=[qo.b], waw=False)
                    else:
                        fw.op("act", lambda e, A=A, j=j: e.activation(out=qo.t[:, j, 0:n], in_=A.t[:, 0:n], func=AF.Copy),
                              reads=[A.b], writes=[qo.b], waw=False)
                c0, c1 = start + t0, start + t0 + n
                for (dst, d0, s0, w) in ((QTv, 0, 0, 4), (KTv, 0, 4, 4), (QTv, 4, 8, 4), (KTv, 4, 12, 1)):
                    fw.dma(dst[:, d0:d0 + w, c0:c1], qo.t[:, s0:s0 + w, 0:n], qo.b, reads=[qo.b],
                           writes=[self.db(dst)], waw=False, stream="pool")
                nsub = n // 128
                for sub in range(nsub):
                    for (cc, w) in ((0, 512), (512, 128)):
                        for k in range(KD):
                            fw.op("pe", lambda e, k=k, cc=cc, w=w, sub=sub: e.matmul(
                                pV.t[:, cc:cc + w], lhsT=hT.t[:, k, sub * 128:(sub + 1) * 128],
                                rhs=W.t[:, k, 2 * ATT_FM + cc:2 * ATT_FM + cc + w], start=(k == 0), stop=(k == KD - 1)),
                                  reads=[W.b, hT.b], writes=[pV.b], ticket=(k == KD - 1 and cc == 512))
                    fw.op("act" if sub % 2 == 0 else "dve",
                          (lambda e, sub=sub: e.activation(out=vv.t[:, sub, :], in_=pV.t[:, 0:640], func=AF.Copy))
                          if sub % 2 == 0 else
                          (lambda e, sub=sub: e.tensor_copy(out=vv.t[:, sub, :], in_=pV.t[:, 0:640])),
                          reads=[pV.b], writes=[vv.b], waw=False)
                r0 = row0 + t0
                fw.dma(self.VT[r0:r0 + n, :].rearrange("(s p) c -> p s c", p=128), vv.t[:, 0:nsub, :], vv.b,
                       reads=[vv.b], writes=[self.db(self.VT)], waw=False, stream="pool")
        self.end()

    def attn_core(self, l):
        fw, cfg = self.fw, self.cfg
        i = l // 2
        lam_init = 0.8 - 0.6 * math.exp(-0.3 * l)
        NTOK, C, T = cfg.NTOK, cfg.C, cfg.T
        NKT = NTOK // 128
        self.begin()
        ro = self.roff["lq1_%d" % i][0]
        rv = self.sb("lqk", [128, 256])
        fw.dma(rv.t[:, :], self.i_rowv[0:1, ro:ro + 256].broadcast_to([128, 256]), rv.b, writes=[rv.b])
        prod = self.sb("prod", [128, 128])
        fw.op("dve", lambda e: e.tensor_tensor(out=prod.t[:, 0:64], in0=rv.t[:, 0:64], in1=rv.t[:, 64:128], op=ALU.mult),
              reads=[rv.b], writes=[prod.b])
        fw.op("dve", lambda e: e.tensor_tensor(out=prod.t[:, 64:128], in0=rv.t[:, 128:192], in1=rv.t[:, 192:256],
                                               op=ALU.mult), reads=[rv.b], writes=[prod.b], waw=False)
        s12 = self.sb("s12", [128, 2])
        for q in range(2):
            fw.op("dve", lambda e, q=q: e.reduce_sum(out=s12.t[:, q:q + 1], in_=prod.t[:, q * 64:(q + 1) * 64],
                                                     axis=mybir.AxisListType.X), reads=[prod.b], writes=[s12.b], waw=False)
        e12 = self.sb("e12", [128, 2])
        fw.op("act", lambda e: e.activation(out=e12.t[:, :], in_=s12.t[:, :], func=AF.Exp), reads=[s12.b], writes=[e12.b])
        nlam = self.sb("nlam", [128, 1])
        fw.op("dve", lambda e: e.tensor_tensor(out=nlam.t[:, :], in0=e12.t[:, 1:2], in1=e12.t[:, 0:1], op=ALU.subtract),
              reads=[e12.b], writes=[nlam.b])
        fw.op("dve", lambda e: e.tensor_scalar(out=nlam.t[:, :], in0=nlam.t[:, :], scalar1=-lam_init, scalar2=None,
                                               op0=ALU.add), reads=[nlam.b], writes=[nlam.b])
        sg = self.sb("sg", [128, 1])
        fw.op("dve", lambda e: e.tensor_scalar(out=sg.t[:, :], in0=self.vec("subg%d" % i), scalar1=1.0 - lam_init,
                                               scalar2=None, op0=ALU.mult), reads=[self.vecs.b], writes=[sg.b])
        Ks = [self.sb("K", [128, NTOK], BF16) for _ in range(2)]
        Vs = [self.sb("V", [128, NKT, 128], BF16) for _ in range(2)]
        Vg = [self.sb("Vg", [128, NKT, 128], BF16) for _ in range(2)]
        Qs = [self.sb("Q", [128, 512], BF16) for _ in range(2)]
        Pt = [self.sb("P", [128, 1024], BF16) for _ in range(2)]
        Sp = [self.ps("S", [128, 1024]) for _ in range(2)]
        Op = [self.ps("O", [128, 512]) for _ in range(2)]
        Lp = [self.ps("L", [128, 512]) for _ in range(2)]
        r_ = [self.sb("r", [128, 512]) for _ in range(2)]
        on = [self.sb("on", [128, 512]) for _ in range(2)]
        oa = self.sb("oa", [128, 512])
        sqo = self.sb("sqo", [128, 512], BF16)
        rs, tmp = self.sb("rso", [128, 512]), self.sb("tmpo", [128, 512])
        Lacc = [self.sb("Lacc", [128, 512]) for _ in range(2)]
        Lb = [self.sb("Lb", [128, 512], BF16) for _ in range(2)]
        Lf = self.sb("Lf", [128, 512])
        hb = self.sb("hb", [128, 512], BF16)
        h32 = self.sb("h32", [128, 512])
        lb = self.sb("lb", [128, 512], BF16)
        ost = [self.sb("ost", [128, 2, 512], BF16) for _ in range(2)]
        QTv = self.QT.rearrange("(k p) n -> p k n", p=128)
        KTv = self.KT.rearrange("(k p) n -> p k n", p=128)
        OTv = self.OT.rearrange("(k p) n -> p k n", p=128)
        v3 = lambda t, n: t.t[:, :].rearrange("p (s c) -> p s c", c=512)[:, :, 0:n]

        def load_kv(u):
            if u > 4:
                return
            K, V = Ks[u % 2], Vs[u % 2]
            kc = u if u < 4 else 4
            vcol = u * 128 if u < 4 else 512
            fw.dma(K.t[:, 0:C], KTv[:, kc, cfg.CS:cfg.CS + C], K.b, reads=[self.db(self.KT)], writes=[K.b])
            fw.dma(K.t[:, C:NTOK], KTv[:, kc, cfg.LS:cfg.LS + T], K.b, reads=[self.db(self.KT)], writes=[K.b], waw=False)
            vsrc = self.VT[:, vcol:vcol + 128].rearrange("(s p) c -> p s c", p=128)
            for (s0, ns) in blocks_of(NKT, 16):
                fw.dma(V.t[:, s0:s0 + ns, :], vsrc[:, s0:s0 + ns, :], V.b, reads=[self.db(self.VT)], writes=[V.b],
                       waw=(s0 == 0))
            if u == 4:
                for sub in range(2):
                    fw.op("pool", lambda e, sub=sub: e.memset(Vg[sub].t[:, :, 64:128], 1.0), writes=[Vg[sub].b])
                    fw.op("pool" if sub == 0 else "dve", lambda e, sub=sub: e.tensor_copy(
                        out=Vg[sub].t[:, :, 0:64], in_=V.t[:, :, sub * 64:(sub + 1) * 64]),
                          reads=[V.b], writes=[Vg[sub].b], waw=False)

        qblocks = []
        for seq in cfg.seqs:
            _, start, ln, v, row0 = seq
            kts = list(range(0, C // 128)) if v == 1 else list(range(NKT))
            for (t0, n) in blocks_of(ln, 512):
                qblocks.append((start + t0, n, kts))
        load_kv(0)
        qi = 0
        for u in range(8):
            if u + 1 < 8:
                load_kv(u + 1)
            K, V = Ks[min(u, 4) % 2], Vs[min(u, 4) % 2]
            diff = u < 4
            for (c0, n, kts) in qblocks:
                Q = Qs[qi % 2]
                os_ = ost[qi % 2]
                qi += 1
                fw.dma(Q.t[:, 0:n], QTv[:, u, c0:c0 + n], Q.b, reads=[self.db(self.QT)], writes=[Q.b])

                def emit_S(kt):
                    S = Sp[kt % 2]
                    for sub in range(2):
                        fw.op("pe", lambda e, sub=sub, S=S, kt=kt: e.matmul(
                            S.t[:, sub * 512:sub * 512 + n], lhsT=K.t[sub * 64:(sub + 1) * 64, kt * 128:(kt + 1) * 128],
                            rhs=Q.t[sub * 64:(sub + 1) * 64, 0:n], start=True, stop=True),
                              reads=[K.b, Q.b], writes=[S.b], ticket=(sub == 1), waw=(sub == 0))

                def emit_E(kt):
                    S, P = Sp[kt % 2], Pt[kt % 2]
                    fw.op("act", lambda e, S=S, P=P: e.activation(out=v3(P, n), in_=v3(S, n), func=AF.Exp,
                                                                  scale=HD ** -0.5), reads=[S.b], writes=[P.b])

                def emit_PV(kt, first, last):
                    P = Pt[kt % 2]
                    for sub in range(2):
                        rhs = P.t[:, sub * 512:sub * 512 + n]
                        if diff:
                            fw.op("pe", lambda e, sub=sub, rhs=rhs: e.matmul(
                                Op[sub].t[:, 0:n], lhsT=V.t[:, kt, :], rhs=rhs, start=first, stop=last),
                                  reads=[V.b, P.b], writes=[Op[sub].b] if (first or last) else [],
                                  ticket=last)
                            La = Lacc[sub]
                            if sub == 1:
                                fw.op("pe", lambda e, sub=sub, rhs=rhs: e.matmul(
                                    Lp[sub].t[:, 0:n], lhsT=self.ones_bf.t[:, :], rhs=rhs, start=first, stop=last),
                                      reads=[self.ones_bf.b, P.b], writes=[Lp[sub].b] if (first or last) else [],
                                      ticket=True)
                            elif first:
                                fw.op("dve", lambda e, La=La, rhs=rhs: e.tensor_copy(out=La.t[:, 0:n], in_=rhs),
                                      reads=[P.b], writes=[La.b])
                            else:
                                fw.op("dve", lambda e, La=La, rhs=rhs: e.tensor_tensor(
                                    out=La.t[:, 0:n], in0=La.t[:, 0:n], in1=rhs, op=ALU.add),
                                      reads=[P.b, La.b], writes=[La.b])
                        else:
                            fw.op("pe", lambda e, sub=sub, rhs=rhs: e.matmul(
                                Op[sub].t[:, 0:n], lhsT=Vg[sub].t[:, kt, :], rhs=rhs, start=first, stop=last),
                                  reads=[Vg[sub].b, P.b], writes=[Op[sub].b] if (first or last) else [],
                                  ticket=(last or sub == 1))

                emit_S(kts[0])
                for idx, kt in enumerate(kts):
                    if idx + 1 < len(kts):
                        emit_S(kts[idx + 1])
                    emit_E(kt)
                    emit_PV(kt, idx == 0, idx == len(kts) - 1)
                if diff:
                    for sub in range(2):
                        if sub == 0:
                            fw.op("dve", lambda e, sub=sub: e.tensor_copy(out=Lb[sub].t[:, 0:n], in_=Lacc[sub].t[:, 0:n]),
                                  reads=[Lacc[sub].b], writes=[Lb[sub].b])
                            fw.op("pe", lambda e, sub=sub: e.matmul(Lp[sub].t[:, 0:n], lhsT=self.ones_bf.t[:, :],
                                                                    rhs=Lb[sub].t[:, 0:n], start=True, stop=True),
                                  reads=[Lb[sub].b, self.ones_bf.b], writes=[Lp[sub].b])
                        fw.op("dve", lambda e, sub=sub: e.reciprocal(out=r_[sub].t[:, 0:n], in_=Lp[sub].t[:, 0:n]),
                              reads=[Lp[sub].b], writes=[r_[sub].b])
                    for sub in range(2):
                        fw.op("dve", lambda e, sub=sub: e.tensor_tensor(
                            out=on[sub].t[:, 0:n], in0=Op[sub].t[:, 0:n], in1=r_[sub].t[:, 0:n], op=ALU.mult),
                              reads=[Op[sub].b, r_[sub].b], writes=[on[sub].b])
                    fw.op("dve", lambda e: e.scalar_tensor_tensor(
                        out=oa.t[:, 0:n], in0=on[1].t[:, 0:n], scalar=nlam.t[:, 0:1], in1=on[0].t[:, 0:n],
                        op0=ALU.mult, op1=ALU.add), reads=[on[0].b, on[1].b, nlam.b], writes=[oa.b])
                    fw.op("act", lambda e: e.activation(out=sqo.t[:, 0:n], in_=oa.t[:, 0:n], func=AF.Square),
                          reads=[oa.b], writes=[sqo.b])
                    ssb = Lp[0]
                    fw.op("pe", lambda e: e.matmul(ssb.t[:, 0:n], lhsT=self.ones_bf.t[:, :], rhs=sqo.t[:, 0:n],
                                                   start=True, stop=True), reads=[sqo.b, self.ones_bf.b], writes=[ssb.b])
                    self.rstd_le(ssb, rs, tmp, 1.0 / 128, n)
                    fw.op("dve", lambda e: e.scalar_tensor_tensor(
                        out=os_.t[:, 0, 0:n], in0=oa.t[:, 0:n], scalar=sg.t[:, 0:1], in1=rs.t[:, 0:n],
                        op0=ALU.mult, op1=ALU.mult), reads=[oa.b, rs.b, sg.b], writes=[os_.b])
                    fw.dma(OTv[:, u, c0:c0 + n], os_.t[:, 0, 0:n], os_.b, reads=[os_.b], writes=[self.db(self.OT)],
                           waw=False, stream="pool")
                else:
                    j = u - 4
                    H = slice(64, 128)
                    idb = self.cmb.t[64:128, 64:128]
                    for sub in range(2):
                        fw.op("act", lambda e, sub=sub: e.activation(out=Lf.t[H, 0:n], in_=Op[sub].t[H, 0:n], func=AF.Copy),
                              reads=[Op[sub].b], writes=[Lf.b])
                        fw.op("dve", lambda e: e.reciprocal(out=Lf.t[H, 0:n], in_=Lf.t[H, 0:n]), reads=[Lf.b], writes=[Lf.b])
                        fw.op("dve", lambda e: e.tensor_copy(out=hb.t[H, 0:n], in_=Lf.t[H, 0:n]), reads=[Lf.b], writes=[hb.b])
                        fw.op("dve", lambda e: e.tensor_copy(out=h32.t[H, 0:n], in_=hb.t[H, 0:n]), reads=[hb.b], writes=[h32.b])
                        fw.op("dve", lambda e: e.tensor_tensor(out=lb.t[H, 0:n], in0=Lf.t[H, 0:n], in1=h32.t[H, 0:n],
                                                               op=ALU.subtract), reads=[Lf.b, h32.b], writes=[lb.b])
                        fw.op("pe", lambda e, sub=sub: e.matmul(Lp[sub].t[0:64, 0:n], lhsT=idb, rhs=hb.t[H, 0:n],
                                                                start=True, stop=False),
                              reads=[hb.b, self.cmb.b], writes=[Lp[sub].b], ticket=False)
                        fw.op("pe", lambda e, sub=sub: e.matmul(Lp[sub].t[0:64, 0:n], lhsT=idb, rhs=lb.t[H, 0:n],
                                                                start=False, stop=True),
                              reads=[lb.b, hb.b, self.cmb.b], writes=[Lp[sub].b])
                        fw.op("act", lambda e, sub=sub: e.activation(out=r_[sub].t[0:64, 0:n], in_=Lp[sub].t[0:64, 0:n],
                                                                     func=AF.Copy), reads=[Lp[sub].b], writes=[r_[sub].b])
                        fw.op("dve", lambda e, sub=sub: e.tensor_tensor(
                            out=os_.t[0:64, sub, 0:n], in0=Op[sub].t[0:64, 0:n], in1=r_[sub].t[0:64, 0:n], op=ALU.mult),
                              reads=[Op[sub].b, r_[sub].b], writes=[os_.b], waw=(sub == 0))
                    for sub in range(2):
                        hd = 4 * sub + j
                        f0 = 512 + hd * 64
                        fw.dma(self.OT[f0:f0 + 64, c0:c0 + n], os_.t[0:64, sub, 0:n], os_.b, reads=[os_.b],
                               writes=[self.db(self.OT)], waw=False, stream="pool")
        self.end()

    def proj_res_norm(self, l, Xin, Xout, SRC, kc, w_rows):
        fw, cfg = self.fw, self.cfg
        self.begin()
        W = self.sb("wo", [128, kc, D], BF16)
        self.load_w(W, w_rows, D)
        nt = self.norm_tiles()
        srcs = [self.sb("src", [128, kc, 512], BF16) for _ in range(2)]
        xs = [self.sb("x", [128, KD, 512], F32) for _ in range(2)]
        x1s = [self.sb("x1", [128, KD, 512], F32) for _ in range(2)]
        hs = [self.sb("h", [128, KD, 512], BF16) for _ in range(2)]
        pp = [self.ps("pp", [128, 512]) for _ in range(2)]
        Xiv = Xin.rearrange("(k p) n -> p k n", p=128)
        Xov = Xout.rearrange("(k p) n -> p k n", p=128)
        Sv = SRC.rearrange("(k p) n -> p k n", p=128)
        HTv = self.HT.rearrange("(k p) n -> p k n", p=128)
        bi = 0
        for seq in cfg.seqs:
            _, start, ln, v, row0 = seq
            if v == 1 and l == DEPTH - 1:
                continue
            for (t0, n) in blocks_of(ln, 512):
                s_, x, x1, h = srcs[bi % 2], xs[bi % 2], x1s[bi % 2], hs[bi % 2]
                bi += 1
                self.load_cols(s_, Sv, seq, t0, n, 0)
                self.load_cols(x, Xiv, seq, t0, n, 0)
                g1 = self.dvv(l, v, 2)
                for c in range(KD):
                    P = pp[c % 2]
                    for k in range(kc):
                        fw.op("pe", lambda e, k=k, c=c, P=P: e.matmul(
                            P.t[:, 0:n], lhsT=W.t[:, k, c * 128:(c + 1) * 128], rhs=s_.t[:, k, 0:n],
                            start=(k == 0), stop=(k == kc - 1)), reads=[W.b, s_.b], writes=[P.b], ticket=(k == kc - 1))
                    fw.op("dve", lambda e, c=c, P=P: e.scalar_tensor_tensor(
                        out=x1.t[:, c, 0:n], in0=P.t[:, 0:n], scalar=g1(c), in1=x.t[:, c, 0:n],
                        op0=ALU.mult, op1=ALU.add), reads=[P.b, x.b], writes=[x1.b], waw=False)
                c0 = start + t0
                fw.dma(Xov[:, :, c0:c0 + n], x1.t[:, :, 0:n], x1.b, reads=[x1.b], writes=[self.db(Xout)], waw=False,
                       stream="pool")
                self.norm_mod(x1, n, nt["ss"], nt["sq"], nt["rs"], nt["tmp"], self.dvv(l, v, 3), self.dvv(l, v, 4), h)
                fw.dma(HTv[:, :, c0:c0 + n], h.t[:, :, 0:n], h.b, reads=[h.b], writes=[self.db(self.HT)], waw=False,
                       stream="pool")
        self.end()

    def ffn_in(self, l):
        fw, cfg = self.fw, self.cfg
        self.begin()
        W = self.sb("wf", [128, KD, 2 * DFF], BF16)
        self.load_w(W, lambda k: self.i_fwi[l, k * 128:(k + 1) * 128, :], 2 * DFF)
        hs = [self.sb("h", [128, KD, 512], BF16) for _ in range(2)]
        uo = [self.sb("uo", [128, NFF, 512], BF16) for _ in range(2)]
        pg = [self.ps("pg", [128, 512]) for _ in range(2)]
        pv = [self.ps("pv", [128, 512]) for _ in range(2)]
        tt = [self.sb("t", [128, 512]) for _ in range(2)]
        ge = [self.sb("ge", [128, 512]) for _ in range(2)]
        HTv = self.HT.rearrange("(k p) n -> p k n", p=128)
        UTv = self.UT.rearrange("(k p) n -> p k n", p=128)
        bi = 0
        for seq in cfg.seqs:
            _, start, ln, v, row0 = seq
            if v == 1 and l == DEPTH - 1:
                continue
            for (t0, n) in blocks_of(ln, 510):
                h, u = hs[bi % 2], uo[bi % 2]
                bi += 1
                self.load_cols(h, HTv, seq, t0, n, 1)
                N = n + 2
                for c in range(NFF):
                    G, Vv, t, g = pg[c % 2], pv[c % 2], tt[c % 2], ge[c % 2]
                    for k in range(KD):
                        fw.op("pe", lambda e, k=k, c=c, G=G: e.matmul(
                            G.t[:, 0:N], lhsT=W.t[:, k, DFF + c * 128:DFF + (c + 1) * 128], rhs=h.t[:, k, 0:N],
                            start=(k == 0), stop=(k == KD - 1)), reads=[W.b, h.b], writes=[G.b], ticket=(k == KD - 1))
                    for k in range(KD):
                        fw.op("pe", lambda e, k=k, c=c, Vv=Vv: e.matmul(
                            Vv.t[:, 0:N], lhsT=W.t[:, k, c * 128:(c + 1) * 128], rhs=h.t[:, k, 0:N],
                            start=(k == 0), stop=(k == KD - 1)), reads=[W.b, h.b], writes=[Vv.b], ticket=(k == KD - 1))
                    w0, w1, w2 = (self.vec("fcw%d_%d" % (l, q), c) for q in range(3))
                    bb = self.vec("fcb%d" % l, c)
                    fw.op("dve", lambda e, G=G, t=t, w0=w0, bb=bb: e.tensor_scalar(
                        out=t.t[:, 0:n], in0=G.t[:, 0:n], scalar1=w0, scalar2=bb, op0=ALU.mult, op1=ALU.add),
                          reads=[G.b], writes=[t.b])
                    fw.op("dve", lambda e, G=G, t=t, w1=w1: e.scalar_tensor_tensor(
                        out=t.t[:, 0:n], in0=G.t[:, 1:n + 1], scalar=w1, in1=t.t[:, 0:n], op0=ALU.mult, op1=ALU.add),
                          reads=[G.b, t.b], writes=[t.b])
                    fw.op("dve", lambda e, G=G, t=t, w2=w2: e.scalar_tensor_tensor(
                        out=t.t[:, 0:n], in0=G.t[:, 2:n + 2], scalar=w2, in1=t.t[:, 0:n], op0=ALU.mult, op1=ALU.add),
                          reads=[G.b, t.b], writes=[t.b])
                    fw.op("act", lambda e, t=t, g=g: e.activation(out=g.t[:, 0:n], in_=t.t[:, 0:n], func=AF.Gelu),
                          reads=[t.b], writes=[g.b])
                    fw.op("dve", lambda e, g=g, Vv=Vv, c=c: e.tensor_tensor(
                        out=u.t[:, c, 0:n], in0=g.t[:, 0:n], in1=Vv.t[:, 1:n + 1], op=ALU.mult),
                          reads=[g.b, Vv.b], writes=[u.b], waw=False)
                c0 = start + t0
                fw.dma(UTv[:, :, c0:c0 + n], u.t[:, :, 0:n], u.b, reads=[u.b], writes=[self.db(self.UT)], waw=False,
                       stream="pool")
        self.end()

    def ffn_out(self, l, Xin, Xout):
        fw, cfg = self.fw, self.cfg
        self.begin()
        W = self.sb("wf2", [128, NFF, D], BF16)
        self.load_w(W, lambda k: self.i_fwo[l, k * 128:(k + 1) * 128, :], D)
        us = [self.sb("u", [128, NFF, 512], BF16) for _ in range(2)]
        xs = [self.sb("x", [128, KD, 512], F32) for _ in range(2)]
        x2s = [self.sb("x2", [128, KD, 512], F32) for _ in range(2)]
        pp = [self.ps("pp", [128, 512]) for _ in range(2)]
        Xiv = Xin.rearrange("(k p) n -> p k n", p=128)
        Xov = Xout.rearrange("(k p) n -> p k n", p=128)
        UTv = self.UT.rearrange("(k p) n -> p k n", p=128)
        bi = 0
        for seq in cfg.seqs:
            _, start, ln, v, row0 = seq
            if v == 1 and l == DEPTH - 1:
                continue
            for (t0, n) in blocks_of(ln, 512):
                u, x, x2 = us[bi % 2], xs[bi % 2], x2s[bi % 2]
                bi += 1
                self.load_cols(u, UTv, seq, t0, n, 0)
                self.load_cols(x, Xiv, seq, t0, n, 0)
                g2 = self.dvv(l, v, 5)
                for c in range(KD):
                    P = pp[c % 2]
                    for k in range(NFF):
                        fw.op("pe", lambda e, k=k, c=c, P=P: e.matmul(
                            P.t[:, 0:n], lhsT=W.t[:, k, c * 128:(c + 1) * 128], rhs=u.t[:, k, 0:n],
                            start=(k == 0), stop=(k == NFF - 1)), reads=[W.b, u.b], writes=[P.b], ticket=(k == NFF - 1))
                    fw.op("dve", lambda e, c=c, P=P: e.scalar_tensor_tensor(
                        out=x2.t[:, c, 0:n], in0=P.t[:, 0:n], scalar=g2(c), in1=x.t[:, c, 0:n],
                        op0=ALU.mult, op1=ALU.add), reads=[P.b, x.b], writes=[x2.b], waw=False)
                c0 = start + t0
                fw.dma(Xov[:, :, c0:c0 + n], x2.t[:, :, 0:n], x2.b, reads=[x2.b], writes=[self.db(Xout)], waw=False,
                       stream="pool")
        self.end()

    def final_norm(self, X):
        fw, cfg = self.fw, self.cfg
        self.begin()
        nt = self.norm_tiles()
        xs = [self.sb("x", [128, KD, 512], F32) for _ in range(2)]
        os_ = [self.sb("o", [128, KD, 512], F32) for _ in range(2)]
        Xv = X.rearrange("(k p) n -> p k n", p=128)
        Ov = self.o_out.rearrange("(k p) n -> p k n", p=128)
        seq = cfg.seqs[1]
        outb = self.fw.buf(name="outT")
        for bi, (t0, n) in enumerate(blocks_of(cfg.T, 512)):
            x, o = xs[bi % 2], os_[bi % 2]
            self.load_cols(x, Xv, seq, t0, n, 0)
            self.norm_mod(x, n, nt["ss"], nt["sq"], nt["rs"], nt["tmp"], lambda k: self.vec("gfin", k), None, o)
            fw.dma(Ov[:, :, t0:t0 + n], o.t[:, :, 0:n], o.b, reads=[o.b], writes=[outb], waw=False, stream="pool")
        self.end()


    def ssm_inproj(self, l, X):
        fw, cfg = self.fw, self.cfg
        i = l // 2
        self.begin()
        W = self.sb("wS", [128, KD, SSM_IN], BF16)
        self.load_w(W, lambda k: self.i_swi[i, k * 128:(k + 1) * 128, :], SSM_IN)
        nt = self.norm_tiles()
        x = self.sb("x", [128, KD, 512], F32)
        hTs = [self.sb("hT", [128, KD, 512], BF16) for _ in range(2)]
        xo = self.sb("xo", [128, 24, 512], BF16)
        zo = [self.sb("zo", [128, SSM_DI], F32) for _ in range(2)]
        dto = [self.sb("dto", [128, 64], F32) for _ in range(2)]
        dta = [self.sb("dta", [128, 64], F32) for _ in range(2)]
        dtb = self.sb("dtb", [128, 64], F32)
        ro = self.roff["dtb%d" % i][0]
        fw.dma(dtb.t[:, :], self.i_rowv[0:1, ro:ro + 64].broadcast_to([128, 64]), dtb.b, writes=[dtb.b])
        pP = [self.ps("pP", [128, 512]) for _ in range(2)]
        pz = [self.ps("pz", [128, 512]) for _ in range(2)]
        pdt = self.ps("pdt", [128, 512])
        tt = [self.sb("t", [128, 512]) for _ in range(2)]
        Xv = X.rearrange("(k p) n -> p k n", p=128)
        XBv = self.XBCT.rearrange("(k p) n -> p k n", p=128)
        bi = 0
        zi = 0
        for seq in cfg.seqs:
            _, start, ln, v, row0 = seq
            for (t0, n) in blocks_of(ln, 510):
                hT = hTs[bi % 2]
                bi += 1
                N = n + 2
                self.load_cols(x, Xv, seq, t0, n, 1)
                self.norm_mod(x, N, nt["ss"], nt["sq"], nt["rs"], nt["tmp"], self.dvv(l, v, 0), self.dvv(l, v, 1), hT)
                if t0 == 0:
                    fw.op("pool", lambda e: e.memset(hT.t[:, :, 0:1], 0.0), writes=[hT.b])
                if t0 + n == ln:
                    fw.op("pool", lambda e: e.memset(hT.t[:, :, n + 1:n + 2], 0.0), writes=[hT.b])
                for c in range(24):
                    P, t = pP[c % 2], tt[c % 2]
                    for k in range(KD):
                        fw.op("pe", lambda e, k=k, c=c, P=P: e.matmul(
                            P.t[:, 0:N], lhsT=W.t[:, k, SSM_DI + c * 128:SSM_DI + (c + 1) * 128], rhs=hT.t[:, k, 0:N],
                            start=(k == 0), stop=(k == KD - 1)), reads=[W.b, hT.b], writes=[P.b], ticket=(k == KD - 1))
                    w0, w1, w2 = (self.vec("scw%d_%d" % (i, q), c) for q in range(3))
                    bb = self.vec("scb%d" % i, c)
                    fw.op("dve", lambda e, P=P, t=t, w0=w0, bb=bb: e.tensor_scalar(
                        out=t.t[:, 0:n], in0=P.t[:, 0:n], scalar1=w0, scalar2=bb, op0=ALU.mult, op1=ALU.add),
                          reads=[P.b], writes=[t.b])
                    fw.op("dve", lambda e, P=P, t=t, w1=w1: e.scalar_tensor_tensor(
                        out=t.t[:, 0:n], in0=P.t[:, 1:n + 1], scalar=w1, in1=t.t[:, 0:n], op0=ALU.mult, op1=ALU.add),
                          reads=[P.b, t.b], writes=[t.b])
                    fw.op("dve", lambda e, P=P, t=t, w2=w2: e.scalar_tensor_tensor(
                        out=t.t[:, 0:n], in0=P.t[:, 2:n + 2], scalar=w2, in1=t.t[:, 0:n], op0=ALU.mult, op1=ALU.add),
                          reads=[P.b, t.b], writes=[t.b])
                    fw.op("act", lambda e, t=t, c=c: e.activation(out=xo.t[:, c, 0:n], in_=t.t[:, 0:n], func=AF.Silu),
                          reads=[t.b], writes=[xo.b], waw=False)
                c0 = start + t0
                fw.dma(XBv[:, :, c0:c0 + n], xo.t[:, :, 0:n], xo.b, reads=[xo.b], writes=[self.db(self.XBCT)],
                       waw=False, stream="pool")
                for (s0, m) in blocks_of(n, 128):
                    z, dt_, da = zo[zi % 2], dto[zi % 2], dta[zi % 2]
                    zi += 1
                    for q in range(4):
                        Pz = pz[q % 2]
                        for k in range(KD):
                            fw.op("pe", lambda e, k=k, q=q, Pz=Pz: e.matmul(
                                Pz.t[0:m, :], lhsT=hT.t[:, k, 1 + s0:1 + s0 + m], rhs=W.t[:, k, q * 512:(q + 1) * 512],
                                start=(k == 0), stop=(k == KD - 1)), reads=[W.b, hT.b], writes=[Pz.b], ticket=(k == KD - 1))
                        if q % 2 == 0:
                            fw.op("act", lambda e, q=q, Pz=Pz: e.activation(out=z.t[0:m, q * 512:(q + 1) * 512],
                                                                            in_=Pz.t[0:m, :], func=AF.Copy),
                                  reads=[Pz.b], writes=[z.b], waw=False)
                        else:
                            fw.op("dve", lambda e, q=q, Pz=Pz: e.tensor_copy(out=z.t[0:m, q * 512:(q + 1) * 512],
                                                                             in_=Pz.t[0:m, :]),
                                  reads=[Pz.b], writes=[z.b], waw=False)
                    r0 = row0 + t0 + s0
                    fw.dma(self.ZT[r0:r0 + m, :], z.t[0:m, :], z.b, reads=[z.b], writes=[self.db(self.ZT)], waw=False,
                           stream="pool")
                    for k in range(KD):
                        fw.op("pe", lambda e, k=k: e.matmul(
                            pdt.t[0:m, 0:64], lhsT=hT.t[:, k, 1 + s0:1 + s0 + m], rhs=W.t[:, k, SSM_DI + SSM_XBC:SSM_IN],
                            start=(k == 0), stop=(k == KD - 1)), reads=[W.b, hT.b], writes=[pdt.b], ticket=(k == KD - 1))
                    fw.op("dve", lambda e: e.tensor_tensor(out=da.t[0:m, :], in0=pdt.t[0:m, 0:64], in1=dtb.t[0:m, :], op=ALU.add),
                          reads=[pdt.b, dtb.b], writes=[da.b])
                    fw.op("act", lambda e: e.activation(out=da.t[0:m, :], in_=da.t[0:m, :], func=AF.Exp),
                          reads=[da.b], writes=[da.b])
                    fw.op("act", lambda e: e.activation(out=dt_.t[0:m, :], in_=da.t[0:m, :], func=AF.Ln, bias=1.0),
                          reads=[da.b], writes=[dt_.b])
                    fw.dma(self.DTT[r0:r0 + m, :], dt_.t[0:m, :], dt_.b, reads=[dt_.b], writes=[self.db(self.DTT)],
                           waw=False, stream="pool")
        self.end()

    def ssm_scan(self, l, d):
        fw, cfg = self.fw, self.cfg
        i = l // 2
        NKT = cfg.NTOK // 128
        nct = cfg.C // 128
        self.begin()
        bc3 = lambda ap: ap.unsqueeze(2).broadcast_to([128, 8, 64])
        ML = self.cmask("ML_f" if d == 0 else "ML_b")
        MR = self.cmask("MR_f" if d == 0 else "MR_b")
        MRf = self.cmask("MR_f" if d == 0 else "MR_b", bf=False)
        arow = self.sb("arow", [128, 32])
        ro = self.roff["alog%d" % i][0] + 32 * d
        fw.dma(arow.t[:, :], self.i_rowv[0:1, ro:ro + 32].broadcast_to([128, 32]), arow.b, writes=[arow.b])
        fw.op("act", lambda e: e.activation(out=arow.t[:, :], in_=arow.t[:, :], func=AF.Exp), reads=[arow.b], writes=[arow.b])
        fw.op("dve", lambda e: e.tensor_scalar(out=arow.t[:, :], in0=arow.t[:, :], scalar1=-1.0, scalar2=None, op0=ALU.mult),
              reads=[arow.b], writes=[arow.b])
        if d == 0:
            drow = self.sb("drow", [128, 32])
            ro = self.roff["ssd%d" % i][0]
            fw.dma(drow.t[:, :], self.i_rowv[0:1, ro:ro + 32].broadcast_to([128, 32]), drow.b, writes=[drow.b])
            DI = self.sb("DI", [128, 32 * 128], BF16)
            fw.op("dve", lambda e: e.tensor_tensor(
                out=DI.t[:, :].rearrange("p (h c) -> p h c", c=128),
                in0=self.cmask("ident", bf=False).unsqueeze(1).broadcast_to([128, 32, 128]),
                in1=drow.t[:, :].unsqueeze(2).broadcast_to([128, 32, 128]), op=ALU.mult),
                  reads=[drow.b, self.cm.b], writes=[DI.b])
        if d == 1:
            ng = self.sb("ng", [128, SSM_DI])
            ro = self.roff["sng%d" % i][0]
            fw.dma(ng.t[:, :], self.i_rowv[0:1, ro:ro + SSM_DI].broadcast_to([128, SSM_DI]), ng.b, writes=[ng.b])
        S = self.sb("S", [128, SSM_DI], F32)
        Sb = self.sb("Sb", [128, SSM_DI], BF16)
        fw.op("pool", lambda e: e.memset(S.t[:, :], 0.0), writes=[S.b])
        fw.op("pool", lambda e: e.memset(Sb.t[:, :], 0.0), writes=[Sb.b])
        xbcs = [self.sb("xbc", [128, 24, 128], BF16) for _ in range(2)]
        dts = [self.sb("dt", [128, 64], F32) for _ in range(2)]
        xtok_l = [self.sb("xtok", [128, 2560], BF16) for _ in range(2)]
        a_l = [self.sb("a", [128, 32]) for _ in range(2)]
        ahb_l = [self.sb("ahb", [128, 32], BF16) for _ in range(2)]
        ah_l = [self.sb("ah", [128, 32]) for _ in range(2)]
        al_l = [self.sb("al", [128, 32]) for _ in range(2)]
        a2_l = [self.sb("a2", [128, 64], BF16) for _ in range(2)]
        E_l = [self.sb("E", [128, 96]) for _ in range(2)]
        e2_l = [self.sb("e2", [128, 32]) for _ in range(2)]
        cbm_l = [self.sb("cbm", [128, 4, 128]) for _ in range(2)]
        Rhs = [self.sb("Rh", [128, 512], BF16) for _ in range(2)]
        Rls = [self.sb("Rl", [128, 512], BF16) for _ in range(2)]
        dec = [self.sb("dec", [128, 512]) for _ in range(2)]
        wT4 = [self.sb("wT4", [128, 512], BF16) for _ in range(2)]
        tmpy = self.sb("tmpy", [128, 512])
        xdt_l = [self.sb("xdt", [128, SSM_DI], BF16) for _ in range(2)]
        xwa_l = [self.sb("xwa", [128, SSM_DI], BF16) for _ in range(2)]
        yo = [self.sb("yo", [128, SSM_DI], F32) for _ in range(2)]
        pT = self.ps("pT", [128, 1024], BF16)
        pE = self.ps("pE", [128, 512])
        pcb = self.ps("pcb", [128, 512])
        pseg = [self.ps("pseg", [128, 512]) for _ in range(2)]
        pYl = [self.ps("pY", [128, 512]) for _ in range(2)]
        pG = self.ps("pG", [128, 512])
        if d == 1:
            yfs = [self.sb("yf", [128, SSM_DI], F32) for _ in range(2)]
            zs = [self.sb("z", [128, SSM_DI], F32) for _ in range(2)]
            sz_l = [self.sb("sz", [128, SSM_DI], F32) for _ in range(2)]
            ssq = self.sb("ssq", [128, 1])
            sqj = self.sb("sqj", [128, SSM_DI], BF16)
            rs1, rt1 = self.sb("rs1", [128, 1]), self.sb("rt1", [128, 1])
            ygn = self.sb("ygn", [128, SSM_DI], BF16)
            ygT = [self.sb("ygT", [128, 16, 128], BF16) for _ in range(2)]
        XBv = self.XBCT.rearrange("(k p) n -> p k n", p=128)
        YGv = self.YGT.rearrange("(k p) n -> p k n", p=128)
        order = list(range(NKT)) if d == 0 else (list(range(nct - 1, -1, -1)) + list(range(NKT - 1, nct - 1, -1)))
        hic = [0]

        def binds(ci):
            kt = order[ci]
            p_ = ci % 2
            r0 = kt * 128
            c0 = (cfg.CS + r0) if kt < nct else (cfg.LS + r0 - cfg.C)
            return (xbcs[p_], dts[p_], r0, c0, xtok_l[p_], a_l[p_], ahb_l[p_], ah_l[p_], al_l[p_], a2_l[p_], E_l[p_],
                    e2_l[p_], cbm_l[p_], xdt_l[p_], xwa_l[p_])

        def prologue(ci):
            xbc, dt_, r0, c0, xtok, a, ahb, ah, al, a2, E, e2, cbm, xdt, xwa = binds(ci)
            fw.dma(xbc.t[:, :, :], XBv[:, :, c0:c0 + 128], xbc.b, reads=[self.db(self.XBCT)], writes=[xbc.b])
            fw.dma(dt_.t[:, :], self.DTT[r0:r0 + 128, :], dt_.b, reads=[self.db(self.DTT)], writes=[dt_.b])
            dtd = dt_.t[:, 32 * d:32 * d + 32]
            if d == 1:
                yf, z = yfs[ci % 2], zs[ci % 2]
                fw.dma(yf.t[:, :], self.YF[r0:r0 + 128, :], yf.b, reads=[self.db(self.YF)], writes=[yf.b])
                fw.dma(z.t[:, :], self.ZT[r0:r0 + 128, :], z.b, reads=[self.db(self.ZT)], writes=[z.b])
            for q in range(5):
                for j in range(4):
                    c = q * 4 + j
                    fw.op("pe", lambda e, c=c, j=j: e.transpose(pT.t[:, j * 128:(j + 1) * 128], xbc.t[:, c, :], self.ident_bf()),
                          reads=[xbc.b, self.cmb.b], writes=[pT.b], ticket=(j == 3), waw=(j == 0))
                if q % 2 == 0:
                    fw.op("act", lambda e, q=q: e.activation(out=xtok.t[:, q * 512:(q + 1) * 512], in_=pT.t[:, 0:512], func=AF.Copy),
                          reads=[pT.b], writes=[xtok.b], waw=False)
                else:
                    fw.op("dve", lambda e, q=q: e.tensor_copy(out=xtok.t[:, q * 512:(q + 1) * 512], in_=pT.t[:, 0:512]),
                          reads=[pT.b], writes=[xtok.b], waw=False)
            fw.op("dve", lambda e: e.tensor_tensor(out=a.t[:, :], in0=dtd, in1=arow.t[:, :], op=ALU.mult),
                  reads=[dt_.b, arow.b], writes=[a.b])
            fw.op("dve", lambda e: e.tensor_copy(out=ahb.t[:, :], in_=a.t[:, :]), reads=[a.b], writes=[ahb.b])
            fw.op("dve", lambda e: e.tensor_copy(out=ah.t[:, :], in_=ahb.t[:, :]), reads=[ahb.b], writes=[ah.b])
            fw.op("dve", lambda e: e.tensor_tensor(out=al.t[:, :], in0=a.t[:, :], in1=ah.t[:, :], op=ALU.subtract),
                  reads=[a.b, ah.b], writes=[al.b])
            fw.op("dve", lambda e: e.tensor_copy(out=a2.t[:, 0:32], in_=ah.t[:, :]), reads=[ah.b], writes=[a2.b])
            fw.op("dve", lambda e: e.tensor_copy(out=a2.t[:, 32:64], in_=al.t[:, :]), reads=[al.b], writes=[a2.b], waw=False)
            for q, lhs in enumerate((MR, ML, self.ones_bf.t[:, :])):
                for hl in range(2):
                    fw.op("pe", lambda e, q=q, lhs=lhs, hl=hl: e.matmul(
                        pE.t[:, q * 32:(q + 1) * 32], lhsT=lhs, rhs=a2.t[:, hl * 32:(hl + 1) * 32],
                        start=(hl == 0), stop=(hl == 1)), reads=[a2.b, self.cmb.b, self.ones_bf.b], writes=[pE.b],
                          ticket=(q == 2 and hl == 1), waw=(q == 0 and hl == 0))
            fw.op("act", lambda e: e.activation(out=E.t[:, :], in_=pE.t[:, 0:96], func=AF.Exp), reads=[pE.b], writes=[E.b])
            fw.op("act", lambda e: e.activation(out=e2.t[:, :], in_=dtd, func=AF.Ln), reads=[dt_.b], writes=[e2.b])
            for g in range(4):
                fw.op("pe", lambda e, g=g: e.matmul(pcb.t[:, g * 128:(g + 1) * 128], lhsT=xbc.t[:, 16 + g, :],
                                                    rhs=xbc.t[:, 20 + g, :], start=True, stop=True),
                      reads=[xbc.b], writes=[pcb.b], ticket=(g == 3), waw=(g == 0))
            for g in range(4):
                fw.op("dve", lambda e, g=g: e.tensor_tensor(out=cbm.t[:, g, :], in0=pcb.t[:, g * 128:(g + 1) * 128],
                                                            in1=MRf, op=ALU.mult),
                      reads=[pcb.b, self.cm.b], writes=[cbm.b], waw=(g == 0))
            if d == 1:
                sz = sz_l[ci % 2]
                fw.op("act", lambda e: e.activation(out=sz.t[:, :], in_=z.t[:, :], func=AF.Silu), reads=[z.b], writes=[sz.b])

        def main(ci):
            xbc, dt_, r0, c0, xtok, a, ahb, ah, al, a2, E, e2, cbm, xdt, xwa = binds(ci)
            dtd = dt_.t[:, 32 * d:32 * d + 32]
            if d == 1:
                yf, z, sz = yfs[ci % 2], zs[ci % 2], sz_l[ci % 2]
            items = [(g, hq) for g in range(4) for hq in range(2)]
            bufs = {}

            def stage1(idx):
                g, hq = items[idx]
                hi = hic[0]
                hic[0] += 1
                ps_, dc, Rh, Rl, w4 = pseg[hi % 2], dec[hi % 2], Rhs[hi % 2], Rls[hi % 2], wT4[hi % 2]
                bufs[idx] = (ps_, dc, w4)
                h0 = g * 8 + hq * 4
                fw.op("dve", lambda e: e.tensor_tensor(
                    out=Rh.t[:, :].rearrange("p (r c) -> p r c", c=128),
                    in0=MR.unsqueeze(1).broadcast_to([128, 4, 128]),
                    in1=ah.t[:, h0:h0 + 4].unsqueeze(2).broadcast_to([128, 4, 128]), op=ALU.mult),
                      reads=[ah.b, self.cmb.b], writes=[Rh.b])
                fw.op("dve", lambda e: e.tensor_tensor(
                    out=Rl.t[:, :].rearrange("p (r c) -> p r c", c=128),
                    in0=MR.unsqueeze(1).broadcast_to([128, 4, 128]),
                    in1=al.t[:, h0:h0 + 4].unsqueeze(2).broadcast_to([128, 4, 128]), op=ALU.mult),
                      reads=[al.b, self.cmb.b], writes=[Rl.b])
                fw.op("pe", lambda e: e.matmul(ps_.t[:, :], lhsT=ML, rhs=Rh.t[:, :], start=True, stop=False),
                      reads=[Rh.b, self.cmb.b], writes=[ps_.b], ticket=False)
                fw.op("pe", lambda e: e.matmul(ps_.t[:, :], lhsT=ML, rhs=Rl.t[:, :], start=False, stop=True),
                      reads=[Rl.b, Rh.b, self.cmb.b], writes=[ps_.b])
                for r4 in range(4):
                    fw.op("act", lambda e, r4=r4: e.activation(
                        out=dc.t[:, r4 * 128:(r4 + 1) * 128], in_=ps_.t[:, r4 * 128:(r4 + 1) * 128], func=AF.Exp,
                        bias=e2.t[:, h0 + r4:h0 + r4 + 1], scale=1.0), reads=[ps_.b, e2.b], writes=[dc.b], waw=(r4 == 0))

            def stage2(idx):
                g, hq = items[idx]
                ps_, dc, w4 = bufs[idx]
                pY = pYl[g % 2]
                fw.op("dve", lambda e: e.tensor_tensor(
                    out=w4.t[:, :].rearrange("p (r c) -> p r c", c=128), in0=dc.t[:, :].rearrange("p (r c) -> p r c", c=128),
                    in1=cbm.t[:, g, :].unsqueeze(1).broadcast_to([128, 4, 128]), op=ALU.mult),
                      reads=[dc.b, cbm.b], writes=[w4.b])
                ecol = 127 if d == 0 else 0
                for r4 in range(4):
                    r = hq * 4 + r4
                    h = g * 8 + r
                    xh = xtok.t[:, h * 64:(h + 1) * 64]
                    fw.op("act", lambda e, r4=r4, h=h, xh=xh: e.activation(
                        out=xwa.t[:, h * 64:(h + 1) * 64], in_=xh, func=AF.Copy,
                        scale=dc.t[:, r4 * 128 + ecol:r4 * 128 + ecol + 1]), reads=[xtok.b, dc.b], writes=[xwa.b], waw=(h == 0))
                    if d == 0:
                        fw.op("pe", lambda e, r=r, r4=r4, xh=xh: e.matmul(
                            pY.t[:, r * 64:(r + 1) * 64], lhsT=w4.t[:, r4 * 128:(r4 + 1) * 128], rhs=xh,
                            start=True, stop=False), reads=[w4.b, xtok.b], writes=[pY.b], ticket=False, waw=(r == 0))
                        fw.op("pe", lambda e, r=r, h=h, xh=xh: e.matmul(
                            pY.t[:, r * 64:(r + 1) * 64], lhsT=DI.t[:, h * 128:(h + 1) * 128], rhs=xh,
                            start=False, stop=True), reads=[w4.b, xtok.b, DI.b], writes=[pY.b], ticket=(r4 == 3), waw=False)
                    else:
                        fw.op("pe", lambda e, r=r, r4=r4, xh=xh: e.matmul(
                            pY.t[:, r * 64:(r + 1) * 64], lhsT=w4.t[:, r4 * 128:(r4 + 1) * 128], rhs=xh,
                            start=True, stop=True), reads=[w4.b, xtok.b], writes=[pY.b], ticket=(r4 == 3), waw=(r == 0))

            def epilogue(g):
                pY = pYl[g % 2]
                pYs = pG
                pdS = pG
                fw.op("pe", lambda e: e.matmul(pYs.t[:, :], lhsT=xbc.t[:, 20 + g, :], rhs=Sb.t[:, g * 512:(g + 1) * 512],
                                               start=True, stop=True), reads=[xbc.b, Sb.b], writes=[pYs.b])
                fw.op("dve", lambda e: e.tensor_tensor(
                    out=tmpy.t[:, :].rearrange("p (r c) -> p r c", c=64), in0=pYs.t[:, :].rearrange("p (r c) -> p r c", c=64),
                    in1=bc3(E.t[:, g * 8:(g + 1) * 8]), op=ALU.mult), reads=[pYs.b, E.b], writes=[tmpy.b])
                yo_ = yo[ci % 2]
                gs_ = slice(g * 512, (g + 1) * 512)
                fw.op("dve", lambda e: e.tensor_tensor(out=yo_.t[:, gs_], in0=pY.t[:, :], in1=tmpy.t[:, :], op=ALU.add),
                      reads=[pY.b, tmpy.b], writes=[yo_.b], waw=(g == 0))
                if d == 1:
                    fw.op("dve", lambda e: e.tensor_tensor(out=yo_.t[:, gs_], in0=yo_.t[:, gs_], in1=yf.t[:, gs_], op=ALU.add),
                          reads=[yo_.b, yf.b], writes=[yo_.b])
                    fw.op("dve", lambda e: e.tensor_tensor(out=yo_.t[:, gs_], in0=yo_.t[:, gs_], in1=sz.t[:, gs_], op=ALU.mult),
                          reads=[yo_.b, sz.b], writes=[yo_.b])
                fw.op("pe", lambda e: e.matmul(
                    pdS.t[:, :], lhsT=xtok.t[:, 2048 + g * 128:2048 + (g + 1) * 128], rhs=xwa.t[:, gs_], start=True, stop=True),
                      reads=[xtok.b, xwa.b], writes=[pdS.b])
                fw.op("dve", lambda e: e.tensor_tensor(
                    out=S.t[:, gs_].rearrange("p (r c) -> p r c", c=64), in0=S.t[:, gs_].rearrange("p (r c) -> p r c", c=64),
                    in1=bc3(E.t[:, 64 + g * 8:64 + (g + 1) * 8]), op=ALU.mult), reads=[S.b, E.b], writes=[S.b])
                fw.op("dve", lambda e: e.tensor_tensor(out=S.t[:, gs_], in0=S.t[:, gs_], in1=pdS.t[:, :], op=ALU.add),
                      reads=[S.b, pdS.b], writes=[S.b])
                fw.op("act", lambda e: e.activation(out=Sb.t[:, gs_], in_=S.t[:, gs_], func=AF.Copy),
                      reads=[S.b], writes=[Sb.b])

            stage1(0)
            for idx in range(8):
                if idx + 1 < 8:
                    stage1(idx + 1)
                stage2(idx)
                if items[idx][1] == 1:
                    epilogue(items[idx][0])
            yo_ = yo[ci % 2]
            if d == 0:
                fw.dma(self.YF[r0:r0 + 128, :], yo_.t[:, :], yo_.b, reads=[yo_.b], writes=[self.db(self.YF)], waw=False,
                       stream="pool")
            else:
                fw.op("pool", lambda e: e.memset(ssq.t[:, :], 0.0), writes=[ssq.b])
                fw.op("act", lambda e: e.activation(out=sqj.t[:, :], in_=yo_.t[:, :], func=AF.Square, accum_out=ssq.t[:, :]),
                      reads=[yo_.b], writes=[sqj.b, ssq.b])
                fw.op("act", lambda e: e.activation(out=rt1.t[:, :], in_=ssq.t[:, :], func=AF.Ln, scale=1.0 / SSM_DI, bias=EPS),
                      reads=[ssq.b], writes=[rt1.b])
                fw.op("act", lambda e: e.activation(out=rs1.t[:, :], in_=rt1.t[:, :], func=AF.Exp, scale=-0.5),
                      reads=[rt1.b], writes=[rs1.b])
                fw.op("dve", lambda e: e.scalar_tensor_tensor(out=ygn.t[:, :], in0=yo_.t[:, :], scalar=rs1.t[:, 0:1],
                                                              in1=ng.t[:, :], op0=ALU.mult, op1=ALU.mult),
                      reads=[yo_.b, rs1.b, ng.b], writes=[ygn.b])
                yt = ygT[ci % 2]
                for q in range(4):
                    for j in range(4):
                        c = q * 4 + j
                        fw.op("pe", lambda e, c=c, j=j: e.transpose(pT.t[:, j * 128:(j + 1) * 128],
                                                                    ygn.t[:, c * 128:(c + 1) * 128], self.ident_bf()),
                              reads=[ygn.b, self.cmb.b], writes=[pT.b], ticket=(j == 3), waw=(j == 0))
                    if q % 2 == 0:
                        fw.op("act", lambda e, q=q: e.activation(out=yt.t[:, q * 4:(q + 1) * 4, :],
                                                                 in_=pT.t[:, 0:512].rearrange("p (a b) -> p a b", b=128), func=AF.Copy),
                              reads=[pT.b], writes=[yt.b], waw=False)
                    else:
                        fw.op("dve", lambda e, q=q: e.tensor_copy(out=yt.t[:, q * 4:(q + 1) * 4, :],
                                                                  in_=pT.t[:, 0:512].rearrange("p (a b) -> p a b", b=128)),
                              reads=[pT.b], writes=[yt.b], waw=False)
                fw.dma(YGv[:, :, c0:c0 + 128], yt.t[:, :, :], yt.b, reads=[yt.b], writes=[self.db(self.YGT)], waw=False,
                       stream="pool")
        prologue(0)
        for ci in range(len(order)):
            if ci + 1 < len(order):
                prologue(ci + 1)
            main(ci)
        self.end()


def build_program(T, voff, roff, nv, nr, n_layers=DEPTH, debug=False):
    p = Prog(T, voff, roff, nv, nr, n_layers=n_layers, debug=debug)
    p.setup()
    p.mod_phase()
    Xa, Xb = p.XA, p.XB
    for l in range(n_layers):
        if l % 2 == 0:
            i = l // 2
            p.attn_inproj(l, Xa)
            p.attn_core(l)
            p.proj_res_norm(l, Xa, Xb, p.OT, KD, lambda k, i=i: p.i_awo[i, k * 128:(k + 1) * 128, :])
        else:
            i = l // 2
            p.ssm_inproj(l, Xa)
            p.ssm_scan(l, 0)
            p.ssm_scan(l, 1)
            p.proj_res_norm(l, Xa, Xb, p.YGT, SSM_DI // 128, lambda k, i=i: p.i_swo[i, k * 128:(k + 1) * 128, :])
        p.ffn_in(l)
        p.ffn_out(l, Xb, Xa)
    p.final_norm(Xa)
    p.pes.close()
    p.fw.close()
    return p


_CACHE = {}


def run(inputs, T=8192, n_layers=DEPTH, debug=False, cores=NCORES):
    cfg, voff, roff, shared = prep_shared(inputs, T)
    key = (T, n_layers, debug)
    if key not in _CACHE:
        _CACHE[key] = build_program(T, voff, roff, shared["vecs"].shape[1], shared["rowv"].shape[1],
                                    n_layers=n_layers, debug=debug)
    p = _CACHE[key]
    in_maps = []
    for b in range(cores):
        m = dict(shared)
        m.update(prep_core(inputs, b, T))
        in_maps.append(m)
    res = run_bass_kernel_spmd(p.nc, in_maps, core_ids=list(range(cores)))
    return p, res


def kernel(**inputs):
    p, res = run(inputs)
    out = np.stack([np.ascontiguousarray(r["outT"].T) for r in res.results], axis=0)
    return out.astype(np.float32)
```
